# Optimizing a Trainium2 kernel written in Bass

```python
import math
import jax, jax.numpy as jnp
from jax import lax
import numpy as np

D_MODEL = 1024
BATCH = 8
SEQ = 4096
DEPTH = 1

NSA_HEADS = 8
NSA_KV_GROUPS = 2
HEAD_DIM = 64
Q_PER_KV = NSA_HEADS // NSA_KV_GROUPS
NSA_WIDTH = NSA_HEADS * HEAD_DIM
KV_WIDTH = NSA_KV_GROUPS * HEAD_DIM
CMP_BLOCK = 32
CMP_STRIDE = 16
CMP_HIDDEN = 256
SLC_BLOCK = 64
SLC_TOPK = 16
SLC_QBLOCK = 64
WINDOW = 512
WIN_BLOCK = 128
FORCE_BONUS = 1000.0
SSM_GROUP = 16
SSM_GROUPS = 32
SSM_STATE = 64
SSM_WIDTH = SSM_GROUPS * SSM_GROUP
D_FF = 2816
RMS_EPS = 1e-6
IN_WIDTH = NSA_WIDTH + 6 * KV_WIDTH + 3 * NSA_HEADS + SSM_WIDTH + 2 * D_MODEL

kernel_name = "nsa_s5_gated_macaron_layer"


def _rmsnorm(x, g):
    xf = x.astype(jnp.float32)
    return xf * lax.rsqrt(jnp.mean(xf * xf, axis=-1, keepdims=True) + RMS_EPS) * g.astype(jnp.float32)


def _swiglu(x, g, w_gate, w_up, w_down):
    h = _rmsnorm(x, g)
    return (jax.nn.silu(h @ w_gate) * (h @ w_up)) @ w_down


def _masked_softmax(s, mask):
    s = jnp.where(mask, s.astype(jnp.float32), -1e30)
    m = jnp.max(s, axis=-1, keepdims=True)
    e = jnp.where(mask, jnp.exp(s - m), 0.0)
    den = jnp.sum(e, axis=-1, keepdims=True)
    return e / jnp.where(den > 0, den, 1.0)


def _compress(kv, pos, w1, w2):
    b, s, g, d = kv.shape
    chunks = kv.reshape(b, s // CMP_STRIDE, CMP_STRIDE, g, d)
    blocks = jnp.concatenate([chunks[:, :-1], chunks[:, 1:]], axis=2) + pos[:, None, :]
    flat = blocks.transpose(0, 1, 3, 2, 4).reshape(b, s // CMP_STRIDE - 1, g, CMP_BLOCK * d)
    return jax.nn.gelu(flat @ w1) @ w2


def _selected_attention(q, k, v, sel, scale):
    b, s, g, r, d = q.shape
    n = sel.shape[-1]
    ns = s // SLC_BLOCK
    nq = s // SLC_QBLOCK
    kb = k.reshape(b, ns, SLC_BLOCK, g, d).transpose(0, 3, 1, 2, 4)
    vb = v.reshape(b, ns, SLC_BLOCK, g, d).transpose(0, 3, 1, 2, 4)
    qc = q.reshape(b, nq, SLC_QBLOCK, g, r, d).transpose(1, 0, 2, 3, 4, 5)
    ic = sel.reshape(b, g, nq, SLC_QBLOCK, n).transpose(2, 0, 1, 3, 4)
    tc = jnp.arange(s).reshape(nq, SLC_QBLOCK)
    bi = jnp.arange(b)[:, None, None, None]
    gi = jnp.arange(g)[None, :, None, None]
    offs = jnp.arange(SLC_BLOCK)

    def step(args):
        qb, ib, tb = args
        kg = kb[bi, gi, ib]
        vg = vb[bi, gi, ib]
        sc = jnp.einsum('bqgrd,bgqnkd->bgrqnk', qb, kg) * scale
        kpos = ib[..., None] * SLC_BLOCK + offs
        mask = (kpos <= tb[:, None, None])[:, :, None]
        p = _masked_softmax(sc.reshape(b, g, r, SLC_QBLOCK, n * SLC_BLOCK),
                            mask.reshape(b, g, 1, SLC_QBLOCK, n * SLC_BLOCK))
        return jnp.einsum('bgrqk,bgqkd->bqgrd', p,
                          vg.reshape(b, g, SLC_QBLOCK, n * SLC_BLOCK, d).astype(jnp.float32))

    o = lax.map(step, (qc, ic, tc))
    return o.transpose(1, 0, 2, 3, 4, 5).reshape(b, s, g, r, d)


def _window_attention(q, k, v, scale):
    b, s, g, r, d = q.shape
    nb = s // WIN_BLOCK
    nprev = WINDOW // WIN_BLOCK
    pad = ((0, 0), (WINDOW, 0), (0, 0), (0, 0))
    kp = jnp.pad(k, pad).reshape(b, nb + nprev, WIN_BLOCK, g, d)
    vp = jnp.pad(v, pad).reshape(b, nb + nprev, WIN_BLOCK, g, d)
    band_k = jnp.concatenate([kp[:, i:i + nb] for i in range(nprev + 1)], axis=2)
    band_v = jnp.concatenate([vp[:, i:i + nb] for i in range(nprev + 1)], axis=2)
    qw = q.reshape(b, nb, WIN_BLOCK, g, r, d)
    sc = jnp.einsum('bnqgrd,bnkgd->bgrnqk', qw, band_k) * scale
    blk = jnp.arange(nb)[:, None] * WIN_BLOCK
    qpos = blk + jnp.arange(WIN_BLOCK)[None, :]
    kpos = blk - WINDOW + jnp.arange((nprev + 1) * WIN_BLOCK)[None, :]
    diff = qpos[:, :, None] - kpos[:, None, :]
    mask = (diff >= 0) & (diff < WINDOW) & (kpos[:, None, :] >= 0)
    p = _masked_softmax(sc, mask)
    o = jnp.einsum('bgrnqk,bnkgd->bnqgrd', p, band_v.astype(jnp.float32))
    return o.reshape(b, s, g, r, d)


def _nsa(q, k_cmp, v_cmp, k_slc, v_slc, k_win, v_win, gate_logits, q_norm,
         k_norm_cmp, k_norm_slc, k_norm_win, cmp_pos_k, cmp_pos_v,
         cmp_k_w1, cmp_k_w2, cmp_v_w1, cmp_v_w2):
    b, s, g, r, d = q.shape
    scale = HEAD_DIM ** -0.5
    t = jnp.arange(s)
    q = _rmsnorm(q, q_norm)
    kc = _rmsnorm(_compress(k_cmp, cmp_pos_k, cmp_k_w1, cmp_k_w2), k_norm_cmp)
    vc = _compress(v_cmp, cmp_pos_v, cmp_v_w1, cmp_v_w2).astype(jnp.float32)
    nc = kc.shape[1]
    ci = jnp.arange(nc)
    sc = jnp.einsum('bsgrd,bcgd->bgrsc', q, kc) * scale
    cmp_mask = (ci * CMP_STRIDE + CMP_BLOCK - 1)[None, :] <= t[:, None]
    p_cmp = _masked_softmax(sc, cmp_mask)
    o_cmp = jnp.einsum('bgrsc,bcgd->bsgrd', p_cmp, vc)
    ns = s // SLC_BLOCK
    j = jnp.arange(ns)
    overlap = ((ci[:, None] * CMP_STRIDE < (j[None, :] + 1) * SLC_BLOCK) &
               (ci[:, None] * CMP_STRIDE + CMP_BLOCK > j[None, :] * SLC_BLOCK)).astype(jnp.float32)
    imp = jnp.einsum('bgrsc,cj->bgsj', p_cmp, overlap)
    qblk = t // SLC_BLOCK
    force = (j[None, :] == 0) | (j[None, :] == qblk[:, None]) | (j[None, :] == qblk[:, None] - 1)
    score = jnp.where(j[None, :] <= qblk[:, None], imp + FORCE_BONUS * force.astype(jnp.float32), -1e30)
    _, sel = lax.top_k(score, min(SLC_TOPK, ns))
    o_slc = _selected_attention(q, _rmsnorm(k_slc, k_norm_slc), v_slc, sel, scale)
    o_win = _window_attention(q, _rmsnorm(k_win, k_norm_win), v_win, scale)
    gates = jax.nn.sigmoid(gate_logits.astype(jnp.float32)).reshape(b, s, 3, g, r, 1)
    o = gates[:, :, 0] * o_cmp + gates[:, :, 1] * o_slc + gates[:, :, 2] * o_win
    return o.reshape(b, s, NSA_WIDTH)


def _complex_affine_combine(e1, e2):
    a1r, a1i, b1r, b1i = e1
    a2r, a2i, b2r, b2i = e2
    return (a2r * a1r - a2i * a1i,
            a2r * a1i + a2i * a1r,
            a2r * b1r - a2i * b1i + b2r,
            a2r * b1i + a2i * b1r + b2i)


def _s5_glu(u, lambda_re, lambda_im, log_step, b_re, b_im, c_re, c_im, d_skip, glu_w):
    b, s, _ = u.shape
    u = u.astype(jnp.float32).reshape(b, s, SSM_GROUPS, SSM_GROUP)
    lr = lambda_re.astype(jnp.float32)
    li = lambda_im.astype(jnp.float32)
    step = jnp.exp(log_step.astype(jnp.float32))[:, None]
    mag = jnp.exp(lr * step)
    ar = mag * jnp.cos(li * step)
    ai = mag * jnp.sin(li * step)
    den = lr * lr + li * li
    cr = ((ar - 1.0) * lr + ai * li) / den
    cim = (ai * lr - (ar - 1.0) * li) / den
    br = b_re.astype(jnp.float32)
    bim = b_im.astype(jnp.float32)
    bbr = cr[..., None] * br - cim[..., None] * bim
    bbi = cr[..., None] * bim + cim[..., None] * br
    ur = jnp.einsum('bsgc,gpc->bsgp', u, bbr)
    ui = jnp.einsum('bsgc,gpc->bsgp', u, bbi)
    a_r = jnp.broadcast_to(ar[None, None], (1, s, SSM_GROUPS, SSM_STATE))
    a_i = jnp.broadcast_to(ai[None, None], (1, s, SSM_GROUPS, SSM_STATE))
    _, _, xr, xi = lax.associative_scan(_complex_affine_combine, (a_r, a_i, ur, ui), axis=1)
    y = (jnp.einsum('bsgp,gcp->bsgc', xr, c_re.astype(jnp.float32))
         - jnp.einsum('bsgp,gcp->bsgc', xi, c_im.astype(jnp.float32))
         + d_skip.astype(jnp.float32) * u)
    hg = jax.nn.gelu(y.reshape(b, s, SSM_WIDTH)) @ glu_w
    val, gate = jnp.split(hg, 2, axis=-1)
    return val * jax.nn.sigmoid(gate)


def _mixing(x, mix_norm, w_in, q_norm, k_norm_cmp, k_norm_slc, k_norm_win, cmp_pos_k, cmp_pos_v,
            cmp_k_w1, cmp_k_w2, cmp_v_w1, cmp_v_w2, w_nsa_proj, ssm_lambda_re, ssm_lambda_im,
            ssm_log_step, ssm_b_re, ssm_b_im, ssm_c_re, ssm_c_im, ssm_d, ssm_glu_w, w_out):
    b, s, _ = x.shape
    h = _rmsnorm(x, mix_norm)
    proj = h @ w_in
    sizes = [NSA_WIDTH] + [KV_WIDTH] * 6 + [3 * NSA_HEADS, SSM_WIDTH, D_MODEL, D_MODEL]
    cuts = [sum(sizes[:i + 1]) for i in range(len(sizes) - 1)]
    q, kc, vc, ks, vs, kw, vw, nsa_gate, ssm_u, gate_nsa, gate_ssm = jnp.split(proj, cuts, axis=-1)
    kvr = lambda z: z.reshape(b, s, NSA_KV_GROUPS, HEAD_DIM)
    o_nsa = _nsa(q.reshape(b, s, NSA_KV_GROUPS, Q_PER_KV, HEAD_DIM), kvr(kc), kvr(vc), kvr(ks), kvr(vs),
                 kvr(kw), kvr(vw), nsa_gate, q_norm, k_norm_cmp, k_norm_slc, k_norm_win,
                 cmp_pos_k, cmp_pos_v, cmp_k_w1, cmp_k_w2, cmp_v_w1, cmp_v_w2)
    y_nsa = o_nsa @ w_nsa_proj
    y_ssm = _s5_glu(ssm_u, ssm_lambda_re, ssm_lambda_im, ssm_log_step, ssm_b_re, ssm_b_im,
                    ssm_c_re, ssm_c_im, ssm_d, ssm_glu_w)
    merged = jax.nn.sigmoid(gate_nsa) * y_nsa + jax.nn.sigmoid(gate_ssm) * y_ssm
    return merged @ w_out


def setup_inputs(seed: int = 0) -> dict:
    key = jax.random.key(seed)
    ks = jax.random.split(key, 32)
    L = DEPTH
    nrm = lambda k, shape, sc: jax.random.normal(k, shape, jnp.float32) * sc
    gain = lambda k, n: 1.0 + 0.01 * jax.random.normal(k, (L, n), jnp.float32)
    lam_n = jnp.arange(SSM_STATE, dtype=jnp.float32)
    return {
        'x': nrm(ks[0], (BATCH, SEQ, D_MODEL), 1.0),
        'ffn1_norm': gain(ks[1], D_MODEL),
        'ffn1_w_gate': nrm(ks[2], (L, D_MODEL, D_FF), D_MODEL ** -0.5),
        'ffn1_w_up': nrm(ks[3], (L, D_MODEL, D_FF), D_MODEL ** -0.5),
        'ffn1_w_down': nrm(ks[4], (L, D_FF, D_MODEL), D_FF ** -0.5),
        'mix_norm': gain(ks[5], D_MODEL),
        'w_in': nrm(ks[6], (L, D_MODEL, IN_WIDTH), D_MODEL ** -0.5),
        'q_norm': gain(ks[7], HEAD_DIM),
        'k_norm_cmp': gain(ks[8], HEAD_DIM),
        'k_norm_slc': gain(ks[9], HEAD_DIM),
        'k_norm_win': gain(ks[10], HEAD_DIM),
        'cmp_pos_k': nrm(ks[11], (L, CMP_BLOCK, HEAD_DIM), 0.1),
        'cmp_pos_v': nrm(ks[12], (L, CMP_BLOCK, HEAD_DIM), 0.1),
        'cmp_k_w1': nrm(ks[13], (L, CMP_BLOCK * HEAD_DIM, CMP_HIDDEN), (CMP_BLOCK * HEAD_DIM) ** -0.5),
        'cmp_k_w2': nrm(ks[14], (L, CMP_HIDDEN, HEAD_DIM), CMP_HIDDEN ** -0.5),
        'cmp_v_w1': nrm(ks[15], (L, CMP_BLOCK * HEAD_DIM, CMP_HIDDEN), (CMP_BLOCK * HEAD_DIM) ** -0.5),
        'cmp_v_w2': nrm(ks[16], (L, CMP_HIDDEN, HEAD_DIM), CMP_HIDDEN ** -0.5),
        'w_nsa_proj': nrm(ks[17], (L, NSA_WIDTH, D_MODEL), NSA_WIDTH ** -0.5),
        'ssm_lambda_re': -0.5 + nrm(ks[18], (L, SSM_GROUPS, SSM_STATE), 0.01),
        'ssm_lambda_im': math.pi * lam_n + nrm(ks[19], (L, SSM_GROUPS, SSM_STATE), 0.01),
        'ssm_log_step': jax.random.uniform(ks[20], (L, SSM_GROUPS), jnp.float32,
                                           minval=math.log(1e-3), maxval=math.log(1e-1)),
        'ssm_b_re': nrm(ks[21], (L, SSM_GROUPS, SSM_STATE, SSM_GROUP), (2 * SSM_GROUP) ** -0.5),
        'ssm_b_im': nrm(ks[22], (L, SSM_GROUPS, SSM_STATE, SSM_GROUP), (2 * SSM_GROUP) ** -0.5),
        'ssm_c_re': nrm(ks[23], (L, SSM_GROUPS, SSM_GROUP, SSM_STATE), (2 * SSM_STATE) ** -0.5),
        'ssm_c_im': nrm(ks[24], (L, SSM_GROUPS, SSM_GROUP, SSM_STATE), (2 * SSM_STATE) ** -0.5),
        'ssm_d': nrm(ks[25], (L, SSM_GROUPS, SSM_GROUP), 1.0),
        'ssm_glu_w': nrm(ks[26], (L, SSM_WIDTH, 2 * D_MODEL), SSM_WIDTH ** -0.5),
        'w_out': nrm(ks[27], (L, D_MODEL, D_MODEL), D_MODEL ** -0.5),
        'ffn2_norm': gain(ks[28], D_MODEL),
        'ffn2_w_gate': nrm(ks[29], (L, D_MODEL, D_FF), D_MODEL ** -0.5),
        'ffn2_w_up': nrm(ks[30], (L, D_MODEL, D_FF), D_MODEL ** -0.5),
        'ffn2_w_down': nrm(ks[31], (L, D_FF, D_MODEL), D_FF ** -0.5),
    }


def reference(x, ffn1_norm, ffn1_w_gate, ffn1_w_up, ffn1_w_down, mix_norm, w_in, q_norm,
              k_norm_cmp, k_norm_slc, k_norm_win, cmp_pos_k, cmp_pos_v, cmp_k_w1, cmp_k_w2,
              cmp_v_w1, cmp_v_w2, w_nsa_proj, ssm_lambda_re, ssm_lambda_im, ssm_log_step,
              ssm_b_re, ssm_b_im, ssm_c_re, ssm_c_im, ssm_d, ssm_glu_w, w_out,
              ffn2_norm, ffn2_w_gate, ffn2_w_up, ffn2_w_down):
    in_dtype = x.dtype
    h = x.astype(jnp.float32)
    for l in range(DEPTH):
        h = h + 0.5 * _swiglu(h, ffn1_norm[l], ffn1_w_gate[l], ffn1_w_up[l], ffn1_w_down[l])
        h = h + _mixing(h, mix_norm[l], w_in[l], q_norm[l], k_norm_cmp[l], k_norm_slc[l],
                        k_norm_win[l], cmp_pos_k[l], cmp_pos_v[l], cmp_k_w1[l], cmp_k_w2[l],
                        cmp_v_w1[l], cmp_v_w2[l], w_nsa_proj[l], ssm_lambda_re[l],
                        ssm_lambda_im[l], ssm_log_step[l], ssm_b_re[l], ssm_b_im[l],
                        ssm_c_re[l], ssm_c_im[l], ssm_d[l], ssm_glu_w[l], w_out[l])
        h = h + 0.5 * _swiglu(h, ffn2_norm[l], ffn2_w_gate[l], ffn2_w_up[l], ffn2_w_down[l])
    return h.astype(in_dtype)
```

```python
import math
import os
from contextlib import ExitStack

import numpy as np
import concourse.bass as bass
import concourse.mybir as mybir
from concourse.bass_utils import run_bass_kernel_spmd

F32 = mybir.dt.float32
BF16 = mybir.dt.bfloat16
AF = mybir.ActivationFunctionType
ALU = mybir.AluOpType

S = 4096
D = 1024
DFF = 2816
NCORES = 8
EPS = 1e-6
DBG = set(os.environ.get("DBGSKIP", "").split(","))


class Tk:
    __slots__ = ("name", "w", "rs", "dsem", "dcnt")

    def __init__(self, name):
        self.name = name
        self.w = []
        self.rs = []
        self.dsem = None
        self.dcnt = 0


class Prog:
    ENG = ("pe", "act", "dve", "pool", "sp")

    def __init__(self, nc, es):
        self.nc = nc
        self.es = es
        self.e = {"pe": nc.tensor, "act": nc.scalar, "dve": nc.vector,
                  "pool": nc.gpsimd, "sp": nc.sync}
        self.nsem = 0
        self.sem = {}
        self.cnt = {}
        for k in self.ENG:
            self.sem[k] = self.new_sem("e_" + k)
            self.cnt[k] = 0
        self.seen = {k: {} for k in self.ENG}
        self.dma_sems = []
        self.n_inst = 0

    def new_sem(self, name):
        self.nsem += 1
        return self.es.enter_context(self.nc.semaphore(name + "_%d" % self.nsem))

    def _wait(self, eng, sem, val):
        d = self.seen[eng]
        key = id(sem)
        if d.get(key, (None, 0))[1] >= val:
            return
        d[key] = (sem, val)
        self.e[eng].wait_ge(sem, val)

    def _deps(self, eng, reads, writes, same_engine_sync=True):
        need = {}

        def add(ev):
            sem, val, src = ev
            if src == eng and not same_engine_sync:
                return
            if isinstance(src, tuple):
                val = src[0].dsem[src[1]][1]
            k = id(sem)
            if k not in need or need[k][1] < val:
                need[k] = (sem, val)

        for t in reads:
            for ev in t.w:
                add(ev)
        for t in writes:
            for ev in t.w:
                add(ev)
            for ev in t.rs:
                add(ev)
        for sem, val in need.values():
            self._wait(eng, sem, val)

    def _roll(self, eng):
        if self.cnt[eng] >= 8000:
            self.sem[eng] = self.new_sem("e_" + eng)
            self.cnt[eng] = 0

    def op(self, eng, fn, reads=(), writes=(), sync_same=True):
        self._roll(eng)
        self._deps(eng, reads, writes, same_engine_sync=sync_same)
        ins = fn(self.e[eng])
        self.cnt[eng] += 1
        ev = (self.sem[eng], self.cnt[eng], eng)
        ins.then_inc(self.sem[eng], 1)
        self.n_inst += 1
        for t in reads:
            t.rs.append(ev)
            if len(t.rs) > 24:
                t.rs = self._compact(t.rs)
        for t in writes:
            t.w = [ev]
            t.rs = []
        return ins

    @staticmethod
    def _compact(evs):
        best = {}
        for sem, val, src in evs:
            k = id(sem)
            if k not in best or best[k][1] < val:
                best[k] = (sem, val, src)
        return list(best.values())

    def dma(self, q, out_ap, in_ap, reads=(), writes=(), chan=None):
        if chan is None:
            chan = writes[0] if writes else reads[0]
        if chan.dsem is None:
            chan.dsem = {}
        if q not in chan.dsem:
            chan.dsem[q] = [self.new_sem("d_" + chan.name + "_" + q), 0]
            self.dma_sems.append((chan, q))
        ent = chan.dsem[q]
        self._deps(q, reads, writes)
        ins = self.e[q].dma_start(out=out_ap, in_=in_ap)
        ent[1] += 16
        ins.then_inc(ent[0], 16)
        ev = (ent[0], ent[1], (chan, q))
        self.n_inst += 1
        for t in reads:
            t.rs.append(ev)
            if len(t.rs) > 24:
                t.rs = self._compact(t.rs)
        for t in writes:
            t.w = [e for e in t.w if isinstance(e[2], tuple) and e[0] is not ent[0]] + [ev]
            t.rs = []
        return ins

    def barrier(self):
        for x in self.ENG:
            for y in self.ENG:
                if y != x and self.cnt[y] > 0:
                    self._wait(x, self.sem[y], self.cnt[y])
            for t, q in self.dma_sems:
                ent = t.dsem[q]
                if ent[1] > 0:
                    self._wait(x, ent[0], ent[1])

    def final_wait(self, tks):
        for t in tks:
            for sem, val, _ in t.w:
                self._wait("sp", sem, val)


def build(stage="full", debug=False):
    nc = bass.Bass("TRN2", target_bir_lowering=False)
    es = ExitStack()
    P = Prog(nc, es)

    def dram_in(name, shape, dt=F32):
        return nc.dram_tensor(name, list(shape), dt, kind="ExternalInput").ap()

    def dram_out(name, shape, dt=F32):
        return nc.dram_tensor(name, list(shape), dt, kind="ExternalOutput").ap()

    def dram_tmp(name, shape, dt=F32):
        kind = "ExternalOutput" if debug else "Internal"
        return nc.dram_tensor(name, list(shape), dt, kind=kind).ap()

    def sb(name, shape, dt, stack=None):
        return (stack or es).enter_context(nc.sbuf_tensor("sb_" + name, list(shape), dt))

    xT = dram_in("xT", [D, S])
    outT = dram_out("outT", [D, S])
    din = {}
    for nm, shp in IN_SHAPES.items():
        din[nm] = dram_in(nm, shp)
    h1T = dram_tmp("h1T", [D, S])
    h2T = dram_tmp("h2T", [D, S])
    uT = dram_tmp("uT", [512, S], BF16)
    onsaT = dram_tmp("onsaT", [512, S], BF16)
    ygT = dram_tmp("ygT", [512, S], BF16)
    t_xT, t_outT, t_h1T, t_h2T, t_uT, t_onsaT, t_ygT = [Tk(n) for n in
        ("xT", "outT", "h1T", "h2T", "uT", "onsaT", "ygT")]

    ps = [es.enter_context(nc.psum_tensor("ps%d" % i, [128, 512], F32)) for i in range(8)]
    t_ps = [Tk("ps%d" % i) for i in range(8)]

    def MM(out, lhsT, rhs, st, sp, reads, writes):
        P.op("pe", lambda e: e.matmul(out, lhsT=lhsT, rhs=rhs, start=st, stop=sp),
             reads, writes, sync_same=False)

    def TR(out, in_, ident, reads, writes):
        P.op("pe", lambda e: e.transpose(out, in_, ident), reads, writes, sync_same=False)

    def TT2(eng, out, a, b, op, reads, writes):
        P.op(eng, lambda e: e.tensor_tensor(out=out, in0=a, in1=b, op=op), reads, writes)

    def TS(eng, out, a, s1, s2, op0, op1, reads, writes):
        if s2 is None:
            P.op(eng, lambda e: e.tensor_scalar(out=out, in0=a, scalar1=s1, scalar2=None, op0=op0),
                 reads, writes)
        else:
            P.op(eng, lambda e: e.tensor_scalar(out=out, in0=a, scalar1=s1, scalar2=s2, op0=op0, op1=op1),
                 reads, writes)

    def STT(out, a, sc, b, op0, op1, reads, writes):
        P.op("dve", lambda e: e.scalar_tensor_tensor(out=out, in0=a, scalar=sc, in1=b, op0=op0, op1=op1),
             reads, writes)

    def ACT(out, in_, func, reads, writes, bias=None, scale=None):
        kw = {}
        if bias is not None:
            kw["bias"] = bias
        if scale is not None:
            kw["scale"] = scale
        P.op("act", lambda e: e.activation(out=out, in_=in_, func=func, **kw), reads, writes)

    def CP(eng, out, in_, reads, writes):
        if eng == "act":
            P.op("act", lambda e: e.activation(out=out, in_=in_, func=AF.Identity), reads, writes)
        else:
            P.op(eng, lambda e: e.tensor_copy(out=out, in_=in_), reads, writes)

    def RECIP(out, in_, reads, writes):
        P.op("dve", lambda e: e.reciprocal(out=out, in_=in_), reads, writes)

    def MEMSET(eng, ap, val, writes):
        P.op(eng, lambda e: e.memset(ap, val), (), writes)

    ones_bf = sb("ones_bf", [128, 128], BF16)
    bd_bf = sb("bd_bf", [128, 128], BF16)
    id_bf = sb("id_bf", [128, 128], BF16)
    id_f = sb("id_f", [128, 128], F32)
    epsc = sb("epsc", [128, 1], F32)
    t_const = Tk("const")
    MEMSET("pool", ones_bf[:], 1.0, [t_const])
    MEMSET("pool", epsc[:], EPS, [t_const])
    P.dma("pool", bd_bf[:], din["c_bd"], writes=[t_const])
    P.dma("pool", id_bf[:], din["c_id"], writes=[t_const])
    P.dma("sp", id_f[:], din["c_id"], writes=[t_const])

    def norm_tile(xb, t_xb, gsb, t_g, xn, t_xn, sq, t_sq, rstd, t_rstd, TT, bank):
        pstat = ps[bank][:, 0:TT]
        for c in range(8):
            TT2("pool", sq[c % 2][:, 0:TT], xb[:, c, :], xb[:, c, :], ALU.mult, [t_xb], [t_sq[c % 2]])
            MM(pstat, ones_bf[:], sq[c % 2][:, 0:TT], c == 0, c == 7, [t_const, t_sq[c % 2]], [t_ps[bank]])
        ACT(rstd[:, 0:TT], pstat, AF.Sqrt, [t_ps[bank], t_const], [t_rstd], bias=epsc[:, 0:1], scale=1.0 / D)
        RECIP(rstd[:, 0:TT], rstd[:, 0:TT], [t_rstd], [t_rstd])
        for c in range(8):
            STT(xn[:, c, :], xb[:, c, :], gsb[:, c:c + 1], rstd[:, 0:TT], ALU.mult, ALU.mult,
                [t_xb, t_g, t_rstd], [t_xn])

    def ffn_phase(tag, src, t_src, dst, t_dst, g_dram, wg_d, wu_d, wd_d):
        TT = 256
        NT = S // TT
        with ExitStack() as fs:
            wg = sb(tag + "wg", [128, 8, DFF], BF16, fs)
            wu = sb(tag + "wu", [128, 8, DFF], BF16, fs)
            wd = sb(tag + "wd", [128, 22, D], BF16, fs)
            gsb = sb(tag + "g", [128, 8], F32, fs)
            xt = [sb(tag + "x%d" % i, [128, 8, TT], F32, fs) for i in range(2)]
            xn = sb(tag + "xn", [128, 8, TT], BF16, fs)
            sq = [sb(tag + "sq%d" % i, [128, TT], BF16, fs) for i in range(2)]
            rstd = sb(tag + "rstd", [128, TT], F32, fs)
            hh = sb(tag + "h", [128, 22, TT], BF16, fs)
            sg = [sb(tag + "sg%d" % i, [128, TT], F32, fs) for i in range(2)]
            ot = sb(tag + "ot", [128, 8, TT], F32, fs)
            t_wg = [Tk(tag + "wg%d" % k) for k in range(8)]
            t_wu = [Tk(tag + "wu%d" % k) for k in range(8)]
            t_wd = [Tk(tag + "wd%d" % k) for k in range(22)]
            t_g = Tk(tag + "g")
            t_xt = [Tk(tag + "xt%d" % i) for i in range(2)]
            t_xn = Tk(tag + "xn")
            t_sq = [Tk(tag + "sq%d" % i) for i in range(2)]
            t_rstd = Tk(tag + "rstd")
            t_hh = [Tk(tag + "hh%d" % i) for i in range(22)]
            t_sg = [Tk(tag + "sg%d" % i) for i in range(2)]
            t_ot = Tk(tag + "ot")
            src_v = src.rearrange("(c p) t -> p c t", p=128)
            dst_v = dst.rearrange("(c p) t -> p c t", p=128)

            def load_x(i):
                b = i % 2
                P.dma("sp", xt[b][:], src_v[:, :, i * TT:(i + 1) * TT], reads=[t_src], writes=[t_xt[b]])

            P.dma("sp", gsb[:], g_dram, writes=[t_g])
            load_x(0)
            wg_v = wg_d.rearrange("(k p) f -> p k f", p=128)
            wu_v = wu_d.rearrange("(k p) f -> p k f", p=128)
            wd_v = wd_d.rearrange("(k p) d -> p k d", p=128)
            ch_a, ch_b = Tk(tag + "chA"), Tk(tag + "chB")
            for k in range(8):
                P.dma("pool", wg[:, k, :], wg_v[:, k, :], writes=[t_wg[k]], chan=ch_a)
                P.dma("pool", wu[:, k, :], wu_v[:, k, :], writes=[t_wu[k]], chan=ch_a)
            for k in range(22):
                P.dma("pool", wd[:, k, :], wd_v[:, k, :], writes=[t_wd[k]], chan=ch_b)
            for i in range(NT):
                b = i % 2
                if i + 1 < NT:
                    load_x(i + 1)
                xb = xt[b]
                norm_tile(xb, t_xt[b], gsb, t_g, xn, t_xn, sq, t_sq, rstd, t_rstd, TT, 6)
                for fc in range(22):
                    pb = fc % 2
                    pg = ps[0 + pb][:, 0:TT]
                    pu = ps[2 + pb][:, 0:TT]
                    for k in range(8):
                        MM(pg, wg[:, k, fc * 128:(fc + 1) * 128], xn[:, k, :], k == 0, k == 7,
                           [t_wg[k], t_xn], [t_ps[0 + pb]])
                    for k in range(8):
                        MM(pu, wu[:, k, fc * 128:(fc + 1) * 128], xn[:, k, :], k == 0, k == 7,
                           [t_wu[k], t_xn], [t_ps[2 + pb]])
                    ACT(sg[pb][:], pg, AF.Silu, [t_ps[0 + pb]], [t_sg[pb]])
                    TT2("dve", hh[:, fc, :], pu, sg[pb][:], ALU.mult, [t_ps[2 + pb], t_sg[pb]], [t_hh[fc]])
                for dc in range(8):
                    pb = dc % 2
                    py = ps[4 + pb][:, 0:TT]
                    for fc in range(22):
                        MM(py, wd[:, fc, dc * 128:(dc + 1) * 128], hh[:, fc, :], fc == 0, fc == 21,
                           [t_wd[fc], t_hh[fc]], [t_ps[4 + pb]])
                    STT(ot[:, dc, :], py, 0.5, xb[:, dc, :], ALU.mult, ALU.add,
                        [t_ps[4 + pb], t_xt[b]], [t_ot])
                P.dma("sp", dst_v[:, :, i * TT:(i + 1) * TT], ot[:], reads=[t_ot], writes=[t_dst], chan=t_ot)
            P.barrier()

    def mixer_attention():
        with ExitStack() as ms:
            Qs = [sb("Qs%d" % h, [128, S], BF16, ms) for h in range(8)]
            Ks = [sb("Ks%d" % g, [128, S], BF16, ms) for g in range(2)]
            Kw = sb("Kw", [128, S], BF16, ms)
            Vs = sb("Vs", [128, 32, 2, 72], BF16, ms)
            Vw = sb("Vw", [128, 32, 2, 72], BF16, ms)
            gl = sb("gl", [128, 32, 24], F32, ms)
            gains = sb("gains", [128, 4], F32, ms)
            gq8 = sb("gq8", [128, 1], F32, ms)
            kcT = sb("kcT", [128, 256], BF16, ms)
            rcmp = [sb("rcmp%d" % g, [128, 2, 129], BF16, ms) for g in range(2)]
            t_Q = Tk("Qq")
            t_Qm = [Tk("Qm%d" % i) for i in range(8)]
            t_Ks, t_Kw, t_Vs, t_Vw, t_gl, t_gains, t_kcT, t_rcmp = [Tk(n) for n in
                ("Ks", "Kw", "Vs", "Vw", "gl", "gains", "kcT", "rcmp")]
            P.dma("sp", gains[:], din["gains"], writes=[t_gains])
            TS("dve", gq8[:], gains[:, 0:1], 0.125, None, ALU.mult, None, [t_gains], [t_gains])
            if "E" not in DBG:
                P.dma("pool", Ks[0][64:128, :], din["c_E"], writes=[t_Ks])
                P.dma("pool", Ks[1][0:64, :], din["c_E"], writes=[t_Ks])
            if "V1" not in DBG:
                MEMSET("pool", Vs[:, :, :, 64:65], 1.0, [t_Vs])
                MEMSET("pool", Vw[:, :, :, 64:65], 1.0, [t_Vw])

            with ExitStack() as bs:
                TT = 512
                sq = [sb("Bsq%d" % i, [128, TT], BF16, bs) for i in range(2)]
                rs = sb("Brs", [128, TT], F32, bs)
                KcA = sb("KcA", [128, S], BF16, bs)
                KcB = sb("KcB", [128, S], BF16, bs)
                VcA = sb("VcA", [128, S], BF16, bs)
                VcB = sb("VcB", [128, S], BF16, bs)
                b2 = ExitStack()
                wB = sb("wB", [128, 8, 1816], BF16, b2)
                mg = sb("mg", [128, 8], F32, b2)
                xt = sb("Bx", [128, 8, TT], F32, b2)
                xn = sb("Bxn", [128, 8, TT], BF16, b2)
                rstd = sb("Brstd", [128, TT], F32, b2)
                ust = sb("Bust", [128, 4, TT], BF16, b2)
                posk = sb("posk", [128, 32], F32, b2)
                posv = sb("posv", [128, 32], F32, b2)
                vstg = sb("vstg", [128, 256], BF16, b2)
                t_vstg = Tk("vstg")
                t_wB, t_mg, t_xt, t_xn, t_rstd, t_rs, t_ust, t_Kc, t_pos = [Tk(n) for n in
                    ("wB", "mg", "Bxt", "Bxn", "Brstd", "Brs", "Bust", "Kc", "pos")]
                t_sq = [Tk("Bsq0"), Tk("Bsq1")]
                wB_v = din["w_inB"].rearrange("(k p) f -> p k f", p=128)
                for k in range(8):
                    P.dma("pool", wB[:, k, :], wB_v[:, k, :], writes=[t_wB])
                P.dma("sp", mg[:], din["mix_g"], writes=[t_mg])
                P.dma("sp", posk[:], din["posk"], writes=[t_pos])
                P.dma("sp", posv[:], din["posv"], writes=[t_pos])
                h1_v = h1T.rearrange("(c p) t -> p c t", p=128)
                uT_v = uT.rearrange("(c p) t -> p c t", p=128)
                for i in range(S // TT):
                    tsl = slice(i * TT, (i + 1) * TT)
                    P.dma("sp", xt[:], h1_v[:, :, tsl], reads=[t_h1T], writes=[t_xt])
                    norm_tile(xt, t_xt, mg, t_mg, xn, t_xn, sq, t_sq, rstd, t_rstd, TT, 6)
                    pssb = [3, 7]

                    def main(ch):
                        pb = ch % 3
                        pq = ps[pb][:, :]
                        for k in range(8):
                            MM(pq, wB[:, k, ch * 128:(ch + 1) * 128], xn[:, k, :], k == 0, k == 7,
                               [t_wB, t_xn], [t_ps[pb]])
                        if ch < 6:
                            ACT(sq[ch % 2][:], pq, AF.Square, [t_ps[pb]], [t_sq[ch % 2]])

                    def finish(ch):
                        pb = ch % 3
                        pq = ps[pb][:, :]
                        if ch < 6:
                            sb_ = pssb[ch % 2]
                            pss = ps[sb_][:, :]
                            MM(pss, bd_bf[:], sq[ch % 2][:], True, True, [t_const, t_sq[ch % 2]], [t_ps[sb_]])
                            ACT(rs[:], pss, AF.Ln, [t_ps[sb_], t_const], [t_rs], bias=epsc[:, 0:1], scale=1.0 / 64)
                            ACT(rs[:], rs[:], AF.Exp, [t_rs], [t_rs], scale=-0.5)
                            if ch < 4:
                                STT(Qs[ch][0:64, tsl], pq[0:64, :], gq8[0:64, 0:1], rs[0:64, :], ALU.mult, ALU.mult,
                                    [t_ps[pb], t_gains, t_rs], [t_Q])
                                STT(Qs[4 + ch][64:128, tsl], pq[64:128, :], gq8[64:128, 0:1], rs[64:128, :],
                                    ALU.mult, ALU.mult, [t_ps[pb], t_gains, t_rs], [t_Q])
                            elif ch == 4:
                                STT(Ks[0][0:64, tsl], pq[0:64, :], gains[0:64, 1:2], rs[0:64, :], ALU.mult, ALU.mult,
                                    [t_ps[pb], t_gains, t_rs], [t_Ks])
                                STT(Ks[1][64:128, tsl], pq[64:128, :], gains[64:128, 1:2], rs[64:128, :],
                                    ALU.mult, ALU.mult, [t_ps[pb], t_gains, t_rs], [t_Ks])
                            else:
                                STT(Kw[:, tsl], pq, gains[:, 2:3], rs[:], ALU.mult, ALU.mult,
                                    [t_ps[pb], t_gains, t_rs], [t_Kw])
                        elif ch < 8:
                            pos = posk if ch == 6 else posv
                            A_, B_ = (KcA, KcB) if ch == 6 else (VcA, VcB)
                            pq3 = pq.rearrange("p (i s) -> p i s", s=16)
                            TT2("dve", A_[:, tsl].rearrange("p (i s) -> p i s", s=16), pq3,
                                pos[:, 0:16].unsqueeze(1).to_broadcast([128, TT // 16, 16]), ALU.add,
                                [t_ps[pb], t_pos], [t_Kc])
                            TT2("dve", B_[:, tsl].rearrange("p (i s) -> p i s", s=16), pq3,
                                pos[:, 16:32].unsqueeze(1).to_broadcast([128, TT // 16, 16]), ALU.add,
                                [t_ps[pb], t_pos], [t_Kc])
                        else:
                            CP("act", ust[:, ch - 8, :], pq, [t_ps[pb]], [t_ust])

                    for ch in range(13):
                        if ch < 12:
                            main(ch)
                        if ch >= 1:
                            finish(ch - 1)
                    P.dma("sp", uT_v[:, :, tsl], ust[:], reads=[t_ust], writes=[t_uT], chan=t_ust)
                    for sub in range(4):
                        if "TM" in DBG:
                            break
                        pb = 4 + sub % 2
                        pt = ps[pb][:, 0:280]
                        for k in range(8):
                            MM(pt, xn[:, k, sub * 128:(sub + 1) * 128], wB[:, k, 1536:1816], k == 0, k == 7,
                               [t_wB, t_xn], [t_ps[pb]])
                        kt = i * 4 + sub
                        CP("act", vstg[:], ps[pb][:, 0:256], [t_ps[pb]], [t_vstg])
                        for g_ in range(2):
                            CP("pool", Vs[:, kt, g_, 0:64], vstg[:, g_ * 64:(g_ + 1) * 64], [t_vstg], [t_Vs])
                            CP("pool", Vw[:, kt, g_, 0:64], vstg[:, 128 + g_ * 64:128 + (g_ + 1) * 64], [t_vstg], [t_Vw])
                        if "cpG" not in DBG:
                            CP("act", gl[:, kt, :], ps[pb][:, 256:280], [t_ps[pb]], [t_gl])
                P.barrier()
                b2.close()
                if stage == "B":
                    return

                w1k = sb("w1k", [128, 32, 256], BF16, bs)
                w1v = sb("w1v", [128, 32, 256], BF16, bs)
                w2k = sb("w2k", [128, 2, 128], BF16, bs)
                w2v = sb("w2v", [128, 2, 64], BF16, bs)
                ovl = sb("ovl", [128, 2, 64], BF16, bs)
                hx = sb("hx", [128, 256], F32, bs)
                hu = sb("hu", [128, 256], F32, bs)
                hid = [[sb("hid%d%d" % (a, b), [128, 256], BF16, bs) for b in range(2)] for a in range(2)]
                t_cw, t_hx, t_hu = Tk("cw"), Tk("hx"), Tk("hu")
                t_hid = [[Tk("hid%d%d" % (a, b)) for b in range(2)] for a in range(2)]
                P.dma("pool", w1k[:], din["w1k"], writes=[t_cw])
                P.dma("pool", w1v[:], din["w1v"], writes=[t_cw])
                P.dma("pool", w2k[:], din["w2k"], writes=[t_cw])
                P.dma("pool", w2v[:], din["w2v"], writes=[t_cw])
                P.dma("pool", ovl[:], din["c_ovl"], writes=[t_cw])
                MEMSET("pool", kcT[:], 0.0, [t_kcT])
                for g in range(2):
                    MEMSET("pool", rcmp[g][:, :, 128:129], 1.0, [t_rcmp])
                    CP("pool", rcmp[g][:, :, 64:128], ovl[:], [t_cw], [t_rcmp])
                for a in range(2):
                    for b in range(2):
                        MEMSET("pool", hid[a][b][:], 0.0, [t_hid[a][b]])
                for kind in range(2):
                    w1 = w1k if kind == 0 else w1v
                    A_, B_ = (KcA, KcB) if kind == 0 else (VcA, VcB)
                    A3 = A_[:].rearrange("p (i s) -> p i s", s=16)
                    B3 = B_[:].rearrange("p (i s) -> p i s", s=16)
                    for g in range(2):
                        R = slice(64 * g, 64 * g + 64)
                        for mc in range(2):
                            ph = ps[mc][:, 0:255]
                            for j in range(32):
                                rhs = A3[R, 0:255, j] if j < 16 else B3[R, 1:256, j - 16]
                                MM(ph, w1[R, j, mc * 128:(mc + 1) * 128], rhs, j == 0, j == 31,
                                   [t_cw, t_Kc], [t_ps[mc]])
                            CP("act", hx[:, 0:255], ph, [t_ps[mc]], [t_hx])
                            TT2("dve", hu[:, 0:255], hx[:, 0:255], hx[:, 0:255], ALU.mult, [t_hx], [t_hu])
                            TS("dve", hu[:, 0:255], hu[:, 0:255], 0.044715, 1.0, ALU.mult, ALU.add, [t_hu], [t_hu])
                            TT2("dve", hu[:, 0:255], hu[:, 0:255], hx[:, 0:255], ALU.mult, [t_hu, t_hx], [t_hu])
                            ACT(hu[:, 0:255], hu[:, 0:255], AF.Sigmoid, [t_hu], [t_hu], scale=1.5957691216057308)
                            TT2("dve", hid[g][mc][:, 0:255], hu[:, 0:255], hx[:, 0:255], ALU.mult,
                                [t_hu, t_hx], [t_hid[g][mc]])
                        if kind == 0:
                            pk = ps[2][:, 0:256]
                            for mc in range(2):
                                MM(pk, w2k[:, mc, :], hid[g][mc][:], mc == 0, mc == 1,
                                   [t_cw, t_hid[g][mc]], [t_ps[2]])
                            sqn = sq[0]
                            ACT(sqn[:, 0:256], pk, AF.Square, [t_ps[2]], [t_sq[0]])
                            pss = ps[3][:, 0:256]
                            MM(pss, bd_bf[:], sqn[:, 0:256], True, True, [t_const, t_sq[0]], [t_ps[3]])
                            ACT(rs[:, 0:256], pss, AF.Sqrt, [t_ps[3], t_const], [t_rs], bias=epsc[:, 0:1], scale=1.0 / 64)
                            RECIP(rs[:, 0:256], rs[:, 0:256], [t_rs], [t_rs])
                            STT(kcT[R, 0:255], pk[R, 0:255], gains[R, 3:4], rs[R, 0:255], ALU.mult, ALU.mult,
                                [t_ps[2], t_gains, t_rs], [t_kcT])
                        else:
                            for it in range(2):
                                pv = ps[2 + it][:, 0:64]
                                for mc in range(2):
                                    MM(pv, hid[g][mc][:, it * 128:(it + 1) * 128], w2v[:, mc, :], mc == 0, mc == 1,
                                       [t_cw, t_hid[g][mc]], [t_ps[2 + it]])
                                CP("act", rcmp[g][:, it, 0:64], pv, [t_ps[2 + it]], [t_rcmp])
                P.barrier()

            if stage == "BC":
                return
            with ExitStack() as ds:
                mwin = sb("mwin", [128, 8, 512], BF16, ds)
                mslc = sb("mslc", [128, 4, 512], BF16, ds)
                mcmp = sb("mcmp", [128, 5, 512], BF16, ds)
                selB = sb("selB", [128, 32, 64], F32, ds)
                gs = sb("gs", [128, 32, 24], F32, ds)
                PT = sb("PT", [128, 32, 512], BF16, ds)
                Pc = [sb("Pc%d" % i, [128, 512], BF16, ds) for i in range(2)]
                onsa = sb("onsa", [128, 4, 512], F32, ds)
                onsab = sb("onsab", [128, 4, 512], BF16, ds)
                oTs = sb("oTs", [128, 4, 512], BF16, ds)
                imp = [sb("imp%d" % g, [128, 4, 64], F32, ds) for g in range(2)]
                scr = sb("scr", [128, 64], F32, ds)
                wk = sb("wk", [128, 64], F32, ds)
                m8 = sb("m8", [128, 16], F32, ds)
                NM = sb("NM", [128, 4, 128], BF16, ds)
                sm = sb("sm", [128, 8], F32, ds)
                ofs_s = sb("ofs_s", [128, 512], F32, ds)
                ofs_w = sb("ofs_w", [128, 512], F32, ds)
                t_ofs_s, t_ofs_w = Tk("ofs_s"), Tk("ofs_w")
                t_msk, t_selB, t_gs, t_onsa, t_onsab, t_oTs, t_scr, t_wk, t_m8, t_NM, t_sm = [Tk(n) for n in
                    ("msk", "selB", "gs", "onsa", "onsab", "oTs", "scr", "wk", "m8", "NM", "sm")]
                t_PT = [Tk("PT%d" % i) for i in range(32)]
                t_Pc = [Tk("Pc0"), Tk("Pc1")]
                t_imp = [Tk("imp0"), Tk("imp1")]
                P.dma("pool", mwin[:], din["c_mwin"], writes=[t_msk])
                P.dma("pool", mslc[:], din["c_mslc"], writes=[t_msk])
                P.dma("pool", mcmp[:], din["c_mcmp"], writes=[t_msk])
                P.dma("sp", selB[:], din["c_selB"], writes=[t_selB])
                ACT(gs[:], gl[:], AF.Sigmoid, [t_gl], [t_gs])
                onsaT_v = onsaT.rearrange("(c p) t -> p c t", p=128)
                BIG = 30000.0
                sbank = [0, 1, 2]
                sctr = [0]

                def next_sbank():
                    b = sbank[sctr[0] % 3]
                    sctr[0] += 1
                    return b

                def evac(po, sub, h, br, kt, first, imp_g=None, r=0):
                    dcol = 128 if br == 0 else 64
                    c0 = (h * 3 + br) % 4 * 2
                    TS("dve", sm[:, c0:c0 + 1], po[:, dcol:dcol + 1], 1e-30, None, ALU.max, None, [t_ps_cur[0]], [t_sm])
                    RECIP(sm[:, c0:c0 + 1], sm[:, c0:c0 + 1], [t_sm], [t_sm])
                    if br == 0:
                        if r == 0:
                            TS("dve", imp_g[:, sub, :], po[:, 64:128], sm[:, c0:c0 + 1], None, ALU.mult, None,
                               [t_ps_cur[0], t_sm], [t_imp[h // 4]])
                        else:
                            STT(imp_g[:, sub, :], po[:, 64:128], sm[:, c0:c0 + 1], imp_g[:, sub, :], ALU.mult, ALU.add,
                                [t_ps_cur[0], t_sm, t_imp[h // 4]], [t_imp[h // 4]])
                    TT2("dve", sm[:, c0 + 1:c0 + 2], sm[:, c0:c0 + 1], gs[:, kt, br * 8 + h:br * 8 + h + 1], ALU.mult,
                        [t_sm, t_gs], [t_sm])
                    dst = onsa[:, sub, h * 64:(h + 1) * 64]
                    if first:
                        TS("dve", dst, po[:, 0:64], sm[:, c0 + 1:c0 + 2], None, ALU.mult, None,
                           [t_ps_cur[0], t_sm], [t_onsa])
                    else:
                        STT(dst, po[:, 0:64], sm[:, c0 + 1:c0 + 2], dst, ALU.mult, ALU.add,
                            [t_ps_cur[0], t_sm, t_onsa], [t_onsa])

                t_ps_cur = [None]
                for Q in range(8):
                    qsl = slice(Q * 512, (Q + 1) * 512)
                    cts = [0] if Q < 4 else [0, 1]
                    for h in range(8):
                        g, r = h // 4, h % 4
                        R = slice(64 * g, 64 * g + 64)
                        for ct in cts:
                            b = next_sbank()
                            MM(ps[b][:, :], kcT[R, ct * 128:(ct + 1) * 128], Qs[h][R, qsl], True, True,
                               [t_kcT, t_Q], [t_ps[b]])
                            ACT(Pc[ct][:], ps[b][:, :], AF.Exp, [t_ps[b]], [t_Pc[ct]])
                            v = Q - 4 * ct
                            if v <= 4:
                                TT2("dve", Pc[ct][:], Pc[ct][:], mcmp[:, v, :], ALU.mult, [t_Pc[ct], t_msk], [t_Pc[ct]])
                        for sub in range(4):
                            bk = 3 + sub // 2
                            po = ps[bk][:, (sub % 2) * 129:(sub % 2) * 129 + 129]
                            for ci, ct in enumerate(cts):
                                MM(po, Pc[ct][:, sub * 128:(sub + 1) * 128], rcmp[g][:, ct, :], ci == 0, ci == len(cts) - 1,
                                   [t_Pc[ct], t_rcmp], [t_ps[bk]])
                        for sub in range(4):
                            bk = 3 + sub // 2
                            po = ps[bk][:, (sub % 2) * 129:(sub % 2) * 129 + 129]
                            t_ps_cur[0] = t_ps[bk]
                            evac(po, sub, h, 0, Q * 4 + sub, True, imp[g], r)
                    for sub in range(4):
                        kt = Q * 4 + sub
                        for g in range(2):
                            TT2("dve", scr[:], imp[g][:, sub, :], selB[:, kt, :], ALU.add, [t_imp[g], t_selB], [t_scr])
                            P.op("dve", lambda e: e.max(out=m8[:, 0:8], in_=scr[:]), [t_scr], [t_m8])
                            P.op("dve", lambda e: e.match_replace(out=wk[:], in_to_replace=m8[:, 0:8],
                                                                   in_values=scr[:], imm_value=-3.0e38),
                                 [t_scr, t_m8], [t_wk])
                            P.op("dve", lambda e: e.max(out=m8[:, 8:16], in_=wk[:]), [t_wk], [t_m8])
                            cs = slice(64, 128) if g == 0 else slice(0, 64)
                            TS("dve", NM[:, sub, cs], scr[:], m8[:, 15:16], -BIG, ALU.is_lt, ALU.mult,
                               [t_scr, t_m8], [t_NM])
                    pT = ps[7][:].bitcast(BF16)
                    for sub in range(4):
                        TR(pT[:, sub * 128:(sub + 1) * 128], NM[:, sub, :], id_bf[:], [t_NM, t_const], [t_ps[7]])
                    for h in range(8):
                        R2 = slice(64, 128) if h < 4 else slice(0, 64)
                        CP("act" if h % 2 == 0 else "dve", Qs[h][R2, qsl], pT[R2, 0:512], [t_ps[7]], [t_Qm[Q]])
                    def pv_branch(h, g, br, items, qk, vsrc, bk_acc, bk_tr, ofs, t_ofs):
                        n = len(items)
                        po = ps[bk_acc][0:65, :]
                        LAG = 2
                        for i in range(n + LAG):
                            if i < n:
                                slot, kt, mk = items[i]
                                b = next_sbank()
                                qk(b, kt)
                                ACT(PT[:, slot, :], ps[b][:, :], AF.Exp, [t_ps[b]], [t_PT[slot]])
                                if mk is not None:
                                    TT2("dve", PT[:, slot, :], PT[:, slot, :], mk, ALU.mult,
                                        [t_PT[slot], t_msk], [t_PT[slot]])
                            j = i - LAG
                            if j >= 0:
                                slot, kt, mk = items[j]
                                MM(po, vsrc[:, kt, g, 0:65], PT[:, slot, :], j == 0, j == n - 1,
                                   [t_PT[slot], t_Vs, t_Vw], [t_ps[bk_acc]])
                        CP("act", ofs[:, :], ps[bk_acc][:, :], [t_ps[bk_acc]], [t_ofs])
                        for sub in range(4):
                            TR(ps[bk_tr][:, sub * 128:(sub + 1) * 128], ofs[:, sub * 128:(sub + 1) * 128],
                               id_f[:], [t_ofs, t_const], [t_ps[bk_tr]])
                        t_ps_cur[0] = t_ps[bk_tr]
                        for sub in range(4):
                            evac(ps[bk_tr][:, sub * 128:sub * 128 + 65], sub, h, br, Q * 4 + sub, False)

                    for h in range(8):
                        g = h // 4
                        R = slice(64 * g, 64 * g + 64)
                        nkt = 4 * Q + 4

                        def qk_slc(b, kt, g=g, h=h):
                            MM(ps[b][:, :], Ks[g][:, kt * 128:(kt + 1) * 128], Qs[h][:, qsl], True, True,
                               [t_Ks, t_Q, t_Qm[Q]], [t_ps[b]])

                        def qk_win(b, kt, R=R, h=h):
                            MM(ps[b][:, :], Kw[R, kt * 128:(kt + 1) * 128], Qs[h][R, qsl], True, True,
                               [t_Kw, t_Q], [t_ps[b]])

                        items = [(kt, kt, (mslc[:, kt - 4 * Q, :] if kt >= 4 * Q else None)) for kt in range(nkt)]
                        pv_branch(h, g, 1, items, qk_slc, Vs, 5, 3, ofs_s, t_ofs_s)
                        items = [(24 + r_, 4 * Q - 4 + r_, mwin[:, r_, :]) for r_ in range(8) if 4 * Q - 4 + r_ >= 0]
                        pv_branch(h, g, 2, items, qk_win, Vw, 6, 4, ofs_w, t_ofs_w)
                    CP("act", onsab[:], onsa[:], [t_onsa], [t_onsab])
                    for fc in range(4):
                        pT2 = ps[7][:].bitcast(BF16)
                        for sub in range(4):
                            TR(pT2[:, sub * 128:(sub + 1) * 128], onsab[:, sub, fc * 128:(fc + 1) * 128], id_bf[:],
                               [t_onsab, t_const], [t_ps[7]])
                        CP("dve" if fc % 2 else "act", oTs[:, fc, :], pT2[:, 0:512], [t_ps[7]], [t_oTs])
                    P.dma("sp", onsaT_v[:, :, qsl], oTs[:], reads=[t_oTs], writes=[t_onsaT], chan=t_oTs)
                P.barrier()

    def ssm_phase():
        PI = math.pi
        with ExitStack() as fs:
            CT = sb("CT", [128, 16, 512], F32, fs)
            ST = sb("ST", [128, 16, 512], F32, fs)
            rhoF = sb("rhoF", [128, 16, 512], F32, fs)
            BBrT = sb("BBrT", [128, 16, 128], BF16, fs)
            BBiT = sb("BBiT", [128, 16, 128], BF16, fs)
            CreT = sb("CreT", [128, 16, 128], BF16, fs)
            CimT = sb("CimT", [128, 16, 128], BF16, fs)
            Dg = sb("Dg", [128, 4, 128], BF16, fs)
            dsk = sb("dsk", [128, 4], F32, fs)
            rho = sb("rho", [128, 16], F32, fs)
            car = [[sb("car%d%d" % (a, b), [128, 16], F32, fs) for b in range(2)] for a in range(2)]
            t_tab, t_bb, t_cc, t_dg, t_rho = [Tk(n) for n in ("tab", "bb", "cc", "dg", "rho")]
            t_car = [[Tk("car%d_%d" % (a, sc)) for sc in range(16)] for a in range(2)]
            P.dma("pool", CreT[:], din["s_creT"], writes=[t_cc])
            P.dma("pool", CimT[:], din["s_cimT"], writes=[t_cc])
            P.dma("sp", dsk[:], din["s_d"], writes=[t_dg])
            for oc in range(4):
                TS("dve", Dg[:, oc, :], id_f[:], dsk[:, oc:oc + 1], None, ALU.mult, None, [t_const, t_dg], [t_dg])
            nCreT = sb("nCreT", [128, 16, 128], BF16, fs)
            nCimT = sb("nCimT", [128, 16, 128], BF16, fs)
            TS("dve", nCreT[:], CreT[:], -1.0, None, ALU.mult, None, [t_cc], [t_cc])
            TS("dve", nCimT[:], CimT[:], -1.0, None, ALU.mult, None, [t_cc], [t_cc])
            with ExitStack() as ss:
                lr = sb("lr", [128, 16], F32, ss)
                li = sb("li", [128, 16], F32, ss)
                stp = sb("stp", [128, 16], F32, ss)
                th = sb("th", [128, 16], F32, ss)
                kk = sb("kk", [128, 16], F32, ss)
                r1 = sb("r1", [128, 16], F32, ss)
                r2 = sb("r2", [128, 16], F32, ss)
                s1 = sb("s1", [128, 16], F32, ss)
                c1 = sb("c1", [128, 16], F32, ss)
                w = [sb("w%d" % i, [128, 16], F32, ss) for i in range(6)]
                bre = sb("bre", [128, 16, 16], F32, ss)
                bim = sb("bim", [128, 16, 16], F32, ss)
                bbr = sb("bbr", [128, 16, 16], F32, ss)
                bbi = sb("bbi", [128, 16, 16], F32, ss)
                bt = [sb("bt%d" % i, [128, 16, 16], F32, ss) for i in range(2)]
                Zr = sb("Zr", [128, 16, 128], F32, ss)
                Zi = sb("Zi", [128, 16, 128], F32, ss)
                tA = sb("tA", [128, 16, 256], F32, ss)
                tB = sb("tB", [128, 16, 256], F32, ss)
                t_p = Tk("ssm_par")
                t_Z = Tk("Z")
                P.dma("sp", lr[:], din["s_lr"], writes=[t_p])
                P.dma("sp", li[:], din["s_li"], writes=[t_p])
                P.dma("sp", stp[:], din["s_ls"], writes=[t_p])
                P.dma("sp", bre[:], din["s_bre"], writes=[t_p])
                P.dma("sp", bim[:], din["s_bim"], writes=[t_p])
                pp = [t_p]
                ACT(stp[:], stp[:], AF.Exp, pp, pp)
                TT2("dve", th[:], li[:], stp[:], ALU.mult, pp, pp)
                TT2("dve", w[0][:], lr[:], stp[:], ALU.mult, pp, pp)
                ACT(rho[:], w[0][:], AF.Exp, pp, [t_rho])
                MEMSET("dve", kk[:], 0.0, pp)
                for m in range(1, 9):
                    STT(kk[:], th[:], (2 * m - 1) * PI, kk[:], ALU.is_gt, ALU.add, pp, pp)
                STT(r1[:], kk[:], -2.0 * PI, th[:], ALU.mult, ALU.add, pp, pp)
                TS("dve", r2[:], r1[:], PI / 2, None, ALU.add, None, pp, pp)
                TS("dve", w[1][:], r2[:], PI, None, ALU.is_gt, None, pp, pp)
                STT(r2[:], w[1][:], -2.0 * PI, r2[:], ALU.mult, ALU.add, pp, pp)
                ACT(s1[:], r1[:], AF.Sin, pp, pp)
                ACT(c1[:], r2[:], AF.Sin, pp, pp)
                ar, ai, arm1, den, cr, cim = w
                TT2("dve", ar[:], rho[:], c1[:], ALU.mult, pp + [t_rho], pp)
                TT2("dve", ai[:], rho[:], s1[:], ALU.mult, pp + [t_rho], pp)
                TS("dve", arm1[:], ar[:], -1.0, None, ALU.add, None, pp, pp)
                TT2("dve", den[:], lr[:], lr[:], ALU.mult, pp, pp)
                TT2("dve", kk[:], li[:], li[:], ALU.mult, pp, pp)
                TT2("dve", den[:], den[:], kk[:], ALU.add, pp, pp)
                RECIP(den[:], den[:], pp, pp)
                TT2("dve", cr[:], arm1[:], lr[:], ALU.mult, pp, pp)
                TT2("dve", kk[:], ai[:], li[:], ALU.mult, pp, pp)
                TT2("dve", cr[:], cr[:], kk[:], ALU.add, pp, pp)
                TT2("dve", cr[:], cr[:], den[:], ALU.mult, pp, pp)
                TT2("dve", cim[:], ai[:], lr[:], ALU.mult, pp, pp)
                TT2("dve", kk[:], arm1[:], li[:], ALU.mult, pp, pp)
                TT2("dve", cim[:], cim[:], kk[:], ALU.subtract, pp, pp)
                TT2("dve", cim[:], cim[:], den[:], ALU.mult, pp, pp)
                crb = cr[:].unsqueeze(2).to_broadcast([128, 16, 16])
                cib = cim[:].unsqueeze(2).to_broadcast([128, 16, 16])
                TT2("dve", bt[0][:], bre[:], crb, ALU.mult, pp, pp)
                TT2("dve", bt[1][:], bim[:], cib, ALU.mult, pp, pp)
                TT2("dve", bbr[:], bt[0][:], bt[1][:], ALU.subtract, pp, pp)
                TT2("dve", bt[0][:], bim[:], crb, ALU.mult, pp, pp)
                TT2("dve", bt[1][:], bre[:], cib, ALU.mult, pp, pp)
                TT2("dve", bbi[:], bt[0][:], bt[1][:], ALU.add, pp, pp)
                MEMSET("pool", Zr[:], 0.0, [t_Z])
                MEMSET("pool", Zi[:], 0.0, [t_Z])
                for sc in range(16):
                    c0 = 32 * (sc % 4)
                    for Z_, bb_ in ((Zr, bbr), (Zi, bbi)):
                        CP("dve", Z_[0:64, sc, c0:c0 + 16], bb_[0:64, sc, :], pp + [t_Z], [t_Z])
                        CP("dve", Z_[64:128, sc, c0 + 16:c0 + 32], bb_[64:128, sc, :], pp + [t_Z], [t_Z])
                for sc in range(16):
                    for bi_, (Z_, BT_) in enumerate(((Zr, BBrT), (Zi, BBiT))):
                        bk = (sc * 2 + bi_) % 4
                        TR(ps[bk][:, 0:128], Z_[:, sc, :], id_f[:], [t_Z, t_const], [t_ps[bk]])
                        CP("act", BT_[:, sc, :], ps[bk][:, 0:128], [t_ps[bk]], [t_bb])
                CP("dve", CT[:, :, 0:1], c1[:].unsqueeze(2), pp, [t_tab])
                CP("dve", ST[:, :, 0:1], s1[:].unsqueeze(2), pp, [t_tab])
                n = 1
                while n < 512:
                    cn = CT[:, :, n - 1:n].to_broadcast([128, 16, n])
                    sn = ST[:, :, n - 1:n].to_broadcast([128, 16, n])
                    tt_ = [t_tab]
                    TT2("dve", tA[:, :, 0:n], CT[:, :, 0:n], cn, ALU.mult, tt_, pp)
                    TT2("pool", tB[:, :, 0:n], ST[:, :, 0:n], sn, ALU.mult, tt_, [t_Z])
                    TT2("dve", CT[:, :, n:2 * n], tA[:, :, 0:n], tB[:, :, 0:n], ALU.subtract, pp + [t_Z], tt_)
                    TT2("dve", tA[:, :, 0:n], ST[:, :, 0:n], cn, ALU.mult, tt_, pp)
                    TT2("pool", tB[:, :, 0:n], CT[:, :, 0:n], sn, ALU.mult, tt_, [t_Z])
                    TT2("dve", ST[:, :, n:2 * n], tA[:, :, 0:n], tB[:, :, 0:n], ALU.add, pp + [t_Z], tt_)
                    n *= 2
                MEMSET("pool", rhoF[:], 1.0, [t_rho])
                for sc in range(16):
                    TS("pool", rhoF[:, sc, :], rhoF[:, sc, :], rho[:, sc:sc + 1], None, ALU.mult, None, [t_rho], [t_rho])
                for a in range(2):
                    for b in range(2):
                        MEMSET("pool", car[a][b][:], 0.0, [t_car[a][sc] for sc in range(16)])
                P.barrier()

            NB = 2
            uTt = [sb("uTt%d" % i, [128, 4, 512], BF16, fs) for i in range(3)]
            ygst = sb("ygst", [128, 4, 512], BF16, fs)
            f = [[sb("sf%d_%d" % (i, b), [128, 512], F32 if i < 8 else BF16, fs) for b in range(NB)] for i in range(12)]
            gx = sb("gx", [128, 512], F32, fs)
            gu = sb("gu", [128, 512], F32, fs)
            cu = sb("cu", [128, 4], F32, fs)
            t_u = [Tk("uTt%d" % i) for i in range(3)]
            t_f = [[Tk("sf%d_%d" % (i, b)) for b in range(NB)] for i in range(12)]
            t_gx, t_gu, t_ygst, t_cu = Tk("gx"), Tk("gu"), Tk("ygst"), Tk("cu")
            uT_v = uT.rearrange("(c p) t -> p c t", p=128)
            yg_v = ygT.rearrange("(c p) t -> p c t", p=128)
            P.dma("sp", uTt[0][:], uT_v[:, :, 0:512], reads=[t_uT], writes=[t_u[0]])
            tiles = [(tt, oc, sci) for tt in range(8) for oc in range(4) for sci in range(4)]

            def S1(i):
                tt, oc, sci = tiles[i]
                sc, b, ub = 4 * oc + sci, i % NB, tt % 3
                if oc == 0 and sci == 0 and tt + 1 < 8:
                    nb_ = (tt + 1) % 3
                    P.dma("sp", uTt[nb_][:], uT_v[:, :, (tt + 1) * 512:(tt + 2) * 512], reads=[t_uT], writes=[t_u[nb_]])
                pb = 2 * (i % 2)
                pur, pui = ps[pb][:, :], ps[pb + 1][:, :]
                C_, S_ = CT[:, sc, :], ST[:, sc, :]
                MM(pur, BBrT[:, sc, :], uTt[ub][:, oc, :], True, True, [t_bb, t_u[ub]], [t_ps[pb]])
                MM(pui, BBiT[:, sc, :], uTt[ub][:, oc, :], True, True, [t_bb, t_u[ub]], [t_ps[pb + 1]])
                TT2("dve", f[0][b][:], pur, C_, ALU.mult, [t_ps[pb], t_tab], [t_f[0][b]])
                TT2("dve", f[1][b][:], pui, S_, ALU.mult, [t_ps[pb + 1], t_tab], [t_f[1][b]])
                TT2("dve", f[2][b][:], pui, C_, ALU.mult, [t_ps[pb + 1], t_tab], [t_f[2][b]])
                TT2("dve", f[3][b][:], pur, S_, ALU.mult, [t_ps[pb], t_tab], [t_f[3][b]])

            def S2(i):
                b = i % NB
                TT2("pool", f[4][b][:], f[0][b][:], f[1][b][:], ALU.add, [t_f[0][b], t_f[1][b]], [t_f[4][b]])
                TT2("pool", f[5][b][:], f[2][b][:], f[3][b][:], ALU.subtract, [t_f[2][b], t_f[3][b]], [t_f[5][b]])

            def S3(i):
                tt, oc, sci = tiles[i]
                sc, b = 4 * oc + sci, i % NB
                cin, cout = car[tt % 2], car[(tt + 1) % 2]
                tcin, tcout = t_car[tt % 2], t_car[(tt + 1) % 2]
                P.op("dve", lambda e: e.tensor_tensor_scan(
                    out=f[6][b][:], data0=rhoF[:, sc, :], data1=f[4][b][:], initial=cin[0][:, sc:sc + 1],
                    op0=ALU.mult, op1=ALU.add), [t_rho, t_f[4][b], tcin[sc]], [t_f[6][b]])
                P.op("dve", lambda e: e.tensor_tensor_scan(
                    out=f[7][b][:], data0=rhoF[:, sc, :], data1=f[5][b][:], initial=cin[1][:, sc:sc + 1],
                    op0=ALU.mult, op1=ALU.add), [t_rho, t_f[5][b], tcin[sc]], [t_f[7][b]])
                zrL, ziL = f[6][b][:, 511:512], f[7][b][:, 511:512]
                cL, sL = CT[:, sc, 511:512], ST[:, sc, 511:512]
                TT2("dve", cu[:, 0:1], ziL, sL, ALU.mult, [t_f[7][b], t_tab], [t_cu])
                STT(cout[0][:, sc:sc + 1], zrL, cL, cu[:, 0:1], ALU.mult, ALU.subtract,
                    [t_f[6][b], t_tab, t_cu], [tcout[sc]])
                TT2("dve", cu[:, 1:2], ziL, cL, ALU.mult, [t_f[7][b], t_tab], [t_cu])
                STT(cout[1][:, sc:sc + 1], zrL, sL, cu[:, 1:2], ALU.mult, ALU.add,
                    [t_f[6][b], t_tab, t_cu], [tcout[sc]])
                TT2("dve", f[11][b][:], f[7][b][:], CT[:, sc, :], ALU.mult, [t_f[7][b], t_tab], [t_f[11][b]])

            def S4(i):
                tt, oc, sci = tiles[i]
                sc, b = 4 * oc + sci, i % NB
                C_, S_ = CT[:, sc, :], ST[:, sc, :]
                TT2("pool", f[8][b][:], f[6][b][:], C_, ALU.mult, [t_f[6][b], t_tab], [t_f[8][b]])
                TT2("pool", f[9][b][:], f[7][b][:], S_, ALU.mult, [t_f[7][b], t_tab], [t_f[9][b]])
                TT2("pool", f[10][b][:], f[6][b][:], S_, ALU.mult, [t_f[6][b], t_tab], [t_f[10][b]])

            def S5(i):
                tt, oc, sci = tiles[i]
                sc, b, ub = 4 * oc + sci, i % NB, tt % 3
                pyb = 4 + oc % 2
                py = ps[pyb][:, :]
                MM(py, CreT[:, sc, :], f[8][b][:], sci == 0, False, [t_cc, t_f[8][b]], [t_ps[pyb]])
                MM(py, nCreT[:, sc, :], f[9][b][:], False, False, [t_cc, t_f[9][b]], [t_ps[pyb]])
                MM(py, nCimT[:, sc, :], f[10][b][:], False, False, [t_cc, t_f[10][b]], [t_ps[pyb]])
                MM(py, nCimT[:, sc, :], f[11][b][:], False, False, [t_cc, t_f[11][b]], [t_ps[pyb]])
                if sci == 3:
                    MM(py, Dg[:, oc, :], uTt[ub][:, oc, :], False, True, [t_dg, t_u[ub]], [t_ps[pyb]])
                    CP("act", gx[:], py, [t_ps[pyb]], [t_gx])
                    ACT(gu[:], py, AF.Square, [t_ps[pyb]], [t_gu])
                    TS("pool", gu[:], gu[:], 0.044715, 1.0, ALU.mult, ALU.add, [t_gu], [t_gu])
                    TT2("pool", gu[:], gu[:], gx[:], ALU.mult, [t_gu, t_gx], [t_gu])
                    ACT(gu[:], gu[:], AF.Sigmoid, [t_gu], [t_gu], scale=1.5957691216057308)
                    TT2("pool", ygst[:, oc, :], gu[:], gx[:], ALU.mult, [t_gu, t_gx], [t_ygst])
                    if oc == 3:
                        P.dma("sp", yg_v[:, :, tt * 512:(tt + 1) * 512], ygst[:], reads=[t_ygst], writes=[t_ygT], chan=t_ygst)

            n_t = len(tiles)
            for k in range(n_t + 2):
                if k < n_t:
                    S1(k)
                    S2(k)
                if 0 <= k - 1 < n_t:
                    S3(k - 1)
                    S4(k - 1)
                if 0 <= k - 2 < n_t:
                    S5(k - 2)
            P.barrier()

    def merge_phase():
        TT = 256
        NT = S // TT
        with ExitStack() as fs:
            wF = sb("wF", [128, 8, 2048], BF16, fs)
            wnsa = sb("wnsa", [128, 4, 1024], BF16, fs)
            wglu = sb("wglu", [128, 4, 2048], BF16, fs)
            wout = sb("wout", [128, 8, 1024], BF16, fs)
            mg = sb("Fmg", [128, 8], F32, fs)
            xt = [sb("Fx%d" % i, [128, 8, TT], F32, fs) for i in range(2)]
            xn = sb("Fxn", [128, 8, TT], BF16, fs)
            sq = [sb("Fsq%d" % i, [128, TT], BF16, fs) for i in range(2)]
            rstd = sb("Frstd", [128, TT], F32, fs)
            ont = [sb("Fon%d" % i, [128, 4, TT], BF16, fs) for i in range(2)]
            ygt = [sb("Fyg%d" % i, [128, 4, TT], BF16, fs) for i in range(2)]
            sg3 = [[sb("Fsg%d_%d" % (i, b), [128, TT], F32, fs) for b in range(2)] for i in range(3)]
            ta = [sb("Fta%d" % b, [128, TT], F32, fs) for b in range(2)]
            tb = [sb("Ftb%d" % b, [128, TT], F32, fs) for b in range(2)]
            mrg = sb("Fmrg", [128, 8, TT], BF16, fs)
            ot = sb("Fot", [128, 8, TT], F32, fs)
            t_w, t_mg, t_xn, t_rstd, t_ot = [Tk(n) for n in ("Fw", "Fmg", "Fxn", "Frstd", "Fot")]
            t_xt = [Tk("Fxt0"), Tk("Fxt1")]
            t_sq = [Tk("Fsq0"), Tk("Fsq1")]
            t_on = [Tk("Fon0"), Tk("Fon1")]
            t_yg = [Tk("Fyg0"), Tk("Fyg1")]
            t_sg3 = [[Tk("Fsg%d_%d" % (i, b)) for b in range(2)] for i in range(3)]
            t_ta = [Tk("Fta0"), Tk("Fta1")]
            t_tb = [Tk("Ftb0"), Tk("Ftb1")]
            t_mrg = [Tk("Fmrg%d" % i) for i in range(8)]
            t_h = [[Tk("Fps%d_%d" % (i, b)) for b in range(2)] for i in range(6)]
            P.dma("sp", mg[:], din["mix_g"], writes=[t_mg])
            wF_v = din["w_inF"].rearrange("(k p) f -> p k f", p=128)
            for k in range(8):
                P.dma("pool", wF[:, k, :], wF_v[:, k, :], writes=[t_w])
            wn_v = din["w_nsa"].rearrange("(k p) f -> p k f", p=128)
            wgl_v = din["w_glu"].rearrange("(k p) f -> p k f", p=128)
            wo_v = din["w_out"].rearrange("(k p) f -> p k f", p=128)
            for k in range(4):
                P.dma("pool", wnsa[:, k, :], wn_v[:, k, :], writes=[t_w])
                P.dma("pool", wglu[:, k, :], wgl_v[:, k, :], writes=[t_w])
            for k in range(8):
                P.dma("pool", wout[:, k, :], wo_v[:, k, :], writes=[t_w])
            h1_v = h1T.rearrange("(c p) t -> p c t", p=128)
            h2_v = h2T.rearrange("(c p) t -> p c t", p=128)
            on_v = onsaT.rearrange("(c p) t -> p c t", p=128)
            yg_v = ygT.rearrange("(c p) t -> p c t", p=128)

            def load(i):
                b = i % 2
                tsl = slice(i * TT, (i + 1) * TT)
                P.dma("sp", xt[b][:], h1_v[:, :, tsl], reads=[t_h1T], writes=[t_xt[b]])
                P.dma("sp", ont[b][:], on_v[:, :, tsl], reads=[t_onsaT], writes=[t_on[b]])
                P.dma("sp", ygt[b][:], yg_v[:, :, tsl], reads=[t_ygT], writes=[t_yg[b]])

            load(0)
            for i in range(NT):
                b = i % 2
                if i + 1 < NT:
                    load(i + 1)
                xb = xt[b]
                norm_tile(xb, t_xt[b], mg, t_mg, xn, t_xn, sq, t_sq, rstd, t_rstd, TT, 6)
                for dc in range(8):
                    if "F1" in DBG:
                        break
                    hb = dc % 2
                    hs = slice(hb * 256, hb * 256 + TT)
                    pgn, pgs, pyn, pval, pgt = [ps[j][:, hs] for j in range(5)]
                    tp = [t_ps[j] for j in range(5)]
                    dsl = slice(dc * 128, (dc + 1) * 128)
                    dsl2 = slice(1024 + dc * 128, 1024 + (dc + 1) * 128)
                    for k in range(8):
                        MM(pgn, wF[:, k, dsl], xn[:, k, :], k == 0, k == 7, [t_w, t_xn], [tp[0]])
                    for k in range(8):
                        MM(pgs, wF[:, k, dsl2], xn[:, k, :], k == 0, k == 7, [t_w, t_xn], [tp[1]])
                    for k in range(4):
                        MM(pyn, wnsa[:, k, dsl], ont[b][:, k, :], k == 0, k == 3, [t_w, t_on[b]], [tp[2]])
                    for k in range(4):
                        MM(pval, wglu[:, k, dsl], ygt[b][:, k, :], k == 0, k == 3, [t_w, t_yg[b]], [tp[3]])
                    for k in range(4):
                        MM(pgt, wglu[:, k, dsl2], ygt[b][:, k, :], k == 0, k == 3, [t_w, t_yg[b]], [tp[4]])
                    if "F2" in DBG:
                        continue
                    ACT(sg3[0][hb][:], pgn, AF.Sigmoid, [tp[0]], [t_sg3[0][hb]])
                    ACT(sg3[1][hb][:], pgs, AF.Sigmoid, [tp[1]], [t_sg3[1][hb]])
                    ACT(sg3[2][hb][:], pgt, AF.Sigmoid, [tp[4]], [t_sg3[2][hb]])
                    TT2("dve", ta[hb][:], pyn, sg3[0][hb][:], ALU.mult, [tp[2], t_sg3[0][hb]], [t_ta[hb]])
                    TT2("dve", tb[hb][:], pval, sg3[2][hb][:], ALU.mult, [tp[3], t_sg3[2][hb]], [t_tb[hb]])
                    TT2("pool", tb[hb][:], tb[hb][:], sg3[1][hb][:], ALU.mult, [t_tb[hb], t_sg3[1][hb]], [t_tb[hb]])
                    TT2("pool", mrg[:, dc, :], ta[hb][:], tb[hb][:], ALU.add, [t_ta[hb], t_tb[hb]], [t_mrg[dc]])
                for dc in range(8):
                    if "F3" in DBG:
                        break
                    hb = dc % 2
                    po = ps[5][:, hb * 256:hb * 256 + TT]
                    for k in range(8):
                        MM(po, wout[:, k, dc * 128:(dc + 1) * 128], mrg[:, k, :], k == 0, k == 7,
                           [t_w, t_mrg[k]], [t_ps[5]])
                    if "F3a" not in DBG:
                        TT2("dve", ot[:, dc, :], po, xb[:, dc, :], ALU.add, [t_ps[5], t_xt[b]], [t_ot])
                if "F3b" not in DBG:
                    P.dma("sp", h2_v[:, :, i * TT:(i + 1) * TT], ot[:], reads=[t_ot], writes=[t_h2T], chan=t_ot)
            P.barrier()

    if stage == "ffn1":
        ffn_phase("f1", xT, t_xT, outT, t_outT, din["f1_g"], din["f1_wg"], din["f1_wu"], din["f1_wd"])
    else:
        ffn_phase("f1", xT, t_xT, h1T, t_h1T, din["f1_g"], din["f1_wg"], din["f1_wu"], din["f1_wd"])
        mixer_attention()
        if stage not in ("B", "BC", "D"):
            ssm_phase()
        if stage not in ("B", "BC", "D", "E"):
            merge_phase()
        if stage == "full":
            ffn_phase("f2", h2T, t_h2T, outT, t_outT, din["f2_g"], din["f2_wg"], din["f2_wu"], din["f2_wd"])

    P.final_wait([t_outT])
    P.barrier()
    es.close()
    return nc, P


def _consts():
    f = np.float32
    c = {}
    c["c_id"] = np.eye(128, dtype=f)
    bd = np.zeros((128, 128), f)
    bd[:64, :64] = 1.0
    bd[64:, 64:] = 1.0
    c["c_bd"] = bd
    k = np.arange(S)
    c["c_E"] = (k[None, :] // 64 == np.arange(64)[:, None]).astype(f)
    ci = np.arange(256)
    j = np.arange(64)
    ovl = ((ci[:, None] * 16 < (j[None, :] + 1) * 64) & (ci[:, None] * 16 + 32 > j[None, :] * 64)).astype(f)
    ovl[255] = 0.0
    c["c_ovl"] = np.ascontiguousarray(ovl.reshape(2, 128, 64).transpose(1, 0, 2))
    t = np.arange(S)
    qblk = t // 64
    force = (j[None, :] == 0) | (j[None, :] == qblk[:, None]) | (j[None, :] == qblk[:, None] - 1)
    selB = np.where(j[None, :] <= qblk[:, None], 1000.0 * force.astype(f), -1e30).astype(f)
    c["c_selB"] = np.ascontiguousarray(selB.reshape(32, 128, 64).transpose(1, 0, 2))
    kl = np.arange(128)[:, None]
    tl = np.arange(512)[None, :]
    mwin = np.zeros((128, 8, 512), f)
    for r in range(8):
        dlt = 512 + tl - 128 * r - kl
        mwin[:, r, :] = ((dlt >= 0) & (dlt < 512)).astype(f)
    c["c_mwin"] = mwin
    mslc = np.zeros((128, 4, 512), f)
    for r in range(4):
        mslc[:, r, :] = ((tl - 128 * r - kl) >= 0).astype(f)
    c["c_mslc"] = mslc
    mcmp = np.zeros((128, 5, 512), f)
    for v in range(5):
        mcmp[:, v, :] = ((512 * v + tl - 16 * kl - 31) >= 0).astype(f)
    c["c_mcmp"] = mcmp
    return c


def _prep_shared(inp):
    f = np.float32
    A = lambda a: np.ascontiguousarray(a, dtype=f)
    sh = dict(_consts())
    for tag, nm in (("f1", "ffn1"), ("f2", "ffn2")):
        sh[tag + "_g"] = A(inp[nm + "_norm"][0].reshape(8, 128).T)
        sh[tag + "_wg"] = A(inp[nm + "_w_gate"][0])
        sh[tag + "_wu"] = A(inp[nm + "_w_up"][0])
        sh[tag + "_wd"] = A(inp[nm + "_w_down"][0])
    sh["mix_g"] = A(inp["mix_norm"][0].reshape(8, 128).T)
    w = inp["w_in"][0]
    q = w[:, 0:512]
    blocks = []
    for r in range(4):
        blocks.append(q[:, r * 64:(r + 1) * 64])
        blocks.append(q[:, (4 + r) * 64:(5 + r) * 64])
    blocks += [w[:, 768:896], w[:, 1024:1152], w[:, 512:640], w[:, 640:768], w[:, 1304:1816],
               w[:, 896:1024], w[:, 1152:1280], w[:, 1280:1304]]
    sh["w_inB"] = A(np.concatenate(blocks, axis=1))
    assert sh["w_inB"].shape == (1024, 1816)
    sh["w_inF"] = A(w[:, 1816:3864])
    rep = lambda v: np.concatenate([v, v])
    sh["gains"] = A(np.stack([rep(inp["q_norm"][0]), rep(inp["k_norm_slc"][0]),
                              rep(inp["k_norm_win"][0]), rep(inp["k_norm_cmp"][0])], axis=1))
    sh["posk"] = A(np.concatenate([inp["cmp_pos_k"][0].T] * 2, axis=0))
    sh["posv"] = A(np.concatenate([inp["cmp_pos_v"][0].T] * 2, axis=0))
    w1 = lambda a: A(np.concatenate([a.reshape(32, 64, 256).transpose(1, 0, 2)] * 2, axis=0))
    sh["w1k"] = w1(inp["cmp_k_w1"][0])
    sh["w1v"] = w1(inp["cmp_v_w1"][0])
    w2k = inp["cmp_k_w2"][0]
    sh["w2k"] = A(np.concatenate([w2k, w2k], axis=1).reshape(2, 128, 128).transpose(1, 0, 2))
    sh["w2v"] = A(inp["cmp_v_w2"][0].reshape(2, 128, 64).transpose(1, 0, 2))
    sh["w_nsa"] = A(inp["w_nsa_proj"][0])
    sh["w_glu"] = A(inp["ssm_glu_w"][0])
    sh["w_out"] = A(inp["w_out"][0])
    sm = lambda a: A(a.reshape(16, 128).T)
    sh["s_lr"] = sm(inp["ssm_lambda_re"][0])
    sh["s_li"] = sm(inp["ssm_lambda_im"][0])
    sh["s_ls"] = sm(np.repeat(inp["ssm_log_step"][0][:, None], 64, axis=1))
    sb_ = lambda a: A(a.reshape(16, 128, 16).transpose(1, 0, 2))
    sh["s_bre"] = sb_(inp["ssm_b_re"][0])
    sh["s_bim"] = sb_(inp["ssm_b_im"][0])

    def cplace(cc):
        o = np.zeros((128, 16, 128), f)
        for g in range(32):
            sc, gl = g // 2, g % 2
            c0 = 32 * (sc % 4) + 16 * gl
            o[gl * 64:(gl + 1) * 64, sc, c0:c0 + 16] = cc[g].T
        return o
    sh["s_creT"] = cplace(inp["ssm_c_re"][0])
    sh["s_cimT"] = cplace(inp["ssm_c_im"][0])
    sh["s_d"] = A(inp["ssm_d"][0].reshape(4, 128).T)
    return sh


IN_SHAPES = {
    "c_id": [128, 128], "c_bd": [128, 128], "c_E": [64, S], "c_ovl": [128, 2, 64], "c_selB": [128, 32, 64],
    "c_mwin": [128, 8, 512], "c_mslc": [128, 4, 512], "c_mcmp": [128, 5, 512],
    "f1_g": [128, 8], "f1_wg": [D, DFF], "f1_wu": [D, DFF], "f1_wd": [DFF, D],
    "f2_g": [128, 8], "f2_wg": [D, DFF], "f2_wu": [D, DFF], "f2_wd": [DFF, D],
    "mix_g": [128, 8], "w_inB": [D, 1816], "w_inF": [D, 2048], "gains": [128, 4],
    "posk": [128, 32], "posv": [128, 32], "w1k": [128, 32, 256], "w1v": [128, 32, 256],
    "w2k": [128, 2, 128], "w2v": [128, 2, 64], "w_nsa": [512, D], "w_glu": [512, 2 * D], "w_out": [D, D],
    "s_lr": [128, 16], "s_li": [128, 16], "s_ls": [128, 16], "s_bre": [128, 16, 16], "s_bim": [128, 16, 16],
    "s_creT": [128, 16, 128], "s_cimT": [128, 16, 128], "s_d": [128, 4],
}

STAGE = "full"


def kernel(**inputs):
    inp = {k: np.asarray(v) for k, v in inputs.items()}
    x = inp["x"]
    nc, P = build(STAGE)
    sh = _prep_shared(inp)
    in_maps = []
    for b in range(NCORES):
        m = dict(sh)
        m["xT"] = np.ascontiguousarray(x[b].T)
        in_maps.append(m)
    res = run_bass_kernel_spmd(nc, in_maps, core_ids=list(range(NCORES)))
    out = np.stack([np.ascontiguousarray(r["outT"].T) for r in res.results], axis=0)
    return out.astype(np.float32)
```

```python
import math
import os
from contextlib import ExitStack

import numpy as np
import concourse.bass as bass
import concourse.mybir as mybir
from concourse.bass_utils import run_bass_kernel_spmd

F32 = mybir.dt.float32
BF16 = mybir.dt.bfloat16
AF = mybir.ActivationFunctionType
ALU = mybir.AluOpType

S = 4096
D = 1024
DFF = 2816
NCORES = 8
EPS = 1e-6
DBG = set(os.environ.get("DBGSKIP", "").split(","))


class Tk:
    __slots__ = ("name", "w", "rs", "dsem", "dcnt")

    def __init__(self, name):
        self.name = name
        self.w = []
        self.rs = []
        self.dsem = None
        self.dcnt = 0


class Prog:
    ENG = ("pe", "act", "dve", "pool", "sp")

    def __init__(self, nc, es):
        self.nc = nc
        self.es = es
        self.e = {"pe": nc.tensor, "act": nc.scalar, "dve": nc.vector,
                  "pool": nc.gpsimd, "sp": nc.sync}
        self.nsem = 0
        self.sem = {}
        self.cnt = {}
        for k in self.ENG:
            self.sem[k] = self.new_sem("e_" + k)
            self.cnt[k] = 0
        self.seen = {k: {} for k in self.ENG}
        self.dma_sems = []
        self.n_inst = 0

    def new_sem(self, name):
        self.nsem += 1
        return self.es.enter_context(self.nc.semaphore(name + "_%d" % self.nsem))

    def _wait(self, eng, sem, val):
        d = self.seen[eng]
        key = id(sem)
        if d.get(key, (None, 0))[1] >= val:
            return
        d[key] = (sem, val)
        self.e[eng].wait_ge(sem, val)

    def _deps(self, eng, reads, writes, same_engine_sync=True):
        need = {}

        def add(ev):
            sem, val, src = ev
            if src == eng and not same_engine_sync:
                return
            if isinstance(src, tuple):
                val = src[0].dsem[src[1]][1]
            k = id(sem)
            if k not in need or need[k][1] < val:
                need[k] = (sem, val)

        for t in reads:
            for ev in t.w:
                add(ev)
        for t in writes:
            for ev in t.w:
                add(ev)
            for ev in t.rs:
                add(ev)
        for sem, val in need.values():
            self._wait(eng, sem, val)

    def _roll(self, eng):
        if self.cnt[eng] >= 8000:
            self.sem[eng] = self.new_sem("e_" + eng)
            self.cnt[eng] = 0

    def op(self, eng, fn, reads=(), writes=(), sync_same=True):
        self._roll(eng)
        self._deps(eng, reads, writes, same_engine_sync=sync_same)
        ins = fn(self.e[eng])
        self.cnt[eng] += 1
        ev = (self.sem[eng], self.cnt[eng], eng)
        ins.then_inc(self.sem[eng], 1)
        self.n_inst += 1
        for t in reads:
            t.rs.append(ev)
            if len(t.rs) > 24:
                t.rs = self._compact(t.rs)
        for t in writes:
            t.w = [ev]
            t.rs = []
        return ins

    @staticmethod
    def _compact(evs):
        best = {}
        for sem, val, src in evs:
            k = id(sem)
            if k not in best or best[k][1] < val:
                best[k] = (sem, val, src)
        return list(best.values())

    def dma(self, q, out_ap, in_ap, reads=(), writes=(), chan=None):
        if chan is None:
            chan = writes[0] if writes else reads[0]
        if chan.dsem is None:
            chan.dsem = {}
        if q not in chan.dsem:
            chan.dsem[q] = [self.new_sem("d_" + chan.name + "_" + q), 0]
            self.dma_sems.append((chan, q))
        ent = chan.dsem[q]
        self._deps(q, reads, writes)
        ins = self.e[q].dma_start(out=out_ap, in_=in_ap)
        ent[1] += 16
        ins.then_inc(ent[0], 16)
        ev = (ent[0], ent[1], (chan, q))
        self.n_inst += 1
        for t in reads:
            t.rs.append(ev)
            if len(t.rs) > 24:
                t.rs = self._compact(t.rs)
        for t in writes:
            t.w = [e for e in t.w if isinstance(e[2], tuple) and e[0] is not ent[0]] + [ev]
            t.rs = []
        return ins

    def barrier(self):
        for x in self.ENG:
            for y in self.ENG:
                if y != x and self.cnt[y] > 0:
                    self._wait(x, self.sem[y], self.cnt[y])
            for t, q in self.dma_sems:
                ent = t.dsem[q]
                if ent[1] > 0:
                    self._wait(x, ent[0], ent[1])

    def final_wait(self, tks):
        for t in tks:
            for sem, val, _ in t.w:
                self._wait("sp", sem, val)


def build(stage="full", debug=False):
    nc = bass.Bass("TRN2", target_bir_lowering=False)
    es = ExitStack()
    P = Prog(nc, es)

    def dram_in(name, shape, dt=F32):
        return nc.dram_tensor(name, list(shape), dt, kind="ExternalInput").ap()

    def dram_out(name, shape, dt=F32):
        return nc.dram_tensor(name, list(shape), dt, kind="ExternalOutput").ap()

    def dram_tmp(name, shape, dt=F32):
        kind = "ExternalOutput" if debug else "Internal"
        return nc.dram_tensor(name, list(shape), dt, kind=kind).ap()

    def sb(name, shape, dt, stack=None):
        return (stack or es).enter_context(nc.sbuf_tensor("sb_" + name, list(shape), dt))

    xT = dram_in("xT", [D, S])
    outT = dram_out("outT", [D, S])
    din = {}
    for nm, shp in IN_SHAPES.items():
        din[nm] = dram_in(nm, shp)
    h1T = dram_tmp("h1T", [D, S])
    h2T = dram_tmp("h2T", [D, S])
    uT = dram_tmp("uT", [512, S], BF16)
    onsaT = dram_tmp("onsaT", [512, S], BF16)
    ygT = dram_tmp("ygT", [512, S], BF16)
    t_xT, t_outT, t_h1T, t_h2T, t_uT, t_onsaT, t_ygT = [Tk(n) for n in
        ("xT", "outT", "h1T", "h2T", "uT", "onsaT", "ygT")]

    ps = [es.enter_context(nc.psum_tensor("ps%d" % i, [128, 512], F32)) for i in range(8)]
    t_ps = [Tk("ps%d" % i) for i in range(8)]

    def MM(out, lhsT, rhs, st, sp, reads, writes):
        P.op("pe", lambda e: e.matmul(out, lhsT=lhsT, rhs=rhs, start=st, stop=sp),
             reads, writes, sync_same=False)

    def TR(out, in_, ident, reads, writes):
        P.op("pe", lambda e: e.transpose(out, in_, ident), reads, writes, sync_same=False)

    def TT2(eng, out, a, b, op, reads, writes):
        P.op(eng, lambda e: e.tensor_tensor(out=out, in0=a, in1=b, op=op), reads, writes)

    def TS(eng, out, a, s1, s2, op0, op1, reads, writes):
        if s2 is None:
            P.op(eng, lambda e: e.tensor_scalar(out=out, in0=a, scalar1=s1, scalar2=None, op0=op0),
                 reads, writes)
        else:
            P.op(eng, lambda e: e.tensor_scalar(out=out, in0=a, scalar1=s1, scalar2=s2, op0=op0, op1=op1),
                 reads, writes)

    def STT(out, a, sc, b, op0, op1, reads, writes):
        P.op("dve", lambda e: e.scalar_tensor_tensor(out=out, in0=a, scalar=sc, in1=b, op0=op0, op1=op1),
             reads, writes)

    def ACT(out, in_, func, reads, writes, bias=None, scale=None):
        kw = {}
        if bias is not None:
            kw["bias"] = bias
        if scale is not None:
            kw["scale"] = scale
        P.op("act", lambda e: e.activation(out=out, in_=in_, func=func, **kw), reads, writes)

    def CP(eng, out, in_, reads, writes):
        if eng == "act":
            P.op("act", lambda e: e.activation(out=out, in_=in_, func=AF.Identity), reads, writes)
        else:
            P.op(eng, lambda e: e.tensor_copy(out=out, in_=in_), reads, writes)

    def RECIP(out, in_, reads, writes):
        P.op("dve", lambda e: e.reciprocal(out=out, in_=in_), reads, writes)

    def MEMSET(eng, ap, val, writes):
        P.op(eng, lambda e: e.memset(ap, val), (), writes)

    ones_bf = sb("ones_bf", [128, 128], BF16)
    bd_bf = sb("bd_bf", [128, 128], BF16)
    id_bf = sb("id_bf", [128, 128], BF16)
    id_f = sb("id_f", [128, 128], F32)
    epsc = sb("epsc", [128, 1], F32)
    t_const = Tk("const")
    MEMSET("pool", ones_bf[:], 1.0, [t_const])
    MEMSET("pool", epsc[:], EPS, [t_const])
    P.dma("pool", bd_bf[:], din["c_bd"], writes=[t_const])
    P.dma("pool", id_bf[:], din["c_id"], writes=[t_const])
    P.dma("sp", id_f[:], din["c_id"], writes=[t_const])

    def norm_tile(xb, t_xb, gsb, t_g, xn, t_xn, sq, t_sq, rstd, t_rstd, TT, bank):
        pstat = ps[bank][:, 0:TT]
        for c in range(8):
            TT2("pool", sq[c % 2][:, 0:TT], xb[:, c, :], xb[:, c, :], ALU.mult, [t_xb], [t_sq[c % 2]])
            MM(pstat, ones_bf[:], sq[c % 2][:, 0:TT], c == 0, c == 7, [t_const, t_sq[c % 2]], [t_ps[bank]])
        ACT(rstd[:, 0:TT], pstat, AF.Sqrt, [t_ps[bank], t_const], [t_rstd], bias=epsc[:, 0:1], scale=1.0 / D)
        RECIP(rstd[:, 0:TT], rstd[:, 0:TT], [t_rstd], [t_rstd])
        for c in range(8):
            STT(xn[:, c, :], xb[:, c, :], gsb[:, c:c + 1], rstd[:, 0:TT], ALU.mult, ALU.mult,
                [t_xb, t_g, t_rstd], [t_xn])

    def ffn_phase(tag, src, t_src, dst, t_dst, g_dram, wg_d, wu_d, wd_d):
        TT = 256
        NT = S // TT
        with ExitStack() as fs:
            wg = sb(tag + "wg", [128, 8, DFF], BF16, fs)
            wu = sb(tag + "wu", [128, 8, DFF], BF16, fs)
            wd = sb(tag + "wd", [128, 22, D], BF16, fs)
            gsb = sb(tag + "g", [128, 8], F32, fs)
            xt = [sb(tag + "x%d" % i, [128, 8, TT], F32, fs) for i in range(2)]
            xn = sb(tag + "xn", [128, 8, TT], BF16, fs)
            sq = [sb(tag + "sq%d" % i, [128, TT], BF16, fs) for i in range(2)]
            rstd = sb(tag + "rstd", [128, TT], F32, fs)
            hh = sb(tag + "h", [128, 22, TT], BF16, fs)
            sg = [sb(tag + "sg%d" % i, [128, TT], F32, fs) for i in range(2)]
            ot = sb(tag + "ot", [128, 8, TT], F32, fs)
            t_wg = [Tk(tag + "wg%d" % k) for k in range(8)]
            t_wu = [Tk(tag + "wu%d" % k) for k in range(8)]
            t_wd = [Tk(tag + "wd%d" % k) for k in range(22)]
            t_g = Tk(tag + "g")
            t_xt = [Tk(tag + "xt%d" % i) for i in range(2)]
            t_xn = Tk(tag + "xn")
            t_sq = [Tk(tag + "sq%d" % i) for i in range(2)]
            t_rstd = Tk(tag + "rstd")
            t_hh = [Tk(tag + "hh%d" % i) for i in range(22)]
            t_sg = [Tk(tag + "sg%d" % i) for i in range(2)]
            t_ot = Tk(tag + "ot")
            src_v = src.rearrange("(c p) t -> p c t", p=128)
            dst_v = dst.rearrange("(c p) t -> p c t", p=128)

            def load_x(i):
                b = i % 2
                P.dma("sp", xt[b][:], src_v[:, :, i * TT:(i + 1) * TT], reads=[t_src], writes=[t_xt[b]])

            P.dma("sp", gsb[:], g_dram, writes=[t_g])
            load_x(0)
            wg_v = wg_d.rearrange("(k p) f -> p k f", p=128)
            wu_v = wu_d.rearrange("(k p) f -> p k f", p=128)
            wd_v = wd_d.rearrange("(k p) d -> p k d", p=128)
            ch_a, ch_b = Tk(tag + "chA"), Tk(tag + "chB")
            for k in range(8):
                P.dma("pool", wg[:, k, :], wg_v[:, k, :], writes=[t_wg[k]], chan=ch_a)
                P.dma("pool", wu[:, k, :], wu_v[:, k, :], writes=[t_wu[k]], chan=ch_a)
            for k in range(22):
                P.dma("pool", wd[:, k, :], wd_v[:, k, :], writes=[t_wd[k]], chan=ch_b)
            for i in range(NT):
                b = i % 2
                if i + 1 < NT:
                    load_x(i + 1)
                xb = xt[b]
                norm_tile(xb, t_xt[b], gsb, t_g, xn, t_xn, sq, t_sq, rstd, t_rstd, TT, 6)
                for fc in range(22):
                    pb = fc % 2
                    pg = ps[0 + pb][:, 0:TT]
                    pu = ps[2 + pb][:, 0:TT]
                    for k in range(8):
                        MM(pg, wg[:, k, fc * 128:(fc + 1) * 128], xn[:, k, :], k == 0, k == 7,
                           [t_wg[k], t_xn], [t_ps[0 + pb]])
                    for k in range(8):
                        MM(pu, wu[:, k, fc * 128:(fc + 1) * 128], xn[:, k, :], k == 0, k == 7,
                           [t_wu[k], t_xn], [t_ps[2 + pb]])
                    ACT(sg[pb][:], pg, AF.Silu, [t_ps[0 + pb]], [t_sg[pb]])
                    TT2("dve", hh[:, fc, :], pu, sg[pb][:], ALU.mult, [t_ps[2 + pb], t_sg[pb]], [t_hh[fc]])
                for dc in range(8):
                    pb = dc % 2
                    py = ps[4 + pb][:, 0:TT]
                    for fc in range(22):
                        MM(py, wd[:, fc, dc * 128:(dc + 1) * 128], hh[:, fc, :], fc == 0, fc == 21,
                           [t_wd[fc], t_hh[fc]], [t_ps[4 + pb]])
                    STT(ot[:, dc, :], py, 0.5, xb[:, dc, :], ALU.mult, ALU.add,
                        [t_ps[4 + pb], t_xt[b]], [t_ot])
                P.dma("sp", dst_v[:, :, i * TT:(i + 1) * TT], ot[:], reads=[t_ot], writes=[t_dst], chan=t_ot)
            P.barrier()

    def mixer_attention():
        with ExitStack() as ms:
            Qs = [sb("Qs%d" % h, [128, S], BF16, ms) for h in range(8)]
            Ks = [sb("Ks%d" % g, [128, S], BF16, ms) for g in range(2)]
            Kw = sb("Kw", [128, S], BF16, ms)
            Vs = sb("Vs", [128, 32, 2, 72], BF16, ms)
            Vw = sb("Vw", [128, 32, 2, 72], BF16, ms)
            gl = sb("gl", [128, 32, 24], F32, ms)
            gains = sb("gains", [128, 4], F32, ms)
            gq8 = sb("gq8", [128, 1], F32, ms)
            kcT = sb("kcT", [128, 256], BF16, ms)
            rcmp = [sb("rcmp%d" % g, [128, 2, 129], BF16, ms) for g in range(2)]
            t_Q = Tk("Qq")
            t_Qm = [Tk("Qm%d" % i) for i in range(8)]
            t_Ks, t_Kw, t_Vs, t_Vw, t_gl, t_gains, t_kcT, t_rcmp = [Tk(n) for n in
                ("Ks", "Kw", "Vs", "Vw", "gl", "gains", "kcT", "rcmp")]
            P.dma("sp", gains[:], din["gains"], writes=[t_gains])
            TS("dve", gq8[:], gains[:, 0:1], 0.125, None, ALU.mult, None, [t_gains], [t_gains])
            if "E" not in DBG:
                P.dma("pool", Ks[0][64:128, :], din["c_E"], writes=[t_Ks])
                P.dma("pool", Ks[1][0:64, :], din["c_E"], writes=[t_Ks])
            if "V1" not in DBG:
                MEMSET("pool", Vs[:, :, :, 64:65], 1.0, [t_Vs])
                MEMSET("pool", Vw[:, :, :, 64:65], 1.0, [t_Vw])

            with ExitStack() as bs:
                TT = 512
                sq = [sb("Bsq%d" % i, [128, TT], BF16, bs) for i in range(2)]
                rs = sb("Brs", [128, TT], F32, bs)
                KcA = sb("KcA", [128, S], BF16, bs)
                KcB = sb("KcB", [128, S], BF16, bs)
                VcA = sb("VcA", [128, S], BF16, bs)
                VcB = sb("VcB", [128, S], BF16, bs)
                b2 = ExitStack()
                wB = sb("wB", [128, 8, 1816], BF16, b2)
                mg = sb("mg", [128, 8], F32, b2)
                xt = sb("Bx", [128, 8, TT], F32, b2)
                xn = sb("Bxn", [128, 8, TT], BF16, b2)
                rstd = sb("Brstd", [128, TT], F32, b2)
                ust = sb("Bust", [128, 4, TT], BF16, b2)
                posk = sb("posk", [128, 32], F32, b2)
                posv = sb("posv", [128, 32], F32, b2)
                vstg = sb("vstg", [128, 256], BF16, b2)
                t_vstg = Tk("vstg")
                t_wB, t_mg, t_xt, t_xn, t_rstd, t_rs, t_ust, t_Kc, t_pos = [Tk(n) for n in
                    ("wB", "mg", "Bxt", "Bxn", "Brstd", "Brs", "Bust", "Kc", "pos")]
                t_sq = [Tk("Bsq0"), Tk("Bsq1")]
                wB_v = din["w_inB"].rearrange("(k p) f -> p k f", p=128)
                for k in range(8):
                    P.dma("pool", wB[:, k, :], wB_v[:, k, :], writes=[t_wB])
                P.dma("sp", mg[:], din["mix_g"], writes=[t_mg])
                P.dma("sp", posk[:], din["posk"], writes=[t_pos])
                P.dma("sp", posv[:], din["posv"], writes=[t_pos])
                h1_v = h1T.rearrange("(c p) t -> p c t", p=128)
                uT_v = uT.rearrange("(c p) t -> p c t", p=128)
                for i in range(S // TT):
                    tsl = slice(i * TT, (i + 1) * TT)
                    P.dma("sp", xt[:], h1_v[:, :, tsl], reads=[t_h1T], writes=[t_xt])
                    norm_tile(xt, t_xt, mg, t_mg, xn, t_xn, sq, t_sq, rstd, t_rstd, TT, 6)
                    pssb = [3, 7]

                    def main(ch):
                        pb = ch % 3
                        pq = ps[pb][:, :]
                        for k in range(8):
                            MM(pq, wB[:, k, ch * 128:(ch + 1) * 128], xn[:, k, :], k == 0, k == 7,
                               [t_wB, t_xn], [t_ps[pb]])
                        if ch < 6:
                            ACT(sq[ch % 2][:], pq, AF.Square, [t_ps[pb]], [t_sq[ch % 2]])

                    def finish(ch):
                        pb = ch % 3
                        pq = ps[pb][:, :]
                        if ch < 6:
                            sb_ = pssb[ch % 2]
                            pss = ps[sb_][:, :]
                            MM(pss, bd_bf[:], sq[ch % 2][:], True, True, [t_const, t_sq[ch % 2]], [t_ps[sb_]])
                            ACT(rs[:], pss, AF.Ln, [t_ps[sb_], t_const], [t_rs], bias=epsc[:, 0:1], scale=1.0 / 64)
                            ACT(rs[:], rs[:], AF.Exp, [t_rs], [t_rs], scale=-0.5)
                            if ch < 4:
                                STT(Qs[ch][0:64, tsl], pq[0:64, :], gq8[0:64, 0:1], rs[0:64, :], ALU.mult, ALU.mult,
                                    [t_ps[pb], t_gains, t_rs], [t_Q])
                                STT(Qs[4 + ch][64:128, tsl], pq[64:128, :], gq8[64:128, 0:1], rs[64:128, :],
                                    ALU.mult, ALU.mult, [t_ps[pb], t_gains, t_rs], [t_Q])
                            elif ch == 4:
                                STT(Ks[0][0:64, tsl], pq[0:64, :], gains[0:64, 1:2], rs[0:64, :], ALU.mult, ALU.mult,
                                    [t_ps[pb], t_gains, t_rs], [t_Ks])
                                STT(Ks[1][64:128, tsl], pq[64:128, :], gains[64:128, 1:2], rs[64:128, :],
                                    ALU.mult, ALU.mult, [t_ps[pb], t_gains, t_rs], [t_Ks])
                            else:
                                STT(Kw[:, tsl], pq, gains[:, 2:3], rs[:], ALU.mult, ALU.mult,
                                    [t_ps[pb], t_gains, t_rs], [t_Kw])
                        elif ch < 8:
                            pos = posk if ch == 6 else posv
                            A_, B_ = (KcA, KcB) if ch == 6 else (VcA, VcB)
                            pq3 = pq.rearrange("p (i s) -> p i s", s=16)
                            TT2("dve", A_[:, tsl].rearrange("p (i s) -> p i s", s=16), pq3,
                                pos[:, 0:16].unsqueeze(1).to_broadcast([128, TT // 16, 16]), ALU.add,
                                [t_ps[pb], t_pos], [t_Kc])
                            TT2("dve", B_[:, tsl].rearrange("p (i s) -> p i s", s=16), pq3,
                                pos[:, 16:32].unsqueeze(1).to_broadcast([128, TT // 16, 16]), ALU.add,
                                [t_ps[pb], t_pos], [t_Kc])
                        else:
                            CP("act", ust[:, ch - 8, :], pq, [t_ps[pb]], [t_ust])

                    for ch in range(13):
                        if ch < 12:
                            main(ch)
                        if ch >= 1:
                            finish(ch - 1)
                    P.dma("sp", uT_v[:, :, tsl], ust[:], reads=[t_ust], writes=[t_uT], chan=t_ust)
                    for sub in range(4):
                        if "TM" in DBG:
                            break
                        pb = 4 + sub % 2
                        pt = ps[pb][:, 0:280]
                        for k in range(8):
                            MM(pt, xn[:, k, sub * 128:(sub + 1) * 128], wB[:, k, 1536:1816], k == 0, k == 7,
                               [t_wB, t_xn], [t_ps[pb]])
                        kt = i * 4 + sub
                        CP("act", vstg[:], ps[pb][:, 0:256], [t_ps[pb]], [t_vstg])
                        for g_ in range(2):
                            CP("pool", Vs[:, kt, g_, 0:64], vstg[:, g_ * 64:(g_ + 1) * 64], [t_vstg], [t_Vs])
                            CP("pool", Vw[:, kt, g_, 0:64], vstg[:, 128 + g_ * 64:128 + (g_ + 1) * 64], [t_vstg], [t_Vw])
                        if "cpG" not in DBG:
                            CP("act", gl[:, kt, :], ps[pb][:, 256:280], [t_ps[pb]], [t_gl])
                P.barrier()
                b2.close()
                if stage == "B":
                    return

                w1k = sb("w1k", [128, 32, 256], BF16, bs)
                w1v = sb("w1v", [128, 32, 256], BF16, bs)
                w2k = sb("w2k", [128, 2, 128], BF16, bs)
                w2v = sb("w2v", [128, 2, 64], BF16, bs)
                ovl = sb("ovl", [128, 2, 64], BF16, bs)
                hx = sb("hx", [128, 256], F32, bs)
                hu = sb("hu", [128, 256], F32, bs)
                hid = [[sb("hid%d%d" % (a, b), [128, 256], BF16, bs) for b in range(2)] for a in range(2)]
                t_cw, t_hx, t_hu = Tk("cw"), Tk("hx"), Tk("hu")
                t_hid = [[Tk("hid%d%d" % (a, b)) for b in range(2)] for a in range(2)]
                P.dma("pool", w1k[:], din["w1k"], writes=[t_cw])
                P.dma("pool", w1v[:], din["w1v"], writes=[t_cw])
                P.dma("pool", w2k[:], din["w2k"], writes=[t_cw])
                P.dma("pool", w2v[:], din["w2v"], writes=[t_cw])
                P.dma("pool", ovl[:], din["c_ovl"], writes=[t_cw])
                MEMSET("pool", kcT[:], 0.0, [t_kcT])
                for g in range(2):
                    MEMSET("pool", rcmp[g][:, :, 128:129], 1.0, [t_rcmp])
                    CP("pool", rcmp[g][:, :, 64:128], ovl[:], [t_cw], [t_rcmp])
                for a in range(2):
                    for b in range(2):
                        MEMSET("pool", hid[a][b][:], 0.0, [t_hid[a][b]])
                for kind in range(2):
                    w1 = w1k if kind == 0 else w1v
                    A_, B_ = (KcA, KcB) if kind == 0 else (VcA, VcB)
                    A3 = A_[:].rearrange("p (i s) -> p i s", s=16)
                    B3 = B_[:].rearrange("p (i s) -> p i s", s=16)
                    for g in range(2):
                        R = slice(64 * g, 64 * g + 64)
                        for mc in range(2):
                            ph = ps[mc][:, 0:255]
                            for j in range(32):
                                rhs = A3[R, 0:255, j] if j < 16 else B3[R, 1:256, j - 16]
                                MM(ph, w1[R, j, mc * 128:(mc + 1) * 128], rhs, j == 0, j == 31,
                                   [t_cw, t_Kc], [t_ps[mc]])
                            CP("act", hx[:, 0:255], ph, [t_ps[mc]], [t_hx])
                            TT2("dve", hu[:, 0:255], hx[:, 0:255], hx[:, 0:255], ALU.mult, [t_hx], [t_hu])
                            TS("dve", hu[:, 0:255], hu[:, 0:255], 0.044715, 1.0, ALU.mult, ALU.add, [t_hu], [t_hu])
                            TT2("dve", hu[:, 0:255], hu[:, 0:255], hx[:, 0:255], ALU.mult, [t_hu, t_hx], [t_hu])
                            ACT(hu[:, 0:255], hu[:, 0:255], AF.Sigmoid, [t_hu], [t_hu], scale=1.5957691216057308)
                            TT2("dve", hid[g][mc][:, 0:255], hu[:, 0:255], hx[:, 0:255], ALU.mult,
                                [t_hu, t_hx], [t_hid[g][mc]])
                        if kind == 0:
                            pk = ps[2][:, 0:256]
                            for mc in range(2):
                                MM(pk, w2k[:, mc, :], hid[g][mc][:], mc == 0, mc == 1,
                                   [t_cw, t_hid[g][mc]], [t_ps[2]])
                            sqn = sq[0]
                            ACT(sqn[:, 0:256], pk, AF.Square, [t_ps[2]], [t_sq[0]])
                            pss = ps[3][:, 0:256]
                            MM(pss, bd_bf[:], sqn[:, 0:256], True, True, [t_const, t_sq[0]], [t_ps[3]])
                            ACT(rs[:, 0:256], pss, AF.Sqrt, [t_ps[3], t_const], [t_rs], bias=epsc[:, 0:1], scale=1.0 / 64)
                            RECIP(rs[:, 0:256], rs[:, 0:256], [t_rs], [t_rs])
                            STT(kcT[R, 0:255], pk[R, 0:255], gains[R, 3:4], rs[R, 0:255], ALU.mult, ALU.mult,
                                [t_ps[2], t_gains, t_rs], [t_kcT])
                        else:
                            for it in range(2):
                                pv = ps[2 + it][:, 0:64]
                                for mc in range(2):
                                    MM(pv, hid[g][mc][:, it * 128:(it + 1) * 128], w2v[:, mc, :], mc == 0, mc == 1,
                                       [t_cw, t_hid[g][mc]], [t_ps[2 + it]])
                                CP("act", rcmp[g][:, it, 0:64], pv, [t_ps[2 + it]], [t_rcmp])
                P.barrier()

            if stage == "BC":
                return
            with ExitStack() as ds:
                mwin = sb("mwin", [128, 8, 512], BF16, ds)
                mslc = sb("mslc", [128, 4, 512], BF16, ds)
                mcmp = sb("mcmp", [128, 5, 512], BF16, ds)
                selB = sb("selB", [128, 32, 64], F32, ds)
                gs = sb("gs", [128, 32, 24], F32, ds)
                PT = sb("PT", [128, 32, 512], BF16, ds)
                Pc = [sb("Pc%d" % i, [128, 512], BF16, ds) for i in range(2)]
                onsa = sb("onsa", [128, 4, 512], F32, ds)
                onsab = sb("onsab", [128, 4, 512], BF16, ds)
                oTs = sb("oTs", [128, 4, 512], BF16, ds)
                imp = [sb("imp%d" % g, [128, 4, 64], F32, ds) for g in range(2)]
                scr = sb("scr", [128, 64], F32, ds)
                wk = sb("wk", [128, 64], F32, ds)
                m8 = sb("m8", [128, 16], F32, ds)
                NM = sb("NM", [128, 4, 128], BF16, ds)
                sm = sb("sm", [128, 8], F32, ds)
                t_msk, t_selB, t_gs, t_onsa, t_onsab, t_oTs, t_scr, t_wk, t_m8, t_NM, t_sm = [Tk(n) for n in
                    ("msk", "selB", "gs", "onsa", "onsab", "oTs", "scr", "wk", "m8", "NM", "sm")]
                t_PT = [Tk("PT%d" % i) for i in range(32)]
                t_Pc = [Tk("Pc0"), Tk("Pc1")]
                t_imp = [Tk("imp0"), Tk("imp1")]
                P.dma("pool", mwin[:], din["c_mwin"], writes=[t_msk])
                P.dma("pool", mslc[:], din["c_mslc"], writes=[t_msk])
                P.dma("pool", mcmp[:], din["c_mcmp"], writes=[t_msk])
                P.dma("sp", selB[:], din["c_selB"], writes=[t_selB])
                ACT(gs[:], gl[:], AF.Sigmoid, [t_gl], [t_gs])
                onsaT_v = onsaT.rearrange("(c p) t -> p c t", p=128)
                BIG = 30000.0
                sbank = [0, 1, 2]
                sctr = [0]

                def next_sbank():
                    b = sbank[sctr[0] % 3]
                    sctr[0] += 1
                    return b

                def evac(po, sub, h, br, kt, first, imp_g=None, r=0):
                    dcol = 128 if br == 0 else 64
                    c0 = (h * 3 + br) % 4 * 2
                    TS("dve", sm[:, c0:c0 + 1], po[:, dcol:dcol + 1], 1e-30, None, ALU.max, None, [t_ps_cur[0]], [t_sm])
                    RECIP(sm[:, c0:c0 + 1], sm[:, c0:c0 + 1], [t_sm], [t_sm])
                    if br == 0:
                        if r == 0:
                            TS("dve", imp_g[:, sub, :], po[:, 64:128], sm[:, c0:c0 + 1], None, ALU.mult, None,
                               [t_ps_cur[0], t_sm], [t_imp[h // 4]])
                        else:
                            STT(imp_g[:, sub, :], po[:, 64:128], sm[:, c0:c0 + 1], imp_g[:, sub, :], ALU.mult, ALU.add,
                                [t_ps_cur[0], t_sm, t_imp[h // 4]], [t_imp[h // 4]])
                    TT2("dve", sm[:, c0 + 1:c0 + 2], sm[:, c0:c0 + 1], gs[:, kt, br * 8 + h:br * 8 + h + 1], ALU.mult,
                        [t_sm, t_gs], [t_sm])
                    dst = onsa[:, sub, h * 64:(h + 1) * 64]
                    if first:
                        TS("dve", dst, po[:, 0:64], sm[:, c0 + 1:c0 + 2], None, ALU.mult, None,
                           [t_ps_cur[0], t_sm], [t_onsa])
                    else:
                        STT(dst, po[:, 0:64], sm[:, c0 + 1:c0 + 2], dst, ALU.mult, ALU.add,
                            [t_ps_cur[0], t_sm, t_onsa], [t_onsa])

                t_ps_cur = [None]
                for Q in range(8):
                    qsl = slice(Q * 512, (Q + 1) * 512)
                    cts = [0] if Q < 4 else [0, 1]
                    for h in range(8):
                        g, r = h // 4, h % 4
                        R = slice(64 * g, 64 * g + 64)
                        for ct in cts:
                            b = next_sbank()
                            MM(ps[b][:, :], kcT[R, ct * 128:(ct + 1) * 128], Qs[h][R, qsl], True, True,
                               [t_kcT, t_Q], [t_ps[b]])
                            ACT(Pc[ct][:], ps[b][:, :], AF.Exp, [t_ps[b]], [t_Pc[ct]])
                            v = Q - 4 * ct
                            if v <= 4:
                                TT2("dve", Pc[ct][:], Pc[ct][:], mcmp[:, v, :], ALU.mult, [t_Pc[ct], t_msk], [t_Pc[ct]])
                        for sub in range(4):
                            bk = 3 + sub // 2
                            po = ps[bk][:, (sub % 2) * 129:(sub % 2) * 129 + 129]
                            for ci, ct in enumerate(cts):
                                MM(po, Pc[ct][:, sub * 128:(sub + 1) * 128], rcmp[g][:, ct, :], ci == 0, ci == len(cts) - 1,
                                   [t_Pc[ct], t_rcmp], [t_ps[bk]])
                        for sub in range(4):
                            bk = 3 + sub // 2
                            po = ps[bk][:, (sub % 2) * 129:(sub % 2) * 129 + 129]
                            t_ps_cur[0] = t_ps[bk]
                            evac(po, sub, h, 0, Q * 4 + sub, True, imp[g], r)
                    for sub in range(4):
                        kt = Q * 4 + sub
                        for g in range(2):
                            TT2("dve", scr[:], imp[g][:, sub, :], selB[:, kt, :], ALU.add, [t_imp[g], t_selB], [t_scr])
                            P.op("dve", lambda e: e.max(out=m8[:, 0:8], in_=scr[:]), [t_scr], [t_m8])
                            P.op("dve", lambda e: e.match_replace(out=wk[:], in_to_replace=m8[:, 0:8],
                                                                   in_values=scr[:], imm_value=-3.0e38),
                                 [t_scr, t_m8], [t_wk])
                            P.op("dve", lambda e: e.max(out=m8[:, 8:16], in_=wk[:]), [t_wk], [t_m8])
                            cs = slice(64, 128) if g == 0 else slice(0, 64)
                            TS("dve", NM[:, sub, cs], scr[:], m8[:, 15:16], -BIG, ALU.is_lt, ALU.mult,
                               [t_scr, t_m8], [t_NM])
                    pT = ps[7][:].bitcast(BF16)
                    for sub in range(4):
                        TR(pT[:, sub * 128:(sub + 1) * 128], NM[:, sub, :], id_bf[:], [t_NM, t_const], [t_ps[7]])
                    for h in range(8):
                        R2 = slice(64, 128) if h < 4 else slice(0, 64)
                        CP("act" if h % 2 == 0 else "dve", Qs[h][R2, qsl], pT[R2, 0:512], [t_ps[7]], [t_Qm[Q]])
                    for h in range(8):
                        g = h // 4
                        R = slice(64 * g, 64 * g + 64)
                        nkt = 4 * Q + 4
                        for kt in range(nkt):
                            b = next_sbank()
                            MM(ps[b][:, :], Ks[g][:, kt * 128:(kt + 1) * 128], Qs[h][:, qsl], True, True,
                               [t_Ks, t_Q, t_Qm[Q]], [t_ps[b]])
                            ACT(PT[:, kt, :], ps[b][:, :], AF.Exp, [t_ps[b]], [t_PT[kt]])
                            if kt >= 4 * Q:
                                TT2("dve", PT[:, kt, :], PT[:, kt, :], mslc[:, kt - 4 * Q, :], ALU.mult,
                                    [t_PT[kt], t_msk], [t_PT[kt]])
                        bk = 5
                        for sub in range(4):
                            po = ps[bk][:, sub * 65:sub * 65 + 65]
                            last = 4 * Q + sub
                            for kt in range(last + 1):
                                MM(po, PT[:, kt, sub * 128:(sub + 1) * 128], Vs[:, kt, g, 0:65], kt == 0, kt == last,
                                   [t_PT[kt], t_Vs], [t_ps[bk]])
                        t_ps_cur[0] = t_ps[bk]
                        for sub in range(4):
                            evac(ps[bk][:, sub * 65:sub * 65 + 65], sub, h, 1, Q * 4 + sub, False)
                        rr = [r_ for r_ in range(8) if 4 * Q - 4 + r_ >= 0]
                        for r_ in rr:
                            kt = 4 * Q - 4 + r_
                            b = next_sbank()
                            MM(ps[b][:, :], Kw[R, kt * 128:(kt + 1) * 128], Qs[h][R, qsl], True, True,
                               [t_Kw, t_Q], [t_ps[b]])
                            ACT(PT[:, r_, :], ps[b][:, :], AF.Exp, [t_ps[b]], [t_PT[r_]])
                            TT2("dve", PT[:, r_, :], PT[:, r_, :], mwin[:, r_, :], ALU.mult, [t_PT[r_], t_msk], [t_PT[r_]])
                        bk = 6
                        for sub in range(4):
                            po = ps[bk][:, sub * 65:sub * 65 + 65]
                            rs_ = [r_ for r_ in rr if sub <= r_ <= sub + 4]
                            for ci, r_ in enumerate(rs_):
                                kt = 4 * Q - 4 + r_
                                MM(po, PT[:, r_, sub * 128:(sub + 1) * 128], Vw[:, kt, g, 0:65], ci == 0, ci == len(rs_) - 1,
                                   [t_PT[r_], t_Vw], [t_ps[bk]])
                        t_ps_cur[0] = t_ps[bk]
                        for sub in range(4):
                            evac(ps[bk][:, sub * 65:sub * 65 + 65], sub, h, 2, Q * 4 + sub, False)
                    CP("act", onsab[:], onsa[:], [t_onsa], [t_onsab])
                    for fc in range(4):
                        pT2 = ps[7][:].bitcast(BF16)
                        for sub in range(4):
                            TR(pT2[:, sub * 128:(sub + 1) * 128], onsab[:, sub, fc * 128:(fc + 1) * 128], id_bf[:],
                               [t_onsab, t_const], [t_ps[7]])
                        CP("dve" if fc % 2 else "act", oTs[:, fc, :], pT2[:, 0:512], [t_ps[7]], [t_oTs])
                    P.dma("sp", onsaT_v[:, :, qsl], oTs[:], reads=[t_oTs], writes=[t_onsaT], chan=t_oTs)
                P.barrier()

    def ssm_phase():
        PI = math.pi
        with ExitStack() as fs:
            CT = sb("CT", [128, 16, 512], F32, fs)
            ST = sb("ST", [128, 16, 512], F32, fs)
            rhoF = sb("rhoF", [128, 16, 512], F32, fs)
            BBrT = sb("BBrT", [128, 16, 128], BF16, fs)
            BBiT = sb("BBiT", [128, 16, 128], BF16, fs)
            CreT = sb("CreT", [128, 16, 128], BF16, fs)
            CimT = sb("CimT", [128, 16, 128], BF16, fs)
            Dg = sb("Dg", [128, 4, 128], BF16, fs)
            dsk = sb("dsk", [128, 4], F32, fs)
            rho = sb("rho", [128, 16], F32, fs)
            car = [[sb("car%d%d" % (a, b), [128, 16], F32, fs) for b in range(2)] for a in range(2)]
            t_tab, t_bb, t_cc, t_dg, t_rho = [Tk(n) for n in ("tab", "bb", "cc", "dg", "rho")]
            t_car = [[Tk("car%d_%d" % (a, sc)) for sc in range(16)] for a in range(2)]
            P.dma("pool", CreT[:], din["s_creT"], writes=[t_cc])
            P.dma("pool", CimT[:], din["s_cimT"], writes=[t_cc])
            P.dma("sp", dsk[:], din["s_d"], writes=[t_dg])
            for oc in range(4):
                TS("dve", Dg[:, oc, :], id_f[:], dsk[:, oc:oc + 1], None, ALU.mult, None, [t_const, t_dg], [t_dg])
            nCreT = sb("nCreT", [128, 16, 128], BF16, fs)
            nCimT = sb("nCimT", [128, 16, 128], BF16, fs)
            TS("dve", nCreT[:], CreT[:], -1.0, None, ALU.mult, None, [t_cc], [t_cc])
            TS("dve", nCimT[:], CimT[:], -1.0, None, ALU.mult, None, [t_cc], [t_cc])
            with ExitStack() as ss:
                lr = sb("lr", [128, 16], F32, ss)
                li = sb("li", [128, 16], F32, ss)
                stp = sb("stp", [128, 16], F32, ss)
                th = sb("th", [128, 16], F32, ss)
                kk = sb("kk", [128, 16], F32, ss)
                r1 = sb("r1", [128, 16], F32, ss)
                r2 = sb("r2", [128, 16], F32, ss)
                s1 = sb("s1", [128, 16], F32, ss)
                c1 = sb("c1", [128, 16], F32, ss)
                w = [sb("w%d" % i, [128, 16], F32, ss) for i in range(6)]
                bre = sb("bre", [128, 16, 16], F32, ss)
                bim = sb("bim", [128, 16, 16], F32, ss)
                bbr = sb("bbr", [128, 16, 16], F32, ss)
                bbi = sb("bbi", [128, 16, 16], F32, ss)
                bt = [sb("bt%d" % i, [128, 16, 16], F32, ss) for i in range(2)]
                Zr = sb("Zr", [128, 16, 128], F32, ss)
                Zi = sb("Zi", [128, 16, 128], F32, ss)
                tA = sb("tA", [128, 16, 256], F32, ss)
                tB = sb("tB", [128, 16, 256], F32, ss)
                t_p = Tk("ssm_par")
                t_Z = Tk("Z")
                P.dma("sp", lr[:], din["s_lr"], writes=[t_p])
                P.dma("sp", li[:], din["s_li"], writes=[t_p])
                P.dma("sp", stp[:], din["s_ls"], writes=[t_p])
                P.dma("sp", bre[:], din["s_bre"], writes=[t_p])
                P.dma("sp", bim[:], din["s_bim"], writes=[t_p])
                pp = [t_p]
                ACT(stp[:], stp[:], AF.Exp, pp, pp)
                TT2("dve", th[:], li[:], stp[:], ALU.mult, pp, pp)
                TT2("dve", w[0][:], lr[:], stp[:], ALU.mult, pp, pp)
                ACT(rho[:], w[0][:], AF.Exp, pp, [t_rho])
                MEMSET("dve", kk[:], 0.0, pp)
                for m in range(1, 9):
                    STT(kk[:], th[:], (2 * m - 1) * PI, kk[:], ALU.is_gt, ALU.add, pp, pp)
                STT(r1[:], kk[:], -2.0 * PI, th[:], ALU.mult, ALU.add, pp, pp)
                TS("dve", r2[:], r1[:], PI / 2, None, ALU.add, None, pp, pp)
                TS("dve", w[1][:], r2[:], PI, None, ALU.is_gt, None, pp, pp)
                STT(r2[:], w[1][:], -2.0 * PI, r2[:], ALU.mult, ALU.add, pp, pp)
                ACT(s1[:], r1[:], AF.Sin, pp, pp)
                ACT(c1[:], r2[:], AF.Sin, pp, pp)
                ar, ai, arm1, den, cr, cim = w
                TT2("dve", ar[:], rho[:], c1[:], ALU.mult, pp + [t_rho], pp)
                TT2("dve", ai[:], rho[:], s1[:], ALU.mult, pp + [t_rho], pp)
                TS("dve", arm1[:], ar[:], -1.0, None, ALU.add, None, pp, pp)
                TT2("dve", den[:], lr[:], lr[:], ALU.mult, pp, pp)
                TT2("dve", kk[:], li[:], li[:], ALU.mult, pp, pp)
                TT2("dve", den[:], den[:], kk[:], ALU.add, pp, pp)
                RECIP(den[:], den[:], pp, pp)
                TT2("dve", cr[:], arm1[:], lr[:], ALU.mult, pp, pp)
                TT2("dve", kk[:], ai[:], li[:], ALU.mult, pp, pp)
                TT2("dve", cr[:], cr[:], kk[:], ALU.add, pp, pp)
                TT2("dve", cr[:], cr[:], den[:], ALU.mult, pp, pp)
                TT2("dve", cim[:], ai[:], lr[:], ALU.mult, pp, pp)
                TT2("dve", kk[:], arm1[:], li[:], ALU.mult, pp, pp)
                TT2("dve", cim[:], cim[:], kk[:], ALU.subtract, pp, pp)
                TT2("dve", cim[:], cim[:], den[:], ALU.mult, pp, pp)
                crb = cr[:].unsqueeze(2).to_broadcast([128, 16, 16])
                cib = cim[:].unsqueeze(2).to_broadcast([128, 16, 16])
                TT2("dve", bt[0][:], bre[:], crb, ALU.mult, pp, pp)
                TT2("dve", bt[1][:], bim[:], cib, ALU.mult, pp, pp)
                TT2("dve", bbr[:], bt[0][:], bt[1][:], ALU.subtract, pp, pp)
                TT2("dve", bt[0][:], bim[:], crb, ALU.mult, pp, pp)
                TT2("dve", bt[1][:], bre[:], cib, ALU.mult, pp, pp)
                TT2("dve", bbi[:], bt[0][:], bt[1][:], ALU.add, pp, pp)
                MEMSET("pool", Zr[:], 0.0, [t_Z])
                MEMSET("pool", Zi[:], 0.0, [t_Z])
                for sc in range(16):
                    c0 = 32 * (sc % 4)
                    for Z_, bb_ in ((Zr, bbr), (Zi, bbi)):
                        CP("dve", Z_[0:64, sc, c0:c0 + 16], bb_[0:64, sc, :], pp + [t_Z], [t_Z])
                        CP("dve", Z_[64:128, sc, c0 + 16:c0 + 32], bb_[64:128, sc, :], pp + [t_Z], [t_Z])
                for sc in range(16):
                    for bi_, (Z_, BT_) in enumerate(((Zr, BBrT), (Zi, BBiT))):
                        bk = (sc * 2 + bi_) % 4
                        TR(ps[bk][:, 0:128], Z_[:, sc, :], id_f[:], [t_Z, t_const], [t_ps[bk]])
                        CP("act", BT_[:, sc, :], ps[bk][:, 0:128], [t_ps[bk]], [t_bb])
                CP("dve", CT[:, :, 0:1], c1[:].unsqueeze(2), pp, [t_tab])
                CP("dve", ST[:, :, 0:1], s1[:].unsqueeze(2), pp, [t_tab])
                n = 1
                while n < 512:
                    cn = CT[:, :, n - 1:n].to_broadcast([128, 16, n])
                    sn = ST[:, :, n - 1:n].to_broadcast([128, 16, n])
                    tt_ = [t_tab]
                    TT2("dve", tA[:, :, 0:n], CT[:, :, 0:n], cn, ALU.mult, tt_, pp)
                    TT2("pool", tB[:, :, 0:n], ST[:, :, 0:n], sn, ALU.mult, tt_, [t_Z])
                    TT2("dve", CT[:, :, n:2 * n], tA[:, :, 0:n], tB[:, :, 0:n], ALU.subtract, pp + [t_Z], tt_)
                    TT2("dve", tA[:, :, 0:n], ST[:, :, 0:n], cn, ALU.mult, tt_, pp)
                    TT2("pool", tB[:, :, 0:n], CT[:, :, 0:n], sn, ALU.mult, tt_, [t_Z])
                    TT2("dve", ST[:, :, n:2 * n], tA[:, :, 0:n], tB[:, :, 0:n], ALU.add, pp + [t_Z], tt_)
                    n *= 2
                MEMSET("pool", rhoF[:], 1.0, [t_rho])
                for sc in range(16):
                    TS("pool", rhoF[:, sc, :], rhoF[:, sc, :], rho[:, sc:sc + 1], None, ALU.mult, None, [t_rho], [t_rho])
                for a in range(2):
                    for b in range(2):
                        MEMSET("pool", car[a][b][:], 0.0, [t_car[a][sc] for sc in range(16)])
                P.barrier()

            NB = 2
            uTt = [sb("uTt%d" % i, [128, 4, 512], BF16, fs) for i in range(3)]
            ygst = sb("ygst", [128, 4, 512], BF16, fs)
            f = [[sb("sf%d_%d" % (i, b), [128, 512], F32 if i < 8 else BF16, fs) for b in range(NB)] for i in range(12)]
            gx = sb("gx", [128, 512], F32, fs)
            gu = sb("gu", [128, 512], F32, fs)
            cu = sb("cu", [128, 4], F32, fs)
            t_u = [Tk("uTt%d" % i) for i in range(3)]
            t_f = [[Tk("sf%d_%d" % (i, b)) for b in range(NB)] for i in range(12)]
            t_gx, t_gu, t_ygst, t_cu = Tk("gx"), Tk("gu"), Tk("ygst"), Tk("cu")
            uT_v = uT.rearrange("(c p) t -> p c t", p=128)
            yg_v = ygT.rearrange("(c p) t -> p c t", p=128)
            P.dma("sp", uTt[0][:], uT_v[:, :, 0:512], reads=[t_uT], writes=[t_u[0]])
            tiles = [(tt, oc, sci) for tt in range(8) for oc in range(4) for sci in range(4)]

            def S1(i):
                tt, oc, sci = tiles[i]
                sc, b, ub = 4 * oc + sci, i % NB, tt % 3
                if oc == 0 and sci == 0 and tt + 1 < 8:
                    nb_ = (tt + 1) % 3
                    P.dma("sp", uTt[nb_][:], uT_v[:, :, (tt + 1) * 512:(tt + 2) * 512], reads=[t_uT], writes=[t_u[nb_]])
                pb = 2 * (i % 2)
                pur, pui = ps[pb][:, :], ps[pb + 1][:, :]
                C_, S_ = CT[:, sc, :], ST[:, sc, :]
                MM(pur, BBrT[:, sc, :], uTt[ub][:, oc, :], True, True, [t_bb, t_u[ub]], [t_ps[pb]])
                MM(pui, BBiT[:, sc, :], uTt[ub][:, oc, :], True, True, [t_bb, t_u[ub]], [t_ps[pb + 1]])
                TT2("dve", f[0][b][:], pur, C_, ALU.mult, [t_ps[pb], t_tab], [t_f[0][b]])
                TT2("dve", f[1][b][:], pui, S_, ALU.mult, [t_ps[pb + 1], t_tab], [t_f[1][b]])
                TT2("dve", f[2][b][:], pui, C_, ALU.mult, [t_ps[pb + 1], t_tab], [t_f[2][b]])
                TT2("dve", f[3][b][:], pur, S_, ALU.mult, [t_ps[pb], t_tab], [t_f[3][b]])

            def S2(i):
                b = i % NB
                TT2("pool", f[4][b][:], f[0][b][:], f[1][b][:], ALU.add, [t_f[0][b], t_f[1][b]], [t_f[4][b]])
                TT2("pool", f[5][b][:], f[2][b][:], f[3][b][:], ALU.subtract, [t_f[2][b], t_f[3][b]], [t_f[5][b]])

            def S3(i):
                tt, oc, sci = tiles[i]
                sc, b = 4 * oc + sci, i % NB
                cin, cout = car[tt % 2], car[(tt + 1) % 2]
                tcin, tcout = t_car[tt % 2], t_car[(tt + 1) % 2]
                P.op("dve", lambda e: e.tensor_tensor_scan(
                    out=f[6][b][:], data0=rhoF[:, sc, :], data1=f[4][b][:], initial=cin[0][:, sc:sc + 1],
                    op0=ALU.mult, op1=ALU.add), [t_rho, t_f[4][b], tcin[sc]], [t_f[6][b]])
                P.op("dve", lambda e: e.tensor_tensor_scan(
                    out=f[7][b][:], data0=rhoF[:, sc, :], data1=f[5][b][:], initial=cin[1][:, sc:sc + 1],
                    op0=ALU.mult, op1=ALU.add), [t_rho, t_f[5][b], tcin[sc]], [t_f[7][b]])
                zrL, ziL = f[6][b][:, 511:512], f[7][b][:, 511:512]
                cL, sL = CT[:, sc, 511:512], ST[:, sc, 511:512]
                TT2("dve", cu[:, 0:1], ziL, sL, ALU.mult, [t_f[7][b], t_tab], [t_cu])
                STT(cout[0][:, sc:sc + 1], zrL, cL, cu[:, 0:1], ALU.mult, ALU.subtract,
                    [t_f[6][b], t_tab, t_cu], [tcout[sc]])
                TT2("dve", cu[:, 1:2], ziL, cL, ALU.mult, [t_f[7][b], t_tab], [t_cu])
                STT(cout[1][:, sc:sc + 1], zrL, sL, cu[:, 1:2], ALU.mult, ALU.add,
                    [t_f[6][b], t_tab, t_cu], [tcout[sc]])
                TT2("dve", f[11][b][:], f[7][b][:], CT[:, sc, :], ALU.mult, [t_f[7][b], t_tab], [t_f[11][b]])

            def S4(i):
                tt, oc, sci = tiles[i]
                sc, b = 4 * oc + sci, i % NB
                C_, S_ = CT[:, sc, :], ST[:, sc, :]
                TT2("pool", f[8][b][:], f[6][b][:], C_, ALU.mult, [t_f[6][b], t_tab], [t_f[8][b]])
                TT2("pool", f[9][b][:], f[7][b][:], S_, ALU.mult, [t_f[7][b], t_tab], [t_f[9][b]])
                TT2("pool", f[10][b][:], f[6][b][:], S_, ALU.mult, [t_f[6][b], t_tab], [t_f[10][b]])

            def S5(i):
                tt, oc, sci = tiles[i]
                sc, b, ub = 4 * oc + sci, i % NB, tt % 3
                pyb = 4 + oc % 2
                py = ps[pyb][:, :]
                MM(py, CreT[:, sc, :], f[8][b][:], sci == 0, False, [t_cc, t_f[8][b]], [t_ps[pyb]])
                MM(py, nCreT[:, sc, :], f[9][b][:], False, False, [t_cc, t_f[9][b]], [t_ps[pyb]])
                MM(py, nCimT[:, sc, :], f[10][b][:], False, False, [t_cc, t_f[10][b]], [t_ps[pyb]])
                MM(py, nCimT[:, sc, :], f[11][b][:], False, False, [t_cc, t_f[11][b]], [t_ps[pyb]])
                if sci == 3:
                    MM(py, Dg[:, oc, :], uTt[ub][:, oc, :], False, True, [t_dg, t_u[ub]], [t_ps[pyb]])
                    CP("act", gx[:], py, [t_ps[pyb]], [t_gx])
                    ACT(gu[:], py, AF.Square, [t_ps[pyb]], [t_gu])
                    TS("pool", gu[:], gu[:], 0.044715, 1.0, ALU.mult, ALU.add, [t_gu], [t_gu])
                    TT2("pool", gu[:], gu[:], gx[:], ALU.mult, [t_gu, t_gx], [t_gu])
                    ACT(gu[:], gu[:], AF.Sigmoid, [t_gu], [t_gu], scale=1.5957691216057308)
                    TT2("pool", ygst[:, oc, :], gu[:], gx[:], ALU.mult, [t_gu, t_gx], [t_ygst])
                    if oc == 3:
                        P.dma("sp", yg_v[:, :, tt * 512:(tt + 1) * 512], ygst[:], reads=[t_ygst], writes=[t_ygT], chan=t_ygst)

            n_t = len(tiles)
            for k in range(n_t + 2):
                if k < n_t:
                    S1(k)
                    S2(k)
                if 0 <= k - 1 < n_t:
                    S3(k - 1)
                    S4(k - 1)
                if 0 <= k - 2 < n_t:
                    S5(k - 2)
            P.barrier()

    def merge_phase():
        TT = 256
        NT = S // TT
        with ExitStack() as fs:
            wF = sb("wF", [128, 8, 2048], BF16, fs)
            wnsa = sb("wnsa", [128, 4, 1024], BF16, fs)
            wglu = sb("wglu", [128, 4, 2048], BF16, fs)
            wout = sb("wout", [128, 8, 1024], BF16, fs)
            mg = sb("Fmg", [128, 8], F32, fs)
            xt = [sb("Fx%d" % i, [128, 8, TT], F32, fs) for i in range(2)]
            xn = sb("Fxn", [128, 8, TT], BF16, fs)
            sq = [sb("Fsq%d" % i, [128, TT], BF16, fs) for i in range(2)]
            rstd = sb("Frstd", [128, TT], F32, fs)
            ont = [sb("Fon%d" % i, [128, 4, TT], BF16, fs) for i in range(2)]
            ygt = [sb("Fyg%d" % i, [128, 4, TT], BF16, fs) for i in range(2)]
            sg3 = [[sb("Fsg%d_%d" % (i, b), [128, TT], F32, fs) for b in range(2)] for i in range(3)]
            ta = [sb("Fta%d" % b, [128, TT], F32, fs) for b in range(2)]
            tb = [sb("Ftb%d" % b, [128, TT], F32, fs) for b in range(2)]
            mrg = sb("Fmrg", [128, 8, TT], BF16, fs)
            sgA = sb("FsgA", [128, 16, TT], F32, fs)
            t_sgA = [Tk("FsgA%d" % j) for j in range(16)]
            ot = sb("Fot", [128, 8, TT], F32, fs)
            t_w, t_mg, t_xn, t_rstd, t_ot = [Tk(n) for n in ("Fw", "Fmg", "Fxn", "Frstd", "Fot")]
            t_xt = [Tk("Fxt0"), Tk("Fxt1")]
            t_sq = [Tk("Fsq0"), Tk("Fsq1")]
            t_on = [Tk("Fon0"), Tk("Fon1")]
            t_yg = [Tk("Fyg0"), Tk("Fyg1")]
            t_sg3 = [[Tk("Fsg%d_%d" % (i, b)) for b in range(2)] for i in range(3)]
            t_ta = [Tk("Fta0"), Tk("Fta1")]
            t_tb = [Tk("Ftb0"), Tk("Ftb1")]
            t_mrg = [Tk("Fmrg%d" % i) for i in range(8)]
            t_h = [[Tk("Fps%d_%d" % (i, b)) for b in range(2)] for i in range(6)]
            P.dma("sp", mg[:], din["mix_g"], writes=[t_mg])
            wF_v = din["w_inF"].rearrange("(k p) f -> p k f", p=128)
            for k in range(8):
                P.dma("pool", wF[:, k, :], wF_v[:, k, :], writes=[t_w])
            wn_v = din["w_nsa"].rearrange("(k p) f -> p k f", p=128)
            wgl_v = din["w_glu"].rearrange("(k p) f -> p k f", p=128)
            wo_v = din["w_out"].rearrange("(k p) f -> p k f", p=128)
            for k in range(4):
                P.dma("pool", wnsa[:, k, :], wn_v[:, k, :], writes=[t_w])
                P.dma("pool", wglu[:, k, :], wgl_v[:, k, :], writes=[t_w])
            for k in range(8):
                P.dma("pool", wout[:, k, :], wo_v[:, k, :], writes=[t_w])
            h1_v = h1T.rearrange("(c p) t -> p c t", p=128)
            h2_v = h2T.rearrange("(c p) t -> p c t", p=128)
            on_v = onsaT.rearrange("(c p) t -> p c t", p=128)
            yg_v = ygT.rearrange("(c p) t -> p c t", p=128)

            def load(i):
                b = i % 2
                tsl = slice(i * TT, (i + 1) * TT)
                P.dma("sp", xt[b][:], h1_v[:, :, tsl], reads=[t_h1T], writes=[t_xt[b]])
                P.dma("sp", ont[b][:], on_v[:, :, tsl], reads=[t_onsaT], writes=[t_on[b]])
                P.dma("sp", ygt[b][:], yg_v[:, :, tsl], reads=[t_ygT], writes=[t_yg[b]])

            load(0)
            for i in range(NT):
                b = i % 2
                if i + 1 < NT:
                    load(i + 1)
                xb = xt[b]
                norm_tile(xb, t_xt[b], mg, t_mg, xn, t_xn, sq, t_sq, rstd, t_rstd, TT, 6)
                for j in range(16):
                    bk = j % 4
                    pg_ = ps[bk][:, 0:TT]
                    cs_ = slice(j * 128, (j + 1) * 128)
                    for k in range(8):
                        MM(pg_, wF[:, k, cs_], xn[:, k, :], k == 0, k == 7, [t_w, t_xn], [t_ps[bk]])
                    ACT(sgA[:, j, :], pg_, AF.Sigmoid, [t_ps[bk]], [t_sgA[j]])
                for dc in range(8):
                    hb = dc % 2
                    bks = (0, 1, 2) if hb == 0 else (3, 4, 5)
                    pyn, pval, pgt = [ps[j][:, 0:TT] for j in bks]
                    tp = [t_ps[j] for j in bks]
                    dsl = slice(dc * 128, (dc + 1) * 128)
                    dsl2 = slice(1024 + dc * 128, 1024 + (dc + 1) * 128)
                    for k in range(4):
                        MM(pyn, wnsa[:, k, dsl], ont[b][:, k, :], k == 0, k == 3, [t_w, t_on[b]], [tp[0]])
                    for k in range(4):
                        MM(pval, wglu[:, k, dsl], ygt[b][:, k, :], k == 0, k == 3, [t_w, t_yg[b]], [tp[1]])
                    for k in range(4):
                        MM(pgt, wglu[:, k, dsl2], ygt[b][:, k, :], k == 0, k == 3, [t_w, t_yg[b]], [tp[2]])
                    ACT(sg3[2][hb][:], pgt, AF.Sigmoid, [tp[2]], [t_sg3[2][hb]])
                    TT2("dve", ta[hb][:], pyn, sgA[:, dc, :], ALU.mult, [tp[0], t_sgA[dc]], [t_ta[hb]])
                    TT2("dve", tb[hb][:], pval, sg3[2][hb][:], ALU.mult, [tp[1], t_sg3[2][hb]], [t_tb[hb]])
                    TT2("pool", tb[hb][:], tb[hb][:], sgA[:, 8 + dc, :], ALU.mult, [t_tb[hb], t_sgA[8 + dc]], [t_tb[hb]])
                    TT2("pool", mrg[:, dc, :], ta[hb][:], tb[hb][:], ALU.add, [t_ta[hb], t_tb[hb]], [t_mrg[dc]])
                for dc in range(8):
                    bk = 7 if dc % 2 == 0 else 6
                    po = ps[bk][:, 0:TT]
                    for k in range(8):
                        MM(po, wout[:, k, dc * 128:(dc + 1) * 128], mrg[:, k, :], k == 0, k == 7,
                           [t_w, t_mrg[k]], [t_ps[bk]])
                    TT2("dve", ot[:, dc, :], po, xb[:, dc, :], ALU.add, [t_ps[bk], t_xt[b]], [t_ot])
                if "F3b" not in DBG:
                    P.dma("sp", h2_v[:, :, i * TT:(i + 1) * TT], ot[:], reads=[t_ot], writes=[t_h2T], chan=t_ot)
            P.barrier()

    if stage == "ffn1":
        ffn_phase("f1", xT, t_xT, outT, t_outT, din["f1_g"], din["f1_wg"], din["f1_wu"], din["f1_wd"])
    else:
        ffn_phase("f1", xT, t_xT, h1T, t_h1T, din["f1_g"], din["f1_wg"], din["f1_wu"], din["f1_wd"])
        mixer_attention()
        if stage not in ("B", "BC", "D"):
            ssm_phase()
        if stage not in ("B", "BC", "D", "E"):
            merge_phase()
        if stage == "full":
            ffn_phase("f2", h2T, t_h2T, outT, t_outT, din["f2_g"], din["f2_wg"], din["f2_wu"], din["f2_wd"])

    P.final_wait([t_outT])
    P.barrier()
    es.close()
    return nc, P


def _consts():
    f = np.float32
    c = {}
    c["c_id"] = np.eye(128, dtype=f)
    bd = np.zeros((128, 128), f)
    bd[:64, :64] = 1.0
    bd[64:, 64:] = 1.0
    c["c_bd"] = bd
    k = np.arange(S)
    c["c_E"] = (k[None, :] // 64 == np.arange(64)[:, None]).astype(f)
    ci = np.arange(256)
    j = np.arange(64)
    ovl = ((ci[:, None] * 16 < (j[None, :] + 1) * 64) & (ci[:, None] * 16 + 32 > j[None, :] * 64)).astype(f)
    ovl[255] = 0.0
    c["c_ovl"] = np.ascontiguousarray(ovl.reshape(2, 128, 64).transpose(1, 0, 2))
    t = np.arange(S)
    qblk = t // 64
    force = (j[None, :] == 0) | (j[None, :] == qblk[:, None]) | (j[None, :] == qblk[:, None] - 1)
    selB = np.where(j[None, :] <= qblk[:, None], 1000.0 * force.astype(f), -1e30).astype(f)
    c["c_selB"] = np.ascontiguousarray(selB.reshape(32, 128, 64).transpose(1, 0, 2))
    kl = np.arange(128)[:, None]
    tl = np.arange(512)[None, :]
    mwin = np.zeros((128, 8, 512), f)
    for r in range(8):
        dlt = 512 + tl - 128 * r - kl
        mwin[:, r, :] = ((dlt >= 0) & (dlt < 512)).astype(f)
    c["c_mwin"] = mwin
    mslc = np.zeros((128, 4, 512), f)
    for r in range(4):
        mslc[:, r, :] = ((tl - 128 * r - kl) >= 0).astype(f)
    c["c_mslc"] = mslc
    mcmp = np.zeros((128, 5, 512), f)
    for v in range(5):
        mcmp[:, v, :] = ((512 * v + tl - 16 * kl - 31) >= 0).astype(f)
    c["c_mcmp"] = mcmp
    return c


def _prep_shared(inp):
    f = np.float32
    A = lambda a: np.ascontiguousarray(a, dtype=f)
    sh = dict(_consts())
    for tag, nm in (("f1", "ffn1"), ("f2", "ffn2")):
        sh[tag + "_g"] = A(inp[nm + "_norm"][0].reshape(8, 128).T)
        sh[tag + "_wg"] = A(inp[nm + "_w_gate"][0])
        sh[tag + "_wu"] = A(inp[nm + "_w_up"][0])
        sh[tag + "_wd"] = A(inp[nm + "_w_down"][0])
    sh["mix_g"] = A(inp["mix_norm"][0].reshape(8, 128).T)
    w = inp["w_in"][0]
    q = w[:, 0:512]
    blocks = []
    for r in range(4):
        blocks.append(q[:, r * 64:(r + 1) * 64])
        blocks.append(q[:, (4 + r) * 64:(5 + r) * 64])
    blocks += [w[:, 768:896], w[:, 1024:1152], w[:, 512:640], w[:, 640:768], w[:, 1304:1816],
               w[:, 896:1024], w[:, 1152:1280], w[:, 1280:1304]]
    sh["w_inB"] = A(np.concatenate(blocks, axis=1))
    assert sh["w_inB"].shape == (1024, 1816)
    sh["w_inF"] = A(w[:, 1816:3864])
    rep = lambda v: np.concatenate([v, v])
    sh["gains"] = A(np.stack([rep(inp["q_norm"][0]), rep(inp["k_norm_slc"][0]),
                              rep(inp["k_norm_win"][0]), rep(inp["k_norm_cmp"][0])], axis=1))
    sh["posk"] = A(np.concatenate([inp["cmp_pos_k"][0].T] * 2, axis=0))
    sh["posv"] = A(np.concatenate([inp["cmp_pos_v"][0].T] * 2, axis=0))
    w1 = lambda a: A(np.concatenate([a.reshape(32, 64, 256).transpose(1, 0, 2)] * 2, axis=0))
    sh["w1k"] = w1(inp["cmp_k_w1"][0])
    sh["w1v"] = w1(inp["cmp_v_w1"][0])
    w2k = inp["cmp_k_w2"][0]
    sh["w2k"] = A(np.concatenate([w2k, w2k], axis=1).reshape(2, 128, 128).transpose(1, 0, 2))
    sh["w2v"] = A(inp["cmp_v_w2"][0].reshape(2, 128, 64).transpose(1, 0, 2))
    sh["w_nsa"] = A(inp["w_nsa_proj"][0])
    sh["w_glu"] = A(inp["ssm_glu_w"][0])
    sh["w_out"] = A(inp["w_out"][0])
    sm = lambda a: A(a.reshape(16, 128).T)
    sh["s_lr"] = sm(inp["ssm_lambda_re"][0])
    sh["s_li"] = sm(inp["ssm_lambda_im"][0])
    sh["s_ls"] = sm(np.repeat(inp["ssm_log_step"][0][:, None], 64, axis=1))
    sb_ = lambda a: A(a.reshape(16, 128, 16).transpose(1, 0, 2))
    sh["s_bre"] = sb_(inp["ssm_b_re"][0])
    sh["s_bim"] = sb_(inp["ssm_b_im"][0])

    def cplace(cc):
        o = np.zeros((128, 16, 128), f)
        for g in range(32):
            sc, gl = g // 2, g % 2
            c0 = 32 * (sc % 4) + 16 * gl
            o[gl * 64:(gl + 1) * 64, sc, c0:c0 + 16] = cc[g].T
        return o
    sh["s_creT"] = cplace(inp["ssm_c_re"][0])
    sh["s_cimT"] = cplace(inp["ssm_c_im"][0])
    sh["s_d"] = A(inp["ssm_d"][0].reshape(4, 128).T)
    return sh


IN_SHAPES = {
    "c_id": [128, 128], "c_bd": [128, 128], "c_E": [64, S], "c_ovl": [128, 2, 64], "c_selB": [128, 32, 64],
    "c_mwin": [128, 8, 512], "c_mslc": [128, 4, 512], "c_mcmp": [128, 5, 512],
    "f1_g": [128, 8], "f1_wg": [D, DFF], "f1_wu": [D, DFF], "f1_wd": [DFF, D],
    "f2_g": [128, 8], "f2_wg": [D, DFF], "f2_wu": [D, DFF], "f2_wd": [DFF, D],
    "mix_g": [128, 8], "w_inB": [D, 1816], "w_inF": [D, 2048], "gains": [128, 4],
    "posk": [128, 32], "posv": [128, 32], "w1k": [128, 32, 256], "w1v": [128, 32, 256],
    "w2k": [128, 2, 128], "w2v": [128, 2, 64], "w_nsa": [512, D], "w_glu": [512, 2 * D], "w_out": [D, D],
    "s_lr": [128, 16], "s_li": [128, 16], "s_ls": [128, 16], "s_bre": [128, 16, 16], "s_bim": [128, 16, 16],
    "s_creT": [128, 16, 128], "s_cimT": [128, 16, 128], "s_d": [128, 4],
}

STAGE = "full"


def kernel(**inputs):
    inp = {k: np.asarray(v) for k, v in inputs.items()}
    x = inp["x"]
    nc, P = build(STAGE)
    sh = _prep_shared(inp)
    in_maps = []
    for b in range(NCORES):
        m = dict(sh)
        m["xT"] = np.ascontiguousarray(x[b].T)
        in_maps.append(m)
    res = run_bass_kernel_spmd(nc, in_maps, core_ids=list(range(NCORES)))
    out = np.stack([np.ascontiguousarray(r["outT"].T) for r in res.results], axis=0)
    return out.astype(np.float32)
```

```python
import math
import os
from contextlib import ExitStack

import numpy as np
import concourse.bass as bass
import concourse.mybir as mybir
from concourse.bass_utils import run_bass_kernel_spmd

F32 = mybir.dt.float32
BF16 = mybir.dt.bfloat16
AF = mybir.ActivationFunctionType
ALU = mybir.AluOpType

S = 4096
D = 1024
DFF = 2816
NCORES = 8
EPS = 1e-6
DBG = set(os.environ.get("DBGSKIP", "").split(","))


class Tk:
    __slots__ = ("name", "w", "rs", "dsem", "dcnt")

    def __init__(self, name):
        self.name = name
        self.w = []
        self.rs = []
        self.dsem = None
        self.dcnt = 0


class Prog:
    ENG = ("pe", "act", "dve", "pool", "sp")

    def __init__(self, nc, es):
        self.nc = nc
        self.es = es
        self.e = {"pe": nc.tensor, "act": nc.scalar, "dve": nc.vector,
                  "pool": nc.gpsimd, "sp": nc.sync}
        self.nsem = 0
        self.sem = {}
        self.cnt = {}
        for k in self.ENG:
            self.sem[k] = self.new_sem("e_" + k)
            self.cnt[k] = 0
        self.seen = {k: {} for k in self.ENG}
        self.dma_sems = []
        self.n_inst = 0

    def new_sem(self, name):
        self.nsem += 1
        return self.es.enter_context(self.nc.semaphore(name + "_%d" % self.nsem))

    def _wait(self, eng, sem, val):
        d = self.seen[eng]
        key = id(sem)
        if d.get(key, (None, 0))[1] >= val:
            return
        d[key] = (sem, val)
        self.e[eng].wait_ge(sem, val)

    def _deps(self, eng, reads, writes, same_engine_sync=True):
        need = {}

        def add(ev):
            sem, val, src = ev
            if src == eng and not same_engine_sync:
                return
            if isinstance(src, tuple):
                val = src[0].dsem[src[1]][1]
            k = id(sem)
            if k not in need or need[k][1] < val:
                need[k] = (sem, val)

        for t in reads:
            for ev in t.w:
                add(ev)
        for t in writes:
            for ev in t.w:
                add(ev)
            for ev in t.rs:
                add(ev)
        for sem, val in need.values():
            self._wait(eng, sem, val)

    def _roll(self, eng):
        if self.cnt[eng] >= 8000:
            self.sem[eng] = self.new_sem("e_" + eng)
            self.cnt[eng] = 0

    def op(self, eng, fn, reads=(), writes=(), sync_same=True):
        self._roll(eng)
        self._deps(eng, reads, writes, same_engine_sync=sync_same)
        ins = fn(self.e[eng])
        self.cnt[eng] += 1
        ev = (self.sem[eng], self.cnt[eng], eng)
        ins.then_inc(self.sem[eng], 1)
        self.n_inst += 1
        for t in reads:
            t.rs.append(ev)
            if len(t.rs) > 24:
                t.rs = self._compact(t.rs)
        for t in writes:
            t.w = [ev]
            t.rs = []
        return ins

    @staticmethod
    def _compact(evs):
        best = {}
        for sem, val, src in evs:
            k = id(sem)
            if k not in best or best[k][1] < val:
                best[k] = (sem, val, src)
        return list(best.values())

    def dma(self, q, out_ap, in_ap, reads=(), writes=(), chan=None):
        if chan is None:
            chan = writes[0] if writes else reads[0]
        if chan.dsem is None:
            chan.dsem = {}
        if q not in chan.dsem:
            chan.dsem[q] = [self.new_sem("d_" + chan.name + "_" + q), 0]
            self.dma_sems.append((chan, q))
        ent = chan.dsem[q]
        self._deps(q, reads, writes)
        ins = self.e[q].dma_start(out=out_ap, in_=in_ap)
        ent[1] += 16
        ins.then_inc(ent[0], 16)
        ev = (ent[0], ent[1], (chan, q))
        self.n_inst += 1
        for t in reads:
            t.rs.append(ev)
            if len(t.rs) > 24:
                t.rs = self._compact(t.rs)
        for t in writes:
            t.w = [e for e in t.w if isinstance(e[2], tuple) and e[0] is not ent[0]] + [ev]
            t.rs = []
        return ins

    def barrier(self):
        for x in self.ENG:
            for y in self.ENG:
                if y != x and self.cnt[y] > 0:
                    self._wait(x, self.sem[y], self.cnt[y])
            for t, q in self.dma_sems:
                ent = t.dsem[q]
                if ent[1] > 0:
                    self._wait(x, ent[0], ent[1])

    def final_wait(self, tks):
        for t in tks:
            for sem, val, _ in t.w:
                self._wait("sp", sem, val)


def build(stage="full", debug=False):
    nc = bass.Bass("TRN2", target_bir_lowering=False)
    es = ExitStack()
    P = Prog(nc, es)

    def dram_in(name, shape, dt=F32):
        return nc.dram_tensor(name, list(shape), dt, kind="ExternalInput").ap()

    def dram_out(name, shape, dt=F32):
        return nc.dram_tensor(name, list(shape), dt, kind="ExternalOutput").ap()

    def dram_tmp(name, shape, dt=F32):
        kind = "ExternalOutput" if debug else "Internal"
        return nc.dram_tensor(name, list(shape), dt, kind=kind).ap()

    def sb(name, shape, dt, stack=None):
        return (stack or es).enter_context(nc.sbuf_tensor("sb_" + name, list(shape), dt))

    xT = dram_in("xT", [D, S])
    outT = dram_out("outT", [D, S])
    din = {}
    for nm, shp in IN_SHAPES.items():
        din[nm] = dram_in(nm, shp)
    h1T = dram_tmp("h1T", [D, S])
    h2T = dram_tmp("h2T", [D, S])
    uT = dram_tmp("uT", [512, S], BF16)
    onsaT = dram_tmp("onsaT", [512, S], BF16)
    ygT = dram_tmp("ygT", [512, S], BF16)
    t_xT, t_outT, t_h1T, t_h2T, t_uT, t_onsaT, t_ygT = [Tk(n) for n in
        ("xT", "outT", "h1T", "h2T", "uT", "onsaT", "ygT")]

    ps = [es.enter_context(nc.psum_tensor("ps%d" % i, [128, 512], F32)) for i in range(8)]
    t_ps = [Tk("ps%d" % i) for i in range(8)]

    def MM(out, lhsT, rhs, st, sp, reads, writes):
        P.op("pe", lambda e: e.matmul(out, lhsT=lhsT, rhs=rhs, start=st, stop=sp),
             reads, writes, sync_same=False)

    def TR(out, in_, ident, reads, writes):
        P.op("pe", lambda e: e.transpose(out, in_, ident), reads, writes, sync_same=False)

    def TT2(eng, out, a, b, op, reads, writes):
        P.op(eng, lambda e: e.tensor_tensor(out=out, in0=a, in1=b, op=op), reads, writes)

    def TS(eng, out, a, s1, s2, op0, op1, reads, writes):
        if s2 is None:
            P.op(eng, lambda e: e.tensor_scalar(out=out, in0=a, scalar1=s1, scalar2=None, op0=op0),
                 reads, writes)
        else:
            P.op(eng, lambda e: e.tensor_scalar(out=out, in0=a, scalar1=s1, scalar2=s2, op0=op0, op1=op1),
                 reads, writes)

    def STT(out, a, sc, b, op0, op1, reads, writes):
        P.op("dve", lambda e: e.scalar_tensor_tensor(out=out, in0=a, scalar=sc, in1=b, op0=op0, op1=op1),
             reads, writes)

    def ACT(out, in_, func, reads, writes, bias=None, scale=None):
        kw = {}
        if bias is not None:
            kw["bias"] = bias
        if scale is not None:
            kw["scale"] = scale
        P.op("act", lambda e: e.activation(out=out, in_=in_, func=func, **kw), reads, writes)

    def CP(eng, out, in_, reads, writes):
        if eng == "act":
            P.op("act", lambda e: e.activation(out=out, in_=in_, func=AF.Identity), reads, writes)
        else:
            P.op(eng, lambda e: e.tensor_copy(out=out, in_=in_), reads, writes)

    def RECIP(out, in_, reads, writes):
        P.op("dve", lambda e: e.reciprocal(out=out, in_=in_), reads, writes)

    def MEMSET(eng, ap, val, writes):
        P.op(eng, lambda e: e.memset(ap, val), (), writes)

    ones_bf = sb("ones_bf", [128, 128], BF16)
    bd_bf = sb("bd_bf", [128, 128], BF16)
    id_bf = sb("id_bf", [128, 128], BF16)
    id_f = sb("id_f", [128, 128], F32)
    epsc = sb("epsc", [128, 1], F32)
    t_const = Tk("const")
    MEMSET("pool", ones_bf[:], 1.0, [t_const])
    MEMSET("pool", epsc[:], EPS, [t_const])
    P.dma("pool", bd_bf[:], din["c_bd"], writes=[t_const])
    P.dma("pool", id_bf[:], din["c_id"], writes=[t_const])
    P.dma("sp", id_f[:], din["c_id"], writes=[t_const])

    def norm_tile(xb, t_xb, gsb, t_g, xn, t_xn, sq, t_sq, rstd, t_rstd, TT, bank):
        pstat = ps[bank][:, 0:TT]
        for c in range(8):
            TT2("pool", sq[c % 2][:, 0:TT], xb[:, c, :], xb[:, c, :], ALU.mult, [t_xb], [t_sq[c % 2]])
            MM(pstat, ones_bf[:], sq[c % 2][:, 0:TT], c == 0, c == 7, [t_const, t_sq[c % 2]], [t_ps[bank]])
        ACT(rstd[:, 0:TT], pstat, AF.Sqrt, [t_ps[bank], t_const], [t_rstd], bias=epsc[:, 0:1], scale=1.0 / D)
        RECIP(rstd[:, 0:TT], rstd[:, 0:TT], [t_rstd], [t_rstd])
        for c in range(8):
            STT(xn[:, c, :], xb[:, c, :], gsb[:, c:c + 1], rstd[:, 0:TT], ALU.mult, ALU.mult,
                [t_xb, t_g, t_rstd], [t_xn])

    def ffn_phase(tag, src, t_src, dst, t_dst, g_dram, wg_d, wu_d, wd_d):
        TT = 256
        NT = S // TT
        with ExitStack() as fs:
            wg = sb(tag + "wg", [128, 8, DFF], BF16, fs)
            wu = sb(tag + "wu", [128, 8, DFF], BF16, fs)
            wd = sb(tag + "wd", [128, 22, D], BF16, fs)
            gsb = sb(tag + "g", [128, 8], F32, fs)
            xt = [sb(tag + "x%d" % i, [128, 8, TT], F32, fs) for i in range(2)]
            xn = [sb(tag + "xn%d" % i, [128, 8, TT], BF16, fs) for i in range(2)]
            sq8 = sb(tag + "sq8", [128, 8, TT], BF16, fs)
            rstd = sb(tag + "rstd", [128, TT], F32, fs)
            hh = sb(tag + "h", [128, 22, TT], BF16, fs)
            sg = [sb(tag + "sg%d" % i, [128, TT], F32, fs) for i in range(2)]
            ot = sb(tag + "ot", [128, 8, TT], F32, fs)
            t_wg = [Tk(tag + "wg%d" % k) for k in range(8)]
            t_wu = [Tk(tag + "wu%d" % k) for k in range(8)]
            t_wd = [Tk(tag + "wd%d" % k) for k in range(22)]
            t_g = Tk(tag + "g")
            t_xt = [Tk(tag + "xt%d" % i) for i in range(2)]
            t_xn = [Tk(tag + "xn%d" % i) for i in range(2)]
            t_sq8 = [Tk(tag + "sq8_%d" % i) for i in range(8)]
            t_rstd = Tk(tag + "rstd")
            t_hh = [Tk(tag + "hh%d" % i) for i in range(22)]
            t_sg = [Tk(tag + "sg%d" % i) for i in range(2)]
            t_ot = Tk(tag + "ot")
            src_v = src.rearrange("(c p) t -> p c t", p=128)
            dst_v = dst.rearrange("(c p) t -> p c t", p=128)

            def load_x(i):
                b = i % 2
                P.dma("sp", xt[b][:], src_v[:, :, i * TT:(i + 1) * TT], reads=[t_src], writes=[t_xt[b]])

            def norm1(i):
                b = i % 2
                for c in range(8):
                    TT2("pool", sq8[:, c, :], xt[b][:, c, :], xt[b][:, c, :], ALU.mult, [t_xt[b]], [t_sq8[c]])

            def norm2(i):
                b = i % 2
                pstat = ps[6][:, 0:TT]
                for c in range(8):
                    MM(pstat, ones_bf[:], sq8[:, c, :], c == 0, c == 7, [t_const, t_sq8[c]], [t_ps[6]])
                ACT(rstd[:], pstat, AF.Sqrt, [t_ps[6], t_const], [t_rstd], bias=epsc[:, 0:1], scale=1.0 / D)
                RECIP(rstd[:], rstd[:], [t_rstd], [t_rstd])
                for c in range(8):
                    STT(xn[b][:, c, :], xt[b][:, c, :], gsb[:, c:c + 1], rstd[:], ALU.mult, ALU.mult,
                        [t_xt[b], t_g, t_rstd], [t_xn[b]])

            P.dma("sp", gsb[:], g_dram, writes=[t_g])
            load_x(0)
            wg_v = wg_d.rearrange("(k p) f -> p k f", p=128)
            wu_v = wu_d.rearrange("(k p) f -> p k f", p=128)
            wd_v = wd_d.rearrange("(k p) d -> p k d", p=128)
            ch_a, ch_b = Tk(tag + "chA"), Tk(tag + "chB")
            for k in range(8):
                P.dma("pool", wg[:, k, :], wg_v[:, k, :], writes=[t_wg[k]], chan=ch_a)
                P.dma("pool", wu[:, k, :], wu_v[:, k, :], writes=[t_wu[k]], chan=ch_a)
            for k in range(22):
                P.dma("pool", wd[:, k, :], wd_v[:, k, :], writes=[t_wd[k]], chan=ch_b)
            norm1(0)
            norm2(0)
            for i in range(NT):
                b = i % 2
                xb = xt[b]
                if i + 1 < NT:
                    load_x(i + 1)
                    norm1(i + 1)
                for fc in range(22):
                    pb = fc % 2
                    pg = ps[0 + pb][:, 0:TT]
                    pu = ps[2 + pb][:, 0:TT]
                    for k in range(8):
                        MM(pg, wg[:, k, fc * 128:(fc + 1) * 128], xn[b][:, k, :], k == 0, k == 7,
                           [t_wg[k], t_xn[b]], [t_ps[0 + pb]])
                    for k in range(8):
                        MM(pu, wu[:, k, fc * 128:(fc + 1) * 128], xn[b][:, k, :], k == 0, k == 7,
                           [t_wu[k], t_xn[b]], [t_ps[2 + pb]])
                    ACT(sg[pb][:], pg, AF.Silu, [t_ps[0 + pb]], [t_sg[pb]])
                    TT2("dve", hh[:, fc, :], pu, sg[pb][:], ALU.mult, [t_ps[2 + pb], t_sg[pb]], [t_hh[fc]])
                if i + 1 < NT:
                    norm2(i + 1)
                for dc in range(8):
                    pb = dc % 2
                    py = ps[4 + pb][:, 0:TT]
                    for fc in range(22):
                        MM(py, wd[:, fc, dc * 128:(dc + 1) * 128], hh[:, fc, :], fc == 0, fc == 21,
                           [t_wd[fc], t_hh[fc]], [t_ps[4 + pb]])
                    STT(ot[:, dc, :], py, 0.5, xb[:, dc, :], ALU.mult, ALU.add,
                        [t_ps[4 + pb], t_xt[b]], [t_ot])
                P.dma("sp", dst_v[:, :, i * TT:(i + 1) * TT], ot[:], reads=[t_ot], writes=[t_dst], chan=t_ot)
            P.barrier()

    def mixer_attention():
        with ExitStack() as ms:
            Qs = [sb("Qs%d" % h, [128, S], BF16, ms) for h in range(8)]
            Ks = [sb("Ks%d" % g, [128, S], BF16, ms) for g in range(2)]
            Kw = sb("Kw", [128, S], BF16, ms)
            Vs = sb("Vs", [128, 32, 2, 72], BF16, ms)
            Vw = sb("Vw", [128, 32, 2, 72], BF16, ms)
            gl = sb("gl", [128, 32, 24], F32, ms)
            gains = sb("gains", [128, 4], F32, ms)
            gq8 = sb("gq8", [128, 1], F32, ms)
            kcT = sb("kcT", [128, 256], BF16, ms)
            rcmp = [sb("rcmp%d" % g, [128, 2, 129], BF16, ms) for g in range(2)]
            t_Q = Tk("Qq")
            t_Qm = [Tk("Qm%d" % i) for i in range(8)]
            t_Ks, t_Kw, t_Vs, t_Vw, t_gl, t_gains, t_kcT, t_rcmp = [Tk(n) for n in
                ("Ks", "Kw", "Vs", "Vw", "gl", "gains", "kcT", "rcmp")]
            P.dma("sp", gains[:], din["gains"], writes=[t_gains])
            TS("dve", gq8[:], gains[:, 0:1], 0.125, None, ALU.mult, None, [t_gains], [t_gains])
            if "E" not in DBG:
                P.dma("pool", Ks[0][64:128, :], din["c_E"], writes=[t_Ks])
                P.dma("pool", Ks[1][0:64, :], din["c_E"], writes=[t_Ks])
            if "V1" not in DBG:
                MEMSET("pool", Vs[:, :, :, 64:65], 1.0, [t_Vs])
                MEMSET("pool", Vw[:, :, :, 64:65], 1.0, [t_Vw])

            with ExitStack() as bs:
                TT = 512
                sq = [sb("Bsq%d" % i, [128, TT], BF16, bs) for i in range(2)]
                rs = sb("Brs", [128, TT], F32, bs)
                KcA = sb("KcA", [128, S], BF16, bs)
                KcB = sb("KcB", [128, S], BF16, bs)
                VcA = sb("VcA", [128, S], BF16, bs)
                VcB = sb("VcB", [128, S], BF16, bs)
                b2 = ExitStack()
                wB = sb("wB", [128, 8, 1816], BF16, b2)
                mg = sb("mg", [128, 8], F32, b2)
                xt = sb("Bx", [128, 8, TT], F32, b2)
                xn = sb("Bxn", [128, 8, TT], BF16, b2)
                rstd = sb("Brstd", [128, TT], F32, b2)
                ust = sb("Bust", [128, 4, TT], BF16, b2)
                posk = sb("posk", [128, 32], F32, b2)
                posv = sb("posv", [128, 32], F32, b2)
                vstg = sb("vstg", [128, 256], BF16, b2)
                t_vstg = Tk("vstg")
                t_wB, t_mg, t_xt, t_xn, t_rstd, t_rs, t_ust, t_Kc, t_pos = [Tk(n) for n in
                    ("wB", "mg", "Bxt", "Bxn", "Brstd", "Brs", "Bust", "Kc", "pos")]
                t_sq = [Tk("Bsq0"), Tk("Bsq1")]
                wB_v = din["w_inB"].rearrange("(k p) f -> p k f", p=128)
                for k in range(8):
                    P.dma("pool", wB[:, k, :], wB_v[:, k, :], writes=[t_wB])
                P.dma("sp", mg[:], din["mix_g"], writes=[t_mg])
                P.dma("sp", posk[:], din["posk"], writes=[t_pos])
                P.dma("sp", posv[:], din["posv"], writes=[t_pos])
                h1_v = h1T.rearrange("(c p) t -> p c t", p=128)
                uT_v = uT.rearrange("(c p) t -> p c t", p=128)
                for i in range(S // TT):
                    tsl = slice(i * TT, (i + 1) * TT)
                    P.dma("sp", xt[:], h1_v[:, :, tsl], reads=[t_h1T], writes=[t_xt])
                    norm_tile(xt, t_xt, mg, t_mg, xn, t_xn, sq, t_sq, rstd, t_rstd, TT, 6)
                    pssb = [3, 7]

                    def main(ch):
                        pb = ch % 3
                        pq = ps[pb][:, :]
                        for k in range(8):
                            MM(pq, wB[:, k, ch * 128:(ch + 1) * 128], xn[:, k, :], k == 0, k == 7,
                               [t_wB, t_xn], [t_ps[pb]])
                        if ch < 6:
                            ACT(sq[ch % 2][:], pq, AF.Square, [t_ps[pb]], [t_sq[ch % 2]])

                    def finish(ch):
                        pb = ch % 3
                        pq = ps[pb][:, :]
                        if ch < 6:
                            sb_ = pssb[ch % 2]
                            pss = ps[sb_][:, :]
                            MM(pss, bd_bf[:], sq[ch % 2][:], True, True, [t_const, t_sq[ch % 2]], [t_ps[sb_]])
                            ACT(rs[:], pss, AF.Ln, [t_ps[sb_], t_const], [t_rs], bias=epsc[:, 0:1], scale=1.0 / 64)
                            ACT(rs[:], rs[:], AF.Exp, [t_rs], [t_rs], scale=-0.5)
                            if ch < 4:
                                STT(Qs[ch][0:64, tsl], pq[0:64, :], gq8[0:64, 0:1], rs[0:64, :], ALU.mult, ALU.mult,
                                    [t_ps[pb], t_gains, t_rs], [t_Q])
                                STT(Qs[4 + ch][64:128, tsl], pq[64:128, :], gq8[64:128, 0:1], rs[64:128, :],
                                    ALU.mult, ALU.mult, [t_ps[pb], t_gains, t_rs], [t_Q])
                            elif ch == 4:
                                STT(Ks[0][0:64, tsl], pq[0:64, :], gains[0:64, 1:2], rs[0:64, :], ALU.mult, ALU.mult,
                                    [t_ps[pb], t_gains, t_rs], [t_Ks])
                                STT(Ks[1][64:128, tsl], pq[64:128, :], gains[64:128, 1:2], rs[64:128, :],
                                    ALU.mult, ALU.mult, [t_ps[pb], t_gains, t_rs], [t_Ks])
                            else:
                                STT(Kw[:, tsl], pq, gains[:, 2:3], rs[:], ALU.mult, ALU.mult,
                                    [t_ps[pb], t_gains, t_rs], [t_Kw])
                        elif ch < 8:
                            pos = posk if ch == 6 else posv
                            A_, B_ = (KcA, KcB) if ch == 6 else (VcA, VcB)
                            pq3 = pq.rearrange("p (i s) -> p i s", s=16)
                            TT2("dve", A_[:, tsl].rearrange("p (i s) -> p i s", s=16), pq3,
                                pos[:, 0:16].unsqueeze(1).to_broadcast([128, TT // 16, 16]), ALU.add,
                                [t_ps[pb], t_pos], [t_Kc])
                            TT2("dve", B_[:, tsl].rearrange("p (i s) -> p i s", s=16), pq3,
                                pos[:, 16:32].unsqueeze(1).to_broadcast([128, TT // 16, 16]), ALU.add,
                                [t_ps[pb], t_pos], [t_Kc])
                        else:
                            CP("act", ust[:, ch - 8, :], pq, [t_ps[pb]], [t_ust])

                    for ch in range(13):
                        if ch < 12:
                            main(ch)
                        if ch >= 1:
                            finish(ch - 1)
                    P.dma("sp", uT_v[:, :, tsl], ust[:], reads=[t_ust], writes=[t_uT], chan=t_ust)
                    for sub in range(4):
                        if "TM" in DBG:
                            break
                        pb = 4 + sub % 2
                        pt = ps[pb][:, 0:280]
                        for k in range(8):
                            MM(pt, xn[:, k, sub * 128:(sub + 1) * 128], wB[:, k, 1536:1816], k == 0, k == 7,
                               [t_wB, t_xn], [t_ps[pb]])
                        kt = i * 4 + sub
                        CP("act", vstg[:], ps[pb][:, 0:256], [t_ps[pb]], [t_vstg])
                        for g_ in range(2):
                            CP("pool", Vs[:, kt, g_, 0:64], vstg[:, g_ * 64:(g_ + 1) * 64], [t_vstg], [t_Vs])
                            CP("pool", Vw[:, kt, g_, 0:64], vstg[:, 128 + g_ * 64:128 + (g_ + 1) * 64], [t_vstg], [t_Vw])
                        if "cpG" not in DBG:
                            CP("act", gl[:, kt, :], ps[pb][:, 256:280], [t_ps[pb]], [t_gl])
                P.barrier()
                b2.close()
                if stage == "B":
                    return

                w1k = sb("w1k", [128, 32, 256], BF16, bs)
                w1v = sb("w1v", [128, 32, 256], BF16, bs)
                w2k = sb("w2k", [128, 2, 128], BF16, bs)
                w2v = sb("w2v", [128, 2, 64], BF16, bs)
                ovl = sb("ovl", [128, 2, 64], BF16, bs)
                hx = sb("hx", [128, 256], F32, bs)
                hu = sb("hu", [128, 256], F32, bs)
                hid = [[sb("hid%d%d" % (a, b), [128, 256], BF16, bs) for b in range(2)] for a in range(2)]
                t_cw, t_hx, t_hu = Tk("cw"), Tk("hx"), Tk("hu")
                t_hid = [[Tk("hid%d%d" % (a, b)) for b in range(2)] for a in range(2)]
                P.dma("pool", w1k[:], din["w1k"], writes=[t_cw])
                P.dma("pool", w1v[:], din["w1v"], writes=[t_cw])
                P.dma("pool", w2k[:], din["w2k"], writes=[t_cw])
                P.dma("pool", w2v[:], din["w2v"], writes=[t_cw])
                P.dma("pool", ovl[:], din["c_ovl"], writes=[t_cw])
                MEMSET("pool", kcT[:], 0.0, [t_kcT])
                for g in range(2):
                    MEMSET("pool", rcmp[g][:, :, 128:129], 1.0, [t_rcmp])
                    CP("pool", rcmp[g][:, :, 64:128], ovl[:], [t_cw], [t_rcmp])
                for a in range(2):
                    for b in range(2):
                        MEMSET("pool", hid[a][b][:], 0.0, [t_hid[a][b]])
                for kind in range(2):
                    w1 = w1k if kind == 0 else w1v
                    A_, B_ = (KcA, KcB) if kind == 0 else (VcA, VcB)
                    A3 = A_[:].rearrange("p (i s) -> p i s", s=16)
                    B3 = B_[:].rearrange("p (i s) -> p i s", s=16)
                    for g in range(2):
                        R = slice(64 * g, 64 * g + 64)
                        for mc in range(2):
                            ph = ps[mc][:, 0:255]
                            for j in range(32):
                                rhs = A3[R, 0:255, j] if j < 16 else B3[R, 1:256, j - 16]
                                MM(ph, w1[R, j, mc * 128:(mc + 1) * 128], rhs, j == 0, j == 31,
                                   [t_cw, t_Kc], [t_ps[mc]])
                            CP("act", hx[:, 0:255], ph, [t_ps[mc]], [t_hx])
                            TT2("dve", hu[:, 0:255], hx[:, 0:255], hx[:, 0:255], ALU.mult, [t_hx], [t_hu])
                            TS("dve", hu[:, 0:255], hu[:, 0:255], 0.044715, 1.0, ALU.mult, ALU.add, [t_hu], [t_hu])
                            TT2("dve", hu[:, 0:255], hu[:, 0:255], hx[:, 0:255], ALU.mult, [t_hu, t_hx], [t_hu])
                            ACT(hu[:, 0:255], hu[:, 0:255], AF.Sigmoid, [t_hu], [t_hu], scale=1.5957691216057308)
                            TT2("dve", hid[g][mc][:, 0:255], hu[:, 0:255], hx[:, 0:255], ALU.mult,
                                [t_hu, t_hx], [t_hid[g][mc]])
                        if kind == 0:
                            pk = ps[2][:, 0:256]
                            for mc in range(2):
                                MM(pk, w2k[:, mc, :], hid[g][mc][:], mc == 0, mc == 1,
                                   [t_cw, t_hid[g][mc]], [t_ps[2]])
                            sqn = sq[0]
                            ACT(sqn[:, 0:256], pk, AF.Square, [t_ps[2]], [t_sq[0]])
                            pss = ps[3][:, 0:256]
                            MM(pss, bd_bf[:], sqn[:, 0:256], True, True, [t_const, t_sq[0]], [t_ps[3]])
                            ACT(rs[:, 0:256], pss, AF.Sqrt, [t_ps[3], t_const], [t_rs], bias=epsc[:, 0:1], scale=1.0 / 64)
                            RECIP(rs[:, 0:256], rs[:, 0:256], [t_rs], [t_rs])
                            STT(kcT[R, 0:255], pk[R, 0:255], gains[R, 3:4], rs[R, 0:255], ALU.mult, ALU.mult,
                                [t_ps[2], t_gains, t_rs], [t_kcT])
                        else:
                            for it in range(2):
                                pv = ps[2 + it][:, 0:64]
                                for mc in range(2):
                                    MM(pv, hid[g][mc][:, it * 128:(it + 1) * 128], w2v[:, mc, :], mc == 0, mc == 1,
                                       [t_cw, t_hid[g][mc]], [t_ps[2 + it]])
                                CP("act", rcmp[g][:, it, 0:64], pv, [t_ps[2 + it]], [t_rcmp])
                P.barrier()

            if stage == "BC":
                return
            with ExitStack() as ds:
                mwin = sb("mwin", [128, 8, 512], BF16, ds)
                mslc = sb("mslc", [128, 4, 512], BF16, ds)
                mcmp = sb("mcmp", [128, 5, 512], BF16, ds)
                selB = sb("selB", [128, 32, 64], F32, ds)
                gs = sb("gs", [128, 32, 24], F32, ds)
                PT = sb("PT", [128, 32, 512], BF16, ds)
                Pc = [sb("Pc%d" % i, [128, 512], BF16, ds) for i in range(2)]
                onsa = sb("onsa", [128, 4, 512], F32, ds)
                onsab = sb("onsab", [128, 4, 512], BF16, ds)
                oTs = sb("oTs", [128, 4, 512], BF16, ds)
                imp = [sb("imp%d" % g, [128, 4, 64], F32, ds) for g in range(2)]
                scr = sb("scr", [128, 64], F32, ds)
                wk = sb("wk", [128, 64], F32, ds)
                m8 = sb("m8", [128, 16], F32, ds)
                NM = sb("NM", [128, 4, 128], BF16, ds)
                sm = sb("sm", [128, 8], F32, ds)
                t_msk, t_selB, t_gs, t_onsa, t_onsab, t_oTs, t_scr, t_wk, t_m8, t_NM, t_sm = [Tk(n) for n in
                    ("msk", "selB", "gs", "onsa", "onsab", "oTs", "scr", "wk", "m8", "NM", "sm")]
                t_PT = [Tk("PT%d" % i) for i in range(32)]
                t_Pc = [Tk("Pc0"), Tk("Pc1")]
                t_imp = [Tk("imp0"), Tk("imp1")]
                P.dma("pool", mwin[:], din["c_mwin"], writes=[t_msk])
                P.dma("pool", mslc[:], din["c_mslc"], writes=[t_msk])
                P.dma("pool", mcmp[:], din["c_mcmp"], writes=[t_msk])
                P.dma("sp", selB[:], din["c_selB"], writes=[t_selB])
                ACT(gs[:], gl[:], AF.Sigmoid, [t_gl], [t_gs])
                onsaT_v = onsaT.rearrange("(c p) t -> p c t", p=128)
                BIG = 30000.0
                sbank = [0, 1, 2]
                sctr = [0]

                def next_sbank():
                    b = sbank[sctr[0] % 3]
                    sctr[0] += 1
                    return b

                def evac(po, sub, h, br, kt, first, imp_g=None, r=0):
                    dcol = 128 if br == 0 else 64
                    c0 = (h * 3 + br) % 4 * 2
                    TS("dve", sm[:, c0:c0 + 1], po[:, dcol:dcol + 1], 1e-30, None, ALU.max, None, [t_ps_cur[0]], [t_sm])
                    RECIP(sm[:, c0:c0 + 1], sm[:, c0:c0 + 1], [t_sm], [t_sm])
                    if br == 0:
                        if r == 0:
                            TS("dve", imp_g[:, sub, :], po[:, 64:128], sm[:, c0:c0 + 1], None, ALU.mult, None,
                               [t_ps_cur[0], t_sm], [t_imp[h // 4]])
                        else:
                            STT(imp_g[:, sub, :], po[:, 64:128], sm[:, c0:c0 + 1], imp_g[:, sub, :], ALU.mult, ALU.add,
                                [t_ps_cur[0], t_sm, t_imp[h // 4]], [t_imp[h // 4]])
                    TT2("dve", sm[:, c0 + 1:c0 + 2], sm[:, c0:c0 + 1], gs[:, kt, br * 8 + h:br * 8 + h + 1], ALU.mult,
                        [t_sm, t_gs], [t_sm])
                    dst = onsa[:, sub, h * 64:(h + 1) * 64]
                    if first:
                        TS("dve", dst, po[:, 0:64], sm[:, c0 + 1:c0 + 2], None, ALU.mult, None,
                           [t_ps_cur[0], t_sm], [t_onsa])
                    else:
                        STT(dst, po[:, 0:64], sm[:, c0 + 1:c0 + 2], dst, ALU.mult, ALU.add,
                            [t_ps_cur[0], t_sm, t_onsa], [t_onsa])

                t_ps_cur = [None]
                for Q in range(8):
                    qsl = slice(Q * 512, (Q + 1) * 512)
                    cts = [0] if Q < 4 else [0, 1]
                    for h in range(8):
                        g, r = h // 4, h % 4
                        R = slice(64 * g, 64 * g + 64)
                        for ct in cts:
                            b = next_sbank()
                            MM(ps[b][:, :], kcT[R, ct * 128:(ct + 1) * 128], Qs[h][R, qsl], True, True,
                               [t_kcT, t_Q], [t_ps[b]])
                            ACT(Pc[ct][:], ps[b][:, :], AF.Exp, [t_ps[b]], [t_Pc[ct]])
                            v = Q - 4 * ct
                            if v <= 4:
                                TT2("dve", Pc[ct][:], Pc[ct][:], mcmp[:, v, :], ALU.mult, [t_Pc[ct], t_msk], [t_Pc[ct]])
                        for sub in range(4):
                            bk = 3 + sub // 2
                            po = ps[bk][:, (sub % 2) * 129:(sub % 2) * 129 + 129]
                            for ci, ct in enumerate(cts):
                                MM(po, Pc[ct][:, sub * 128:(sub + 1) * 128], rcmp[g][:, ct, :], ci == 0, ci == len(cts) - 1,
                                   [t_Pc[ct], t_rcmp], [t_ps[bk]])
                        for sub in range(4):
                            bk = 3 + sub // 2
                            po = ps[bk][:, (sub % 2) * 129:(sub % 2) * 129 + 129]
                            t_ps_cur[0] = t_ps[bk]
                            evac(po, sub, h, 0, Q * 4 + sub, True, imp[g], r)
                    for sub in range(4):
                        kt = Q * 4 + sub
                        for g in range(2):
                            TT2("dve", scr[:], imp[g][:, sub, :], selB[:, kt, :], ALU.add, [t_imp[g], t_selB], [t_scr])
                            P.op("dve", lambda e: e.max(out=m8[:, 0:8], in_=scr[:]), [t_scr], [t_m8])
                            P.op("dve", lambda e: e.match_replace(out=wk[:], in_to_replace=m8[:, 0:8],
                                                                   in_values=scr[:], imm_value=-3.0e38),
                                 [t_scr, t_m8], [t_wk])
                            P.op("dve", lambda e: e.max(out=m8[:, 8:16], in_=wk[:]), [t_wk], [t_m8])
                            cs = slice(64, 128) if g == 0 else slice(0, 64)
                            TS("dve", NM[:, sub, cs], scr[:], m8[:, 15:16], -BIG, ALU.is_lt, ALU.mult,
                               [t_scr, t_m8], [t_NM])
                    pT = ps[7][:].bitcast(BF16)
                    for sub in range(4):
                        TR(pT[:, sub * 128:(sub + 1) * 128], NM[:, sub, :], id_bf[:], [t_NM, t_const], [t_ps[7]])
                    for h in range(8):
                        R2 = slice(64, 128) if h < 4 else slice(0, 64)
                        CP("act" if h % 2 == 0 else "dve", Qs[h][R2, qsl], pT[R2, 0:512], [t_ps[7]], [t_Qm[Q]])
                    for h in range(8):
                        g = h // 4
                        R = slice(64 * g, 64 * g + 64)
                        nkt = 4 * Q + 4
                        for kt in range(nkt):
                            b = next_sbank()
                            MM(ps[b][:, :], Ks[g][:, kt * 128:(kt + 1) * 128], Qs[h][:, qsl], True, True,
                               [t_Ks, t_Q, t_Qm[Q]], [t_ps[b]])
                            ACT(PT[:, kt, :], ps[b][:, :], AF.Exp, [t_ps[b]], [t_PT[kt]])
                            if kt >= 4 * Q:
                                TT2("dve", PT[:, kt, :], PT[:, kt, :], mslc[:, kt - 4 * Q, :], ALU.mult,
                                    [t_PT[kt], t_msk], [t_PT[kt]])
                        bk = 5
                        for sub in range(4):
                            po = ps[bk][:, sub * 65:sub * 65 + 65]
                            last = 4 * Q + sub
                            for kt in range(last + 1):
                                MM(po, PT[:, kt, sub * 128:(sub + 1) * 128], Vs[:, kt, g, 0:65], kt == 0, kt == last,
                                   [t_PT[kt], t_Vs], [t_ps[bk]])
                        t_ps_cur[0] = t_ps[bk]
                        for sub in range(4):
                            evac(ps[bk][:, sub * 65:sub * 65 + 65], sub, h, 1, Q * 4 + sub, False)
                        rr = [r_ for r_ in range(8) if 4 * Q - 4 + r_ >= 0]
                        for r_ in rr:
                            kt = 4 * Q - 4 + r_
                            b = next_sbank()
                            MM(ps[b][:, :], Kw[R, kt * 128:(kt + 1) * 128], Qs[h][R, qsl], True, True,
                               [t_Kw, t_Q], [t_ps[b]])
                            ACT(PT[:, r_, :], ps[b][:, :], AF.Exp, [t_ps[b]], [t_PT[r_]])
                            TT2("dve", PT[:, r_, :], PT[:, r_, :], mwin[:, r_, :], ALU.mult, [t_PT[r_], t_msk], [t_PT[r_]])
                        bk = 6
                        for sub in range(4):
                            po = ps[bk][:, sub * 65:sub * 65 + 65]
                            rs_ = [r_ for r_ in rr if sub <= r_ <= sub + 4]
                            for ci, r_ in enumerate(rs_):
                                kt = 4 * Q - 4 + r_
                                MM(po, PT[:, r_, sub * 128:(sub + 1) * 128], Vw[:, kt, g, 0:65], ci == 0, ci == len(rs_) - 1,
                                   [t_PT[r_], t_Vw], [t_ps[bk]])
                        t_ps_cur[0] = t_ps[bk]
                        for sub in range(4):
                            evac(ps[bk][:, sub * 65:sub * 65 + 65], sub, h, 2, Q * 4 + sub, False)
                    CP("act", onsab[:], onsa[:], [t_onsa], [t_onsab])
                    for fc in range(4):
                        pT2 = ps[7][:].bitcast(BF16)
                        for sub in range(4):
                            TR(pT2[:, sub * 128:(sub + 1) * 128], onsab[:, sub, fc * 128:(fc + 1) * 128], id_bf[:],
                               [t_onsab, t_const], [t_ps[7]])
                        CP("dve" if fc % 2 else "act", oTs[:, fc, :], pT2[:, 0:512], [t_ps[7]], [t_oTs])
                    P.dma("sp", onsaT_v[:, :, qsl], oTs[:], reads=[t_oTs], writes=[t_onsaT], chan=t_oTs)
                P.barrier()

    def ssm_phase():
        PI = math.pi
        with ExitStack() as fs:
            CT = sb("CT", [128, 16, 512], F32, fs)
            ST = sb("ST", [128, 16, 512], F32, fs)
            rhoF = sb("rhoF", [128, 16, 512], F32, fs)
            BBrT = sb("BBrT", [128, 16, 128], BF16, fs)
            BBiT = sb("BBiT", [128, 16, 128], BF16, fs)
            CreT = sb("CreT", [128, 16, 128], BF16, fs)
            CimT = sb("CimT", [128, 16, 128], BF16, fs)
            Dg = sb("Dg", [128, 4, 128], BF16, fs)
            dsk = sb("dsk", [128, 4], F32, fs)
            rho = sb("rho", [128, 16], F32, fs)
            car = [[sb("car%d%d" % (a, b), [128, 16], F32, fs) for b in range(2)] for a in range(2)]
            t_tab, t_bb, t_cc, t_dg, t_rho = [Tk(n) for n in ("tab", "bb", "cc", "dg", "rho")]
            t_car = [[Tk("car%d_%d" % (a, sc)) for sc in range(16)] for a in range(2)]
            P.dma("pool", CreT[:], din["s_creT"], writes=[t_cc])
            P.dma("pool", CimT[:], din["s_cimT"], writes=[t_cc])
            P.dma("sp", dsk[:], din["s_d"], writes=[t_dg])
            for oc in range(4):
                TS("dve", Dg[:, oc, :], id_f[:], dsk[:, oc:oc + 1], None, ALU.mult, None, [t_const, t_dg], [t_dg])
            nCreT = sb("nCreT", [128, 16, 128], BF16, fs)
            nCimT = sb("nCimT", [128, 16, 128], BF16, fs)
            TS("dve", nCreT[:], CreT[:], -1.0, None, ALU.mult, None, [t_cc], [t_cc])
            TS("dve", nCimT[:], CimT[:], -1.0, None, ALU.mult, None, [t_cc], [t_cc])
            with ExitStack() as ss:
                lr = sb("lr", [128, 16], F32, ss)
                li = sb("li", [128, 16], F32, ss)
                stp = sb("stp", [128, 16], F32, ss)
                th = sb("th", [128, 16], F32, ss)
                kk = sb("kk", [128, 16], F32, ss)
                r1 = sb("r1", [128, 16], F32, ss)
                r2 = sb("r2", [128, 16], F32, ss)
                s1 = sb("s1", [128, 16], F32, ss)
                c1 = sb("c1", [128, 16], F32, ss)
                w = [sb("w%d" % i, [128, 16], F32, ss) for i in range(6)]
                bre = sb("bre", [128, 16, 16], F32, ss)
                bim = sb("bim", [128, 16, 16], F32, ss)
                bbr = sb("bbr", [128, 16, 16], F32, ss)
                bbi = sb("bbi", [128, 16, 16], F32, ss)
                bt = [sb("bt%d" % i, [128, 16, 16], F32, ss) for i in range(2)]
                Zr = sb("Zr", [128, 16, 128], F32, ss)
                Zi = sb("Zi", [128, 16, 128], F32, ss)
                tA = sb("tA", [128, 16, 256], F32, ss)
                tB = sb("tB", [128, 16, 256], F32, ss)
                t_p = Tk("ssm_par")
                t_Z = Tk("Z")
                P.dma("sp", lr[:], din["s_lr"], writes=[t_p])
                P.dma("sp", li[:], din["s_li"], writes=[t_p])
                P.dma("sp", stp[:], din["s_ls"], writes=[t_p])
                P.dma("sp", bre[:], din["s_bre"], writes=[t_p])
                P.dma("sp", bim[:], din["s_bim"], writes=[t_p])
                pp = [t_p]
                ACT(stp[:], stp[:], AF.Exp, pp, pp)
                TT2("dve", th[:], li[:], stp[:], ALU.mult, pp, pp)
                TT2("dve", w[0][:], lr[:], stp[:], ALU.mult, pp, pp)
                ACT(rho[:], w[0][:], AF.Exp, pp, [t_rho])
                MEMSET("dve", kk[:], 0.0, pp)
                for m in range(1, 9):
                    STT(kk[:], th[:], (2 * m - 1) * PI, kk[:], ALU.is_gt, ALU.add, pp, pp)
                STT(r1[:], kk[:], -2.0 * PI, th[:], ALU.mult, ALU.add, pp, pp)
                TS("dve", r2[:], r1[:], PI / 2, None, ALU.add, None, pp, pp)
                TS("dve", w[1][:], r2[:], PI, None, ALU.is_gt, None, pp, pp)
                STT(r2[:], w[1][:], -2.0 * PI, r2[:], ALU.mult, ALU.add, pp, pp)
                ACT(s1[:], r1[:], AF.Sin, pp, pp)
                ACT(c1[:], r2[:], AF.Sin, pp, pp)
                ar, ai, arm1, den, cr, cim = w
                TT2("dve", ar[:], rho[:], c1[:], ALU.mult, pp + [t_rho], pp)
                TT2("dve", ai[:], rho[:], s1[:], ALU.mult, pp + [t_rho], pp)
                TS("dve", arm1[:], ar[:], -1.0, None, ALU.add, None, pp, pp)
                TT2("dve", den[:], lr[:], lr[:], ALU.mult, pp, pp)
                TT2("dve", kk[:], li[:], li[:], ALU.mult, pp, pp)
                TT2("dve", den[:], den[:], kk[:], ALU.add, pp, pp)
                RECIP(den[:], den[:], pp, pp)
                TT2("dve", cr[:], arm1[:], lr[:], ALU.mult, pp, pp)
                TT2("dve", kk[:], ai[:], li[:], ALU.mult, pp, pp)
                TT2("dve", cr[:], cr[:], kk[:], ALU.add, pp, pp)
                TT2("dve", cr[:], cr[:], den[:], ALU.mult, pp, pp)
                TT2("dve", cim[:], ai[:], lr[:], ALU.mult, pp, pp)
                TT2("dve", kk[:], arm1[:], li[:], ALU.mult, pp, pp)
                TT2("dve", cim[:], cim[:], kk[:], ALU.subtract, pp, pp)
                TT2("dve", cim[:], cim[:], den[:], ALU.mult, pp, pp)
                crb = cr[:].unsqueeze(2).to_broadcast([128, 16, 16])
                cib = cim[:].unsqueeze(2).to_broadcast([128, 16, 16])
                TT2("dve", bt[0][:], bre[:], crb, ALU.mult, pp, pp)
                TT2("dve", bt[1][:], bim[:], cib, ALU.mult, pp, pp)
                TT2("dve", bbr[:], bt[0][:], bt[1][:], ALU.subtract, pp, pp)
                TT2("dve", bt[0][:], bim[:], crb, ALU.mult, pp, pp)
                TT2("dve", bt[1][:], bre[:], cib, ALU.mult, pp, pp)
                TT2("dve", bbi[:], bt[0][:], bt[1][:], ALU.add, pp, pp)
                MEMSET("pool", Zr[:], 0.0, [t_Z])
                MEMSET("pool", Zi[:], 0.0, [t_Z])
                for sc in range(16):
                    c0 = 32 * (sc % 4)
                    for Z_, bb_ in ((Zr, bbr), (Zi, bbi)):
                        CP("dve", Z_[0:64, sc, c0:c0 + 16], bb_[0:64, sc, :], pp + [t_Z], [t_Z])
                        CP("dve", Z_[64:128, sc, c0 + 16:c0 + 32], bb_[64:128, sc, :], pp + [t_Z], [t_Z])
                for sc in range(16):
                    for bi_, (Z_, BT_) in enumerate(((Zr, BBrT), (Zi, BBiT))):
                        bk = (sc * 2 + bi_) % 4
                        TR(ps[bk][:, 0:128], Z_[:, sc, :], id_f[:], [t_Z, t_const], [t_ps[bk]])
                        CP("act", BT_[:, sc, :], ps[bk][:, 0:128], [t_ps[bk]], [t_bb])
                CP("dve", CT[:, :, 0:1], c1[:].unsqueeze(2), pp, [t_tab])
                CP("dve", ST[:, :, 0:1], s1[:].unsqueeze(2), pp, [t_tab])
                n = 1
                while n < 512:
                    cn = CT[:, :, n - 1:n].to_broadcast([128, 16, n])
                    sn = ST[:, :, n - 1:n].to_broadcast([128, 16, n])
                    tt_ = [t_tab]
                    TT2("dve", tA[:, :, 0:n], CT[:, :, 0:n], cn, ALU.mult, tt_, pp)
                    TT2("pool", tB[:, :, 0:n], ST[:, :, 0:n], sn, ALU.mult, tt_, [t_Z])
                    TT2("dve", CT[:, :, n:2 * n], tA[:, :, 0:n], tB[:, :, 0:n], ALU.subtract, pp + [t_Z], tt_)
                    TT2("dve", tA[:, :, 0:n], ST[:, :, 0:n], cn, ALU.mult, tt_, pp)
                    TT2("pool", tB[:, :, 0:n], CT[:, :, 0:n], sn, ALU.mult, tt_, [t_Z])
                    TT2("dve", ST[:, :, n:2 * n], tA[:, :, 0:n], tB[:, :, 0:n], ALU.add, pp + [t_Z], tt_)
                    n *= 2
                MEMSET("pool", rhoF[:], 1.0, [t_rho])
                for sc in range(16):
                    TS("pool", rhoF[:, sc, :], rhoF[:, sc, :], rho[:, sc:sc + 1], None, ALU.mult, None, [t_rho], [t_rho])
                for a in range(2):
                    for b in range(2):
                        MEMSET("pool", car[a][b][:], 0.0, [t_car[a][sc] for sc in range(16)])
                P.barrier()

            NB = 2
            uTt = [sb("uTt%d" % i, [128, 4, 512], BF16, fs) for i in range(3)]
            ygst = sb("ygst", [128, 4, 512], BF16, fs)
            f = [[sb("sf%d_%d" % (i, b), [128, 512], F32 if i < 8 else BF16, fs) for b in range(NB)] for i in range(12)]
            gx = sb("gx", [128, 512], F32, fs)
            gu = sb("gu", [128, 512], F32, fs)
            cu = sb("cu", [128, 4], F32, fs)
            t_u = [Tk("uTt%d" % i) for i in range(3)]
            t_f = [[Tk("sf%d_%d" % (i, b)) for b in range(NB)] for i in range(12)]
            t_gx, t_gu, t_ygst, t_cu = Tk("gx"), Tk("gu"), Tk("ygst"), Tk("cu")
            uT_v = uT.rearrange("(c p) t -> p c t", p=128)
            yg_v = ygT.rearrange("(c p) t -> p c t", p=128)
            P.dma("sp", uTt[0][:], uT_v[:, :, 0:512], reads=[t_uT], writes=[t_u[0]])
            tiles = [(tt, oc, sci) for tt in range(8) for oc in range(4) for sci in range(4)]

            def S1(i):
                tt, oc, sci = tiles[i]
                sc, b, ub = 4 * oc + sci, i % NB, tt % 3
                if oc == 0 and sci == 0 and tt + 1 < 8:
                    nb_ = (tt + 1) % 3
                    P.dma("sp", uTt[nb_][:], uT_v[:, :, (tt + 1) * 512:(tt + 2) * 512], reads=[t_uT], writes=[t_u[nb_]])
                pb = 2 * (i % 2)
                pur, pui = ps[pb][:, :], ps[pb + 1][:, :]
                C_, S_ = CT[:, sc, :], ST[:, sc, :]
                MM(pur, BBrT[:, sc, :], uTt[ub][:, oc, :], True, True, [t_bb, t_u[ub]], [t_ps[pb]])
                MM(pui, BBiT[:, sc, :], uTt[ub][:, oc, :], True, True, [t_bb, t_u[ub]], [t_ps[pb + 1]])
                TT2("dve", f[0][b][:], pur, C_, ALU.mult, [t_ps[pb], t_tab], [t_f[0][b]])
                TT2("dve", f[1][b][:], pui, S_, ALU.mult, [t_ps[pb + 1], t_tab], [t_f[1][b]])
                TT2("dve", f[2][b][:], pui, C_, ALU.mult, [t_ps[pb + 1], t_tab], [t_f[2][b]])
                TT2("dve", f[3][b][:], pur, S_, ALU.mult, [t_ps[pb], t_tab], [t_f[3][b]])

            def S2(i):
                b = i % NB
                TT2("pool", f[4][b][:], f[0][b][:], f[1][b][:], ALU.add, [t_f[0][b], t_f[1][b]], [t_f[4][b]])
                TT2("pool", f[5][b][:], f[2][b][:], f[3][b][:], ALU.subtract, [t_f[2][b], t_f[3][b]], [t_f[5][b]])

            def S3(i):
                tt, oc, sci = tiles[i]
                sc, b = 4 * oc + sci, i % NB
                cin, cout = car[tt % 2], car[(tt + 1) % 2]
                tcin, tcout = t_car[tt % 2], t_car[(tt + 1) % 2]
                P.op("dve", lambda e: e.tensor_tensor_scan(
                    out=f[6][b][:], data0=rhoF[:, sc, :], data1=f[4][b][:], initial=cin[0][:, sc:sc + 1],
                    op0=ALU.mult, op1=ALU.add), [t_rho, t_f[4][b], tcin[sc]], [t_f[6][b]])
                P.op("dve", lambda e: e.tensor_tensor_scan(
                    out=f[7][b][:], data0=rhoF[:, sc, :], data1=f[5][b][:], initial=cin[1][:, sc:sc + 1],
                    op0=ALU.mult, op1=ALU.add), [t_rho, t_f[5][b], tcin[sc]], [t_f[7][b]])
                zrL, ziL = f[6][b][:, 511:512], f[7][b][:, 511:512]
                cL, sL = CT[:, sc, 511:512], ST[:, sc, 511:512]
                TT2("dve", cu[:, 0:1], ziL, sL, ALU.mult, [t_f[7][b], t_tab], [t_cu])
                STT(cout[0][:, sc:sc + 1], zrL, cL, cu[:, 0:1], ALU.mult, ALU.subtract,
                    [t_f[6][b], t_tab, t_cu], [tcout[sc]])
                TT2("dve", cu[:, 1:2], ziL, cL, ALU.mult, [t_f[7][b], t_tab], [t_cu])
                STT(cout[1][:, sc:sc + 1], zrL, sL, cu[:, 1:2], ALU.mult, ALU.add,
                    [t_f[6][b], t_tab, t_cu], [tcout[sc]])
                TT2("dve", f[11][b][:], f[7][b][:], CT[:, sc, :], ALU.mult, [t_f[7][b], t_tab], [t_f[11][b]])

            def S4(i):
                tt, oc, sci = tiles[i]
                sc, b = 4 * oc + sci, i % NB
                C_, S_ = CT[:, sc, :], ST[:, sc, :]
                TT2("pool", f[8][b][:], f[6][b][:], C_, ALU.mult, [t_f[6][b], t_tab], [t_f[8][b]])
                TT2("pool", f[9][b][:], f[7][b][:], S_, ALU.mult, [t_f[7][b], t_tab], [t_f[9][b]])
                TT2("pool", f[10][b][:], f[6][b][:], S_, ALU.mult, [t_f[6][b], t_tab], [t_f[10][b]])

            def S5(i):
                tt, oc, sci = tiles[i]
                sc, b, ub = 4 * oc + sci, i % NB, tt % 3
                pyb = 4 + oc % 2
                py = ps[pyb][:, :]
                MM(py, CreT[:, sc, :], f[8][b][:], sci == 0, False, [t_cc, t_f[8][b]], [t_ps[pyb]])
                MM(py, nCreT[:, sc, :], f[9][b][:], False, False, [t_cc, t_f[9][b]], [t_ps[pyb]])
                MM(py, nCimT[:, sc, :], f[10][b][:], False, False, [t_cc, t_f[10][b]], [t_ps[pyb]])
                MM(py, nCimT[:, sc, :], f[11][b][:], False, False, [t_cc, t_f[11][b]], [t_ps[pyb]])
                if sci == 3:
                    MM(py, Dg[:, oc, :], uTt[ub][:, oc, :], False, True, [t_dg, t_u[ub]], [t_ps[pyb]])
                    CP("act", gx[:], py, [t_ps[pyb]], [t_gx])
                    ACT(gu[:], py, AF.Square, [t_ps[pyb]], [t_gu])
                    TS("pool", gu[:], gu[:], 0.044715, 1.0, ALU.mult, ALU.add, [t_gu], [t_gu])
                    TT2("pool", gu[:], gu[:], gx[:], ALU.mult, [t_gu, t_gx], [t_gu])
                    ACT(gu[:], gu[:], AF.Sigmoid, [t_gu], [t_gu], scale=1.5957691216057308)
                    TT2("pool", ygst[:, oc, :], gu[:], gx[:], ALU.mult, [t_gu, t_gx], [t_ygst])
                    if oc == 3:
                        P.dma("sp", yg_v[:, :, tt * 512:(tt + 1) * 512], ygst[:], reads=[t_ygst], writes=[t_ygT], chan=t_ygst)

            n_t = len(tiles)
            for k in range(n_t + 2):
                if k < n_t:
                    S1(k)
                    S2(k)
                if 0 <= k - 1 < n_t:
                    S3(k - 1)
                    S4(k - 1)
                if 0 <= k - 2 < n_t:
                    S5(k - 2)
            P.barrier()

    def merge_phase():
        TT = 256
        NT = S // TT
        with ExitStack() as fs:
            wF = sb("wF", [128, 8, 2048], BF16, fs)
            wnsa = sb("wnsa", [128, 4, 1024], BF16, fs)
            wglu = sb("wglu", [128, 4, 2048], BF16, fs)
            wout = sb("wout", [128, 8, 1024], BF16, fs)
            mg = sb("Fmg", [128, 8], F32, fs)
            xt = [sb("Fx%d" % i, [128, 8, TT], F32, fs) for i in range(2)]
            xn = sb("Fxn", [128, 8, TT], BF16, fs)
            sq = [sb("Fsq%d" % i, [128, TT], BF16, fs) for i in range(2)]
            rstd = sb("Frstd", [128, TT], F32, fs)
            ont = [sb("Fon%d" % i, [128, 4, TT], BF16, fs) for i in range(2)]
            ygt = [sb("Fyg%d" % i, [128, 4, TT], BF16, fs) for i in range(2)]
            sg3 = [[sb("Fsg%d_%d" % (i, b), [128, TT], F32, fs) for b in range(2)] for i in range(3)]
            ta = [sb("Fta%d" % b, [128, TT], F32, fs) for b in range(2)]
            tb = [sb("Ftb%d" % b, [128, TT], F32, fs) for b in range(2)]
            mrg = sb("Fmrg", [128, 8, TT], BF16, fs)
            sgA = sb("FsgA", [128, 16, TT], F32, fs)
            t_sgA = [Tk("FsgA%d" % j) for j in range(16)]
            ot = sb("Fot", [128, 8, TT], F32, fs)
            t_w, t_mg, t_xn, t_rstd, t_ot = [Tk(n) for n in ("Fw", "Fmg", "Fxn", "Frstd", "Fot")]
            t_xt = [Tk("Fxt0"), Tk("Fxt1")]
            t_sq = [Tk("Fsq0"), Tk("Fsq1")]
            t_on = [Tk("Fon0"), Tk("Fon1")]
            t_yg = [Tk("Fyg0"), Tk("Fyg1")]
            t_sg3 = [[Tk("Fsg%d_%d" % (i, b)) for b in range(2)] for i in range(3)]
            t_ta = [Tk("Fta0"), Tk("Fta1")]
            t_tb = [Tk("Ftb0"), Tk("Ftb1")]
            t_mrg = [Tk("Fmrg%d" % i) for i in range(8)]
            t_h = [[Tk("Fps%d_%d" % (i, b)) for b in range(2)] for i in range(6)]
            P.dma("sp", mg[:], din["mix_g"], writes=[t_mg])
            wF_v = din["w_inF"].rearrange("(k p) f -> p k f", p=128)
            for k in range(8):
                P.dma("pool", wF[:, k, :], wF_v[:, k, :], writes=[t_w])
            wn_v = din["w_nsa"].rearrange("(k p) f -> p k f", p=128)
            wgl_v = din["w_glu"].rearrange("(k p) f -> p k f", p=128)
            wo_v = din["w_out"].rearrange("(k p) f -> p k f", p=128)
            for k in range(4):
                P.dma("pool", wnsa[:, k, :], wn_v[:, k, :], writes=[t_w])
                P.dma("pool", wglu[:, k, :], wgl_v[:, k, :], writes=[t_w])
            for k in range(8):
                P.dma("pool", wout[:, k, :], wo_v[:, k, :], writes=[t_w])
            h1_v = h1T.rearrange("(c p) t -> p c t", p=128)
            h2_v = h2T.rearrange("(c p) t -> p c t", p=128)
            on_v = onsaT.rearrange("(c p) t -> p c t", p=128)
            yg_v = ygT.rearrange("(c p) t -> p c t", p=128)

            def load(i):
                b = i % 2
                tsl = slice(i * TT, (i + 1) * TT)
                P.dma("sp", xt[b][:], h1_v[:, :, tsl], reads=[t_h1T], writes=[t_xt[b]])
                P.dma("sp", ont[b][:], on_v[:, :, tsl], reads=[t_onsaT], writes=[t_on[b]])
                P.dma("sp", ygt[b][:], yg_v[:, :, tsl], reads=[t_ygT], writes=[t_yg[b]])

            load(0)
            for i in range(NT):
                b = i % 2
                if i + 1 < NT:
                    load(i + 1)
                xb = xt[b]
                norm_tile(xb, t_xt[b], mg, t_mg, xn, t_xn, sq, t_sq, rstd, t_rstd, TT, 6)
                for j in range(16):
                    bk = j % 4
                    pg_ = ps[bk][:, 0:TT]
                    cs_ = slice(j * 128, (j + 1) * 128)
                    for k in range(8):
                        MM(pg_, wF[:, k, cs_], xn[:, k, :], k == 0, k == 7, [t_w, t_xn], [t_ps[bk]])
                    ACT(sgA[:, j, :], pg_, AF.Sigmoid, [t_ps[bk]], [t_sgA[j]])
                for dc in range(8):
                    hb = dc % 2
                    bks = (0, 1, 2) if hb == 0 else (3, 4, 5)
                    pyn, pval, pgt = [ps[j][:, 0:TT] for j in bks]
                    tp = [t_ps[j] for j in bks]
                    dsl = slice(dc * 128, (dc + 1) * 128)
                    dsl2 = slice(1024 + dc * 128, 1024 + (dc + 1) * 128)
                    for k in range(4):
                        MM(pyn, wnsa[:, k, dsl], ont[b][:, k, :], k == 0, k == 3, [t_w, t_on[b]], [tp[0]])
                    for k in range(4):
                        MM(pval, wglu[:, k, dsl], ygt[b][:, k, :], k == 0, k == 3, [t_w, t_yg[b]], [tp[1]])
                    for k in range(4):
                        MM(pgt, wglu[:, k, dsl2], ygt[b][:, k, :], k == 0, k == 3, [t_w, t_yg[b]], [tp[2]])
                    ACT(sg3[2][hb][:], pgt, AF.Sigmoid, [tp[2]], [t_sg3[2][hb]])
                    TT2("dve", ta[hb][:], pyn, sgA[:, dc, :], ALU.mult, [tp[0], t_sgA[dc]], [t_ta[hb]])
                    TT2("dve", tb[hb][:], pval, sg3[2][hb][:], ALU.mult, [tp[1], t_sg3[2][hb]], [t_tb[hb]])
                    TT2("pool", tb[hb][:], tb[hb][:], sgA[:, 8 + dc, :], ALU.mult, [t_tb[hb], t_sgA[8 + dc]], [t_tb[hb]])
                    TT2("pool", mrg[:, dc, :], ta[hb][:], tb[hb][:], ALU.add, [t_ta[hb], t_tb[hb]], [t_mrg[dc]])
                for dc in range(8):
                    bk = 7 if dc % 2 == 0 else 6
                    po = ps[bk][:, 0:TT]
                    for k in range(8):
                        MM(po, wout[:, k, dc * 128:(dc + 1) * 128], mrg[:, k, :], k == 0, k == 7,
                           [t_w, t_mrg[k]], [t_ps[bk]])
                    TT2("dve", ot[:, dc, :], po, xb[:, dc, :], ALU.add, [t_ps[bk], t_xt[b]], [t_ot])
                if "F3b" not in DBG:
                    P.dma("sp", h2_v[:, :, i * TT:(i + 1) * TT], ot[:], reads=[t_ot], writes=[t_h2T], chan=t_ot)
            P.barrier()

    if stage == "ffn1":
        ffn_phase("f1", xT, t_xT, outT, t_outT, din["f1_g"], din["f1_wg"], din["f1_wu"], din["f1_wd"])
    else:
        ffn_phase("f1", xT, t_xT, h1T, t_h1T, din["f1_g"], din["f1_wg"], din["f1_wu"], din["f1_wd"])
        mixer_attention()
        if stage not in ("B", "BC", "D"):
            ssm_phase()
        if stage not in ("B", "BC", "D", "E"):
            merge_phase()
        if stage == "full":
            ffn_phase("f2", h2T, t_h2T, outT, t_outT, din["f2_g"], din["f2_wg"], din["f2_wu"], din["f2_wd"])

    P.final_wait([t_outT])
    P.barrier()
    es.close()
    return nc, P


def _consts():
    f = np.float32
    c = {}
    c["c_id"] = np.eye(128, dtype=f)
    bd = np.zeros((128, 128), f)
    bd[:64, :64] = 1.0
    bd[64:, 64:] = 1.0
    c["c_bd"] = bd
    k = np.arange(S)
    c["c_E"] = (k[None, :] // 64 == np.arange(64)[:, None]).astype(f)
    ci = np.arange(256)
    j = np.arange(64)
    ovl = ((ci[:, None] * 16 < (j[None, :] + 1) * 64) & (ci[:, None] * 16 + 32 > j[None, :] * 64)).astype(f)
    ovl[255] = 0.0
    c["c_ovl"] = np.ascontiguousarray(ovl.reshape(2, 128, 64).transpose(1, 0, 2))
    t = np.arange(S)
    qblk = t // 64
    force = (j[None, :] == 0) | (j[None, :] == qblk[:, None]) | (j[None, :] == qblk[:, None] - 1)
    selB = np.where(j[None, :] <= qblk[:, None], 1000.0 * force.astype(f), -1e30).astype(f)
    c["c_selB"] = np.ascontiguousarray(selB.reshape(32, 128, 64).transpose(1, 0, 2))
    kl = np.arange(128)[:, None]
    tl = np.arange(512)[None, :]
    mwin = np.zeros((128, 8, 512), f)
    for r in range(8):
        dlt = 512 + tl - 128 * r - kl
        mwin[:, r, :] = ((dlt >= 0) & (dlt < 512)).astype(f)
    c["c_mwin"] = mwin
    mslc = np.zeros((128, 4, 512), f)
    for r in range(4):
        mslc[:, r, :] = ((tl - 128 * r - kl) >= 0).astype(f)
    c["c_mslc"] = mslc
    mcmp = np.zeros((128, 5, 512), f)
    for v in range(5):
        mcmp[:, v, :] = ((512 * v + tl - 16 * kl - 31) >= 0).astype(f)
    c["c_mcmp"] = mcmp
    return c


def _prep_shared(inp):
    f = np.float32
    A = lambda a: np.ascontiguousarray(a, dtype=f)
    sh = dict(_consts())
    for tag, nm in (("f1", "ffn1"), ("f2", "ffn2")):
        sh[tag + "_g"] = A(inp[nm + "_norm"][0].reshape(8, 128).T)
        sh[tag + "_wg"] = A(inp[nm + "_w_gate"][0])
        sh[tag + "_wu"] = A(inp[nm + "_w_up"][0])
        sh[tag + "_wd"] = A(inp[nm + "_w_down"][0])
    sh["mix_g"] = A(inp["mix_norm"][0].reshape(8, 128).T)
    w = inp["w_in"][0]
    q = w[:, 0:512]
    blocks = []
    for r in range(4):
        blocks.append(q[:, r * 64:(r + 1) * 64])
        blocks.append(q[:, (4 + r) * 64:(5 + r) * 64])
    blocks += [w[:, 768:896], w[:, 1024:1152], w[:, 512:640], w[:, 640:768], w[:, 1304:1816],
               w[:, 896:1024], w[:, 1152:1280], w[:, 1280:1304]]
    sh["w_inB"] = A(np.concatenate(blocks, axis=1))
    assert sh["w_inB"].shape == (1024, 1816)
    sh["w_inF"] = A(w[:, 1816:3864])
    rep = lambda v: np.concatenate([v, v])
    sh["gains"] = A(np.stack([rep(inp["q_norm"][0]), rep(inp["k_norm_slc"][0]),
                              rep(inp["k_norm_win"][0]), rep(inp["k_norm_cmp"][0])], axis=1))
    sh["posk"] = A(np.concatenate([inp["cmp_pos_k"][0].T] * 2, axis=0))
    sh["posv"] = A(np.concatenate([inp["cmp_pos_v"][0].T] * 2, axis=0))
    w1 = lambda a: A(np.concatenate([a.reshape(32, 64, 256).transpose(1, 0, 2)] * 2, axis=0))
    sh["w1k"] = w1(inp["cmp_k_w1"][0])
    sh["w1v"] = w1(inp["cmp_v_w1"][0])
    w2k = inp["cmp_k_w2"][0]
    sh["w2k"] = A(np.concatenate([w2k, w2k], axis=1).reshape(2, 128, 128).transpose(1, 0, 2))
    sh["w2v"] = A(inp["cmp_v_w2"][0].reshape(2, 128, 64).transpose(1, 0, 2))
    sh["w_nsa"] = A(inp["w_nsa_proj"][0])
    sh["w_glu"] = A(inp["ssm_glu_w"][0])
    sh["w_out"] = A(inp["w_out"][0])
    sm = lambda a: A(a.reshape(16, 128).T)
    sh["s_lr"] = sm(inp["ssm_lambda_re"][0])
    sh["s_li"] = sm(inp["ssm_lambda_im"][0])
    sh["s_ls"] = sm(np.repeat(inp["ssm_log_step"][0][:, None], 64, axis=1))
    sb_ = lambda a: A(a.reshape(16, 128, 16).transpose(1, 0, 2))
    sh["s_bre"] = sb_(inp["ssm_b_re"][0])
    sh["s_bim"] = sb_(inp["ssm_b_im"][0])

    def cplace(cc):
        o = np.zeros((128, 16, 128), f)
        for g in range(32):
            sc, gl = g // 2, g % 2
            c0 = 32 * (sc % 4) + 16 * gl
            o[gl * 64:(gl + 1) * 64, sc, c0:c0 + 16] = cc[g].T
        return o
    sh["s_creT"] = cplace(inp["ssm_c_re"][0])
    sh["s_cimT"] = cplace(inp["ssm_c_im"][0])
    sh["s_d"] = A(inp["ssm_d"][0].reshape(4, 128).T)
    return sh


IN_SHAPES = {
    "c_id": [128, 128], "c_bd": [128, 128], "c_E": [64, S], "c_ovl": [128, 2, 64], "c_selB": [128, 32, 64],
    "c_mwin": [128, 8, 512], "c_mslc": [128, 4, 512], "c_mcmp": [128, 5, 512],
    "f1_g": [128, 8], "f1_wg": [D, DFF], "f1_wu": [D, DFF], "f1_wd": [DFF, D],
    "f2_g": [128, 8], "f2_wg": [D, DFF], "f2_wu": [D, DFF], "f2_wd": [DFF, D],
    "mix_g": [128, 8], "w_inB": [D, 1816], "w_inF": [D, 2048], "gains": [128, 4],
    "posk": [128, 32], "posv": [128, 32], "w1k": [128, 32, 256], "w1v": [128, 32, 256],
    "w2k": [128, 2, 128], "w2v": [128, 2, 64], "w_nsa": [512, D], "w_glu": [512, 2 * D], "w_out": [D, D],
    "s_lr": [128, 16], "s_li": [128, 16], "s_ls": [128, 16], "s_bre": [128, 16, 16], "s_bim": [128, 16, 16],
    "s_creT": [128, 16, 128], "s_cimT": [128, 16, 128], "s_d": [128, 4],
}

STAGE = "full"


def kernel(**inputs):
    inp = {k: np.asarray(v) for k, v in inputs.items()}
    x = inp["x"]
    nc, P = build(STAGE)
    sh = _prep_shared(inp)
    in_maps = []
    for b in range(NCORES):
        m = dict(sh)
        m["xT"] = np.ascontiguousarray(x[b].T)
        in_maps.append(m)
    res = run_bass_kernel_spmd(nc, in_maps, core_ids=list(range(NCORES)))
    out = np.stack([np.ascontiguousarray(r["outT"].T) for r in res.results], axis=0)
    return out.astype(np.float32)
```

```python
import math
import os
from contextlib import ExitStack

import numpy as np
import concourse.bass as bass
import concourse.mybir as mybir
from concourse.bass_utils import run_bass_kernel_spmd

F32 = mybir.dt.float32
BF16 = mybir.dt.bfloat16
AF = mybir.ActivationFunctionType
ALU = mybir.AluOpType

S = 4096
D = 1024
DFF = 2816
NCORES = 8
EPS = 1e-6
DBG = set(os.environ.get("DBGSKIP", "").split(","))


class Tk:
    __slots__ = ("name", "w", "rs", "dsem", "dcnt")

    def __init__(self, name):
        self.name = name
        self.w = []
        self.rs = []
        self.dsem = None
        self.dcnt = 0


class Prog:
    ENG = ("pe", "act", "dve", "pool", "sp")

    def __init__(self, nc, es):
        self.nc = nc
        self.es = es
        self.e = {"pe": nc.tensor, "act": nc.scalar, "dve": nc.vector,
                  "pool": nc.gpsimd, "sp": nc.sync}
        self.nsem = 0
        self.sem = {}
        self.cnt = {}
        for k in self.ENG:
            self.sem[k] = self.new_sem("e_" + k)
            self.cnt[k] = 0
        self.seen = {k: {} for k in self.ENG}
        self.dma_sems = []
        self.n_inst = 0

    def new_sem(self, name):
        self.nsem += 1
        return self.es.enter_context(self.nc.semaphore(name + "_%d" % self.nsem))

    def _wait(self, eng, sem, val):
        d = self.seen[eng]
        key = id(sem)
        if d.get(key, (None, 0))[1] >= val:
            return
        d[key] = (sem, val)
        self.e[eng].wait_ge(sem, val)

    def _deps(self, eng, reads, writes, same_engine_sync=True):
        need = {}

        def add(ev):
            sem, val, src = ev
            if src == eng and not same_engine_sync:
                return
            if isinstance(src, tuple):
                val = src[0].dsem[src[1]][1]
            k = id(sem)
            if k not in need or need[k][1] < val:
                need[k] = (sem, val)

        for t in reads:
            for ev in t.w:
                add(ev)
        for t in writes:
            for ev in t.w:
                add(ev)
            for ev in t.rs:
                add(ev)
        for sem, val in need.values():
            self._wait(eng, sem, val)

    def _roll(self, eng):
        if self.cnt[eng] >= 8000:
            self.sem[eng] = self.new_sem("e_" + eng)
            self.cnt[eng] = 0

    def op(self, eng, fn, reads=(), writes=(), sync_same=True):
        self._roll(eng)
        self._deps(eng, reads, writes, same_engine_sync=sync_same)
        ins = fn(self.e[eng])
        self.cnt[eng] += 1
        ev = (self.sem[eng], self.cnt[eng], eng)
        ins.then_inc(self.sem[eng], 1)
        self.n_inst += 1
        for t in reads:
            t.rs.append(ev)
            if len(t.rs) > 24:
                t.rs = self._compact(t.rs)
        for t in writes:
            t.w = [ev]
            t.rs = []
        return ins

    @staticmethod
    def _compact(evs):
        best = {}
        for sem, val, src in evs:
            k = id(sem)
            if k not in best or best[k][1] < val:
                best[k] = (sem, val, src)
        return list(best.values())

    def dma(self, q, out_ap, in_ap, reads=(), writes=(), chan=None):
        if chan is None:
            chan = writes[0] if writes else reads[0]
        if chan.dsem is None:
            chan.dsem = {}
        if q not in chan.dsem:
            chan.dsem[q] = [self.new_sem("d_" + chan.name + "_" + q), 0]
            self.dma_sems.append((chan, q))
        ent = chan.dsem[q]
        self._deps(q, reads, writes)
        ins = self.e[q].dma_start(out=out_ap, in_=in_ap)
        ent[1] += 16
        ins.then_inc(ent[0], 16)
        ev = (ent[0], ent[1], (chan, q))
        self.n_inst += 1
        for t in reads:
            t.rs.append(ev)
            if len(t.rs) > 24:
                t.rs = self._compact(t.rs)
        for t in writes:
            t.w = [e for e in t.w if isinstance(e[2], tuple) and e[0] is not ent[0]] + [ev]
            t.rs = []
        return ins

    def barrier(self):
        for x in self.ENG:
            for y in self.ENG:
                if y != x and self.cnt[y] > 0:
                    self._wait(x, self.sem[y], self.cnt[y])
            for t, q in self.dma_sems:
                ent = t.dsem[q]
                if ent[1] > 0:
                    self._wait(x, ent[0], ent[1])

    def final_wait(self, tks):
        for t in tks:
            for sem, val, _ in t.w:
                self._wait("sp", sem, val)


def build(stage="full", debug=False):
    nc = bass.Bass("TRN2", target_bir_lowering=False)
    es = ExitStack()
    P = Prog(nc, es)

    def dram_in(name, shape, dt=F32):
        return nc.dram_tensor(name, list(shape), dt, kind="ExternalInput").ap()

    def dram_out(name, shape, dt=F32):
        return nc.dram_tensor(name, list(shape), dt, kind="ExternalOutput").ap()

    def dram_tmp(name, shape, dt=F32):
        kind = "ExternalOutput" if debug else "Internal"
        return nc.dram_tensor(name, list(shape), dt, kind=kind).ap()

    def sb(name, shape, dt, stack=None):
        return (stack or es).enter_context(nc.sbuf_tensor("sb_" + name, list(shape), dt))

    xT = dram_in("xT", [D, S])
    outT = dram_out("outT", [D, S])
    din = {}
    for nm, shp in IN_SHAPES.items():
        din[nm] = dram_in(nm, shp)
    h1T = dram_tmp("h1T", [D, S])
    h2T = dram_tmp("h2T", [D, S])
    uT = dram_tmp("uT", [512, S], BF16)
    onsaT = dram_tmp("onsaT", [512, S], BF16)
    ygT = dram_tmp("ygT", [512, S], BF16)
    t_xT, t_outT, t_h1T, t_h2T, t_uT, t_onsaT, t_ygT = [Tk(n) for n in
        ("xT", "outT", "h1T", "h2T", "uT", "onsaT", "ygT")]

    ps = [es.enter_context(nc.psum_tensor("ps%d" % i, [128, 512], F32)) for i in range(8)]
    t_ps = [Tk("ps%d" % i) for i in range(8)]

    def MM(out, lhsT, rhs, st, sp, reads, writes):
        P.op("pe", lambda e: e.matmul(out, lhsT=lhsT, rhs=rhs, start=st, stop=sp),
             reads, writes, sync_same=False)

    def TR(out, in_, ident, reads, writes):
        P.op("pe", lambda e: e.transpose(out, in_, ident), reads, writes, sync_same=False)

    def TT2(eng, out, a, b, op, reads, writes):
        P.op(eng, lambda e: e.tensor_tensor(out=out, in0=a, in1=b, op=op), reads, writes)

    def TS(eng, out, a, s1, s2, op0, op1, reads, writes):
        if s2 is None:
            P.op(eng, lambda e: e.tensor_scalar(out=out, in0=a, scalar1=s1, scalar2=None, op0=op0),
                 reads, writes)
        else:
            P.op(eng, lambda e: e.tensor_scalar(out=out, in0=a, scalar1=s1, scalar2=s2, op0=op0, op1=op1),
                 reads, writes)

    def STT(out, a, sc, b, op0, op1, reads, writes):
        P.op("dve", lambda e: e.scalar_tensor_tensor(out=out, in0=a, scalar=sc, in1=b, op0=op0, op1=op1),
             reads, writes)

    def ACT(out, in_, func, reads, writes, bias=None, scale=None):
        kw = {}
        if bias is not None:
            kw["bias"] = bias
        if scale is not None:
            kw["scale"] = scale
        P.op("act", lambda e: e.activation(out=out, in_=in_, func=func, **kw), reads, writes)

    def CP(eng, out, in_, reads, writes):
        if eng == "act":
            P.op("act", lambda e: e.activation(out=out, in_=in_, func=AF.Identity), reads, writes)
        else:
            P.op(eng, lambda e: e.tensor_copy(out=out, in_=in_), reads, writes)

    def RECIP(out, in_, reads, writes):
        P.op("dve", lambda e: e.reciprocal(out=out, in_=in_), reads, writes)

    def MEMSET(eng, ap, val, writes):
        P.op(eng, lambda e: e.memset(ap, val), (), writes)

    ones_bf = sb("ones_bf", [128, 128], BF16)
    bd_bf = sb("bd_bf", [128, 128], BF16)
    id_bf = sb("id_bf", [128, 128], BF16)
    id_f = sb("id_f", [128, 128], F32)
    epsc = sb("epsc", [128, 1], F32)
    t_const = Tk("const")
    MEMSET("pool", ones_bf[:], 1.0, [t_const])
    MEMSET("pool", epsc[:], EPS, [t_const])
    P.dma("pool", bd_bf[:], din["c_bd"], writes=[t_const])
    P.dma("pool", id_bf[:], din["c_id"], writes=[t_const])
    P.dma("sp", id_f[:], din["c_id"], writes=[t_const])

    def norm_tile(xb, t_xb, gsb, t_g, xn, t_xn, sq, t_sq, rstd, t_rstd, TT, bank):
        pstat = ps[bank][:, 0:TT]
        for c in range(8):
            TT2("pool", sq[c % 2][:, 0:TT], xb[:, c, :], xb[:, c, :], ALU.mult, [t_xb], [t_sq[c % 2]])
            MM(pstat, ones_bf[:], sq[c % 2][:, 0:TT], c == 0, c == 7, [t_const, t_sq[c % 2]], [t_ps[bank]])
        ACT(rstd[:, 0:TT], pstat, AF.Sqrt, [t_ps[bank], t_const], [t_rstd], bias=epsc[:, 0:1], scale=1.0 / D)
        RECIP(rstd[:, 0:TT], rstd[:, 0:TT], [t_rstd], [t_rstd])
        for c in range(8):
            STT(xn[:, c, :], xb[:, c, :], gsb[:, c:c + 1], rstd[:, 0:TT], ALU.mult, ALU.mult,
                [t_xb, t_g, t_rstd], [t_xn])

    def ffn_phase(tag, src, t_src, dst, t_dst, g_dram, wg_d, wu_d, wd_d):
        TT = 256
        NT = S // TT
        with ExitStack() as fs:
            wg = sb(tag + "wg", [128, 8, DFF], BF16, fs)
            wu = sb(tag + "wu", [128, 8, DFF], BF16, fs)
            wd = sb(tag + "wd", [128, 22, D], BF16, fs)
            gsb = sb(tag + "g", [128, 8], F32, fs)
            xt = [sb(tag + "x%d" % i, [128, 8, TT], F32, fs) for i in range(2)]
            xn = [sb(tag + "xn%d" % i, [128, 8, TT], BF16, fs) for i in range(2)]
            sq8 = sb(tag + "sq8", [128, 8, TT], BF16, fs)
            rstd = sb(tag + "rstd", [128, TT], F32, fs)
            hh = sb(tag + "h", [128, 22, TT], BF16, fs)
            sg = [sb(tag + "sg%d" % i, [128, TT], F32, fs) for i in range(2)]
            ot = sb(tag + "ot", [128, 8, TT], F32, fs)
            t_wg = [Tk(tag + "wg%d" % k) for k in range(8)]
            t_wu = [Tk(tag + "wu%d" % k) for k in range(8)]
            t_wd = [Tk(tag + "wd%d" % k) for k in range(22)]
            t_g = Tk(tag + "g")
            t_xt = [Tk(tag + "xt%d" % i) for i in range(2)]
            t_xn = [Tk(tag + "xn%d" % i) for i in range(2)]
            t_sq8 = [Tk(tag + "sq8_%d" % i) for i in range(8)]
            t_rstd = Tk(tag + "rstd")
            t_hh = [Tk(tag + "hh%d" % i) for i in range(22)]
            t_sg = [Tk(tag + "sg%d" % i) for i in range(2)]
            t_ot = Tk(tag + "ot")
            src_v = src.rearrange("(c p) t -> p c t", p=128)
            dst_v = dst.rearrange("(c p) t -> p c t", p=128)

            def load_x(i):
                b = i % 2
                P.dma("sp", xt[b][:], src_v[:, :, i * TT:(i + 1) * TT], reads=[t_src], writes=[t_xt[b]])

            def norm1(i):
                b = i % 2
                for c in range(8):
                    TT2("pool", sq8[:, c, :], xt[b][:, c, :], xt[b][:, c, :], ALU.mult, [t_xt[b]], [t_sq8[c]])

            def norm2(i):
                b = i % 2
                pstat = ps[6][:, 0:TT]
                for c in range(8):
                    MM(pstat, ones_bf[:], sq8[:, c, :], c == 0, c == 7, [t_const, t_sq8[c]], [t_ps[6]])
                ACT(rstd[:], pstat, AF.Sqrt, [t_ps[6], t_const], [t_rstd], bias=epsc[:, 0:1], scale=1.0 / D)
                RECIP(rstd[:], rstd[:], [t_rstd], [t_rstd])
                for c in range(8):
                    STT(xn[b][:, c, :], xt[b][:, c, :], gsb[:, c:c + 1], rstd[:], ALU.mult, ALU.mult,
                        [t_xt[b], t_g, t_rstd], [t_xn[b]])

            P.dma("sp", gsb[:], g_dram, writes=[t_g])
            load_x(0)
            wg_v = wg_d.rearrange("(k p) f -> p k f", p=128)
            wu_v = wu_d.rearrange("(k p) f -> p k f", p=128)
            wd_v = wd_d.rearrange("(k p) d -> p k d", p=128)
            ch_a, ch_b = Tk(tag + "chA"), Tk(tag + "chB")
            for k in range(8):
                P.dma("pool", wg[:, k, :], wg_v[:, k, :], writes=[t_wg[k]], chan=ch_a)
                P.dma("pool", wu[:, k, :], wu_v[:, k, :], writes=[t_wu[k]], chan=ch_a)
            for k in range(22):
                P.dma("pool", wd[:, k, :], wd_v[:, k, :], writes=[t_wd[k]], chan=ch_b)
            norm1(0)
            norm2(0)
            for i in range(NT):
                b = i % 2
                xb = xt[b]
                if i + 1 < NT:
                    load_x(i + 1)
                    norm1(i + 1)
                for fc in range(22):
                    pb = fc % 2
                    pg = ps[0 + pb][:, 0:TT]
                    pu = ps[2 + pb][:, 0:TT]
                    for k in range(8):
                        MM(pg, wg[:, k, fc * 128:(fc + 1) * 128], xn[b][:, k, :], k == 0, k == 7,
                           [t_wg[k], t_xn[b]], [t_ps[0 + pb]])
                    for k in range(8):
                        MM(pu, wu[:, k, fc * 128:(fc + 1) * 128], xn[b][:, k, :], k == 0, k == 7,
                           [t_wu[k], t_xn[b]], [t_ps[2 + pb]])
                    ACT(sg[pb][:], pg, AF.Silu, [t_ps[0 + pb]], [t_sg[pb]])
                    TT2("dve", hh[:, fc, :], pu, sg[pb][:], ALU.mult, [t_ps[2 + pb], t_sg[pb]], [t_hh[fc]])
                if i + 1 < NT:
                    norm2(i + 1)
                for dc in range(8):
                    pb = dc % 2
                    py = ps[4 + pb][:, 0:TT]
                    for fc in range(22):
                        MM(py, wd[:, fc, dc * 128:(dc + 1) * 128], hh[:, fc, :], fc == 0, fc == 21,
                           [t_wd[fc], t_hh[fc]], [t_ps[4 + pb]])
                    STT(ot[:, dc, :], py, 0.5, xb[:, dc, :], ALU.mult, ALU.add,
                        [t_ps[4 + pb], t_xt[b]], [t_ot])
                P.dma("sp", dst_v[:, :, i * TT:(i + 1) * TT], ot[:], reads=[t_ot], writes=[t_dst], chan=t_ot)
            P.barrier()

    def mixer_attention():
        with ExitStack() as ms:
            Qs = [sb("Qs%d" % h, [128, S], BF16, ms) for h in range(8)]
            Ks = [sb("Ks%d" % g, [128, S], BF16, ms) for g in range(2)]
            Kw = sb("Kw", [128, S], BF16, ms)
            Vs = sb("Vs", [128, 32, 2, 72], BF16, ms)
            Vw = sb("Vw", [128, 32, 2, 72], BF16, ms)
            gl = sb("gl", [128, 32, 24], F32, ms)
            gains = sb("gains", [128, 4], F32, ms)
            gq8 = sb("gq8", [128, 1], F32, ms)
            kcT = sb("kcT", [128, 256], BF16, ms)
            rcmp = [sb("rcmp%d" % g, [128, 2, 129], BF16, ms) for g in range(2)]
            t_Q = Tk("Qq")
            t_Qm = [Tk("Qm%d" % i) for i in range(8)]
            t_Ks, t_Kw, t_Vs, t_Vw, t_gl, t_gains, t_kcT, t_rcmp = [Tk(n) for n in
                ("Ks", "Kw", "Vs", "Vw", "gl", "gains", "kcT", "rcmp")]
            P.dma("sp", gains[:], din["gains"], writes=[t_gains])
            TS("dve", gq8[:], gains[:, 0:1], 0.125, None, ALU.mult, None, [t_gains], [t_gains])
            if "E" not in DBG:
                P.dma("pool", Ks[0][64:128, :], din["c_E"], writes=[t_Ks])
                P.dma("pool", Ks[1][0:64, :], din["c_E"], writes=[t_Ks])
            if "V1" not in DBG:
                MEMSET("pool", Vs[:, :, :, 64:65], 1.0, [t_Vs])
                MEMSET("pool", Vw[:, :, :, 64:65], 1.0, [t_Vw])

            with ExitStack() as bs:
                TT = 512
                sq = [sb("Bsq%d" % i, [128, TT], BF16, bs) for i in range(2)]
                rs = sb("Brs", [128, TT], F32, bs)
                KcA = sb("KcA", [128, S], BF16, bs)
                KcB = sb("KcB", [128, S], BF16, bs)
                VcA = sb("VcA", [128, S], BF16, bs)
                VcB = sb("VcB", [128, S], BF16, bs)
                b2 = ExitStack()
                wB = sb("wB", [128, 8, 1816], BF16, b2)
                mg = sb("mg", [128, 8], F32, b2)
                xt = sb("Bx", [128, 8, TT], F32, b2)
                xn = sb("Bxn", [128, 8, TT], BF16, b2)
                rstd = sb("Brstd", [128, TT], F32, b2)
                ust = sb("Bust", [128, 4, TT], BF16, b2)
                posk = sb("posk", [128, 32], F32, b2)
                posv = sb("posv", [128, 32], F32, b2)
                vstg = sb("vstg", [128, 256], BF16, b2)
                t_vstg = Tk("vstg")
                t_wB, t_mg, t_xt, t_xn, t_rstd, t_rs, t_ust, t_Kc, t_pos = [Tk(n) for n in
                    ("wB", "mg", "Bxt", "Bxn", "Brstd", "Brs", "Bust", "Kc", "pos")]
                t_sq = [Tk("Bsq0"), Tk("Bsq1")]
                wB_v = din["w_inB"].rearrange("(k p) f -> p k f", p=128)
                for k in range(8):
                    P.dma("pool", wB[:, k, :], wB_v[:, k, :], writes=[t_wB])
                P.dma("sp", mg[:], din["mix_g"], writes=[t_mg])
                P.dma("sp", posk[:], din["posk"], writes=[t_pos])
                P.dma("sp", posv[:], din["posv"], writes=[t_pos])
                h1_v = h1T.rearrange("(c p) t -> p c t", p=128)
                uT_v = uT.rearrange("(c p) t -> p c t", p=128)
                for i in range(S // TT):
                    tsl = slice(i * TT, (i + 1) * TT)
                    P.dma("sp", xt[:], h1_v[:, :, tsl], reads=[t_h1T], writes=[t_xt])
                    norm_tile(xt, t_xt, mg, t_mg, xn, t_xn, sq, t_sq, rstd, t_rstd, TT, 6)
                    pssb = [3, 7]

                    def main(ch):
                        pb = ch % 3
                        pq = ps[pb][:, :]
                        for k in range(8):
                            MM(pq, wB[:, k, ch * 128:(ch + 1) * 128], xn[:, k, :], k == 0, k == 7,
                               [t_wB, t_xn], [t_ps[pb]])
                        if ch < 6:
                            ACT(sq[ch % 2][:], pq, AF.Square, [t_ps[pb]], [t_sq[ch % 2]])

                    def finish(ch):
                        pb = ch % 3
                        pq = ps[pb][:, :]
                        if ch < 6:
                            sb_ = pssb[ch % 2]
                            pss = ps[sb_][:, :]
                            MM(pss, bd_bf[:], sq[ch % 2][:], True, True, [t_const, t_sq[ch % 2]], [t_ps[sb_]])
                            ACT(rs[:], pss, AF.Ln, [t_ps[sb_], t_const], [t_rs], bias=epsc[:, 0:1], scale=1.0 / 64)
                            ACT(rs[:], rs[:], AF.Exp, [t_rs], [t_rs], scale=-0.5)
                            if ch < 4:
                                STT(Qs[ch][0:64, tsl], pq[0:64, :], gq8[0:64, 0:1], rs[0:64, :], ALU.mult, ALU.mult,
                                    [t_ps[pb], t_gains, t_rs], [t_Q])
                                STT(Qs[4 + ch][64:128, tsl], pq[64:128, :], gq8[64:128, 0:1], rs[64:128, :],
                                    ALU.mult, ALU.mult, [t_ps[pb], t_gains, t_rs], [t_Q])
                            elif ch == 4:
                                STT(Ks[0][0:64, tsl], pq[0:64, :], gains[0:64, 1:2], rs[0:64, :], ALU.mult, ALU.mult,
                                    [t_ps[pb], t_gains, t_rs], [t_Ks])
                                STT(Ks[1][64:128, tsl], pq[64:128, :], gains[64:128, 1:2], rs[64:128, :],
                                    ALU.mult, ALU.mult, [t_ps[pb], t_gains, t_rs], [t_Ks])
                            else:
                                STT(Kw[:, tsl], pq, gains[:, 2:3], rs[:], ALU.mult, ALU.mult,
                                    [t_ps[pb], t_gains, t_rs], [t_Kw])
                        elif ch < 8:
                            pos = posk if ch == 6 else posv
                            A_, B_ = (KcA, KcB) if ch == 6 else (VcA, VcB)
                            pq3 = pq.rearrange("p (i s) -> p i s", s=16)
                            TT2("dve", A_[:, tsl].rearrange("p (i s) -> p i s", s=16), pq3,
                                pos[:, 0:16].unsqueeze(1).to_broadcast([128, TT // 16, 16]), ALU.add,
                                [t_ps[pb], t_pos], [t_Kc])
                            TT2("dve", B_[:, tsl].rearrange("p (i s) -> p i s", s=16), pq3,
                                pos[:, 16:32].unsqueeze(1).to_broadcast([128, TT // 16, 16]), ALU.add,
                                [t_ps[pb], t_pos], [t_Kc])
                        else:
                            CP("act", ust[:, ch - 8, :], pq, [t_ps[pb]], [t_ust])

                    for ch in range(13):
                        if ch < 12:
                            main(ch)
                        if ch >= 1:
                            finish(ch - 1)
                    P.dma("sp", uT_v[:, :, tsl], ust[:], reads=[t_ust], writes=[t_uT], chan=t_ust)
                    for sub in range(4):
                        if "TM" in DBG:
                            break
                        pb = 4 + sub % 2
                        pt = ps[pb][:, 0:280]
                        for k in range(8):
                            MM(pt, xn[:, k, sub * 128:(sub + 1) * 128], wB[:, k, 1536:1816], k == 0, k == 7,
                               [t_wB, t_xn], [t_ps[pb]])
                        kt = i * 4 + sub
                        CP("act", vstg[:], ps[pb][:, 0:256], [t_ps[pb]], [t_vstg])
                        for g_ in range(2):
                            CP("pool", Vs[:, kt, g_, 0:64], vstg[:, g_ * 64:(g_ + 1) * 64], [t_vstg], [t_Vs])
                            CP("pool", Vw[:, kt, g_, 0:64], vstg[:, 128 + g_ * 64:128 + (g_ + 1) * 64], [t_vstg], [t_Vw])
                        if "cpG" not in DBG:
                            CP("act", gl[:, kt, :], ps[pb][:, 256:280], [t_ps[pb]], [t_gl])
                P.barrier()
                b2.close()
                if stage == "B":
                    return

                w1k = sb("w1k", [128, 32, 256], BF16, bs)
                w1v = sb("w1v", [128, 32, 256], BF16, bs)
                w2k = sb("w2k", [128, 2, 128], BF16, bs)
                w2v = sb("w2v", [128, 2, 64], BF16, bs)
                ovl = sb("ovl", [128, 2, 64], BF16, bs)
                hx = sb("hx", [128, 256], F32, bs)
                hu = sb("hu", [128, 256], F32, bs)
                hid = [[sb("hid%d%d" % (a, b), [128, 256], BF16, bs) for b in range(2)] for a in range(2)]
                t_cw, t_hx, t_hu = Tk("cw"), Tk("hx"), Tk("hu")
                t_hid = [[Tk("hid%d%d" % (a, b)) for b in range(2)] for a in range(2)]
                P.dma("pool", w1k[:], din["w1k"], writes=[t_cw])
                P.dma("pool", w1v[:], din["w1v"], writes=[t_cw])
                P.dma("pool", w2k[:], din["w2k"], writes=[t_cw])
                P.dma("pool", w2v[:], din["w2v"], writes=[t_cw])
                P.dma("pool", ovl[:], din["c_ovl"], writes=[t_cw])
                MEMSET("pool", kcT[:], 0.0, [t_kcT])
                for g in range(2):
                    MEMSET("pool", rcmp[g][:, :, 128:129], 1.0, [t_rcmp])
                    CP("pool", rcmp[g][:, :, 64:128], ovl[:], [t_cw], [t_rcmp])
                for a in range(2):
                    for b in range(2):
                        MEMSET("pool", hid[a][b][:], 0.0, [t_hid[a][b]])
                for kind in range(2):
                    w1 = w1k if kind == 0 else w1v
                    A_, B_ = (KcA, KcB) if kind == 0 else (VcA, VcB)
                    A3 = A_[:].rearrange("p (i s) -> p i s", s=16)
                    B3 = B_[:].rearrange("p (i s) -> p i s", s=16)
                    for g in range(2):
                        R = slice(64 * g, 64 * g + 64)
                        for mc in range(2):
                            ph = ps[mc][:, 0:255]
                            for j in range(32):
                                rhs = A3[R, 0:255, j] if j < 16 else B3[R, 1:256, j - 16]
                                MM(ph, w1[R, j, mc * 128:(mc + 1) * 128], rhs, j == 0, j == 31,
                                   [t_cw, t_Kc], [t_ps[mc]])
                            CP("act", hx[:, 0:255], ph, [t_ps[mc]], [t_hx])
                            TT2("dve", hu[:, 0:255], hx[:, 0:255], hx[:, 0:255], ALU.mult, [t_hx], [t_hu])
                            TS("dve", hu[:, 0:255], hu[:, 0:255], 0.044715, 1.0, ALU.mult, ALU.add, [t_hu], [t_hu])
                            TT2("dve", hu[:, 0:255], hu[:, 0:255], hx[:, 0:255], ALU.mult, [t_hu, t_hx], [t_hu])
                            ACT(hu[:, 0:255], hu[:, 0:255], AF.Sigmoid, [t_hu], [t_hu], scale=1.5957691216057308)
                            TT2("dve", hid[g][mc][:, 0:255], hu[:, 0:255], hx[:, 0:255], ALU.mult,
                                [t_hu, t_hx], [t_hid[g][mc]])
                        if kind == 0:
                            pk = ps[2][:, 0:256]
                            for mc in range(2):
                                MM(pk, w2k[:, mc, :], hid[g][mc][:], mc == 0, mc == 1,
                                   [t_cw, t_hid[g][mc]], [t_ps[2]])
                            sqn = sq[0]
                            ACT(sqn[:, 0:256], pk, AF.Square, [t_ps[2]], [t_sq[0]])
                            pss = ps[3][:, 0:256]
                            MM(pss, bd_bf[:], sqn[:, 0:256], True, True, [t_const, t_sq[0]], [t_ps[3]])
                            ACT(rs[:, 0:256], pss, AF.Sqrt, [t_ps[3], t_const], [t_rs], bias=epsc[:, 0:1], scale=1.0 / 64)
                            RECIP(rs[:, 0:256], rs[:, 0:256], [t_rs], [t_rs])
                            STT(kcT[R, 0:255], pk[R, 0:255], gains[R, 3:4], rs[R, 0:255], ALU.mult, ALU.mult,
                                [t_ps[2], t_gains, t_rs], [t_kcT])
                        else:
                            for it in range(2):
                                pv = ps[2 + it][:, 0:64]
                                for mc in range(2):
                                    MM(pv, hid[g][mc][:, it * 128:(it + 1) * 128], w2v[:, mc, :], mc == 0, mc == 1,
                                       [t_cw, t_hid[g][mc]], [t_ps[2 + it]])
                                CP("act", rcmp[g][:, it, 0:64], pv, [t_ps[2 + it]], [t_rcmp])
                P.barrier()

            if stage == "BC":
                return
            with ExitStack() as ds:
                mwin = sb("mwin", [128, 8, 512], BF16, ds)
                mslc = sb("mslc", [128, 4, 512], BF16, ds)
                mcmp = sb("mcmp", [128, 5, 512], BF16, ds)
                selB = sb("selB", [128, 32, 64], F32, ds)
                gs = sb("gs", [128, 32, 24], F32, ds)
                PT = sb("PT", [128, 32, 512], BF16, ds)
                Pc = [sb("Pc%d" % i, [128, 512], BF16, ds) for i in range(2)]
                onsa = sb("onsa", [128, 4, 512], F32, ds)
                onsab = sb("onsab", [128, 4, 512], BF16, ds)
                oTs = sb("oTs", [128, 4, 512], BF16, ds)
                imp = [sb("imp%d" % g, [128, 4, 64], F32, ds) for g in range(2)]
                scr = sb("scr", [128, 64], F32, ds)
                wk = sb("wk", [128, 64], F32, ds)
                m8 = sb("m8", [128, 16], F32, ds)
                NM = sb("NM", [128, 4, 128], BF16, ds)
                sm = sb("sm", [128, 8], F32, ds)
                t_msk, t_selB, t_gs, t_onsa, t_onsab, t_oTs, t_scr, t_wk, t_m8, t_NM, t_sm = [Tk(n) for n in
                    ("msk", "selB", "gs", "onsa", "onsab", "oTs", "scr", "wk", "m8", "NM", "sm")]
                t_PT = [Tk("PT%d" % i) for i in range(32)]
                t_Pc = [Tk("Pc0"), Tk("Pc1")]
                t_imp = [Tk("imp0"), Tk("imp1")]
                P.dma("pool", mwin[:], din["c_mwin"], writes=[t_msk])
                P.dma("pool", mslc[:], din["c_mslc"], writes=[t_msk])
                P.dma("pool", mcmp[:], din["c_mcmp"], writes=[t_msk])
                P.dma("sp", selB[:], din["c_selB"], writes=[t_selB])
                ACT(gs[:], gl[:], AF.Sigmoid, [t_gl], [t_gs])
                onsaT_v = onsaT.rearrange("(c p) t -> p c t", p=128)
                BIG = 30000.0
                sbank = [0, 1, 2]
                sctr = [0]

                def next_sbank():
                    b = sbank[sctr[0] % 3]
                    sctr[0] += 1
                    return b

                def evac(po, sub, h, br, kt, first, imp_g=None, r=0):
                    dcol = 128 if br == 0 else 64
                    c0 = (h * 3 + br) % 4 * 2
                    TS("dve", sm[:, c0:c0 + 1], po[:, dcol:dcol + 1], 1e-30, None, ALU.max, None, [t_ps_cur[0]], [t_sm])
                    RECIP(sm[:, c0:c0 + 1], sm[:, c0:c0 + 1], [t_sm], [t_sm])
                    if br == 0:
                        if r == 0:
                            TS("dve", imp_g[:, sub, :], po[:, 64:128], sm[:, c0:c0 + 1], None, ALU.mult, None,
                               [t_ps_cur[0], t_sm], [t_imp[h // 4]])
                        else:
                            STT(imp_g[:, sub, :], po[:, 64:128], sm[:, c0:c0 + 1], imp_g[:, sub, :], ALU.mult, ALU.add,
                                [t_ps_cur[0], t_sm, t_imp[h // 4]], [t_imp[h // 4]])
                    TT2("dve", sm[:, c0 + 1:c0 + 2], sm[:, c0:c0 + 1], gs[:, kt, br * 8 + h:br * 8 + h + 1], ALU.mult,
                        [t_sm, t_gs], [t_sm])
                    dst = onsa[:, sub, h * 64:(h + 1) * 64]
                    if first:
                        TS("dve", dst, po[:, 0:64], sm[:, c0 + 1:c0 + 2], None, ALU.mult, None,
                           [t_ps_cur[0], t_sm], [t_onsa])
                    else:
                        STT(dst, po[:, 0:64], sm[:, c0 + 1:c0 + 2], dst, ALU.mult, ALU.add,
                            [t_ps_cur[0], t_sm, t_onsa], [t_onsa])

                t_ps_cur = [None]
                for Q in range(8):
                    qsl = slice(Q * 512, (Q + 1) * 512)
                    cts = [0] if Q < 4 else [0, 1]
                    for h in range(8):
                        g, r = h // 4, h % 4
                        R = slice(64 * g, 64 * g + 64)
                        for ct in cts:
                            b = next_sbank()
                            MM(ps[b][:, :], kcT[R, ct * 128:(ct + 1) * 128], Qs[h][R, qsl], True, True,
                               [t_kcT, t_Q], [t_ps[b]])
                            ACT(Pc[ct][:], ps[b][:, :], AF.Exp, [t_ps[b]], [t_Pc[ct]])
                            v = Q - 4 * ct
                            if v <= 4:
                                TT2("dve", Pc[ct][:], Pc[ct][:], mcmp[:, v, :], ALU.mult, [t_Pc[ct], t_msk], [t_Pc[ct]])
                        for sub in range(4):
                            bk = 3 + sub // 2
                            po = ps[bk][:, (sub % 2) * 129:(sub % 2) * 129 + 129]
                            for ci, ct in enumerate(cts):
                                MM(po, Pc[ct][:, sub * 128:(sub + 1) * 128], rcmp[g][:, ct, :], ci == 0, ci == len(cts) - 1,
                                   [t_Pc[ct], t_rcmp], [t_ps[bk]])
                        for sub in range(4):
                            bk = 3 + sub // 2
                            po = ps[bk][:, (sub % 2) * 129:(sub % 2) * 129 + 129]
                            t_ps_cur[0] = t_ps[bk]
                            evac(po, sub, h, 0, Q * 4 + sub, True, imp[g], r)
                    for sub in range(4):
                        kt = Q * 4 + sub
                        for g in range(2):
                            TT2("dve", scr[:], imp[g][:, sub, :], selB[:, kt, :], ALU.add, [t_imp[g], t_selB], [t_scr])
                            P.op("dve", lambda e: e.max(out=m8[:, 0:8], in_=scr[:]), [t_scr], [t_m8])
                            P.op("dve", lambda e: e.match_replace(out=wk[:], in_to_replace=m8[:, 0:8],
                                                                   in_values=scr[:], imm_value=-3.0e38),
                                 [t_scr, t_m8], [t_wk])
                            P.op("dve", lambda e: e.max(out=m8[:, 8:16], in_=wk[:]), [t_wk], [t_m8])
                            cs = slice(64, 128) if g == 0 else slice(0, 64)
                            TS("dve", NM[:, sub, cs], scr[:], m8[:, 15:16], -BIG, ALU.is_lt, ALU.mult,
                               [t_scr, t_m8], [t_NM])
                    pT = ps[7][:].bitcast(BF16)
                    for sub in range(4):
                        TR(pT[:, sub * 128:(sub + 1) * 128], NM[:, sub, :], id_bf[:], [t_NM, t_const], [t_ps[7]])
                    for h in range(8):
                        R2 = slice(64, 128) if h < 4 else slice(0, 64)
                        CP("act" if h % 2 == 0 else "dve", Qs[h][R2, qsl], pT[R2, 0:512], [t_ps[7]], [t_Qm[Q]])
                    for h in range(8):
                        g = h // 4
                        R = slice(64 * g, 64 * g + 64)
                        nkt = 4 * Q + 4
                        for kt in range(nkt):
                            b = next_sbank()
                            MM(ps[b][:, :], Ks[g][:, kt * 128:(kt + 1) * 128], Qs[h][:, qsl], True, True,
                               [t_Ks, t_Q, t_Qm[Q]], [t_ps[b]])
                            ACT(PT[:, kt, :], ps[b][:, :], AF.Exp, [t_ps[b]], [t_PT[kt]])
                            if kt >= 4 * Q:
                                TT2("dve", PT[:, kt, :], PT[:, kt, :], mslc[:, kt - 4 * Q, :], ALU.mult,
                                    [t_PT[kt], t_msk], [t_PT[kt]])
                        bk = 5
                        for sub in range(4):
                            po = ps[bk][:, sub * 65:sub * 65 + 65]
                            last = 4 * Q + sub
                            for kt in range(last + 1):
                                MM(po, PT[:, kt, sub * 128:(sub + 1) * 128], Vs[:, kt, g, 0:65], kt == 0, kt == last,
                                   [t_PT[kt], t_Vs], [t_ps[bk]])
                        t_ps_cur[0] = t_ps[bk]
                        for sub in range(4):
                            evac(ps[bk][:, sub * 65:sub * 65 + 65], sub, h, 1, Q * 4 + sub, False)
                        rr = [r_ for r_ in range(8) if 4 * Q - 4 + r_ >= 0]
                        for r_ in rr:
                            kt = 4 * Q - 4 + r_
                            b = next_sbank()
                            MM(ps[b][:, :], Kw[R, kt * 128:(kt + 1) * 128], Qs[h][R, qsl], True, True,
                               [t_Kw, t_Q], [t_ps[b]])
                            ACT(PT[:, r_, :], ps[b][:, :], AF.Exp, [t_ps[b]], [t_PT[r_]])
                            TT2("dve", PT[:, r_, :], PT[:, r_, :], mwin[:, r_, :], ALU.mult, [t_PT[r_], t_msk], [t_PT[r_]])
                        bk = 6
                        for sub in range(4):
                            po = ps[bk][:, sub * 65:sub * 65 + 65]
                            rs_ = [r_ for r_ in rr if sub <= r_ <= sub + 4]
                            for ci, r_ in enumerate(rs_):
                                kt = 4 * Q - 4 + r_
                                MM(po, PT[:, r_, sub * 128:(sub + 1) * 128], Vw[:, kt, g, 0:65], ci == 0, ci == len(rs_) - 1,
                                   [t_PT[r_], t_Vw], [t_ps[bk]])
                        t_ps_cur[0] = t_ps[bk]
                        for sub in range(4):
                            evac(ps[bk][:, sub * 65:sub * 65 + 65], sub, h, 2, Q * 4 + sub, False)
                    CP("act", onsab[:], onsa[:], [t_onsa], [t_onsab])
                    for fc in range(4):
                        pT2 = ps[7][:].bitcast(BF16)
                        for sub in range(4):
                            TR(pT2[:, sub * 128:(sub + 1) * 128], onsab[:, sub, fc * 128:(fc + 1) * 128], id_bf[:],
                               [t_onsab, t_const], [t_ps[7]])
                        CP("dve" if fc % 2 else "act", oTs[:, fc, :], pT2[:, 0:512], [t_ps[7]], [t_oTs])
                    P.dma("sp", onsaT_v[:, :, qsl], oTs[:], reads=[t_oTs], writes=[t_onsaT], chan=t_oTs)
                P.barrier()

    def ssm_phase():
        PI = math.pi
        with ExitStack() as fs:
            CT = sb("CT", [128, 16, 512], F32, fs)
            ST = sb("ST", [128, 16, 512], F32, fs)
            rhoF = sb("rhoF", [128, 16, 512], F32, fs)
            BBrT = sb("BBrT", [128, 16, 128], BF16, fs)
            BBiT = sb("BBiT", [128, 16, 128], BF16, fs)
            CreT = sb("CreT", [128, 16, 128], BF16, fs)
            CimT = sb("CimT", [128, 16, 128], BF16, fs)
            Dg = sb("Dg", [128, 4, 128], BF16, fs)
            dsk = sb("dsk", [128, 4], F32, fs)
            rho = sb("rho", [128, 16], F32, fs)
            car = [[sb("car%d%d" % (a, b), [128, 16], F32, fs) for b in range(2)] for a in range(2)]
            t_tab, t_bb, t_cc, t_dg, t_rho = [Tk(n) for n in ("tab", "bb", "cc", "dg", "rho")]
            t_car = [[Tk("car%d_%d" % (a, sc)) for sc in range(16)] for a in range(2)]
            P.dma("pool", CreT[:], din["s_creT"], writes=[t_cc])
            P.dma("pool", CimT[:], din["s_cimT"], writes=[t_cc])
            P.dma("sp", dsk[:], din["s_d"], writes=[t_dg])
            for oc in range(4):
                TS("dve", Dg[:, oc, :], id_f[:], dsk[:, oc:oc + 1], None, ALU.mult, None, [t_const, t_dg], [t_dg])
            nCreT = sb("nCreT", [128, 16, 128], BF16, fs)
            nCimT = sb("nCimT", [128, 16, 128], BF16, fs)
            TS("dve", nCreT[:], CreT[:], -1.0, None, ALU.mult, None, [t_cc], [t_cc])
            TS("dve", nCimT[:], CimT[:], -1.0, None, ALU.mult, None, [t_cc], [t_cc])
            with ExitStack() as ss:
                lr = sb("lr", [128, 16], F32, ss)
                li = sb("li", [128, 16], F32, ss)
                stp = sb("stp", [128, 16], F32, ss)
                th = sb("th", [128, 16], F32, ss)
                kk = sb("kk", [128, 16], F32, ss)
                r1 = sb("r1", [128, 16], F32, ss)
                r2 = sb("r2", [128, 16], F32, ss)
                s1 = sb("s1", [128, 16], F32, ss)
                c1 = sb("c1", [128, 16], F32, ss)
                w = [sb("w%d" % i, [128, 16], F32, ss) for i in range(6)]
                bre = sb("bre", [128, 16, 16], F32, ss)
                bim = sb("bim", [128, 16, 16], F32, ss)
                bbr = sb("bbr", [128, 16, 16], F32, ss)
                bbi = sb("bbi", [128, 16, 16], F32, ss)
                bt = [sb("bt%d" % i, [128, 16, 16], F32, ss) for i in range(2)]
                Zr = sb("Zr", [128, 16, 128], F32, ss)
                Zi = sb("Zi", [128, 16, 128], F32, ss)
                tA = sb("tA", [128, 16, 256], F32, ss)
                tB = sb("tB", [128, 16, 256], F32, ss)
                t_p = Tk("ssm_par")
                t_Z = Tk("Z")
                P.dma("sp", lr[:], din["s_lr"], writes=[t_p])
                P.dma("sp", li[:], din["s_li"], writes=[t_p])
                P.dma("sp", stp[:], din["s_ls"], writes=[t_p])
                P.dma("sp", bre[:], din["s_bre"], writes=[t_p])
                P.dma("sp", bim[:], din["s_bim"], writes=[t_p])
                pp = [t_p]
                ACT(stp[:], stp[:], AF.Exp, pp, pp)
                TT2("dve", th[:], li[:], stp[:], ALU.mult, pp, pp)
                TT2("dve", w[0][:], lr[:], stp[:], ALU.mult, pp, pp)
                ACT(rho[:], w[0][:], AF.Exp, pp, [t_rho])
                MEMSET("dve", kk[:], 0.0, pp)
                for m in range(1, 9):
                    STT(kk[:], th[:], (2 * m - 1) * PI, kk[:], ALU.is_gt, ALU.add, pp, pp)
                STT(r1[:], kk[:], -2.0 * PI, th[:], ALU.mult, ALU.add, pp, pp)
                TS("dve", r2[:], r1[:], PI / 2, None, ALU.add, None, pp, pp)
                TS("dve", w[1][:], r2[:], PI, None, ALU.is_gt, None, pp, pp)
                STT(r2[:], w[1][:], -2.0 * PI, r2[:], ALU.mult, ALU.add, pp, pp)
                ACT(s1[:], r1[:], AF.Sin, pp, pp)
                ACT(c1[:], r2[:], AF.Sin, pp, pp)
                ar, ai, arm1, den, cr, cim = w
                TT2("dve", ar[:], rho[:], c1[:], ALU.mult, pp + [t_rho], pp)
                TT2("dve", ai[:], rho[:], s1[:], ALU.mult, pp + [t_rho], pp)
                TS("dve", arm1[:], ar[:], -1.0, None, ALU.add, None, pp, pp)
                TT2("dve", den[:], lr[:], lr[:], ALU.mult, pp, pp)
                TT2("dve", kk[:], li[:], li[:], ALU.mult, pp, pp)
                TT2("dve", den[:], den[:], kk[:], ALU.add, pp, pp)
                RECIP(den[:], den[:], pp, pp)
                TT2("dve", cr[:], arm1[:], lr[:], ALU.mult, pp, pp)
                TT2("dve", kk[:], ai[:], li[:], ALU.mult, pp, pp)
                TT2("dve", cr[:], cr[:], kk[:], ALU.add, pp, pp)
                TT2("dve", cr[:], cr[:], den[:], ALU.mult, pp, pp)
                TT2("dve", cim[:], ai[:], lr[:], ALU.mult, pp, pp)
                TT2("dve", kk[:], arm1[:], li[:], ALU.mult, pp, pp)
                TT2("dve", cim[:], cim[:], kk[:], ALU.subtract, pp, pp)
                TT2("dve", cim[:], cim[:], den[:], ALU.mult, pp, pp)
                crb = cr[:].unsqueeze(2).to_broadcast([128, 16, 16])
                cib = cim[:].unsqueeze(2).to_broadcast([128, 16, 16])
                TT2("dve", bt[0][:], bre[:], crb, ALU.mult, pp, pp)
                TT2("dve", bt[1][:], bim[:], cib, ALU.mult, pp, pp)
                TT2("dve", bbr[:], bt[0][:], bt[1][:], ALU.subtract, pp, pp)
                TT2("dve", bt[0][:], bim[:], crb, ALU.mult, pp, pp)
                TT2("dve", bt[1][:], bre[:], cib, ALU.mult, pp, pp)
                TT2("dve", bbi[:], bt[0][:], bt[1][:], ALU.add, pp, pp)
                MEMSET("pool", Zr[:], 0.0, [t_Z])
                MEMSET("pool", Zi[:], 0.0, [t_Z])
                for sc in range(16):
                    c0 = 32 * (sc % 4)
                    for Z_, bb_ in ((Zr, bbr), (Zi, bbi)):
                        CP("dve", Z_[0:64, sc, c0:c0 + 16], bb_[0:64, sc, :], pp + [t_Z], [t_Z])
                        CP("dve", Z_[64:128, sc, c0 + 16:c0 + 32], bb_[64:128, sc, :], pp + [t_Z], [t_Z])
                for sc in range(16):
                    for bi_, (Z_, BT_) in enumerate(((Zr, BBrT), (Zi, BBiT))):
                        bk = (sc * 2 + bi_) % 4
                        TR(ps[bk][:, 0:128], Z_[:, sc, :], id_f[:], [t_Z, t_const], [t_ps[bk]])
                        CP("act", BT_[:, sc, :], ps[bk][:, 0:128], [t_ps[bk]], [t_bb])
                CP("dve", CT[:, :, 0:1], c1[:].unsqueeze(2), pp, [t_tab])
                CP("dve", ST[:, :, 0:1], s1[:].unsqueeze(2), pp, [t_tab])
                n = 1
                while n < 512:
                    cn = CT[:, :, n - 1:n].to_broadcast([128, 16, n])
                    sn = ST[:, :, n - 1:n].to_broadcast([128, 16, n])
                    tt_ = [t_tab]
                    TT2("dve", tA[:, :, 0:n], CT[:, :, 0:n], cn, ALU.mult, tt_, pp)
                    TT2("pool", tB[:, :, 0:n], ST[:, :, 0:n], sn, ALU.mult, tt_, [t_Z])
                    TT2("dve", CT[:, :, n:2 * n], tA[:, :, 0:n], tB[:, :, 0:n], ALU.subtract, pp + [t_Z], tt_)
                    TT2("dve", tA[:, :, 0:n], ST[:, :, 0:n], cn, ALU.mult, tt_, pp)
                    TT2("pool", tB[:, :, 0:n], CT[:, :, 0:n], sn, ALU.mult, tt_, [t_Z])
                    TT2("dve", ST[:, :, n:2 * n], tA[:, :, 0:n], tB[:, :, 0:n], ALU.add, pp + [t_Z], tt_)
                    n *= 2
                MEMSET("pool", rhoF[:], 1.0, [t_rho])
                for sc in range(16):
                    TS("pool", rhoF[:, sc, :], rhoF[:, sc, :], rho[:, sc:sc + 1], None, ALU.mult, None, [t_rho], [t_rho])
                for a in range(2):
                    for b in range(2):
                        MEMSET("pool", car[a][b][:], 0.0, [t_car[a][sc] for sc in range(16)])
                P.barrier()

            NB = 2
            uTt = [sb("uTt%d" % i, [128, 4, 512], BF16, fs) for i in range(3)]
            ygst = sb("ygst", [128, 4, 512], BF16, fs)
            f = [[sb("sf%d_%d" % (i, b), [128, 512], F32 if i < 8 else BF16, fs) for b in range(NB)] for i in range(12)]
            gx = sb("gx", [128, 512], F32, fs)
            gu = sb("gu", [128, 512], F32, fs)
            cu = sb("cu", [128, 4], F32, fs)
            t_u = [Tk("uTt%d" % i) for i in range(3)]
            t_f = [[Tk("sf%d_%d" % (i, b)) for b in range(NB)] for i in range(12)]
            t_gx, t_gu, t_ygst, t_cu = Tk("gx"), Tk("gu"), Tk("ygst"), Tk("cu")
            uT_v = uT.rearrange("(c p) t -> p c t", p=128)
            yg_v = ygT.rearrange("(c p) t -> p c t", p=128)
            P.dma("sp", uTt[0][:], uT_v[:, :, 0:512], reads=[t_uT], writes=[t_u[0]])
            tiles = [(tt, oc, sci) for tt in range(8) for oc in range(4) for sci in range(4)]

            def S1(i):
                tt, oc, sci = tiles[i]
                sc, b, ub = 4 * oc + sci, i % NB, tt % 3
                if oc == 0 and sci == 0 and tt + 1 < 8:
                    nb_ = (tt + 1) % 3
                    P.dma("sp", uTt[nb_][:], uT_v[:, :, (tt + 1) * 512:(tt + 2) * 512], reads=[t_uT], writes=[t_u[nb_]])
                pb = 2 * (i % 2)
                pur, pui = ps[pb][:, :], ps[pb + 1][:, :]
                C_, S_ = CT[:, sc, :], ST[:, sc, :]
                MM(pur, BBrT[:, sc, :], uTt[ub][:, oc, :], True, True, [t_bb, t_u[ub]], [t_ps[pb]])
                MM(pui, BBiT[:, sc, :], uTt[ub][:, oc, :], True, True, [t_bb, t_u[ub]], [t_ps[pb + 1]])
                TT2("dve", f[0][b][:], pur, C_, ALU.mult, [t_ps[pb], t_tab], [t_f[0][b]])
                TT2("dve", f[1][b][:], pui, S_, ALU.mult, [t_ps[pb + 1], t_tab], [t_f[1][b]])
                TT2("dve", f[2][b][:], pui, C_, ALU.mult, [t_ps[pb + 1], t_tab], [t_f[2][b]])
                TT2("dve", f[3][b][:], pur, S_, ALU.mult, [t_ps[pb], t_tab], [t_f[3][b]])

            def S2(i):
                b = i % NB
                TT2("pool", f[4][b][:], f[0][b][:], f[1][b][:], ALU.add, [t_f[0][b], t_f[1][b]], [t_f[4][b]])
                TT2("pool", f[5][b][:], f[2][b][:], f[3][b][:], ALU.subtract, [t_f[2][b], t_f[3][b]], [t_f[5][b]])

            def S3(i):
                tt, oc, sci = tiles[i]
                sc, b = 4 * oc + sci, i % NB
                cin, cout = car[tt % 2], car[(tt + 1) % 2]
                tcin, tcout = t_car[tt % 2], t_car[(tt + 1) % 2]
                P.op("dve", lambda e: e.tensor_tensor_scan(
                    out=f[6][b][:], data0=rhoF[:, sc, :], data1=f[4][b][:], initial=cin[0][:, sc:sc + 1],
                    op0=ALU.mult, op1=ALU.add), [t_rho, t_f[4][b], tcin[sc]], [t_f[6][b]])
                P.op("dve", lambda e: e.tensor_tensor_scan(
                    out=f[7][b][:], data0=rhoF[:, sc, :], data1=f[5][b][:], initial=cin[1][:, sc:sc + 1],
                    op0=ALU.mult, op1=ALU.add), [t_rho, t_f[5][b], tcin[sc]], [t_f[7][b]])
                zrL, ziL = f[6][b][:, 511:512], f[7][b][:, 511:512]
                cL, sL = CT[:, sc, 511:512], ST[:, sc, 511:512]
                TT2("dve", cu[:, 0:1], ziL, sL, ALU.mult, [t_f[7][b], t_tab], [t_cu])
                STT(cout[0][:, sc:sc + 1], zrL, cL, cu[:, 0:1], ALU.mult, ALU.subtract,
                    [t_f[6][b], t_tab, t_cu], [tcout[sc]])
                TT2("dve", cu[:, 1:2], ziL, cL, ALU.mult, [t_f[7][b], t_tab], [t_cu])
                STT(cout[1][:, sc:sc + 1], zrL, sL, cu[:, 1:2], ALU.mult, ALU.add,
                    [t_f[6][b], t_tab, t_cu], [tcout[sc]])
                TT2("dve", f[11][b][:], f[7][b][:], CT[:, sc, :], ALU.mult, [t_f[7][b], t_tab], [t_f[11][b]])

            def S4(i):
                tt, oc, sci = tiles[i]
                sc, b = 4 * oc + sci, i % NB
                C_, S_ = CT[:, sc, :], ST[:, sc, :]
                TT2("pool", f[8][b][:], f[6][b][:], C_, ALU.mult, [t_f[6][b], t_tab], [t_f[8][b]])
                TT2("pool", f[9][b][:], f[7][b][:], S_, ALU.mult, [t_f[7][b], t_tab], [t_f[9][b]])
                TT2("pool", f[10][b][:], f[6][b][:], S_, ALU.mult, [t_f[6][b], t_tab], [t_f[10][b]])

            def S5(i):
                tt, oc, sci = tiles[i]
                sc, b, ub = 4 * oc + sci, i % NB, tt % 3
                pyb = 4 + oc % 2
                py = ps[pyb][:, :]
                MM(py, CreT[:, sc, :], f[8][b][:], sci == 0, False, [t_cc, t_f[8][b]], [t_ps[pyb]])
                MM(py, nCreT[:, sc, :], f[9][b][:], False, False, [t_cc, t_f[9][b]], [t_ps[pyb]])
                MM(py, nCimT[:, sc, :], f[10][b][:], False, False, [t_cc, t_f[10][b]], [t_ps[pyb]])
                MM(py, nCimT[:, sc, :], f[11][b][:], False, False, [t_cc, t_f[11][b]], [t_ps[pyb]])
                if sci == 3:
                    MM(py, Dg[:, oc, :], uTt[ub][:, oc, :], False, True, [t_dg, t_u[ub]], [t_ps[pyb]])
                    CP("act", gx[:], py, [t_ps[pyb]], [t_gx])
                    ACT(gu[:], py, AF.Square, [t_ps[pyb]], [t_gu])
                    TS("pool", gu[:], gu[:], 0.044715, 1.0, ALU.mult, ALU.add, [t_gu], [t_gu])
                    TT2("pool", gu[:], gu[:], gx[:], ALU.mult, [t_gu, t_gx], [t_gu])
                    ACT(gu[:], gu[:], AF.Sigmoid, [t_gu], [t_gu], scale=1.5957691216057308)
                    TT2("pool", ygst[:, oc, :], gu[:], gx[:], ALU.mult, [t_gu, t_gx], [t_ygst])
                    if oc == 3:
                        P.dma("sp", yg_v[:, :, tt * 512:(tt + 1) * 512], ygst[:], reads=[t_ygst], writes=[t_ygT], chan=t_ygst)

            n_t = len(tiles)
            for k in range(n_t + 2):
                if k < n_t:
                    S1(k)
                    S2(k)
                if 0 <= k - 1 < n_t:
                    S3(k - 1)
                    S4(k - 1)
                if 0 <= k - 2 < n_t:
                    S5(k - 2)
            P.barrier()

    def merge_phase():
        TT = 256
        NT = S // TT
        with ExitStack() as fs:
            wF = sb("wF", [128, 8, 2048], BF16, fs)
            wnsa = sb("wnsa", [128, 4, 1024], BF16, fs)
            wglu = sb("wglu", [128, 4, 2048], BF16, fs)
            wout = sb("wout", [128, 8, 1024], BF16, fs)
            mg = sb("Fmg", [128, 8], F32, fs)
            xt = [sb("Fx%d" % i, [128, 8, TT], F32, fs) for i in range(2)]
            xn = sb("Fxn", [128, 8, TT], BF16, fs)
            sq = [sb("Fsq%d" % i, [128, TT], BF16, fs) for i in range(2)]
            rstd = sb("Frstd", [128, TT], F32, fs)
            ont = [sb("Fon%d" % i, [128, 4, TT], BF16, fs) for i in range(2)]
            ygt = [sb("Fyg%d" % i, [128, 4, TT], BF16, fs) for i in range(2)]
            sg3 = [[sb("Fsg%d_%d" % (i, b), [128, TT], F32, fs) for b in range(2)] for i in range(3)]
            ta = [sb("Fta%d" % b, [128, TT], F32, fs) for b in range(2)]
            tb = [sb("Ftb%d" % b, [128, TT], F32, fs) for b in range(2)]
            mrg = sb("Fmrg", [128, 8, TT], BF16, fs)
            sgA = sb("FsgA", [128, 16, TT], F32, fs)
            t_sgA = [Tk("FsgA%d" % j) for j in range(16)]
            ot = sb("Fot", [128, 8, TT], F32, fs)
            t_w, t_mg, t_xn, t_rstd, t_ot = [Tk(n) for n in ("Fw", "Fmg", "Fxn", "Frstd", "Fot")]
            t_xt = [Tk("Fxt0"), Tk("Fxt1")]
            t_sq = [Tk("Fsq0"), Tk("Fsq1")]
            t_on = [Tk("Fon0"), Tk("Fon1")]
            t_yg = [Tk("Fyg0"), Tk("Fyg1")]
            t_sg3 = [[Tk("Fsg%d_%d" % (i, b)) for b in range(2)] for i in range(3)]
            t_ta = [Tk("Fta0"), Tk("Fta1")]
            t_tb = [Tk("Ftb0"), Tk("Ftb1")]
            t_mrg = [Tk("Fmrg%d" % i) for i in range(8)]
            t_h = [[Tk("Fps%d_%d" % (i, b)) for b in range(2)] for i in range(6)]
            P.dma("sp", mg[:], din["mix_g"], writes=[t_mg])
            wF_v = din["w_inF"].rearrange("(k p) f -> p k f", p=128)
            for k in range(8):
                P.dma("pool", wF[:, k, :], wF_v[:, k, :], writes=[t_w])
            wn_v = din["w_nsa"].rearrange("(k p) f -> p k f", p=128)
            wgl_v = din["w_glu"].rearrange("(k p) f -> p k f", p=128)
            wo_v = din["w_out"].rearrange("(k p) f -> p k f", p=128)
            for k in range(4):
                P.dma("pool", wnsa[:, k, :], wn_v[:, k, :], writes=[t_w])
                P.dma("pool", wglu[:, k, :], wgl_v[:, k, :], writes=[t_w])
            for k in range(8):
                P.dma("pool", wout[:, k, :], wo_v[:, k, :], writes=[t_w])
            h1_v = h1T.rearrange("(c p) t -> p c t", p=128)
            h2_v = h2T.rearrange("(c p) t -> p c t", p=128)
            on_v = onsaT.rearrange("(c p) t -> p c t", p=128)
            yg_v = ygT.rearrange("(c p) t -> p c t", p=128)

            def load(i):
                b = i % 2
                tsl = slice(i * TT, (i + 1) * TT)
                P.dma("sp", xt[b][:], h1_v[:, :, tsl], reads=[t_h1T], writes=[t_xt[b]])
                P.dma("sp", ont[b][:], on_v[:, :, tsl], reads=[t_onsaT], writes=[t_on[b]])
                P.dma("sp", ygt[b][:], yg_v[:, :, tsl], reads=[t_ygT], writes=[t_yg[b]])

            sq8 = sb("Fsq8", [128, 8, TT], BF16, fs)
            t_sq8 = [Tk("Fsq8_%d" % c) for c in range(8)]

            def fnorm1(i):
                bb = i % 2
                for c in range(8):
                    TT2("pool", sq8[:, c, :], xt[bb][:, c, :], xt[bb][:, c, :], ALU.mult, [t_xt[bb]], [t_sq8[c]])

            def fnorm2(i):
                bb = i % 2
                pstat = ps[6][:, 0:TT]
                for c in range(8):
                    MM(pstat, ones_bf[:], sq8[:, c, :], c == 0, c == 7, [t_const, t_sq8[c]], [t_ps[6]])
                ACT(rstd[:], pstat, AF.Sqrt, [t_ps[6], t_const], [t_rstd], bias=epsc[:, 0:1], scale=1.0 / D)
                RECIP(rstd[:], rstd[:], [t_rstd], [t_rstd])
                for c in range(8):
                    STT(xn[:, c, :], xt[bb][:, c, :], mg[:, c:c + 1], rstd[:], ALU.mult, ALU.mult,
                        [t_xt[bb], t_mg, t_rstd], [t_xn])

            load(0)
            fnorm1(0)
            fnorm2(0)
            for i in range(NT):
                b = i % 2
                if i + 1 < NT:
                    load(i + 1)
                xb = xt[b]
                if i + 1 < NT:
                    fnorm1(i + 1)
                for j in range(16):
                    bk = j % 4
                    pg_ = ps[bk][:, 0:TT]
                    cs_ = slice(j * 128, (j + 1) * 128)
                    for k in range(8):
                        MM(pg_, wF[:, k, cs_], xn[:, k, :], k == 0, k == 7, [t_w, t_xn], [t_ps[bk]])
                    ACT(sgA[:, j, :], pg_, AF.Sigmoid, [t_ps[bk]], [t_sgA[j]])
                if i + 1 < NT:
                    fnorm2(i + 1)
                for dc in range(8):
                    hb = dc % 2
                    bks = (0, 1, 2) if hb == 0 else (3, 4, 5)
                    pyn, pval, pgt = [ps[j][:, 0:TT] for j in bks]
                    tp = [t_ps[j] for j in bks]
                    dsl = slice(dc * 128, (dc + 1) * 128)
                    dsl2 = slice(1024 + dc * 128, 1024 + (dc + 1) * 128)
                    for k in range(4):
                        MM(pyn, wnsa[:, k, dsl], ont[b][:, k, :], k == 0, k == 3, [t_w, t_on[b]], [tp[0]])
                    for k in range(4):
                        MM(pval, wglu[:, k, dsl], ygt[b][:, k, :], k == 0, k == 3, [t_w, t_yg[b]], [tp[1]])
                    for k in range(4):
                        MM(pgt, wglu[:, k, dsl2], ygt[b][:, k, :], k == 0, k == 3, [t_w, t_yg[b]], [tp[2]])
                    ACT(sg3[2][hb][:], pgt, AF.Sigmoid, [tp[2]], [t_sg3[2][hb]])
                    TT2("dve", ta[hb][:], pyn, sgA[:, dc, :], ALU.mult, [tp[0], t_sgA[dc]], [t_ta[hb]])
                    TT2("dve", tb[hb][:], pval, sg3[2][hb][:], ALU.mult, [tp[1], t_sg3[2][hb]], [t_tb[hb]])
                    TT2("pool", tb[hb][:], tb[hb][:], sgA[:, 8 + dc, :], ALU.mult, [t_tb[hb], t_sgA[8 + dc]], [t_tb[hb]])
                    TT2("pool", mrg[:, dc, :], ta[hb][:], tb[hb][:], ALU.add, [t_ta[hb], t_tb[hb]], [t_mrg[dc]])
                for dc in range(8):
                    bk = 7 if dc % 2 == 0 else 6
                    po = ps[bk][:, 0:TT]
                    for k in range(8):
                        MM(po, wout[:, k, dc * 128:(dc + 1) * 128], mrg[:, k, :], k == 0, k == 7,
                           [t_w, t_mrg[k]], [t_ps[bk]])
                    TT2("dve", ot[:, dc, :], po, xb[:, dc, :], ALU.add, [t_ps[bk], t_xt[b]], [t_ot])
                if "F3b" not in DBG:
                    P.dma("sp", h2_v[:, :, i * TT:(i + 1) * TT], ot[:], reads=[t_ot], writes=[t_h2T], chan=t_ot)
            P.barrier()

    if stage == "ffn1":
        ffn_phase("f1", xT, t_xT, outT, t_outT, din["f1_g"], din["f1_wg"], din["f1_wu"], din["f1_wd"])
    else:
        ffn_phase("f1", xT, t_xT, h1T, t_h1T, din["f1_g"], din["f1_wg"], din["f1_wu"], din["f1_wd"])
        mixer_attention()
        if stage not in ("B", "BC", "D"):
            ssm_phase()
        if stage not in ("B", "BC", "D", "E"):
            merge_phase()
        if stage == "full":
            ffn_phase("f2", h2T, t_h2T, outT, t_outT, din["f2_g"], din["f2_wg"], din["f2_wu"], din["f2_wd"])

    P.final_wait([t_outT])
    P.barrier()
    es.close()
    return nc, P


def _consts():
    f = np.float32
    c = {}
    c["c_id"] = np.eye(128, dtype=f)
    bd = np.zeros((128, 128), f)
    bd[:64, :64] = 1.0
    bd[64:, 64:] = 1.0
    c["c_bd"] = bd
    k = np.arange(S)
    c["c_E"] = (k[None, :] // 64 == np.arange(64)[:, None]).astype(f)
    ci = np.arange(256)
    j = np.arange(64)
    ovl = ((ci[:, None] * 16 < (j[None, :] + 1) * 64) & (ci[:, None] * 16 + 32 > j[None, :] * 64)).astype(f)
    ovl[255] = 0.0
    c["c_ovl"] = np.ascontiguousarray(ovl.reshape(2, 128, 64).transpose(1, 0, 2))
    t = np.arange(S)
    qblk = t // 64
    force = (j[None, :] == 0) | (j[None, :] == qblk[:, None]) | (j[None, :] == qblk[:, None] - 1)
    selB = np.where(j[None, :] <= qblk[:, None], 1000.0 * force.astype(f), -1e30).astype(f)
    c["c_selB"] = np.ascontiguousarray(selB.reshape(32, 128, 64).transpose(1, 0, 2))
    kl = np.arange(128)[:, None]
    tl = np.arange(512)[None, :]
    mwin = np.zeros((128, 8, 512), f)
    for r in range(8):
        dlt = 512 + tl - 128 * r - kl
        mwin[:, r, :] = ((dlt >= 0) & (dlt < 512)).astype(f)
    c["c_mwin"] = mwin
    mslc = np.zeros((128, 4, 512), f)
    for r in range(4):
        mslc[:, r, :] = ((tl - 128 * r - kl) >= 0).astype(f)
    c["c_mslc"] = mslc
    mcmp = np.zeros((128, 5, 512), f)
    for v in range(5):
        mcmp[:, v, :] = ((512 * v + tl - 16 * kl - 31) >= 0).astype(f)
    c["c_mcmp"] = mcmp
    return c


def _prep_shared(inp):
    f = np.float32
    A = lambda a: np.ascontiguousarray(a, dtype=f)
    sh = dict(_consts())
    for tag, nm in (("f1", "ffn1"), ("f2", "ffn2")):
        sh[tag + "_g"] = A(inp[nm + "_norm"][0].reshape(8, 128).T)
        sh[tag + "_wg"] = A(inp[nm + "_w_gate"][0])
        sh[tag + "_wu"] = A(inp[nm + "_w_up"][0])
        sh[tag + "_wd"] = A(inp[nm + "_w_down"][0])
    sh["mix_g"] = A(inp["mix_norm"][0].reshape(8, 128).T)
    w = inp["w_in"][0]
    q = w[:, 0:512]
    blocks = []
    for r in range(4):
        blocks.append(q[:, r * 64:(r + 1) * 64])
        blocks.append(q[:, (4 + r) * 64:(5 + r) * 64])
    blocks += [w[:, 768:896], w[:, 1024:1152], w[:, 512:640], w[:, 640:768], w[:, 1304:1816],
               w[:, 896:1024], w[:, 1152:1280], w[:, 1280:1304]]
    sh["w_inB"] = A(np.concatenate(blocks, axis=1))
    assert sh["w_inB"].shape == (1024, 1816)
    sh["w_inF"] = A(w[:, 1816:3864])
    rep = lambda v: np.concatenate([v, v])
    sh["gains"] = A(np.stack([rep(inp["q_norm"][0]), rep(inp["k_norm_slc"][0]),
                              rep(inp["k_norm_win"][0]), rep(inp["k_norm_cmp"][0])], axis=1))
    sh["posk"] = A(np.concatenate([inp["cmp_pos_k"][0].T] * 2, axis=0))
    sh["posv"] = A(np.concatenate([inp["cmp_pos_v"][0].T] * 2, axis=0))
    w1 = lambda a: A(np.concatenate([a.reshape(32, 64, 256).transpose(1, 0, 2)] * 2, axis=0))
    sh["w1k"] = w1(inp["cmp_k_w1"][0])
    sh["w1v"] = w1(inp["cmp_v_w1"][0])
    w2k = inp["cmp_k_w2"][0]
    sh["w2k"] = A(np.concatenate([w2k, w2k], axis=1).reshape(2, 128, 128).transpose(1, 0, 2))
    sh["w2v"] = A(inp["cmp_v_w2"][0].reshape(2, 128, 64).transpose(1, 0, 2))
    sh["w_nsa"] = A(inp["w_nsa_proj"][0])
    sh["w_glu"] = A(inp["ssm_glu_w"][0])
    sh["w_out"] = A(inp["w_out"][0])
    sm = lambda a: A(a.reshape(16, 128).T)
    sh["s_lr"] = sm(inp["ssm_lambda_re"][0])
    sh["s_li"] = sm(inp["ssm_lambda_im"][0])
    sh["s_ls"] = sm(np.repeat(inp["ssm_log_step"][0][:, None], 64, axis=1))
    sb_ = lambda a: A(a.reshape(16, 128, 16).transpose(1, 0, 2))
    sh["s_bre"] = sb_(inp["ssm_b_re"][0])
    sh["s_bim"] = sb_(inp["ssm_b_im"][0])

    def cplace(cc):
        o = np.zeros((128, 16, 128), f)
        for g in range(32):
            sc, gl = g // 2, g % 2
            c0 = 32 * (sc % 4) + 16 * gl
            o[gl * 64:(gl + 1) * 64, sc, c0:c0 + 16] = cc[g].T
        return o
    sh["s_creT"] = cplace(inp["ssm_c_re"][0])
    sh["s_cimT"] = cplace(inp["ssm_c_im"][0])
    sh["s_d"] = A(inp["ssm_d"][0].reshape(4, 128).T)
    return sh


IN_SHAPES = {
    "c_id": [128, 128], "c_bd": [128, 128], "c_E": [64, S], "c_ovl": [128, 2, 64], "c_selB": [128, 32, 64],
    "c_mwin": [128, 8, 512], "c_mslc": [128, 4, 512], "c_mcmp": [128, 5, 512],
    "f1_g": [128, 8], "f1_wg": [D, DFF], "f1_wu": [D, DFF], "f1_wd": [DFF, D],
    "f2_g": [128, 8], "f2_wg": [D, DFF], "f2_wu": [D, DFF], "f2_wd": [DFF, D],
    "mix_g": [128, 8], "w_inB": [D, 1816], "w_inF": [D, 2048], "gains": [128, 4],
    "posk": [128, 32], "posv": [128, 32], "w1k": [128, 32, 256], "w1v": [128, 32, 256],
    "w2k": [128, 2, 128], "w2v": [128, 2, 64], "w_nsa": [512, D], "w_glu": [512, 2 * D], "w_out": [D, D],
    "s_lr": [128, 16], "s_li": [128, 16], "s_ls": [128, 16], "s_bre": [128, 16, 16], "s_bim": [128, 16, 16],
    "s_creT": [128, 16, 128], "s_cimT": [128, 16, 128], "s_d": [128, 4],
}

STAGE = "full"


def kernel(**inputs):
    inp = {k: np.asarray(v) for k, v in inputs.items()}
    x = inp["x"]
    nc, P = build(STAGE)
    sh = _prep_shared(inp)
    in_maps = []
    for b in range(NCORES):
        m = dict(sh)
        m["xT"] = np.ascontiguousarray(x[b].T)
        in_maps.append(m)
    res = run_bass_kernel_spmd(nc, in_maps, core_ids=list(range(NCORES)))
    out = np.stack([np.ascontiguousarray(r["outT"].T) for r in res.results], axis=0)
    return out.astype(np.float32)
```

```python
import math
import os
from contextlib import ExitStack

import numpy as np
import concourse.bass as bass
import concourse.mybir as mybir
from concourse.bass_utils import run_bass_kernel_spmd

F32 = mybir.dt.float32
BF16 = mybir.dt.bfloat16
AF = mybir.ActivationFunctionType
ALU = mybir.AluOpType

S = 4096
D = 1024
DFF = 2816
NCORES = 8
EPS = 1e-6
DBG = set(os.environ.get("DBGSKIP", "").split(","))


class Tk:
    __slots__ = ("name", "w", "rs", "dsem", "dcnt")

    def __init__(self, name):
        self.name = name
        self.w = []
        self.rs = []
        self.dsem = None
        self.dcnt = 0


class Prog:
    ENG = ("pe", "act", "dve", "pool", "sp")

    def __init__(self, nc, es):
        self.nc = nc
        self.es = es
        self.e = {"pe": nc.tensor, "act": nc.scalar, "dve": nc.vector,
                  "pool": nc.gpsimd, "sp": nc.sync}
        self.nsem = 0
        self.sem = {}
        self.cnt = {}
        for k in self.ENG:
            self.sem[k] = self.new_sem("e_" + k)
            self.cnt[k] = 0
        self.seen = {k: {} for k in self.ENG}
        self.dma_sems = []
        self.n_inst = 0

    def new_sem(self, name):
        self.nsem += 1
        return self.es.enter_context(self.nc.semaphore(name + "_%d" % self.nsem))

    def _wait(self, eng, sem, val):
        d = self.seen[eng]
        key = id(sem)
        if d.get(key, (None, 0))[1] >= val:
            return
        d[key] = (sem, val)
        self.e[eng].wait_ge(sem, val)

    def _deps(self, eng, reads, writes, same_engine_sync=True):
        need = {}

        def add(ev):
            sem, val, src = ev
            if src == eng and not same_engine_sync:
                return
            if isinstance(src, tuple):
                val = src[0].dsem[src[1]][1]
            k = id(sem)
            if k not in need or need[k][1] < val:
                need[k] = (sem, val)

        for t in reads:
            for ev in t.w:
                add(ev)
        for t in writes:
            for ev in t.w:
                add(ev)
            for ev in t.rs:
                add(ev)
        for sem, val in need.values():
            self._wait(eng, sem, val)

    def _roll(self, eng):
        if self.cnt[eng] >= 8000:
            self.sem[eng] = self.new_sem("e_" + eng)
            self.cnt[eng] = 0

    def op(self, eng, fn, reads=(), writes=(), sync_same=True):
        self._roll(eng)
        self._deps(eng, reads, writes, same_engine_sync=sync_same)
        ins = fn(self.e[eng])
        self.cnt[eng] += 1
        ev = (self.sem[eng], self.cnt[eng], eng)
        ins.then_inc(self.sem[eng], 1)
        self.n_inst += 1
        for t in reads:
            t.rs.append(ev)
            if len(t.rs) > 24:
                t.rs = self._compact(t.rs)
        for t in writes:
            t.w = [ev]
            t.rs = []
        return ins

    @staticmethod
    def _compact(evs):
        best = {}
        for sem, val, src in evs:
            k = id(sem)
            if k not in best or best[k][1] < val:
                best[k] = (sem, val, src)
        return list(best.values())

    def dma(self, q, out_ap, in_ap, reads=(), writes=(), chan=None):
        if chan is None:
            chan = writes[0] if writes else reads[0]
        if chan.dsem is None:
            chan.dsem = {}
        if q not in chan.dsem:
            chan.dsem[q] = [self.new_sem("d_" + chan.name + "_" + q), 0]
            self.dma_sems.append((chan, q))
        ent = chan.dsem[q]
        self._deps(q, reads, writes)
        ins = self.e[q].dma_start(out=out_ap, in_=in_ap)
        ent[1] += 16
        ins.then_inc(ent[0], 16)
        ev = (ent[0], ent[1], (chan, q))
        self.n_inst += 1
        for t in reads:
            t.rs.append(ev)
            if len(t.rs) > 24:
                t.rs = self._compact(t.rs)
        for t in writes:
            t.w = [e for e in t.w if isinstance(e[2], tuple) and e[0] is not ent[0]] + [ev]
            t.rs = []
        return ins

    def barrier(self):
        for x in self.ENG:
            for y in self.ENG:
                if y != x and self.cnt[y] > 0:
                    self._wait(x, self.sem[y], self.cnt[y])
            for t, q in self.dma_sems:
                ent = t.dsem[q]
                if ent[1] > 0:
                    self._wait(x, ent[0], ent[1])

    def final_wait(self, tks):
        for t in tks:
            for sem, val, _ in t.w:
                self._wait("sp", sem, val)


def build(stage="full", debug=False):
    nc = bass.Bass("TRN2", target_bir_lowering=False)
    es = ExitStack()
    P = Prog(nc, es)

    def dram_in(name, shape, dt=F32):
        return nc.dram_tensor(name, list(shape), dt, kind="ExternalInput").ap()

    def dram_out(name, shape, dt=F32):
        return nc.dram_tensor(name, list(shape), dt, kind="ExternalOutput").ap()

    def dram_tmp(name, shape, dt=F32):
        kind = "ExternalOutput" if debug else "Internal"
        return nc.dram_tensor(name, list(shape), dt, kind=kind).ap()

    def sb(name, shape, dt, stack=None):
        return (stack or es).enter_context(nc.sbuf_tensor("sb_" + name, list(shape), dt))

    xT = dram_in("xT", [D, S])
    outT = dram_out("outT", [D, S])
    din = {}
    for nm, shp in IN_SHAPES.items():
        din[nm] = dram_in(nm, shp)
    h1T = dram_tmp("h1T", [D, S])
    h2T = dram_tmp("h2T", [D, S])
    uT = dram_tmp("uT", [512, S], BF16)
    onsaT = dram_tmp("onsaT", [512, S], BF16)
    ygT = dram_tmp("ygT", [512, S], BF16)
    t_xT, t_outT, t_h1T, t_h2T, t_uT, t_onsaT, t_ygT = [Tk(n) for n in
        ("xT", "outT", "h1T", "h2T", "uT", "onsaT", "ygT")]

    ps = [es.enter_context(nc.psum_tensor("ps%d" % i, [128, 512], F32)) for i in range(8)]
    t_ps = [Tk("ps%d" % i) for i in range(8)]

    def MM(out, lhsT, rhs, st, sp, reads, writes):
        P.op("pe", lambda e: e.matmul(out, lhsT=lhsT, rhs=rhs, start=st, stop=sp),
             reads, writes, sync_same=False)

    def TR(out, in_, ident, reads, writes):
        P.op("pe", lambda e: e.transpose(out, in_, ident), reads, writes, sync_same=False)

    def TT2(eng, out, a, b, op, reads, writes):
        P.op(eng, lambda e: e.tensor_tensor(out=out, in0=a, in1=b, op=op), reads, writes)

    def TS(eng, out, a, s1, s2, op0, op1, reads, writes):
        if s2 is None:
            P.op(eng, lambda e: e.tensor_scalar(out=out, in0=a, scalar1=s1, scalar2=None, op0=op0),
                 reads, writes)
        else:
            P.op(eng, lambda e: e.tensor_scalar(out=out, in0=a, scalar1=s1, scalar2=s2, op0=op0, op1=op1),
                 reads, writes)

    def STT(out, a, sc, b, op0, op1, reads, writes):
        P.op("dve", lambda e: e.scalar_tensor_tensor(out=out, in0=a, scalar=sc, in1=b, op0=op0, op1=op1),
             reads, writes)

    def ACT(out, in_, func, reads, writes, bias=None, scale=None):
        kw = {}
        if bias is not None:
            kw["bias"] = bias
        if scale is not None:
            kw["scale"] = scale
        P.op("act", lambda e: e.activation(out=out, in_=in_, func=func, **kw), reads, writes)

    def CP(eng, out, in_, reads, writes):
        if eng == "act":
            P.op("act", lambda e: e.activation(out=out, in_=in_, func=AF.Identity), reads, writes)
        else:
            P.op(eng, lambda e: e.tensor_copy(out=out, in_=in_), reads, writes)

    def RECIP(out, in_, reads, writes):
        P.op("dve", lambda e: e.reciprocal(out=out, in_=in_), reads, writes)

    def MEMSET(eng, ap, val, writes):
        P.op(eng, lambda e: e.memset(ap, val), (), writes)

    ones_bf = sb("ones_bf", [128, 128], BF16)
    bd_bf = sb("bd_bf", [128, 128], BF16)
    id_bf = sb("id_bf", [128, 128], BF16)
    id_f = sb("id_f", [128, 128], F32)
    epsc = sb("epsc", [128, 1], F32)
    t_const = Tk("const")
    MEMSET("pool", ones_bf[:], 1.0, [t_const])
    MEMSET("pool", epsc[:], EPS, [t_const])
    P.dma("pool", bd_bf[:], din["c_bd"], writes=[t_const])
    P.dma("pool", id_bf[:], din["c_id"], writes=[t_const])
    P.dma("sp", id_f[:], din["c_id"], writes=[t_const])

    def norm_tile(xb, t_xb, gsb, t_g, xn, t_xn, sq, t_sq, rstd, t_rstd, TT, bank):
        pstat = ps[bank][:, 0:TT]
        for c in range(8):
            TT2("pool", sq[c % 2][:, 0:TT], xb[:, c, :], xb[:, c, :], ALU.mult, [t_xb], [t_sq[c % 2]])
            MM(pstat, ones_bf[:], sq[c % 2][:, 0:TT], c == 0, c == 7, [t_const, t_sq[c % 2]], [t_ps[bank]])
        ACT(rstd[:, 0:TT], pstat, AF.Sqrt, [t_ps[bank], t_const], [t_rstd], bias=epsc[:, 0:1], scale=1.0 / D)
        RECIP(rstd[:, 0:TT], rstd[:, 0:TT], [t_rstd], [t_rstd])
        for c in range(8):
            STT(xn[:, c, :], xb[:, c, :], gsb[:, c:c + 1], rstd[:, 0:TT], ALU.mult, ALU.mult,
                [t_xb, t_g, t_rstd], [t_xn])

    def ffn_phase(tag, src, t_src, dst, t_dst, g_dram, wg_d, wu_d, wd_d):
        TT = 256
        NT = S // TT
        with ExitStack() as fs:
            wg = sb(tag + "wg", [128, 8, DFF], BF16, fs)
            wu = sb(tag + "wu", [128, 8, DFF], BF16, fs)
            wd = sb(tag + "wd", [128, 22, D], BF16, fs)
            gsb = sb(tag + "g", [128, 8], F32, fs)
            xt = [sb(tag + "x%d" % i, [128, 8, TT], F32, fs) for i in range(2)]
            xn = [sb(tag + "xn%d" % i, [128, 8, TT], BF16, fs) for i in range(2)]
            sq8 = sb(tag + "sq8", [128, 8, TT], BF16, fs)
            rstd = sb(tag + "rstd", [128, TT], F32, fs)
            hh = sb(tag + "h", [128, 22, TT], BF16, fs)
            sg = [sb(tag + "sg%d" % i, [128, TT], F32, fs) for i in range(2)]
            ot = sb(tag + "ot", [128, 8, TT], F32, fs)
            t_wg = [Tk(tag + "wg%d" % k) for k in range(8)]
            t_wu = [Tk(tag + "wu%d" % k) for k in range(8)]
            t_wd = [Tk(tag + "wd%d" % k) for k in range(22)]
            t_g = Tk(tag + "g")
            t_xt = [Tk(tag + "xt%d" % i) for i in range(2)]
            t_xn = [Tk(tag + "xn%d" % i) for i in range(2)]
            t_sq8 = [Tk(tag + "sq8_%d" % i) for i in range(8)]
            t_rstd = Tk(tag + "rstd")
            t_hh = [Tk(tag + "hh%d" % i) for i in range(22)]
            t_sg = [Tk(tag + "sg%d" % i) for i in range(2)]
            t_ot = Tk(tag + "ot")
            src_v = src.rearrange("(c p) t -> p c t", p=128)
            dst_v = dst.rearrange("(c p) t -> p c t", p=128)

            def load_x(i):
                b = i % 2
                P.dma("sp", xt[b][:], src_v[:, :, i * TT:(i + 1) * TT], reads=[t_src], writes=[t_xt[b]])

            def norm1(i):
                b = i % 2
                for c in range(8):
                    TT2("pool", sq8[:, c, :], xt[b][:, c, :], xt[b][:, c, :], ALU.mult, [t_xt[b]], [t_sq8[c]])

            def norm2(i):
                b = i % 2
                pstat = ps[6][:, 0:TT]
                for c in range(8):
                    MM(pstat, ones_bf[:], sq8[:, c, :], c == 0, c == 7, [t_const, t_sq8[c]], [t_ps[6]])
                ACT(rstd[:], pstat, AF.Sqrt, [t_ps[6], t_const], [t_rstd], bias=epsc[:, 0:1], scale=1.0 / D)
                RECIP(rstd[:], rstd[:], [t_rstd], [t_rstd])
                for c in range(8):
                    STT(xn[b][:, c, :], xt[b][:, c, :], gsb[:, c:c + 1], rstd[:], ALU.mult, ALU.mult,
                        [t_xt[b], t_g, t_rstd], [t_xn[b]])

            P.dma("sp", gsb[:], g_dram, writes=[t_g])
            load_x(0)
            wg_v = wg_d.rearrange("(k p) f -> p k f", p=128)
            wu_v = wu_d.rearrange("(k p) f -> p k f", p=128)
            wd_v = wd_d.rearrange("(k p) d -> p k d", p=128)
            ch_a, ch_b = Tk(tag + "chA"), Tk(tag + "chB")
            for k in range(8):
                P.dma("pool", wg[:, k, :], wg_v[:, k, :], writes=[t_wg[k]], chan=ch_a)
                P.dma("pool", wu[:, k, :], wu_v[:, k, :], writes=[t_wu[k]], chan=ch_a)
            for k in range(22):
                P.dma("pool", wd[:, k, :], wd_v[:, k, :], writes=[t_wd[k]], chan=ch_b)
            norm1(0)
            norm2(0)
            for i in range(NT):
                b = i % 2
                xb = xt[b]
                if i + 1 < NT:
                    load_x(i + 1)
                    norm1(i + 1)
                for fc in range(22):
                    pb = fc % 2
                    pg = ps[0 + pb][:, 0:TT]
                    pu = ps[2 + pb][:, 0:TT]
                    for k in range(8):
                        MM(pg, wg[:, k, fc * 128:(fc + 1) * 128], xn[b][:, k, :], k == 0, k == 7,
                           [t_wg[k], t_xn[b]], [t_ps[0 + pb]])
                    for k in range(8):
                        MM(pu, wu[:, k, fc * 128:(fc + 1) * 128], xn[b][:, k, :], k == 0, k == 7,
                           [t_wu[k], t_xn[b]], [t_ps[2 + pb]])
                    ACT(sg[pb][:], pg, AF.Silu, [t_ps[0 + pb]], [t_sg[pb]])
                    TT2("dve", hh[:, fc, :], pu, sg[pb][:], ALU.mult, [t_ps[2 + pb], t_sg[pb]], [t_hh[fc]])
                if i + 1 < NT:
                    norm2(i + 1)
                for dc in range(8):
                    pb = dc % 2
                    py = ps[4 + pb][:, 0:TT]
                    for fc in range(22):
                        MM(py, wd[:, fc, dc * 128:(dc + 1) * 128], hh[:, fc, :], fc == 0, fc == 21,
                           [t_wd[fc], t_hh[fc]], [t_ps[4 + pb]])
                    STT(ot[:, dc, :], py, 0.5, xb[:, dc, :], ALU.mult, ALU.add,
                        [t_ps[4 + pb], t_xt[b]], [t_ot])
                P.dma("sp", dst_v[:, :, i * TT:(i + 1) * TT], ot[:], reads=[t_ot], writes=[t_dst], chan=t_ot)
            P.barrier()

    def mixer_attention():
        with ExitStack() as ms:
            Qs = [sb("Qs%d" % h, [128, S], BF16, ms) for h in range(8)]
            Ks = [sb("Ks%d" % g, [128, S], BF16, ms) for g in range(2)]
            Kw = sb("Kw", [128, S], BF16, ms)
            Vs = sb("Vs", [128, 32, 2, 72], BF16, ms)
            Vw = sb("Vw", [128, 32, 2, 72], BF16, ms)
            gl = sb("gl", [128, 32, 24], F32, ms)
            gains = sb("gains", [128, 4], F32, ms)
            gq8 = sb("gq8", [128, 1], F32, ms)
            kcT = sb("kcT", [128, 256], BF16, ms)
            rcmp = [sb("rcmp%d" % g, [128, 2, 129], BF16, ms) for g in range(2)]
            t_Q = Tk("Qq")
            t_Qm = [Tk("Qm%d" % i) for i in range(8)]
            t_Ks, t_Kw, t_Vs, t_Vw, t_gl, t_gains, t_kcT, t_rcmp = [Tk(n) for n in
                ("Ks", "Kw", "Vs", "Vw", "gl", "gains", "kcT", "rcmp")]
            P.dma("sp", gains[:], din["gains"], writes=[t_gains])
            TS("dve", gq8[:], gains[:, 0:1], 0.125, None, ALU.mult, None, [t_gains], [t_gains])
            if "E" not in DBG:
                P.dma("pool", Ks[0][64:128, :], din["c_E"], writes=[t_Ks])
                P.dma("pool", Ks[1][0:64, :], din["c_E"], writes=[t_Ks])
            if "V1" not in DBG:
                MEMSET("pool", Vs[:, :, :, 64:65], 1.0, [t_Vs])
                MEMSET("pool", Vw[:, :, :, 64:65], 1.0, [t_Vw])

            with ExitStack() as bs:
                TT = 512
                sq = [sb("Bsq%d" % i, [128, TT], BF16, bs) for i in range(2)]
                rs = sb("Brs", [128, TT], F32, bs)
                KcA = sb("KcA", [128, S], BF16, bs)
                KcB = sb("KcB", [128, S], BF16, bs)
                VcA = sb("VcA", [128, S], BF16, bs)
                VcB = sb("VcB", [128, S], BF16, bs)
                b2 = ExitStack()
                wB = sb("wB", [128, 8, 1816], BF16, b2)
                mg = sb("mg", [128, 8], F32, b2)
                xt = sb("Bx", [128, 8, TT], F32, b2)
                xn = sb("Bxn", [128, 8, TT], BF16, b2)
                rstd = sb("Brstd", [128, TT], F32, b2)
                ust = sb("Bust", [128, 4, TT], BF16, b2)
                posk = sb("posk", [128, 32], F32, b2)
                posv = sb("posv", [128, 32], F32, b2)
                vstg = sb("vstg", [128, 256], BF16, b2)
                t_vstg = Tk("vstg")
                t_wB, t_mg, t_xt, t_xn, t_rstd, t_rs, t_ust, t_Kc, t_pos = [Tk(n) for n in
                    ("wB", "mg", "Bxt", "Bxn", "Brstd", "Brs", "Bust", "Kc", "pos")]
                t_sq = [Tk("Bsq0"), Tk("Bsq1")]
                wB_v = din["w_inB"].rearrange("(k p) f -> p k f", p=128)
                for k in range(8):
                    P.dma("pool", wB[:, k, :], wB_v[:, k, :], writes=[t_wB])
                P.dma("sp", mg[:], din["mix_g"], writes=[t_mg])
                P.dma("sp", posk[:], din["posk"], writes=[t_pos])
                P.dma("sp", posv[:], din["posv"], writes=[t_pos])
                h1_v = h1T.rearrange("(c p) t -> p c t", p=128)
                uT_v = uT.rearrange("(c p) t -> p c t", p=128)
                for i in range(S // TT):
                    tsl = slice(i * TT, (i + 1) * TT)
                    P.dma("sp", xt[:], h1_v[:, :, tsl], reads=[t_h1T], writes=[t_xt])
                    norm_tile(xt, t_xt, mg, t_mg, xn, t_xn, sq, t_sq, rstd, t_rstd, TT, 6)
                    pssb = [3, 7]

                    def main(ch):
                        pb = ch % 3
                        pq = ps[pb][:, :]
                        for k in range(8):
                            MM(pq, wB[:, k, ch * 128:(ch + 1) * 128], xn[:, k, :], k == 0, k == 7,
                               [t_wB, t_xn], [t_ps[pb]])
                        if ch < 6:
                            ACT(sq[ch % 2][:], pq, AF.Square, [t_ps[pb]], [t_sq[ch % 2]])

                    def finish(ch):
                        pb = ch % 3
                        pq = ps[pb][:, :]
                        if ch < 6:
                            sb_ = pssb[ch % 2]
                            pss = ps[sb_][:, :]
                            MM(pss, bd_bf[:], sq[ch % 2][:], True, True, [t_const, t_sq[ch % 2]], [t_ps[sb_]])
                            ACT(rs[:], pss, AF.Ln, [t_ps[sb_], t_const], [t_rs], bias=epsc[:, 0:1], scale=1.0 / 64)
                            ACT(rs[:], rs[:], AF.Exp, [t_rs], [t_rs], scale=-0.5)
                            if ch < 4:
                                STT(Qs[ch][0:64, tsl], pq[0:64, :], gq8[0:64, 0:1], rs[0:64, :], ALU.mult, ALU.mult,
                                    [t_ps[pb], t_gains, t_rs], [t_Q])
                                STT(Qs[4 + ch][64:128, tsl], pq[64:128, :], gq8[64:128, 0:1], rs[64:128, :],
                                    ALU.mult, ALU.mult, [t_ps[pb], t_gains, t_rs], [t_Q])
                            elif ch == 4:
                                STT(Ks[0][0:64, tsl], pq[0:64, :], gains[0:64, 1:2], rs[0:64, :], ALU.mult, ALU.mult,
                                    [t_ps[pb], t_gains, t_rs], [t_Ks])
                                STT(Ks[1][64:128, tsl], pq[64:128, :], gains[64:128, 1:2], rs[64:128, :],
                                    ALU.mult, ALU.mult, [t_ps[pb], t_gains, t_rs], [t_Ks])
                            else:
                                STT(Kw[:, tsl], pq, gains[:, 2:3], rs[:], ALU.mult, ALU.mult,
                                    [t_ps[pb], t_gains, t_rs], [t_Kw])
                        elif ch < 8:
                            pos = posk if ch == 6 else posv
                            A_, B_ = (KcA, KcB) if ch == 6 else (VcA, VcB)
                            pq3 = pq.rearrange("p (i s) -> p i s", s=16)
                            TT2("dve", A_[:, tsl].rearrange("p (i s) -> p i s", s=16), pq3,
                                pos[:, 0:16].unsqueeze(1).to_broadcast([128, TT // 16, 16]), ALU.add,
                                [t_ps[pb], t_pos], [t_Kc])
                            TT2("dve", B_[:, tsl].rearrange("p (i s) -> p i s", s=16), pq3,
                                pos[:, 16:32].unsqueeze(1).to_broadcast([128, TT // 16, 16]), ALU.add,
                                [t_ps[pb], t_pos], [t_Kc])
                        else:
                            CP("act", ust[:, ch - 8, :], pq, [t_ps[pb]], [t_ust])

                    for ch in range(13):
                        if ch < 12:
                            main(ch)
                        if ch >= 1:
                            finish(ch - 1)
                    P.dma("sp", uT_v[:, :, tsl], ust[:], reads=[t_ust], writes=[t_uT], chan=t_ust)
                    for sub in range(4):
                        if "TM" in DBG:
                            break
                        pb = 4 + sub % 2
                        pt = ps[pb][:, 0:280]
                        for k in range(8):
                            MM(pt, xn[:, k, sub * 128:(sub + 1) * 128], wB[:, k, 1536:1816], k == 0, k == 7,
                               [t_wB, t_xn], [t_ps[pb]])
                        kt = i * 4 + sub
                        CP("act", vstg[:], ps[pb][:, 0:256], [t_ps[pb]], [t_vstg])
                        for g_ in range(2):
                            CP("pool", Vs[:, kt, g_, 0:64], vstg[:, g_ * 64:(g_ + 1) * 64], [t_vstg], [t_Vs])
                            CP("pool", Vw[:, kt, g_, 0:64], vstg[:, 128 + g_ * 64:128 + (g_ + 1) * 64], [t_vstg], [t_Vw])
                        if "cpG" not in DBG:
                            CP("act", gl[:, kt, :], ps[pb][:, 256:280], [t_ps[pb]], [t_gl])
                P.barrier()
                b2.close()
                if stage == "B":
                    return

                w1k = sb("w1k", [128, 32, 256], BF16, bs)
                w1v = sb("w1v", [128, 32, 256], BF16, bs)
                w2k = sb("w2k", [128, 2, 128], BF16, bs)
                w2v = sb("w2v", [128, 2, 64], BF16, bs)
                ovl = sb("ovl", [128, 2, 64], BF16, bs)
                hx = sb("hx", [128, 256], F32, bs)
                hu = sb("hu", [128, 256], F32, bs)
                hid = [[sb("hid%d%d" % (a, b), [128, 256], BF16, bs) for b in range(2)] for a in range(2)]
                t_cw, t_hx, t_hu = Tk("cw"), Tk("hx"), Tk("hu")
                t_hid = [[Tk("hid%d%d" % (a, b)) for b in range(2)] for a in range(2)]
                P.dma("pool", w1k[:], din["w1k"], writes=[t_cw])
                P.dma("pool", w1v[:], din["w1v"], writes=[t_cw])
                P.dma("pool", w2k[:], din["w2k"], writes=[t_cw])
                P.dma("pool", w2v[:], din["w2v"], writes=[t_cw])
                P.dma("pool", ovl[:], din["c_ovl"], writes=[t_cw])
                MEMSET("pool", kcT[:], 0.0, [t_kcT])
                for g in range(2):
                    MEMSET("pool", rcmp[g][:, :, 128:129], 1.0, [t_rcmp])
                    CP("pool", rcmp[g][:, :, 64:128], ovl[:], [t_cw], [t_rcmp])
                for a in range(2):
                    for b in range(2):
                        MEMSET("pool", hid[a][b][:], 0.0, [t_hid[a][b]])
                for kind in range(2):
                    w1 = w1k if kind == 0 else w1v
                    A_, B_ = (KcA, KcB) if kind == 0 else (VcA, VcB)
                    A3 = A_[:].rearrange("p (i s) -> p i s", s=16)
                    B3 = B_[:].rearrange("p (i s) -> p i s", s=16)
                    for g in range(2):
                        R = slice(64 * g, 64 * g + 64)
                        for mc in range(2):
                            ph = ps[mc][:, 0:255]
                            for j in range(32):
                                rhs = A3[R, 0:255, j] if j < 16 else B3[R, 1:256, j - 16]
                                MM(ph, w1[R, j, mc * 128:(mc + 1) * 128], rhs, j == 0, j == 31,
                                   [t_cw, t_Kc], [t_ps[mc]])
                            CP("act", hx[:, 0:255], ph, [t_ps[mc]], [t_hx])
                            TT2("dve", hu[:, 0:255], hx[:, 0:255], hx[:, 0:255], ALU.mult, [t_hx], [t_hu])
                            TS("dve", hu[:, 0:255], hu[:, 0:255], 0.044715, 1.0, ALU.mult, ALU.add, [t_hu], [t_hu])
                            TT2("dve", hu[:, 0:255], hu[:, 0:255], hx[:, 0:255], ALU.mult, [t_hu, t_hx], [t_hu])
                            ACT(hu[:, 0:255], hu[:, 0:255], AF.Sigmoid, [t_hu], [t_hu], scale=1.5957691216057308)
                            TT2("dve", hid[g][mc][:, 0:255], hu[:, 0:255], hx[:, 0:255], ALU.mult,
                                [t_hu, t_hx], [t_hid[g][mc]])
                        if kind == 0:
                            pk = ps[2][:, 0:256]
                            for mc in range(2):
                                MM(pk, w2k[:, mc, :], hid[g][mc][:], mc == 0, mc == 1,
                                   [t_cw, t_hid[g][mc]], [t_ps[2]])
                            sqn = sq[0]
                            ACT(sqn[:, 0:256], pk, AF.Square, [t_ps[2]], [t_sq[0]])
                            pss = ps[3][:, 0:256]
                            MM(pss, bd_bf[:], sqn[:, 0:256], True, True, [t_const, t_sq[0]], [t_ps[3]])
                            ACT(rs[:, 0:256], pss, AF.Sqrt, [t_ps[3], t_const], [t_rs], bias=epsc[:, 0:1], scale=1.0 / 64)
                            RECIP(rs[:, 0:256], rs[:, 0:256], [t_rs], [t_rs])
                            STT(kcT[R, 0:255], pk[R, 0:255], gains[R, 3:4], rs[R, 0:255], ALU.mult, ALU.mult,
                                [t_ps[2], t_gains, t_rs], [t_kcT])
                        else:
                            for it in range(2):
                                pv = ps[2 + it][:, 0:64]
                                for mc in range(2):
                                    MM(pv, hid[g][mc][:, it * 128:(it + 1) * 128], w2v[:, mc, :], mc == 0, mc == 1,
                                       [t_cw, t_hid[g][mc]], [t_ps[2 + it]])
                                CP("act", rcmp[g][:, it, 0:64], pv, [t_ps[2 + it]], [t_rcmp])
                P.barrier()

            if stage == "BC":
                return
            with ExitStack() as ds:
                mwin = sb("mwin", [128, 8, 512], BF16, ds)
                mslc = sb("mslc", [128, 4, 512], BF16, ds)
                mcmp = sb("mcmp", [128, 5, 512], BF16, ds)
                selB = sb("selB", [128, 32, 64], F32, ds)
                gs = sb("gs", [128, 32, 24], F32, ds)
                PT = sb("PT", [128, 32, 512], BF16, ds)
                Pc = [sb("Pc%d" % i, [128, 512], BF16, ds) for i in range(2)]
                onsa = sb("onsa", [128, 4, 512], F32, ds)
                onsab = sb("onsab", [128, 4, 512], BF16, ds)
                oTs = sb("oTs", [128, 4, 512], BF16, ds)
                imp = [sb("imp%d" % g, [128, 4, 64], F32, ds) for g in range(2)]
                scr = sb("scr", [128, 64], F32, ds)
                wk = sb("wk", [128, 64], F32, ds)
                m8 = sb("m8", [128, 16], F32, ds)
                NM = sb("NM", [128, 4, 128], BF16, ds)
                sm = sb("sm", [128, 8], F32, ds)
                t_msk, t_selB, t_gs, t_onsa, t_onsab, t_oTs, t_scr, t_wk, t_m8, t_NM, t_sm = [Tk(n) for n in
                    ("msk", "selB", "gs", "onsa", "onsab", "oTs", "scr", "wk", "m8", "NM", "sm")]
                t_PT = [Tk("PT%d" % i) for i in range(32)]
                t_Pc = [Tk("Pc0"), Tk("Pc1")]
                t_imp = [Tk("imp0"), Tk("imp1")]
                P.dma("pool", mwin[:], din["c_mwin"], writes=[t_msk])
                P.dma("pool", mslc[:], din["c_mslc"], writes=[t_msk])
                P.dma("pool", mcmp[:], din["c_mcmp"], writes=[t_msk])
                P.dma("sp", selB[:], din["c_selB"], writes=[t_selB])
                ACT(gs[:], gl[:], AF.Sigmoid, [t_gl], [t_gs])
                onsaT_v = onsaT.rearrange("(c p) t -> p c t", p=128)
                BIG = 30000.0
                sbank = [0, 1, 2]
                sctr = [0]

                def next_sbank():
                    b = sbank[sctr[0] % 3]
                    sctr[0] += 1
                    return b

                def evac(po, sub, h, br, kt, first, imp_g=None, r=0):
                    dcol = 128 if br == 0 else 64
                    c0 = (h * 3 + br) % 4 * 2
                    TS("dve", sm[:, c0:c0 + 1], po[:, dcol:dcol + 1], 1e-30, None, ALU.max, None, [t_ps_cur[0]], [t_sm])
                    RECIP(sm[:, c0:c0 + 1], sm[:, c0:c0 + 1], [t_sm], [t_sm])
                    if br == 0:
                        if r == 0:
                            TS("dve", imp_g[:, sub, :], po[:, 64:128], sm[:, c0:c0 + 1], None, ALU.mult, None,
                               [t_ps_cur[0], t_sm], [t_imp[h // 4]])
                        else:
                            STT(imp_g[:, sub, :], po[:, 64:128], sm[:, c0:c0 + 1], imp_g[:, sub, :], ALU.mult, ALU.add,
                                [t_ps_cur[0], t_sm, t_imp[h // 4]], [t_imp[h // 4]])
                    TT2("dve", sm[:, c0 + 1:c0 + 2], sm[:, c0:c0 + 1], gs[:, kt, br * 8 + h:br * 8 + h + 1], ALU.mult,
                        [t_sm, t_gs], [t_sm])
                    dst = onsa[:, sub, h * 64:(h + 1) * 64]
                    if first:
                        TS("dve", dst, po[:, 0:64], sm[:, c0 + 1:c0 + 2], None, ALU.mult, None,
                           [t_ps_cur[0], t_sm], [t_onsa])
                    else:
                        STT(dst, po[:, 0:64], sm[:, c0 + 1:c0 + 2], dst, ALU.mult, ALU.add,
                            [t_ps_cur[0], t_sm, t_onsa], [t_onsa])

                t_ps_cur = [None]
                for Q in range(8):
                    qsl = slice(Q * 512, (Q + 1) * 512)
                    cts = [0] if Q < 4 else [0, 1]
                    for h in range(8):
                        g, r = h // 4, h % 4
                        R = slice(64 * g, 64 * g + 64)
                        for ct in cts:
                            b = next_sbank()
                            MM(ps[b][:, :], kcT[R, ct * 128:(ct + 1) * 128], Qs[h][R, qsl], True, True,
                               [t_kcT, t_Q], [t_ps[b]])
                            ACT(Pc[ct][:], ps[b][:, :], AF.Exp, [t_ps[b]], [t_Pc[ct]])
                            v = Q - 4 * ct
                            if v <= 4:
                                TT2("dve", Pc[ct][:], Pc[ct][:], mcmp[:, v, :], ALU.mult, [t_Pc[ct], t_msk], [t_Pc[ct]])
                        for sub in range(4):
                            bk = 3 + sub // 2
                            po = ps[bk][:, (sub % 2) * 129:(sub % 2) * 129 + 129]
                            for ci, ct in enumerate(cts):
                                MM(po, Pc[ct][:, sub * 128:(sub + 1) * 128], rcmp[g][:, ct, :], ci == 0, ci == len(cts) - 1,
                                   [t_Pc[ct], t_rcmp], [t_ps[bk]])
                        for sub in range(4):
                            bk = 3 + sub // 2
                            po = ps[bk][:, (sub % 2) * 129:(sub % 2) * 129 + 129]
                            t_ps_cur[0] = t_ps[bk]
                            evac(po, sub, h, 0, Q * 4 + sub, True, imp[g], r)
                    for sub in range(4):
                        kt = Q * 4 + sub
                        for g in range(2):
                            TT2("dve", scr[:], imp[g][:, sub, :], selB[:, kt, :], ALU.add, [t_imp[g], t_selB], [t_scr])
                            P.op("dve", lambda e: e.max(out=m8[:, 0:8], in_=scr[:]), [t_scr], [t_m8])
                            P.op("dve", lambda e: e.match_replace(out=wk[:], in_to_replace=m8[:, 0:8],
                                                                   in_values=scr[:], imm_value=-3.0e38),
                                 [t_scr, t_m8], [t_wk])
                            P.op("dve", lambda e: e.max(out=m8[:, 8:16], in_=wk[:]), [t_wk], [t_m8])
                            cs = slice(64, 128) if g == 0 else slice(0, 64)
                            TS("dve", NM[:, sub, cs], scr[:], m8[:, 15:16], -BIG, ALU.is_lt, ALU.mult,
                               [t_scr, t_m8], [t_NM])
                    pT = ps[7][:].bitcast(BF16)
                    for sub in range(4):
                        TR(pT[:, sub * 128:(sub + 1) * 128], NM[:, sub, :], id_bf[:], [t_NM, t_const], [t_ps[7]])
                    for h in range(8):
                        R2 = slice(64, 128) if h < 4 else slice(0, 64)
                        CP("act" if h % 2 == 0 else "dve", Qs[h][R2, qsl], pT[R2, 0:512], [t_ps[7]], [t_Qm[Q]])
                    for h in range(8):
                        g = h // 4
                        R = slice(64 * g, 64 * g + 64)
                        nkt = 4 * Q + 4
                        for kt in range(nkt):
                            b = next_sbank()
                            MM(ps[b][:, :], Ks[g][:, kt * 128:(kt + 1) * 128], Qs[h][:, qsl], True, True,
                               [t_Ks, t_Q, t_Qm[Q]], [t_ps[b]])
                            ACT(PT[:, kt, :], ps[b][:, :], AF.Exp, [t_ps[b]], [t_PT[kt]])
                            if kt >= 4 * Q:
                                TT2("dve", PT[:, kt, :], PT[:, kt, :], mslc[:, kt - 4 * Q, :], ALU.mult,
                                    [t_PT[kt], t_msk], [t_PT[kt]])
                        bk = 5
                        for sub in range(4):
                            po = ps[bk][:, sub * 65:sub * 65 + 65]
                            last = 4 * Q + sub
                            for kt in range(last + 1):
                                MM(po, PT[:, kt, sub * 128:(sub + 1) * 128], Vs[:, kt, g, 0:65], kt == 0, kt == last,
                                   [t_PT[kt], t_Vs], [t_ps[bk]])
                        t_ps_cur[0] = t_ps[bk]
                        for sub in range(4):
                            evac(ps[bk][:, sub * 65:sub * 65 + 65], sub, h, 1, Q * 4 + sub, False)
                        rr = [r_ for r_ in range(8) if 4 * Q - 4 + r_ >= 0]
                        for r_ in rr:
                            kt = 4 * Q - 4 + r_
                            b = next_sbank()
                            MM(ps[b][:, :], Kw[R, kt * 128:(kt + 1) * 128], Qs[h][R, qsl], True, True,
                               [t_Kw, t_Q], [t_ps[b]])
                            ACT(PT[:, r_, :], ps[b][:, :], AF.Exp, [t_ps[b]], [t_PT[r_]])
                            TT2("dve", PT[:, r_, :], PT[:, r_, :], mwin[:, r_, :], ALU.mult, [t_PT[r_], t_msk], [t_PT[r_]])
                        bk = 6
                        for sub in range(4):
                            po = ps[bk][:, sub * 65:sub * 65 + 65]
                            rs_ = [r_ for r_ in rr if sub <= r_ <= sub + 4]
                            for ci, r_ in enumerate(rs_):
                                kt = 4 * Q - 4 + r_
                                MM(po, PT[:, r_, sub * 128:(sub + 1) * 128], Vw[:, kt, g, 0:65], ci == 0, ci == len(rs_) - 1,
                                   [t_PT[r_], t_Vw], [t_ps[bk]])
                        t_ps_cur[0] = t_ps[bk]
                        for sub in range(4):
                            evac(ps[bk][:, sub * 65:sub * 65 + 65], sub, h, 2, Q * 4 + sub, False)
                    CP("act", onsab[:], onsa[:], [t_onsa], [t_onsab])
                    for fc in range(4):
                        pT2 = ps[7][:].bitcast(BF16)
                        for sub in range(4):
                            TR(pT2[:, sub * 128:(sub + 1) * 128], onsab[:, sub, fc * 128:(fc + 1) * 128], id_bf[:],
                               [t_onsab, t_const], [t_ps[7]])
                        CP("dve" if fc % 2 else "act", oTs[:, fc, :], pT2[:, 0:512], [t_ps[7]], [t_oTs])
                    P.dma("sp", onsaT_v[:, :, qsl], oTs[:], reads=[t_oTs], writes=[t_onsaT], chan=t_oTs)
                P.barrier()

    def ssm_phase():
        PI = math.pi
        with ExitStack() as fs:
            CT = sb("CT", [128, 16, 512], F32, fs)
            ST = sb("ST", [128, 16, 512], F32, fs)
            rhoF = sb("rhoF", [128, 16, 512], F32, fs)
            BBrT = sb("BBrT", [128, 16, 128], BF16, fs)
            BBiT = sb("BBiT", [128, 16, 128], BF16, fs)
            CreT = sb("CreT", [128, 16, 128], BF16, fs)
            CimT = sb("CimT", [128, 16, 128], BF16, fs)
            Dg = sb("Dg", [128, 4, 128], BF16, fs)
            dsk = sb("dsk", [128, 4], F32, fs)
            rho = sb("rho", [128, 16], F32, fs)
            car = [[sb("car%d%d" % (a, b), [128, 16], F32, fs) for b in range(2)] for a in range(2)]
            t_tab, t_bb, t_cc, t_dg, t_rho = [Tk(n) for n in ("tab", "bb", "cc", "dg", "rho")]
            t_car = [[Tk("car%d_%d" % (a, sc)) for sc in range(16)] for a in range(2)]
            P.dma("pool", CreT[:], din["s_creT"], writes=[t_cc])
            P.dma("pool", CimT[:], din["s_cimT"], writes=[t_cc])
            P.dma("sp", dsk[:], din["s_d"], writes=[t_dg])
            for oc in range(4):
                TS("dve", Dg[:, oc, :], id_f[:], dsk[:, oc:oc + 1], None, ALU.mult, None, [t_const, t_dg], [t_dg])
            nCreT = sb("nCreT", [128, 16, 128], BF16, fs)
            nCimT = sb("nCimT", [128, 16, 128], BF16, fs)
            TS("dve", nCreT[:], CreT[:], -1.0, None, ALU.mult, None, [t_cc], [t_cc])
            TS("dve", nCimT[:], CimT[:], -1.0, None, ALU.mult, None, [t_cc], [t_cc])
            with ExitStack() as ss:
                lr = sb("lr", [128, 16], F32, ss)
                li = sb("li", [128, 16], F32, ss)
                stp = sb("stp", [128, 16], F32, ss)
                th = sb("th", [128, 16], F32, ss)
                kk = sb("kk", [128, 16], F32, ss)
                r1 = sb("r1", [128, 16], F32, ss)
                r2 = sb("r2", [128, 16], F32, ss)
                s1 = sb("s1", [128, 16], F32, ss)
                c1 = sb("c1", [128, 16], F32, ss)
                w = [sb("w%d" % i, [128, 16], F32, ss) for i in range(6)]
                bre = sb("bre", [128, 16, 16], F32, ss)
                bim = sb("bim", [128, 16, 16], F32, ss)
                bbr = sb("bbr", [128, 16, 16], F32, ss)
                bbi = sb("bbi", [128, 16, 16], F32, ss)
                bt = [sb("bt%d" % i, [128, 16, 16], F32, ss) for i in range(2)]
                Zr = sb("Zr", [128, 16, 128], F32, ss)
                Zi = sb("Zi", [128, 16, 128], F32, ss)
                tA = sb("tA", [128, 16, 256], F32, ss)
                tB = sb("tB", [128, 16, 256], F32, ss)
                t_p = Tk("ssm_par")
                t_Z = Tk("Z")
                P.dma("sp", lr[:], din["s_lr"], writes=[t_p])
                P.dma("sp", li[:], din["s_li"], writes=[t_p])
                P.dma("sp", stp[:], din["s_ls"], writes=[t_p])
                P.dma("sp", bre[:], din["s_bre"], writes=[t_p])
                P.dma("sp", bim[:], din["s_bim"], writes=[t_p])
                pp = [t_p]
                ACT(stp[:], stp[:], AF.Exp, pp, pp)
                TT2("dve", th[:], li[:], stp[:], ALU.mult, pp, pp)
                TT2("dve", w[0][:], lr[:], stp[:], ALU.mult, pp, pp)
                ACT(rho[:], w[0][:], AF.Exp, pp, [t_rho])
                MEMSET("dve", kk[:], 0.0, pp)
                for m in range(1, 9):
                    STT(kk[:], th[:], (2 * m - 1) * PI, kk[:], ALU.is_gt, ALU.add, pp, pp)
                STT(r1[:], kk[:], -2.0 * PI, th[:], ALU.mult, ALU.add, pp, pp)
                TS("dve", r2[:], r1[:], PI / 2, None, ALU.add, None, pp, pp)
                TS("dve", w[1][:], r2[:], PI, None, ALU.is_gt, None, pp, pp)
                STT(r2[:], w[1][:], -2.0 * PI, r2[:], ALU.mult, ALU.add, pp, pp)
                ACT(s1[:], r1[:], AF.Sin, pp, pp)
                ACT(c1[:], r2[:], AF.Sin, pp, pp)
                ar, ai, arm1, den, cr, cim = w
                TT2("dve", ar[:], rho[:], c1[:], ALU.mult, pp + [t_rho], pp)
                TT2("dve", ai[:], rho[:], s1[:], ALU.mult, pp + [t_rho], pp)
                TS("dve", arm1[:], ar[:], -1.0, None, ALU.add, None, pp, pp)
                TT2("dve", den[:], lr[:], lr[:], ALU.mult, pp, pp)
                TT2("dve", kk[:], li[:], li[:], ALU.mult, pp, pp)
                TT2("dve", den[:], den[:], kk[:], ALU.add, pp, pp)
                RECIP(den[:], den[:], pp, pp)
                TT2("dve", cr[:], arm1[:], lr[:], ALU.mult, pp, pp)
                TT2("dve", kk[:], ai[:], li[:], ALU.mult, pp, pp)
                TT2("dve", cr[:], cr[:], kk[:], ALU.add, pp, pp)
                TT2("dve", cr[:], cr[:], den[:], ALU.mult, pp, pp)
                TT2("dve", cim[:], ai[:], lr[:], ALU.mult, pp, pp)
                TT2("dve", kk[:], arm1[:], li[:], ALU.mult, pp, pp)
                TT2("dve", cim[:], cim[:], kk[:], ALU.subtract, pp, pp)
                TT2("dve", cim[:], cim[:], den[:], ALU.mult, pp, pp)
                crb = cr[:].unsqueeze(2).to_broadcast([128, 16, 16])
                cib = cim[:].unsqueeze(2).to_broadcast([128, 16, 16])
                TT2("dve", bt[0][:], bre[:], crb, ALU.mult, pp, pp)
                TT2("dve", bt[1][:], bim[:], cib, ALU.mult, pp, pp)
                TT2("dve", bbr[:], bt[0][:], bt[1][:], ALU.subtract, pp, pp)
                TT2("dve", bt[0][:], bim[:], crb, ALU.mult, pp, pp)
                TT2("dve", bt[1][:], bre[:], cib, ALU.mult, pp, pp)
                TT2("dve", bbi[:], bt[0][:], bt[1][:], ALU.add, pp, pp)
                MEMSET("pool", Zr[:], 0.0, [t_Z])
                MEMSET("pool", Zi[:], 0.0, [t_Z])
                for sc in range(16):
                    c0 = 32 * (sc % 4)
                    for Z_, bb_ in ((Zr, bbr), (Zi, bbi)):
                        CP("dve", Z_[0:64, sc, c0:c0 + 16], bb_[0:64, sc, :], pp + [t_Z], [t_Z])
                        CP("dve", Z_[64:128, sc, c0 + 16:c0 + 32], bb_[64:128, sc, :], pp + [t_Z], [t_Z])
                for sc in range(16):
                    for bi_, (Z_, BT_) in enumerate(((Zr, BBrT), (Zi, BBiT))):
                        bk = (sc * 2 + bi_) % 4
                        TR(ps[bk][:, 0:128], Z_[:, sc, :], id_f[:], [t_Z, t_const], [t_ps[bk]])
                        CP("act", BT_[:, sc, :], ps[bk][:, 0:128], [t_ps[bk]], [t_bb])
                CP("dve", CT[:, :, 0:1], c1[:].unsqueeze(2), pp, [t_tab])
                CP("dve", ST[:, :, 0:1], s1[:].unsqueeze(2), pp, [t_tab])
                n = 1
                while n < 512:
                    cn = CT[:, :, n - 1:n].to_broadcast([128, 16, n])
                    sn = ST[:, :, n - 1:n].to_broadcast([128, 16, n])
                    tt_ = [t_tab]
                    TT2("dve", tA[:, :, 0:n], CT[:, :, 0:n], cn, ALU.mult, tt_, pp)
                    TT2("pool", tB[:, :, 0:n], ST[:, :, 0:n], sn, ALU.mult, tt_, [t_Z])
                    TT2("dve", CT[:, :, n:2 * n], tA[:, :, 0:n], tB[:, :, 0:n], ALU.subtract, pp + [t_Z], tt_)
                    TT2("dve", tA[:, :, 0:n], ST[:, :, 0:n], cn, ALU.mult, tt_, pp)
                    TT2("pool", tB[:, :, 0:n], CT[:, :, 0:n], sn, ALU.mult, tt_, [t_Z])
                    TT2("dve", ST[:, :, n:2 * n], tA[:, :, 0:n], tB[:, :, 0:n], ALU.add, pp + [t_Z], tt_)
                    n *= 2
                MEMSET("pool", rhoF[:], 1.0, [t_rho])
                for sc in range(16):
                    ACT(rhoF[:, sc, :], rhoF[:, sc, :], AF.Identity, [t_rho], [t_rho], scale=rho[:, sc:sc + 1])
                for a in range(2):
                    for b in range(2):
                        MEMSET("pool", car[a][b][:], 0.0, [t_car[a][sc] for sc in range(16)])
                P.barrier()

            NB = 2
            uTt = [sb("uTt%d" % i, [128, 4, 512], BF16, fs) for i in range(3)]
            ygst = sb("ygst", [128, 4, 512], BF16, fs)
            f = [[sb("sf%d_%d" % (i, b), [128, 512], F32 if i < 8 else BF16, fs) for b in range(NB)] for i in range(12)]
            gx = sb("gx", [128, 512], F32, fs)
            gu = sb("gu", [128, 512], F32, fs)
            cu = sb("cu", [128, 4], F32, fs)
            t_u = [Tk("uTt%d" % i) for i in range(3)]
            t_f = [[Tk("sf%d_%d" % (i, b)) for b in range(NB)] for i in range(12)]
            t_gx, t_gu, t_ygst, t_cu = Tk("gx"), Tk("gu"), Tk("ygst"), Tk("cu")
            uT_v = uT.rearrange("(c p) t -> p c t", p=128)
            yg_v = ygT.rearrange("(c p) t -> p c t", p=128)
            P.dma("sp", uTt[0][:], uT_v[:, :, 0:512], reads=[t_uT], writes=[t_u[0]])
            tiles = [(tt, oc, sci) for tt in range(8) for oc in range(4) for sci in range(4)]

            def S1(i):
                tt, oc, sci = tiles[i]
                sc, b, ub = 4 * oc + sci, i % NB, tt % 3
                if oc == 0 and sci == 0 and tt + 1 < 8:
                    nb_ = (tt + 1) % 3
                    P.dma("sp", uTt[nb_][:], uT_v[:, :, (tt + 1) * 512:(tt + 2) * 512], reads=[t_uT], writes=[t_u[nb_]])
                pb = 2 * (i % 2)
                pur, pui = ps[pb][:, :], ps[pb + 1][:, :]
                C_, S_ = CT[:, sc, :], ST[:, sc, :]
                MM(pur, BBrT[:, sc, :], uTt[ub][:, oc, :], True, True, [t_bb, t_u[ub]], [t_ps[pb]])
                MM(pui, BBiT[:, sc, :], uTt[ub][:, oc, :], True, True, [t_bb, t_u[ub]], [t_ps[pb + 1]])
                TT2("dve", f[0][b][:], pur, C_, ALU.mult, [t_ps[pb], t_tab], [t_f[0][b]])
                TT2("dve", f[1][b][:], pui, S_, ALU.mult, [t_ps[pb + 1], t_tab], [t_f[1][b]])
                TT2("dve", f[2][b][:], pui, C_, ALU.mult, [t_ps[pb + 1], t_tab], [t_f[2][b]])
                TT2("dve", f[3][b][:], pur, S_, ALU.mult, [t_ps[pb], t_tab], [t_f[3][b]])

            def S2(i):
                b = i % NB
                TT2("pool", f[4][b][:], f[0][b][:], f[1][b][:], ALU.add, [t_f[0][b], t_f[1][b]], [t_f[4][b]])
                TT2("pool", f[5][b][:], f[2][b][:], f[3][b][:], ALU.subtract, [t_f[2][b], t_f[3][b]], [t_f[5][b]])

            def S3(i):
                tt, oc, sci = tiles[i]
                sc, b = 4 * oc + sci, i % NB
                cin, cout = car[tt % 2], car[(tt + 1) % 2]
                tcin, tcout = t_car[tt % 2], t_car[(tt + 1) % 2]
                P.op("dve", lambda e: e.tensor_tensor_scan(
                    out=f[6][b][:], data0=rhoF[:, sc, :], data1=f[4][b][:], initial=cin[0][:, sc:sc + 1],
                    op0=ALU.mult, op1=ALU.add), [t_rho, t_f[4][b], tcin[sc]], [t_f[6][b]])
                P.op("dve", lambda e: e.tensor_tensor_scan(
                    out=f[7][b][:], data0=rhoF[:, sc, :], data1=f[5][b][:], initial=cin[1][:, sc:sc + 1],
                    op0=ALU.mult, op1=ALU.add), [t_rho, t_f[5][b], tcin[sc]], [t_f[7][b]])
                zrL, ziL = f[6][b][:, 511:512], f[7][b][:, 511:512]
                cL, sL = CT[:, sc, 511:512], ST[:, sc, 511:512]
                TT2("dve", cu[:, 0:1], ziL, sL, ALU.mult, [t_f[7][b], t_tab], [t_cu])
                STT(cout[0][:, sc:sc + 1], zrL, cL, cu[:, 0:1], ALU.mult, ALU.subtract,
                    [t_f[6][b], t_tab, t_cu], [tcout[sc]])
                TT2("dve", cu[:, 1:2], ziL, cL, ALU.mult, [t_f[7][b], t_tab], [t_cu])
                STT(cout[1][:, sc:sc + 1], zrL, sL, cu[:, 1:2], ALU.mult, ALU.add,
                    [t_f[6][b], t_tab, t_cu], [tcout[sc]])
                TT2("dve", f[11][b][:], f[7][b][:], CT[:, sc, :], ALU.mult, [t_f[7][b], t_tab], [t_f[11][b]])

            def S4(i):
                tt, oc, sci = tiles[i]
                sc, b = 4 * oc + sci, i % NB
                C_, S_ = CT[:, sc, :], ST[:, sc, :]
                TT2("pool", f[8][b][:], f[6][b][:], C_, ALU.mult, [t_f[6][b], t_tab], [t_f[8][b]])
                TT2("pool", f[9][b][:], f[7][b][:], S_, ALU.mult, [t_f[7][b], t_tab], [t_f[9][b]])
                TT2("pool", f[10][b][:], f[6][b][:], S_, ALU.mult, [t_f[6][b], t_tab], [t_f[10][b]])

            def S5(i):
                tt, oc, sci = tiles[i]
                sc, b, ub = 4 * oc + sci, i % NB, tt % 3
                pyb = 4 + oc % 2
                py = ps[pyb][:, :]
                MM(py, CreT[:, sc, :], f[8][b][:], sci == 0, False, [t_cc, t_f[8][b]], [t_ps[pyb]])
                MM(py, nCreT[:, sc, :], f[9][b][:], False, False, [t_cc, t_f[9][b]], [t_ps[pyb]])
                MM(py, nCimT[:, sc, :], f[10][b][:], False, False, [t_cc, t_f[10][b]], [t_ps[pyb]])
                MM(py, nCimT[:, sc, :], f[11][b][:], False, False, [t_cc, t_f[11][b]], [t_ps[pyb]])
                if sci == 3:
                    MM(py, Dg[:, oc, :], uTt[ub][:, oc, :], False, True, [t_dg, t_u[ub]], [t_ps[pyb]])
                    CP("act", gx[:], py, [t_ps[pyb]], [t_gx])
                    ACT(gu[:], py, AF.Square, [t_ps[pyb]], [t_gu])
                    TS("pool", gu[:], gu[:], 0.044715, 1.0, ALU.mult, ALU.add, [t_gu], [t_gu])
                    TT2("pool", gu[:], gu[:], gx[:], ALU.mult, [t_gu, t_gx], [t_gu])
                    ACT(gu[:], gu[:], AF.Sigmoid, [t_gu], [t_gu], scale=1.5957691216057308)
                    TT2("pool", ygst[:, oc, :], gu[:], gx[:], ALU.mult, [t_gu, t_gx], [t_ygst])
                    if oc == 3:
                        P.dma("sp", yg_v[:, :, tt * 512:(tt + 1) * 512], ygst[:], reads=[t_ygst], writes=[t_ygT], chan=t_ygst)

            n_t = len(tiles)
            for k in range(n_t + 2):
                if k < n_t:
                    S1(k)
                    S2(k)
                if 0 <= k - 1 < n_t:
                    S3(k - 1)
                    S4(k - 1)
                if 0 <= k - 2 < n_t:
                    S5(k - 2)
            P.barrier()

    def merge_phase():
        TT = 256
        NT = S // TT
        with ExitStack() as fs:
            wF = sb("wF", [128, 8, 2048], BF16, fs)
            wnsa = sb("wnsa", [128, 4, 1024], BF16, fs)
            wglu = sb("wglu", [128, 4, 2048], BF16, fs)
            wout = sb("wout", [128, 8, 1024], BF16, fs)
            mg = sb("Fmg", [128, 8], F32, fs)
            xt = [sb("Fx%d" % i, [128, 8, TT], F32, fs) for i in range(2)]
            xn = sb("Fxn", [128, 8, TT], BF16, fs)
            sq = [sb("Fsq%d" % i, [128, TT], BF16, fs) for i in range(2)]
            rstd = sb("Frstd", [128, TT], F32, fs)
            ont = [sb("Fon%d" % i, [128, 4, TT], BF16, fs) for i in range(2)]
            ygt = [sb("Fyg%d" % i, [128, 4, TT], BF16, fs) for i in range(2)]
            sg3 = [[sb("Fsg%d_%d" % (i, b), [128, TT], F32, fs) for b in range(2)] for i in range(3)]
            ta = [sb("Fta%d" % b, [128, TT], F32, fs) for b in range(2)]
            tb = [sb("Ftb%d" % b, [128, TT], F32, fs) for b in range(2)]
            mrg = sb("Fmrg", [128, 8, TT], BF16, fs)
            sgA = sb("FsgA", [128, 16, TT], F32, fs)
            t_sgA = [Tk("FsgA%d" % j) for j in range(16)]
            ot = sb("Fot", [128, 8, TT], F32, fs)
            t_w, t_mg, t_xn, t_rstd, t_ot = [Tk(n) for n in ("Fw", "Fmg", "Fxn", "Frstd", "Fot")]
            t_xt = [Tk("Fxt0"), Tk("Fxt1")]
            t_sq = [Tk("Fsq0"), Tk("Fsq1")]
            t_on = [Tk("Fon0"), Tk("Fon1")]
            t_yg = [Tk("Fyg0"), Tk("Fyg1")]
            t_sg3 = [[Tk("Fsg%d_%d" % (i, b)) for b in range(2)] for i in range(3)]
            t_ta = [Tk("Fta0"), Tk("Fta1")]
            t_tb = [Tk("Ftb0"), Tk("Ftb1")]
            t_mrg = [Tk("Fmrg%d" % i) for i in range(8)]
            t_h = [[Tk("Fps%d_%d" % (i, b)) for b in range(2)] for i in range(6)]
            P.dma("sp", mg[:], din["mix_g"], writes=[t_mg])
            wF_v = din["w_inF"].rearrange("(k p) f -> p k f", p=128)
            for k in range(8):
                P.dma("pool", wF[:, k, :], wF_v[:, k, :], writes=[t_w])
            wn_v = din["w_nsa"].rearrange("(k p) f -> p k f", p=128)
            wgl_v = din["w_glu"].rearrange("(k p) f -> p k f", p=128)
            wo_v = din["w_out"].rearrange("(k p) f -> p k f", p=128)
            for k in range(4):
                P.dma("pool", wnsa[:, k, :], wn_v[:, k, :], writes=[t_w])
                P.dma("pool", wglu[:, k, :], wgl_v[:, k, :], writes=[t_w])
            for k in range(8):
                P.dma("pool", wout[:, k, :], wo_v[:, k, :], writes=[t_w])
            h1_v = h1T.rearrange("(c p) t -> p c t", p=128)
            h2_v = h2T.rearrange("(c p) t -> p c t", p=128)
            on_v = onsaT.rearrange("(c p) t -> p c t", p=128)
            yg_v = ygT.rearrange("(c p) t -> p c t", p=128)

            def load(i):
                b = i % 2
                tsl = slice(i * TT, (i + 1) * TT)
                P.dma("sp", xt[b][:], h1_v[:, :, tsl], reads=[t_h1T], writes=[t_xt[b]])
                P.dma("sp", ont[b][:], on_v[:, :, tsl], reads=[t_onsaT], writes=[t_on[b]])
                P.dma("sp", ygt[b][:], yg_v[:, :, tsl], reads=[t_ygT], writes=[t_yg[b]])

            sq8 = sb("Fsq8", [128, 8, TT], BF16, fs)
            t_sq8 = [Tk("Fsq8_%d" % c) for c in range(8)]

            def fnorm1(i):
                bb = i % 2
                for c in range(8):
                    TT2("pool", sq8[:, c, :], xt[bb][:, c, :], xt[bb][:, c, :], ALU.mult, [t_xt[bb]], [t_sq8[c]])

            def fnorm2(i):
                bb = i % 2
                pstat = ps[6][:, 0:TT]
                for c in range(8):
                    MM(pstat, ones_bf[:], sq8[:, c, :], c == 0, c == 7, [t_const, t_sq8[c]], [t_ps[6]])
                ACT(rstd[:], pstat, AF.Sqrt, [t_ps[6], t_const], [t_rstd], bias=epsc[:, 0:1], scale=1.0 / D)
                RECIP(rstd[:], rstd[:], [t_rstd], [t_rstd])
                for c in range(8):
                    STT(xn[:, c, :], xt[bb][:, c, :], mg[:, c:c + 1], rstd[:], ALU.mult, ALU.mult,
                        [t_xt[bb], t_mg, t_rstd], [t_xn])

            load(0)
            fnorm1(0)
            fnorm2(0)
            for i in range(NT):
                b = i % 2
                if i + 1 < NT:
                    load(i + 1)
                xb = xt[b]
                if i + 1 < NT:
                    fnorm1(i + 1)
                for j in range(16):
                    bk = j % 4
                    pg_ = ps[bk][:, 0:TT]
                    cs_ = slice(j * 128, (j + 1) * 128)
                    for k in range(8):
                        MM(pg_, wF[:, k, cs_], xn[:, k, :], k == 0, k == 7, [t_w, t_xn], [t_ps[bk]])
                    ACT(sgA[:, j, :], pg_, AF.Sigmoid, [t_ps[bk]], [t_sgA[j]])
                if i + 1 < NT:
                    fnorm2(i + 1)
                for dc in range(8):
                    hb = dc % 2
                    bks = (0, 1, 2) if hb == 0 else (3, 4, 5)
                    pyn, pval, pgt = [ps[j][:, 0:TT] for j in bks]
                    tp = [t_ps[j] for j in bks]
                    dsl = slice(dc * 128, (dc + 1) * 128)
                    dsl2 = slice(1024 + dc * 128, 1024 + (dc + 1) * 128)
                    for k in range(4):
                        MM(pyn, wnsa[:, k, dsl], ont[b][:, k, :], k == 0, k == 3, [t_w, t_on[b]], [tp[0]])
                    for k in range(4):
                        MM(pval, wglu[:, k, dsl], ygt[b][:, k, :], k == 0, k == 3, [t_w, t_yg[b]], [tp[1]])
                    for k in range(4):
                        MM(pgt, wglu[:, k, dsl2], ygt[b][:, k, :], k == 0, k == 3, [t_w, t_yg[b]], [tp[2]])
                    ACT(sg3[2][hb][:], pgt, AF.Sigmoid, [tp[2]], [t_sg3[2][hb]])
                    TT2("dve", ta[hb][:], pyn, sgA[:, dc, :], ALU.mult, [tp[0], t_sgA[dc]], [t_ta[hb]])
                    TT2("dve", tb[hb][:], pval, sg3[2][hb][:], ALU.mult, [tp[1], t_sg3[2][hb]], [t_tb[hb]])
                    TT2("pool", tb[hb][:], tb[hb][:], sgA[:, 8 + dc, :], ALU.mult, [t_tb[hb], t_sgA[8 + dc]], [t_tb[hb]])
                    TT2("pool", mrg[:, dc, :], ta[hb][:], tb[hb][:], ALU.add, [t_ta[hb], t_tb[hb]], [t_mrg[dc]])
                for dc in range(8):
                    bk = 7 if dc % 2 == 0 else 6
                    po = ps[bk][:, 0:TT]
                    for k in range(8):
                        MM(po, wout[:, k, dc * 128:(dc + 1) * 128], mrg[:, k, :], k == 0, k == 7,
                           [t_w, t_mrg[k]], [t_ps[bk]])
                    TT2("dve", ot[:, dc, :], po, xb[:, dc, :], ALU.add, [t_ps[bk], t_xt[b]], [t_ot])
                if "F3b" not in DBG:
                    P.dma("sp", h2_v[:, :, i * TT:(i + 1) * TT], ot[:], reads=[t_ot], writes=[t_h2T], chan=t_ot)
            P.barrier()

    if stage == "ffn1":
        ffn_phase("f1", xT, t_xT, outT, t_outT, din["f1_g"], din["f1_wg"], din["f1_wu"], din["f1_wd"])
    else:
        ffn_phase("f1", xT, t_xT, h1T, t_h1T, din["f1_g"], din["f1_wg"], din["f1_wu"], din["f1_wd"])
        mixer_attention()
        if stage not in ("B", "BC", "D"):
            ssm_phase()
        if stage not in ("B", "BC", "D", "E"):
            merge_phase()
        if stage == "full":
            ffn_phase("f2", h2T, t_h2T, outT, t_outT, din["f2_g"], din["f2_wg"], din["f2_wu"], din["f2_wd"])

    P.final_wait([t_outT])
    P.barrier()
    es.close()
    return nc, P


def _consts():
    f = np.float32
    c = {}
    c["c_id"] = np.eye(128, dtype=f)
    bd = np.zeros((128, 128), f)
    bd[:64, :64] = 1.0
    bd[64:, 64:] = 1.0
    c["c_bd"] = bd
    k = np.arange(S)
    c["c_E"] = (k[None, :] // 64 == np.arange(64)[:, None]).astype(f)
    ci = np.arange(256)
    j = np.arange(64)
    ovl = ((ci[:, None] * 16 < (j[None, :] + 1) * 64) & (ci[:, None] * 16 + 32 > j[None, :] * 64)).astype(f)
    ovl[255] = 0.0
    c["c_ovl"] = np.ascontiguousarray(ovl.reshape(2, 128, 64).transpose(1, 0, 2))
    t = np.arange(S)
    qblk = t // 64
    force = (j[None, :] == 0) | (j[None, :] == qblk[:, None]) | (j[None, :] == qblk[:, None] - 1)
    selB = np.where(j[None, :] <= qblk[:, None], 1000.0 * force.astype(f), -1e30).astype(f)
    c["c_selB"] = np.ascontiguousarray(selB.reshape(32, 128, 64).transpose(1, 0, 2))
    kl = np.arange(128)[:, None]
    tl = np.arange(512)[None, :]
    mwin = np.zeros((128, 8, 512), f)
    for r in range(8):
        dlt = 512 + tl - 128 * r - kl
        mwin[:, r, :] = ((dlt >= 0) & (dlt < 512)).astype(f)
    c["c_mwin"] = mwin
    mslc = np.zeros((128, 4, 512), f)
    for r in range(4):
        mslc[:, r, :] = ((tl - 128 * r - kl) >= 0).astype(f)
    c["c_mslc"] = mslc
    mcmp = np.zeros((128, 5, 512), f)
    for v in range(5):
        mcmp[:, v, :] = ((512 * v + tl - 16 * kl - 31) >= 0).astype(f)
    c["c_mcmp"] = mcmp
    return c


def _prep_shared(inp):
    f = np.float32
    A = lambda a: np.ascontiguousarray(a, dtype=f)
    sh = dict(_consts())
    for tag, nm in (("f1", "ffn1"), ("f2", "ffn2")):
        sh[tag + "_g"] = A(inp[nm + "_norm"][0].reshape(8, 128).T)
        sh[tag + "_wg"] = A(inp[nm + "_w_gate"][0])
        sh[tag + "_wu"] = A(inp[nm + "_w_up"][0])
        sh[tag + "_wd"] = A(inp[nm + "_w_down"][0])
    sh["mix_g"] = A(inp["mix_norm"][0].reshape(8, 128).T)
    w = inp["w_in"][0]
    q = w[:, 0:512]
    blocks = []
    for r in range(4):
        blocks.append(q[:, r * 64:(r + 1) * 64])
        blocks.append(q[:, (4 + r) * 64:(5 + r) * 64])
    blocks += [w[:, 768:896], w[:, 1024:1152], w[:, 512:640], w[:, 640:768], w[:, 1304:1816],
               w[:, 896:1024], w[:, 1152:1280], w[:, 1280:1304]]
    sh["w_inB"] = A(np.concatenate(blocks, axis=1))
    assert sh["w_inB"].shape == (1024, 1816)
    sh["w_inF"] = A(w[:, 1816:3864])
    rep = lambda v: np.concatenate([v, v])
    sh["gains"] = A(np.stack([rep(inp["q_norm"][0]), rep(inp["k_norm_slc"][0]),
                              rep(inp["k_norm_win"][0]), rep(inp["k_norm_cmp"][0])], axis=1))
    sh["posk"] = A(np.concatenate([inp["cmp_pos_k"][0].T] * 2, axis=0))
    sh["posv"] = A(np.concatenate([inp["cmp_pos_v"][0].T] * 2, axis=0))
    w1 = lambda a: A(np.concatenate([a.reshape(32, 64, 256).transpose(1, 0, 2)] * 2, axis=0))
    sh["w1k"] = w1(inp["cmp_k_w1"][0])
    sh["w1v"] = w1(inp["cmp_v_w1"][0])
    w2k = inp["cmp_k_w2"][0]
    sh["w2k"] = A(np.concatenate([w2k, w2k], axis=1).reshape(2, 128, 128).transpose(1, 0, 2))
    sh["w2v"] = A(inp["cmp_v_w2"][0].reshape(2, 128, 64).transpose(1, 0, 2))
    sh["w_nsa"] = A(inp["w_nsa_proj"][0])
    sh["w_glu"] = A(inp["ssm_glu_w"][0])
    sh["w_out"] = A(inp["w_out"][0])
    sm = lambda a: A(a.reshape(16, 128).T)
    sh["s_lr"] = sm(inp["ssm_lambda_re"][0])
    sh["s_li"] = sm(inp["ssm_lambda_im"][0])
    sh["s_ls"] = sm(np.repeat(inp["ssm_log_step"][0][:, None], 64, axis=1))
    sb_ = lambda a: A(a.reshape(16, 128, 16).transpose(1, 0, 2))
    sh["s_bre"] = sb_(inp["ssm_b_re"][0])
    sh["s_bim"] = sb_(inp["ssm_b_im"][0])

    def cplace(cc):
        o = np.zeros((128, 16, 128), f)
        for g in range(32):
            sc, gl = g // 2, g % 2
            c0 = 32 * (sc % 4) + 16 * gl
            o[gl * 64:(gl + 1) * 64, sc, c0:c0 + 16] = cc[g].T
        return o
    sh["s_creT"] = cplace(inp["ssm_c_re"][0])
    sh["s_cimT"] = cplace(inp["ssm_c_im"][0])
    sh["s_d"] = A(inp["ssm_d"][0].reshape(4, 128).T)
    return sh


IN_SHAPES = {
    "c_id": [128, 128], "c_bd": [128, 128], "c_E": [64, S], "c_ovl": [128, 2, 64], "c_selB": [128, 32, 64],
    "c_mwin": [128, 8, 512], "c_mslc": [128, 4, 512], "c_mcmp": [128, 5, 512],
    "f1_g": [128, 8], "f1_wg": [D, DFF], "f1_wu": [D, DFF], "f1_wd": [DFF, D],
    "f2_g": [128, 8], "f2_wg": [D, DFF], "f2_wu": [D, DFF], "f2_wd": [DFF, D],
    "mix_g": [128, 8], "w_inB": [D, 1816], "w_inF": [D, 2048], "gains": [128, 4],
    "posk": [128, 32], "posv": [128, 32], "w1k": [128, 32, 256], "w1v": [128, 32, 256],
    "w2k": [128, 2, 128], "w2v": [128, 2, 64], "w_nsa": [512, D], "w_glu": [512, 2 * D], "w_out": [D, D],
    "s_lr": [128, 16], "s_li": [128, 16], "s_ls": [128, 16], "s_bre": [128, 16, 16], "s_bim": [128, 16, 16],
    "s_creT": [128, 16, 128], "s_cimT": [128, 16, 128], "s_d": [128, 4],
}

STAGE = "full"


def kernel(**inputs):
    inp = {k: np.asarray(v) for k, v in inputs.items()}
    x = inp["x"]
    nc, P = build(STAGE)
    sh = _prep_shared(inp)
    in_maps = []
    for b in range(NCORES):
        m = dict(sh)
        m["xT"] = np.ascontiguousarray(x[b].T)
        in_maps.append(m)
    res = run_bass_kernel_spmd(nc, in_maps, core_ids=list(range(NCORES)))
    out = np.stack([np.ascontiguousarray(r["outT"].T) for r in res.results], axis=0)
    return out.astype(np.float32)
```

```python
import math
import os
from contextlib import ExitStack

import numpy as np
import concourse.bass as bass
import concourse.mybir as mybir
from concourse.bass_utils import run_bass_kernel_spmd

F32 = mybir.dt.float32
BF16 = mybir.dt.bfloat16
AF = mybir.ActivationFunctionType
ALU = mybir.AluOpType

S = 4096
D = 1024
DFF = 2816
NCORES = 8
EPS = 1e-6
DBG = set(os.environ.get("DBGSKIP", "").split(","))


class Tk:
    __slots__ = ("name", "w", "rs", "dsem", "dcnt")

    def __init__(self, name):
        self.name = name
        self.w = []
        self.rs = []
        self.dsem = None
        self.dcnt = 0


class Prog:
    ENG = ("pe", "act", "dve", "pool", "sp")

    def __init__(self, nc, es):
        self.nc = nc
        self.es = es
        self.e = {"pe": nc.tensor, "act": nc.scalar, "dve": nc.vector,
                  "pool": nc.gpsimd, "sp": nc.sync}
        self.nsem = 0
        self.sem = {}
        self.cnt = {}
        for k in self.ENG:
            self.sem[k] = self.new_sem("e_" + k)
            self.cnt[k] = 0
        self.seen = {k: {} for k in self.ENG}
        self.dma_sems = []
        self.n_inst = 0

    def new_sem(self, name):
        self.nsem += 1
        return self.es.enter_context(self.nc.semaphore(name + "_%d" % self.nsem))

    def _wait(self, eng, sem, val):
        d = self.seen[eng]
        key = id(sem)
        if d.get(key, (None, 0))[1] >= val:
            return
        d[key] = (sem, val)
        self.e[eng].wait_ge(sem, val)

    def _deps(self, eng, reads, writes, same_engine_sync=True):
        need = {}

        def add(ev):
            sem, val, src = ev
            if src == eng and not same_engine_sync:
                return
            if isinstance(src, tuple):
                val = src[0].dsem[src[1]][1]
            k = id(sem)
            if k not in need or need[k][1] < val:
                need[k] = (sem, val)

        for t in reads:
            for ev in t.w:
                add(ev)
        for t in writes:
            for ev in t.w:
                add(ev)
            for ev in t.rs:
                add(ev)
        for sem, val in need.values():
            self._wait(eng, sem, val)

    def _roll(self, eng):
        if self.cnt[eng] >= 8000:
            self.sem[eng] = self.new_sem("e_" + eng)
            self.cnt[eng] = 0

    def op(self, eng, fn, reads=(), writes=(), sync_same=True):
        self._roll(eng)
        self._deps(eng, reads, writes, same_engine_sync=sync_same)
        ins = fn(self.e[eng])
        self.cnt[eng] += 1
        ev = (self.sem[eng], self.cnt[eng], eng)
        ins.then_inc(self.sem[eng], 1)
        self.n_inst += 1
        for t in reads:
            t.rs.append(ev)
            if len(t.rs) > 24:
                t.rs = self._compact(t.rs)
        for t in writes:
            t.w = [ev]
            t.rs = []
        return ins

    @staticmethod
    def _compact(evs):
        best = {}
        for sem, val, src in evs:
            k = id(sem)
            if k not in best or best[k][1] < val:
                best[k] = (sem, val, src)
        return list(best.values())

    def dma(self, q, out_ap, in_ap, reads=(), writes=(), chan=None):
        if chan is None:
            chan = writes[0] if writes else reads[0]
        if chan.dsem is None:
            chan.dsem = {}
        if q not in chan.dsem:
            chan.dsem[q] = [self.new_sem("d_" + chan.name + "_" + q), 0]
            self.dma_sems.append((chan, q))
        ent = chan.dsem[q]
        self._deps(q, reads, writes)
        ins = self.e[q].dma_start(out=out_ap, in_=in_ap)
        ent[1] += 16
        ins.then_inc(ent[0], 16)
        ev = (ent[0], ent[1], (chan, q))
        self.n_inst += 1
        for t in reads:
            t.rs.append(ev)
            if len(t.rs) > 24:
                t.rs = self._compact(t.rs)
        for t in writes:
            t.w = [e for e in t.w if isinstance(e[2], tuple) and e[0] is not ent[0]] + [ev]
            t.rs = []
        return ins

    def barrier(self):
        for x in self.ENG:
            for y in self.ENG:
                if y != x and self.cnt[y] > 0:
                    self._wait(x, self.sem[y], self.cnt[y])
            for t, q in self.dma_sems:
                ent = t.dsem[q]
                if ent[1] > 0:
                    self._wait(x, ent[0], ent[1])

    def final_wait(self, tks):
        for t in tks:
            for sem, val, _ in t.w:
                self._wait("sp", sem, val)


def build(stage="full", debug=False):
    nc = bass.Bass("TRN2", target_bir_lowering=False)
    es = ExitStack()
    P = Prog(nc, es)

    def dram_in(name, shape, dt=F32):
        return nc.dram_tensor(name, list(shape), dt, kind="ExternalInput").ap()

    def dram_out(name, shape, dt=F32):
        return nc.dram_tensor(name, list(shape), dt, kind="ExternalOutput").ap()

    def dram_tmp(name, shape, dt=F32):
        kind = "ExternalOutput" if debug else "Internal"
        return nc.dram_tensor(name, list(shape), dt, kind=kind).ap()

    def sb(name, shape, dt, stack=None):
        return (stack or es).enter_context(nc.sbuf_tensor("sb_" + name, list(shape), dt))

    xT = dram_in("xT", [D, S])
    outT = dram_out("outT", [D, S])
    din = {}
    for nm, shp in IN_SHAPES.items():
        din[nm] = dram_in(nm, shp)
    h1T = dram_tmp("h1T", [D, S])
    h2T = dram_tmp("h2T", [D, S])
    uT = dram_tmp("uT", [512, S], BF16)
    onsaT = dram_tmp("onsaT", [512, S], BF16)
    ygT = dram_tmp("ygT", [512, S], BF16)
    t_xT, t_outT, t_h1T, t_h2T, t_uT, t_onsaT, t_ygT = [Tk(n) for n in
        ("xT", "outT", "h1T", "h2T", "uT", "onsaT", "ygT")]

    ps = [es.enter_context(nc.psum_tensor("ps%d" % i, [128, 512], F32)) for i in range(8)]
    t_ps = [Tk("ps%d" % i) for i in range(8)]

    def MM(out, lhsT, rhs, st, sp, reads, writes):
        P.op("pe", lambda e: e.matmul(out, lhsT=lhsT, rhs=rhs, start=st, stop=sp),
             reads, writes, sync_same=False)

    def TR(out, in_, ident, reads, writes):
        P.op("pe", lambda e: e.transpose(out, in_, ident), reads, writes, sync_same=False)

    def TT2(eng, out, a, b, op, reads, writes):
        P.op(eng, lambda e: e.tensor_tensor(out=out, in0=a, in1=b, op=op), reads, writes)

    def TS(eng, out, a, s1, s2, op0, op1, reads, writes):
        if s2 is None:
            P.op(eng, lambda e: e.tensor_scalar(out=out, in0=a, scalar1=s1, scalar2=None, op0=op0),
                 reads, writes)
        else:
            P.op(eng, lambda e: e.tensor_scalar(out=out, in0=a, scalar1=s1, scalar2=s2, op0=op0, op1=op1),
                 reads, writes)

    def STT(out, a, sc, b, op0, op1, reads, writes):
        P.op("dve", lambda e: e.scalar_tensor_tensor(out=out, in0=a, scalar=sc, in1=b, op0=op0, op1=op1),
             reads, writes)

    def ACT(out, in_, func, reads, writes, bias=None, scale=None):
        kw = {}
        if bias is not None:
            kw["bias"] = bias
        if scale is not None:
            kw["scale"] = scale
        P.op("act", lambda e: e.activation(out=out, in_=in_, func=func, **kw), reads, writes)

    def CP(eng, out, in_, reads, writes):
        if eng == "act":
            P.op("act", lambda e: e.activation(out=out, in_=in_, func=AF.Identity), reads, writes)
        else:
            P.op(eng, lambda e: e.tensor_copy(out=out, in_=in_), reads, writes)

    def RECIP(out, in_, reads, writes):
        P.op("dve", lambda e: e.reciprocal(out=out, in_=in_), reads, writes)

    def MEMSET(eng, ap, val, writes):
        P.op(eng, lambda e: e.memset(ap, val), (), writes)

    ones_bf = sb("ones_bf", [128, 128], BF16)
    bd_bf = sb("bd_bf", [128, 128], BF16)
    id_bf = sb("id_bf", [128, 128], BF16)
    id_f = sb("id_f", [128, 128], F32)
    epsc = sb("epsc", [128, 1], F32)
    t_const = Tk("const")
    MEMSET("pool", ones_bf[:], 1.0, [t_const])
    MEMSET("pool", epsc[:], EPS, [t_const])
    P.dma("pool", bd_bf[:], din["c_bd"], writes=[t_const])
    P.dma("pool", id_bf[:], din["c_id"], writes=[t_const])
    P.dma("sp", id_f[:], din["c_id"], writes=[t_const])

    def norm_tile(xb, t_xb, gsb, t_g, xn, t_xn, sq, t_sq, rstd, t_rstd, TT, bank):
        pstat = ps[bank][:, 0:TT]
        for c in range(8):
            TT2("pool", sq[c % 2][:, 0:TT], xb[:, c, :], xb[:, c, :], ALU.mult, [t_xb], [t_sq[c % 2]])
            MM(pstat, ones_bf[:], sq[c % 2][:, 0:TT], c == 0, c == 7, [t_const, t_sq[c % 2]], [t_ps[bank]])
        ACT(rstd[:, 0:TT], pstat, AF.Ln, [t_ps[bank], t_const], [t_rstd], bias=epsc[:, 0:1], scale=1.0 / D)
        ACT(rstd[:, 0:TT], rstd[:, 0:TT], AF.Exp, [t_rstd], [t_rstd], scale=-0.5)
        for c in range(8):
            STT(xn[:, c, :], xb[:, c, :], gsb[:, c:c + 1], rstd[:, 0:TT], ALU.mult, ALU.mult,
                [t_xb, t_g, t_rstd], [t_xn])

    def ffn_phase(tag, src, t_src, dst, t_dst, g_dram, wg_d, wu_d, wd_d):
        TT = 256
        NT = S // TT
        with ExitStack() as fs:
            wg = sb(tag + "wg", [128, 8, DFF], BF16, fs)
            wu = sb(tag + "wu", [128, 8, DFF], BF16, fs)
            wd = sb(tag + "wd", [128, 22, D], BF16, fs)
            gsb = sb(tag + "g", [128, 8], F32, fs)
            xt = [sb(tag + "x%d" % i, [128, 8, TT], F32, fs) for i in range(2)]
            xn = [sb(tag + "xn%d" % i, [128, 8, TT], BF16, fs) for i in range(2)]
            sq8 = sb(tag + "sq8", [128, 8, TT], BF16, fs)
            rstd = sb(tag + "rstd", [128, TT], F32, fs)
            hh = sb(tag + "h", [128, 22, TT], BF16, fs)
            sg = [sb(tag + "sg%d" % i, [128, TT], F32, fs) for i in range(2)]
            ot = sb(tag + "ot", [128, 8, TT], F32, fs)
            t_wg = [Tk(tag + "wg%d" % k) for k in range(8)]
            t_wu = [Tk(tag + "wu%d" % k) for k in range(8)]
            t_wd = [Tk(tag + "wd%d" % k) for k in range(22)]
            t_g = Tk(tag + "g")
            t_xt = [Tk(tag + "xt%d" % i) for i in range(2)]
            t_xn = [Tk(tag + "xn%d" % i) for i in range(2)]
            t_sq8 = [Tk(tag + "sq8_%d" % i) for i in range(8)]
            t_rstd = Tk(tag + "rstd")
            t_hh = [Tk(tag + "hh%d" % i) for i in range(22)]
            t_sg = [Tk(tag + "sg%d" % i) for i in range(2)]
            t_ot = Tk(tag + "ot")
            src_v = src.rearrange("(c p) t -> p c t", p=128)
            dst_v = dst.rearrange("(c p) t -> p c t", p=128)

            def load_x(i):
                b = i % 2
                P.dma("sp", xt[b][:], src_v[:, :, i * TT:(i + 1) * TT], reads=[t_src], writes=[t_xt[b]])

            def norm1(i):
                b = i % 2
                for c in range(8):
                    TT2("pool", sq8[:, c, :], xt[b][:, c, :], xt[b][:, c, :], ALU.mult, [t_xt[b]], [t_sq8[c]])

            def norm2(i):
                b = i % 2
                pstat = ps[6][:, 0:TT]
                for c in range(8):
                    MM(pstat, ones_bf[:], sq8[:, c, :], c == 0, c == 7, [t_const, t_sq8[c]], [t_ps[6]])
                ACT(rstd[:], pstat, AF.Sqrt, [t_ps[6], t_const], [t_rstd], bias=epsc[:, 0:1], scale=1.0 / D)
                RECIP(rstd[:], rstd[:], [t_rstd], [t_rstd])
                for c in range(8):
                    STT(xn[b][:, c, :], xt[b][:, c, :], gsb[:, c:c + 1], rstd[:], ALU.mult, ALU.mult,
                        [t_xt[b], t_g, t_rstd], [t_xn[b]])

            P.dma("sp", gsb[:], g_dram, writes=[t_g])
            load_x(0)
            wg_v = wg_d.rearrange("(k p) f -> p k f", p=128)
            wu_v = wu_d.rearrange("(k p) f -> p k f", p=128)
            wd_v = wd_d.rearrange("(k p) d -> p k d", p=128)
            ch_a, ch_b = Tk(tag + "chA"), Tk(tag + "chB")
            for k in range(8):
                P.dma("pool", wg[:, k, :], wg_v[:, k, :], writes=[t_wg[k]], chan=ch_a)
                P.dma("pool", wu[:, k, :], wu_v[:, k, :], writes=[t_wu[k]], chan=ch_a)
            for k in range(22):
                P.dma("pool", wd[:, k, :], wd_v[:, k, :], writes=[t_wd[k]], chan=ch_b)
            norm1(0)
            norm2(0)
            for i in range(NT):
                b = i % 2
                xb = xt[b]
                if i + 1 < NT:
                    load_x(i + 1)
                    norm1(i + 1)
                for fc in range(22):
                    pb = fc % 2
                    pg = ps[0 + pb][:, 0:TT]
                    pu = ps[2 + pb][:, 0:TT]
                    for k in range(8):
                        MM(pg, wg[:, k, fc * 128:(fc + 1) * 128], xn[b][:, k, :], k == 0, k == 7,
                           [t_wg[k], t_xn[b]], [t_ps[0 + pb]])
                    for k in range(8):
                        MM(pu, wu[:, k, fc * 128:(fc + 1) * 128], xn[b][:, k, :], k == 0, k == 7,
                           [t_wu[k], t_xn[b]], [t_ps[2 + pb]])
                    ACT(sg[pb][:], pg, AF.Silu, [t_ps[0 + pb]], [t_sg[pb]])
                    TT2("dve", hh[:, fc, :], pu, sg[pb][:], ALU.mult, [t_ps[2 + pb], t_sg[pb]], [t_hh[fc]])
                if i + 1 < NT:
                    norm2(i + 1)
                for dc in range(8):
                    pb = dc % 2
                    py = ps[4 + pb][:, 0:TT]
                    for fc in range(22):
                        MM(py, wd[:, fc, dc * 128:(dc + 1) * 128], hh[:, fc, :], fc == 0, fc == 21,
                           [t_wd[fc], t_hh[fc]], [t_ps[4 + pb]])
                    STT(ot[:, dc, :], py, 0.5, xb[:, dc, :], ALU.mult, ALU.add,
                        [t_ps[4 + pb], t_xt[b]], [t_ot])
                P.dma("sp", dst_v[:, :, i * TT:(i + 1) * TT], ot[:], reads=[t_ot], writes=[t_dst], chan=t_ot)
            P.barrier()

    def mixer_attention():
        with ExitStack() as ms:
            Qs = [sb("Qs%d" % h, [128, S], BF16, ms) for h in range(8)]
            Ks = [sb("Ks%d" % g, [128, S], BF16, ms) for g in range(2)]
            Kw = sb("Kw", [128, S], BF16, ms)
            Vs = sb("Vs", [128, 32, 2, 72], BF16, ms)
            Vw = sb("Vw", [128, 32, 2, 72], BF16, ms)
            gl = sb("gl", [128, 32, 24], F32, ms)
            gains = sb("gains", [128, 4], F32, ms)
            gq8 = sb("gq8", [128, 1], F32, ms)
            kcT = sb("kcT", [128, 256], BF16, ms)
            rcmp = [sb("rcmp%d" % g, [128, 2, 129], BF16, ms) for g in range(2)]
            t_Q = Tk("Qq")
            t_Qm = [Tk("Qm%d" % i) for i in range(8)]
            t_Ks, t_Kw, t_Vs, t_Vw, t_gl, t_gains, t_kcT, t_rcmp = [Tk(n) for n in
                ("Ks", "Kw", "Vs", "Vw", "gl", "gains", "kcT", "rcmp")]
            P.dma("sp", gains[:], din["gains"], writes=[t_gains])
            TS("dve", gq8[:], gains[:, 0:1], 0.125, None, ALU.mult, None, [t_gains], [t_gains])
            if "E" not in DBG:
                P.dma("pool", Ks[0][64:128, :], din["c_E"], writes=[t_Ks])
                P.dma("pool", Ks[1][0:64, :], din["c_E"], writes=[t_Ks])
            if "V1" not in DBG:
                MEMSET("pool", Vs[:, :, :, 64:65], 1.0, [t_Vs])
                MEMSET("pool", Vw[:, :, :, 64:65], 1.0, [t_Vw])

            with ExitStack() as bs:
                TT = 512
                sq = [sb("Bsq%d" % i, [128, TT], BF16, bs) for i in range(2)]
                rs = sb("Brs", [128, TT], F32, bs)
                KcA = sb("KcA", [128, S], BF16, bs)
                KcB = sb("KcB", [128, S], BF16, bs)
                VcA = sb("VcA", [128, S], BF16, bs)
                VcB = sb("VcB", [128, S], BF16, bs)
                b2 = ExitStack()
                wB = sb("wB", [128, 8, 1816], BF16, b2)
                mg = sb("mg", [128, 8], F32, b2)
                xt = sb("Bx", [128, 8, TT], F32, b2)
                xn = sb("Bxn", [128, 8, TT], BF16, b2)
                rstd = sb("Brstd", [128, TT], F32, b2)
                ust = sb("Bust", [128, 4, TT], BF16, b2)
                posk = sb("posk", [128, 32], F32, b2)
                posv = sb("posv", [128, 32], F32, b2)
                vstg = sb("vstg", [128, 256], BF16, b2)
                t_vstg = Tk("vstg")
                t_wB, t_mg, t_xt, t_xn, t_rstd, t_rs, t_ust, t_Kc, t_pos = [Tk(n) for n in
                    ("wB", "mg", "Bxt", "Bxn", "Brstd", "Brs", "Bust", "Kc", "pos")]
                t_sq = [Tk("Bsq0"), Tk("Bsq1")]
                wB_v = din["w_inB"].rearrange("(k p) f -> p k f", p=128)
                for k in range(8):
                    P.dma("pool", wB[:, k, :], wB_v[:, k, :], writes=[t_wB])
                P.dma("sp", mg[:], din["mix_g"], writes=[t_mg])
                P.dma("sp", posk[:], din["posk"], writes=[t_pos])
                P.dma("sp", posv[:], din["posv"], writes=[t_pos])
                h1_v = h1T.rearrange("(c p) t -> p c t", p=128)
                uT_v = uT.rearrange("(c p) t -> p c t", p=128)
                for i in range(S // TT):
                    tsl = slice(i * TT, (i + 1) * TT)
                    P.dma("sp", xt[:], h1_v[:, :, tsl], reads=[t_h1T], writes=[t_xt])
                    norm_tile(xt, t_xt, mg, t_mg, xn, t_xn, sq, t_sq, rstd, t_rstd, TT, 6)
                    pssb = [3, 7]

                    def main(ch):
                        pb = ch % 3
                        pq = ps[pb][:, :]
                        for k in range(8):
                            MM(pq, wB[:, k, ch * 128:(ch + 1) * 128], xn[:, k, :], k == 0, k == 7,
                               [t_wB, t_xn], [t_ps[pb]])
                        if ch < 6:
                            ACT(sq[ch % 2][:], pq, AF.Square, [t_ps[pb]], [t_sq[ch % 2]])

                    def finish(ch):
                        pb = ch % 3
                        pq = ps[pb][:, :]
                        if ch < 6:
                            sb_ = pssb[ch % 2]
                            pss = ps[sb_][:, :]
                            MM(pss, bd_bf[:], sq[ch % 2][:], True, True, [t_const, t_sq[ch % 2]], [t_ps[sb_]])
                            ACT(rs[:], pss, AF.Ln, [t_ps[sb_], t_const], [t_rs], bias=epsc[:, 0:1], scale=1.0 / 64)
                            ACT(rs[:], rs[:], AF.Exp, [t_rs], [t_rs], scale=-0.5)
                            if ch < 4:
                                STT(Qs[ch][0:64, tsl], pq[0:64, :], gq8[0:64, 0:1], rs[0:64, :], ALU.mult, ALU.mult,
                                    [t_ps[pb], t_gains, t_rs], [t_Q])
                                STT(Qs[4 + ch][64:128, tsl], pq[64:128, :], gq8[64:128, 0:1], rs[64:128, :],
                                    ALU.mult, ALU.mult, [t_ps[pb], t_gains, t_rs], [t_Q])
                            elif ch == 4:
                                STT(Ks[0][0:64, tsl], pq[0:64, :], gains[0:64, 1:2], rs[0:64, :], ALU.mult, ALU.mult,
                                    [t_ps[pb], t_gains, t_rs], [t_Ks])
                                STT(Ks[1][64:128, tsl], pq[64:128, :], gains[64:128, 1:2], rs[64:128, :],
                                    ALU.mult, ALU.mult, [t_ps[pb], t_gains, t_rs], [t_Ks])
                            else:
                                STT(Kw[:, tsl], pq, gains[:, 2:3], rs[:], ALU.mult, ALU.mult,
                                    [t_ps[pb], t_gains, t_rs], [t_Kw])
                        elif ch < 8:
                            pos = posk if ch == 6 else posv
                            A_, B_ = (KcA, KcB) if ch == 6 else (VcA, VcB)
                            pq3 = pq.rearrange("p (i s) -> p i s", s=16)
                            TT2("dve", A_[:, tsl].rearrange("p (i s) -> p i s", s=16), pq3,
                                pos[:, 0:16].unsqueeze(1).to_broadcast([128, TT // 16, 16]), ALU.add,
                                [t_ps[pb], t_pos], [t_Kc])
                            TT2("dve", B_[:, tsl].rearrange("p (i s) -> p i s", s=16), pq3,
                                pos[:, 16:32].unsqueeze(1).to_broadcast([128, TT // 16, 16]), ALU.add,
                                [t_ps[pb], t_pos], [t_Kc])
                        else:
                            CP("act", ust[:, ch - 8, :], pq, [t_ps[pb]], [t_ust])

                    for ch in range(13):
                        if ch < 12:
                            main(ch)
                        if ch >= 1:
                            finish(ch - 1)
                    P.dma("sp", uT_v[:, :, tsl], ust[:], reads=[t_ust], writes=[t_uT], chan=t_ust)
                    for sub in range(4):
                        if "TM" in DBG:
                            break
                        pb = 4 + sub % 2
                        pt = ps[pb][:, 0:280]
                        for k in range(8):
                            MM(pt, xn[:, k, sub * 128:(sub + 1) * 128], wB[:, k, 1536:1816], k == 0, k == 7,
                               [t_wB, t_xn], [t_ps[pb]])
                        kt = i * 4 + sub
                        CP("act", vstg[:], ps[pb][:, 0:256], [t_ps[pb]], [t_vstg])
                        for g_ in range(2):
                            CP("pool", Vs[:, kt, g_, 0:64], vstg[:, g_ * 64:(g_ + 1) * 64], [t_vstg], [t_Vs])
                            CP("pool", Vw[:, kt, g_, 0:64], vstg[:, 128 + g_ * 64:128 + (g_ + 1) * 64], [t_vstg], [t_Vw])
                        if "cpG" not in DBG:
                            CP("act", gl[:, kt, :], ps[pb][:, 256:280], [t_ps[pb]], [t_gl])
                P.barrier()
                b2.close()
                if stage == "B":
                    return

                w1k = sb("w1k", [128, 32, 256], BF16, bs)
                w1v = sb("w1v", [128, 32, 256], BF16, bs)
                w2k = sb("w2k", [128, 2, 128], BF16, bs)
                w2v = sb("w2v", [128, 2, 64], BF16, bs)
                ovl = sb("ovl", [128, 2, 64], BF16, bs)
                hx = sb("hx", [128, 256], F32, bs)
                hu = sb("hu", [128, 256], F32, bs)
                hid = [[sb("hid%d%d" % (a, b), [128, 256], BF16, bs) for b in range(2)] for a in range(2)]
                t_cw, t_hx, t_hu = Tk("cw"), Tk("hx"), Tk("hu")
                t_hid = [[Tk("hid%d%d" % (a, b)) for b in range(2)] for a in range(2)]
                P.dma("pool", w1k[:], din["w1k"], writes=[t_cw])
                P.dma("pool", w1v[:], din["w1v"], writes=[t_cw])
                P.dma("pool", w2k[:], din["w2k"], writes=[t_cw])
                P.dma("pool", w2v[:], din["w2v"], writes=[t_cw])
                P.dma("pool", ovl[:], din["c_ovl"], writes=[t_cw])
                MEMSET("pool", kcT[:], 0.0, [t_kcT])
                for g in range(2):
                    MEMSET("pool", rcmp[g][:, :, 128:129], 1.0, [t_rcmp])
                    CP("pool", rcmp[g][:, :, 64:128], ovl[:], [t_cw], [t_rcmp])
                for a in range(2):
                    for b in range(2):
                        MEMSET("pool", hid[a][b][:], 0.0, [t_hid[a][b]])
                for kind in range(2):
                    w1 = w1k if kind == 0 else w1v
                    A_, B_ = (KcA, KcB) if kind == 0 else (VcA, VcB)
                    A3 = A_[:].rearrange("p (i s) -> p i s", s=16)
                    B3 = B_[:].rearrange("p (i s) -> p i s", s=16)
                    for g in range(2):
                        R = slice(64 * g, 64 * g + 64)
                        for mc in range(2):
                            ph = ps[mc][:, 0:255]
                            for j in range(32):
                                rhs = A3[R, 0:255, j] if j < 16 else B3[R, 1:256, j - 16]
                                MM(ph, w1[R, j, mc * 128:(mc + 1) * 128], rhs, j == 0, j == 31,
                                   [t_cw, t_Kc], [t_ps[mc]])
                            CP("act", hx[:, 0:255], ph, [t_ps[mc]], [t_hx])
                            TT2("dve", hu[:, 0:255], hx[:, 0:255], hx[:, 0:255], ALU.mult, [t_hx], [t_hu])
                            TS("dve", hu[:, 0:255], hu[:, 0:255], 0.044715, 1.0, ALU.mult, ALU.add, [t_hu], [t_hu])
                            TT2("dve", hu[:, 0:255], hu[:, 0:255], hx[:, 0:255], ALU.mult, [t_hu, t_hx], [t_hu])
                            ACT(hu[:, 0:255], hu[:, 0:255], AF.Sigmoid, [t_hu], [t_hu], scale=1.5957691216057308)
                            TT2("dve", hid[g][mc][:, 0:255], hu[:, 0:255], hx[:, 0:255], ALU.mult,
                                [t_hu, t_hx], [t_hid[g][mc]])
                        if kind == 0:
                            pk = ps[2][:, 0:256]
                            for mc in range(2):
                                MM(pk, w2k[:, mc, :], hid[g][mc][:], mc == 0, mc == 1,
                                   [t_cw, t_hid[g][mc]], [t_ps[2]])
                            sqn = sq[0]
                            ACT(sqn[:, 0:256], pk, AF.Square, [t_ps[2]], [t_sq[0]])
                            pss = ps[3][:, 0:256]
                            MM(pss, bd_bf[:], sqn[:, 0:256], True, True, [t_const, t_sq[0]], [t_ps[3]])
                            ACT(rs[:, 0:256], pss, AF.Sqrt, [t_ps[3], t_const], [t_rs], bias=epsc[:, 0:1], scale=1.0 / 64)
                            RECIP(rs[:, 0:256], rs[:, 0:256], [t_rs], [t_rs])
                            STT(kcT[R, 0:255], pk[R, 0:255], gains[R, 3:4], rs[R, 0:255], ALU.mult, ALU.mult,
                                [t_ps[2], t_gains, t_rs], [t_kcT])
                        else:
                            for it in range(2):
                                pv = ps[2 + it][:, 0:64]
                                for mc in range(2):
                                    MM(pv, hid[g][mc][:, it * 128:(it + 1) * 128], w2v[:, mc, :], mc == 0, mc == 1,
                                       [t_cw, t_hid[g][mc]], [t_ps[2 + it]])
                                CP("act", rcmp[g][:, it, 0:64], pv, [t_ps[2 + it]], [t_rcmp])
                P.barrier()

            if stage == "BC":
                return
            with ExitStack() as ds:
                mwin = sb("mwin", [128, 8, 512], BF16, ds)
                mslc = sb("mslc", [128, 4, 512], BF16, ds)
                mcmp = sb("mcmp", [128, 5, 512], BF16, ds)
                selB = sb("selB", [128, 32, 64], F32, ds)
                gs = sb("gs", [128, 32, 24], F32, ds)
                PT = sb("PT", [128, 32, 512], BF16, ds)
                Pc = [sb("Pc%d" % i, [128, 512], BF16, ds) for i in range(2)]
                onsa = sb("onsa", [128, 4, 512], F32, ds)
                onsab = sb("onsab", [128, 4, 512], BF16, ds)
                oTs = sb("oTs", [128, 4, 512], BF16, ds)
                imp = [sb("imp%d" % g, [128, 4, 64], F32, ds) for g in range(2)]
                scr = sb("scr", [128, 64], F32, ds)
                wk = sb("wk", [128, 64], F32, ds)
                m8 = sb("m8", [128, 16], F32, ds)
                NM = sb("NM", [128, 4, 128], BF16, ds)
                sm = sb("sm", [128, 8], F32, ds)
                t_msk, t_selB, t_gs, t_onsa, t_onsab, t_oTs, t_scr, t_wk, t_m8, t_NM, t_sm = [Tk(n) for n in
                    ("msk", "selB", "gs", "onsa", "onsab", "oTs", "scr", "wk", "m8", "NM", "sm")]
                t_PT = [Tk("PT%d" % i) for i in range(32)]
                t_Pc = [Tk("Pc0"), Tk("Pc1")]
                t_imp = [Tk("imp0"), Tk("imp1")]
                P.dma("pool", mwin[:], din["c_mwin"], writes=[t_msk])
                P.dma("pool", mslc[:], din["c_mslc"], writes=[t_msk])
                P.dma("pool", mcmp[:], din["c_mcmp"], writes=[t_msk])
                P.dma("sp", selB[:], din["c_selB"], writes=[t_selB])
                ACT(gs[:], gl[:], AF.Sigmoid, [t_gl], [t_gs])
                onsaT_v = onsaT.rearrange("(c p) t -> p c t", p=128)
                BIG = 30000.0
                sbank = [0, 1, 2]
                sctr = [0]

                def next_sbank():
                    b = sbank[sctr[0] % 3]
                    sctr[0] += 1
                    return b

                def evac(po, sub, h, br, kt, first, imp_g=None, r=0):
                    dcol = 128 if br == 0 else 64
                    c0 = (h * 3 + br) % 4 * 2
                    TS("dve", sm[:, c0:c0 + 1], po[:, dcol:dcol + 1], 1e-30, None, ALU.max, None, [t_ps_cur[0]], [t_sm])
                    RECIP(sm[:, c0:c0 + 1], sm[:, c0:c0 + 1], [t_sm], [t_sm])
                    if br == 0:
                        if r == 0:
                            TS("dve", imp_g[:, sub, :], po[:, 64:128], sm[:, c0:c0 + 1], None, ALU.mult, None,
                               [t_ps_cur[0], t_sm], [t_imp[h // 4]])
                        else:
                            STT(imp_g[:, sub, :], po[:, 64:128], sm[:, c0:c0 + 1], imp_g[:, sub, :], ALU.mult, ALU.add,
                                [t_ps_cur[0], t_sm, t_imp[h // 4]], [t_imp[h // 4]])
                    TT2("dve", sm[:, c0 + 1:c0 + 2], sm[:, c0:c0 + 1], gs[:, kt, br * 8 + h:br * 8 + h + 1], ALU.mult,
                        [t_sm, t_gs], [t_sm])
                    dst = onsa[:, sub, h * 64:(h + 1) * 64]
                    if first:
                        TS("dve", dst, po[:, 0:64], sm[:, c0 + 1:c0 + 2], None, ALU.mult, None,
                           [t_ps_cur[0], t_sm], [t_onsa])
                    else:
                        STT(dst, po[:, 0:64], sm[:, c0 + 1:c0 + 2], dst, ALU.mult, ALU.add,
                            [t_ps_cur[0], t_sm, t_onsa], [t_onsa])

                t_ps_cur = [None]
                for Q in range(8):
                    qsl = slice(Q * 512, (Q + 1) * 512)
                    cts = [0] if Q < 4 else [0, 1]
                    for h in range(8):
                        g, r = h // 4, h % 4
                        R = slice(64 * g, 64 * g + 64)
                        for ct in cts:
                            b = next_sbank()
                            MM(ps[b][:, :], kcT[R, ct * 128:(ct + 1) * 128], Qs[h][R, qsl], True, True,
                               [t_kcT, t_Q], [t_ps[b]])
                            ACT(Pc[ct][:], ps[b][:, :], AF.Exp, [t_ps[b]], [t_Pc[ct]])
                            v = Q - 4 * ct
                            if v <= 4:
                                TT2("dve", Pc[ct][:], Pc[ct][:], mcmp[:, v, :], ALU.mult, [t_Pc[ct], t_msk], [t_Pc[ct]])
                        for sub in range(4):
                            bk = 3 + sub // 2
                            po = ps[bk][:, (sub % 2) * 129:(sub % 2) * 129 + 129]
                            for ci, ct in enumerate(cts):
                                MM(po, Pc[ct][:, sub * 128:(sub + 1) * 128], rcmp[g][:, ct, :], ci == 0, ci == len(cts) - 1,
                                   [t_Pc[ct], t_rcmp], [t_ps[bk]])
                        for sub in range(4):
                            bk = 3 + sub // 2
                            po = ps[bk][:, (sub % 2) * 129:(sub % 2) * 129 + 129]
                            t_ps_cur[0] = t_ps[bk]
                            evac(po, sub, h, 0, Q * 4 + sub, True, imp[g], r)
                    for sub in range(4):
                        kt = Q * 4 + sub
                        for g in range(2):
                            TT2("dve", scr[:], imp[g][:, sub, :], selB[:, kt, :], ALU.add, [t_imp[g], t_selB], [t_scr])
                            P.op("dve", lambda e: e.max(out=m8[:, 0:8], in_=scr[:]), [t_scr], [t_m8])
                            P.op("dve", lambda e: e.match_replace(out=wk[:], in_to_replace=m8[:, 0:8],
                                                                   in_values=scr[:], imm_value=-3.0e38),
                                 [t_scr, t_m8], [t_wk])
                            P.op("dve", lambda e: e.max(out=m8[:, 8:16], in_=wk[:]), [t_wk], [t_m8])
                            cs = slice(64, 128) if g == 0 else slice(0, 64)
                            TS("dve", NM[:, sub, cs], scr[:], m8[:, 15:16], -BIG, ALU.is_lt, ALU.mult,
                               [t_scr, t_m8], [t_NM])
                    pT = ps[7][:].bitcast(BF16)
                    for sub in range(4):
                        TR(pT[:, sub * 128:(sub + 1) * 128], NM[:, sub, :], id_bf[:], [t_NM, t_const], [t_ps[7]])
                    for h in range(8):
                        R2 = slice(64, 128) if h < 4 else slice(0, 64)
                        CP("act" if h % 2 == 0 else "dve", Qs[h][R2, qsl], pT[R2, 0:512], [t_ps[7]], [t_Qm[Q]])
                    for h in range(8):
                        g = h // 4
                        R = slice(64 * g, 64 * g + 64)
                        nkt = 4 * Q + 4
                        for kt in range(nkt):
                            b = next_sbank()
                            MM(ps[b][:, :], Ks[g][:, kt * 128:(kt + 1) * 128], Qs[h][:, qsl], True, True,
                               [t_Ks, t_Q, t_Qm[Q]], [t_ps[b]])
                            ACT(PT[:, kt, :], ps[b][:, :], AF.Exp, [t_ps[b]], [t_PT[kt]])
                            if kt >= 4 * Q:
                                TT2("dve", PT[:, kt, :], PT[:, kt, :], mslc[:, kt - 4 * Q, :], ALU.mult,
                                    [t_PT[kt], t_msk], [t_PT[kt]])
                        bk = 5
                        for sub in range(4):
                            po = ps[bk][:, sub * 65:sub * 65 + 65]
                            last = 4 * Q + sub
                            for kt in range(last + 1):
                                MM(po, PT[:, kt, sub * 128:(sub + 1) * 128], Vs[:, kt, g, 0:65], kt == 0, kt == last,
                                   [t_PT[kt], t_Vs], [t_ps[bk]])
                        t_ps_cur[0] = t_ps[bk]
                        for sub in range(4):
                            evac(ps[bk][:, sub * 65:sub * 65 + 65], sub, h, 1, Q * 4 + sub, False)
                        rr = [r_ for r_ in range(8) if 4 * Q - 4 + r_ >= 0]
                        for r_ in rr:
                            kt = 4 * Q - 4 + r_
                            b = next_sbank()
                            MM(ps[b][:, :], Kw[R, kt * 128:(kt + 1) * 128], Qs[h][R, qsl], True, True,
                               [t_Kw, t_Q], [t_ps[b]])
                            ACT(PT[:, r_, :], ps[b][:, :], AF.Exp, [t_ps[b]], [t_PT[r_]])
                            TT2("dve", PT[:, r_, :], PT[:, r_, :], mwin[:, r_, :], ALU.mult, [t_PT[r_], t_msk], [t_PT[r_]])
                        bk = 6
                        for sub in range(4):
                            po = ps[bk][:, sub * 65:sub * 65 + 65]
                            rs_ = [r_ for r_ in rr if sub <= r_ <= sub + 4]
                            for ci, r_ in enumerate(rs_):
                                kt = 4 * Q - 4 + r_
                                MM(po, PT[:, r_, sub * 128:(sub + 1) * 128], Vw[:, kt, g, 0:65], ci == 0, ci == len(rs_) - 1,
                                   [t_PT[r_], t_Vw], [t_ps[bk]])
                        t_ps_cur[0] = t_ps[bk]
                        for sub in range(4):
                            evac(ps[bk][:, sub * 65:sub * 65 + 65], sub, h, 2, Q * 4 + sub, False)
                    CP("act", onsab[:], onsa[:], [t_onsa], [t_onsab])
                    for fc in range(4):
                        pT2 = ps[7][:].bitcast(BF16)
                        for sub in range(4):
                            TR(pT2[:, sub * 128:(sub + 1) * 128], onsab[:, sub, fc * 128:(fc + 1) * 128], id_bf[:],
                               [t_onsab, t_const], [t_ps[7]])
                        CP("dve" if fc % 2 else "act", oTs[:, fc, :], pT2[:, 0:512], [t_ps[7]], [t_oTs])
                    P.dma("sp", onsaT_v[:, :, qsl], oTs[:], reads=[t_oTs], writes=[t_onsaT], chan=t_oTs)
                P.barrier()

    def ssm_phase():
        PI = math.pi
        with ExitStack() as fs:
            CT = sb("CT", [128, 16, 512], F32, fs)
            ST = sb("ST", [128, 16, 512], F32, fs)
            rhoF = sb("rhoF", [128, 16, 512], F32, fs)
            BBrT = sb("BBrT", [128, 16, 128], BF16, fs)
            BBiT = sb("BBiT", [128, 16, 128], BF16, fs)
            CreT = sb("CreT", [128, 16, 128], BF16, fs)
            CimT = sb("CimT", [128, 16, 128], BF16, fs)
            Dg = sb("Dg", [128, 4, 128], BF16, fs)
            dsk = sb("dsk", [128, 4], F32, fs)
            rho = sb("rho", [128, 16], F32, fs)
            car = [[sb("car%d%d" % (a, b), [128, 16], F32, fs) for b in range(2)] for a in range(2)]
            t_tab, t_bb, t_cc, t_dg, t_rho = [Tk(n) for n in ("tab", "bb", "cc", "dg", "rho")]
            t_car = [[Tk("car%d_%d" % (a, sc)) for sc in range(16)] for a in range(2)]
            P.dma("pool", CreT[:], din["s_creT"], writes=[t_cc])
            P.dma("pool", CimT[:], din["s_cimT"], writes=[t_cc])
            P.dma("sp", dsk[:], din["s_d"], writes=[t_dg])
            for oc in range(4):
                TS("dve", Dg[:, oc, :], id_f[:], dsk[:, oc:oc + 1], None, ALU.mult, None, [t_const, t_dg], [t_dg])
            nCreT = sb("nCreT", [128, 16, 128], BF16, fs)
            nCimT = sb("nCimT", [128, 16, 128], BF16, fs)
            TS("dve", nCreT[:], CreT[:], -1.0, None, ALU.mult, None, [t_cc], [t_cc])
            TS("dve", nCimT[:], CimT[:], -1.0, None, ALU.mult, None, [t_cc], [t_cc])
            with ExitStack() as ss:
                lr = sb("lr", [128, 16], F32, ss)
                li = sb("li", [128, 16], F32, ss)
                stp = sb("stp", [128, 16], F32, ss)
                th = sb("th", [128, 16], F32, ss)
                kk = sb("kk", [128, 16], F32, ss)
                r1 = sb("r1", [128, 16], F32, ss)
                r2 = sb("r2", [128, 16], F32, ss)
                s1 = sb("s1", [128, 16], F32, ss)
                c1 = sb("c1", [128, 16], F32, ss)
                w = [sb("w%d" % i, [128, 16], F32, ss) for i in range(6)]
                bre = sb("bre", [128, 16, 16], F32, ss)
                bim = sb("bim", [128, 16, 16], F32, ss)
                bbr = sb("bbr", [128, 16, 16], F32, ss)
                bbi = sb("bbi", [128, 16, 16], F32, ss)
                bt = [sb("bt%d" % i, [128, 16, 16], F32, ss) for i in range(2)]
                Zr = sb("Zr", [128, 16, 128], F32, ss)
                Zi = sb("Zi", [128, 16, 128], F32, ss)
                tA = sb("tA", [128, 16, 256], F32, ss)
                tB = sb("tB", [128, 16, 256], F32, ss)
                t_p = Tk("ssm_par")
                t_Z = Tk("Z")
                P.dma("sp", lr[:], din["s_lr"], writes=[t_p])
                P.dma("sp", li[:], din["s_li"], writes=[t_p])
                P.dma("sp", stp[:], din["s_ls"], writes=[t_p])
                P.dma("sp", bre[:], din["s_bre"], writes=[t_p])
                P.dma("sp", bim[:], din["s_bim"], writes=[t_p])
                pp = [t_p]
                ACT(stp[:], stp[:], AF.Exp, pp, pp)
                TT2("dve", th[:], li[:], stp[:], ALU.mult, pp, pp)
                TT2("dve", w[0][:], lr[:], stp[:], ALU.mult, pp, pp)
                ACT(rho[:], w[0][:], AF.Exp, pp, [t_rho])
                MEMSET("dve", kk[:], 0.0, pp)
                for m in range(1, 9):
                    STT(kk[:], th[:], (2 * m - 1) * PI, kk[:], ALU.is_gt, ALU.add, pp, pp)
                STT(r1[:], kk[:], -2.0 * PI, th[:], ALU.mult, ALU.add, pp, pp)
                TS("dve", r2[:], r1[:], PI / 2, None, ALU.add, None, pp, pp)
                TS("dve", w[1][:], r2[:], PI, None, ALU.is_gt, None, pp, pp)
                STT(r2[:], w[1][:], -2.0 * PI, r2[:], ALU.mult, ALU.add, pp, pp)
                ACT(s1[:], r1[:], AF.Sin, pp, pp)
                ACT(c1[:], r2[:], AF.Sin, pp, pp)
                ar, ai, arm1, den, cr, cim = w
                TT2("dve", ar[:], rho[:], c1[:], ALU.mult, pp + [t_rho], pp)
                TT2("dve", ai[:], rho[:], s1[:], ALU.mult, pp + [t_rho], pp)
                TS("dve", arm1[:], ar[:], -1.0, None, ALU.add, None, pp, pp)
                TT2("dve", den[:], lr[:], lr[:], ALU.mult, pp, pp)
                TT2("dve", kk[:], li[:], li[:], ALU.mult, pp, pp)
                TT2("dve", den[:], den[:], kk[:], ALU.add, pp, pp)
                RECIP(den[:], den[:], pp, pp)
                TT2("dve", cr[:], arm1[:], lr[:], ALU.mult, pp, pp)
                TT2("dve", kk[:], ai[:], li[:], ALU.mult, pp, pp)
                TT2("dve", cr[:], cr[:], kk[:], ALU.add, pp, pp)
                TT2("dve", cr[:], cr[:], den[:], ALU.mult, pp, pp)
                TT2("dve", cim[:], ai[:], lr[:], ALU.mult, pp, pp)
                TT2("dve", kk[:], arm1[:], li[:], ALU.mult, pp, pp)
                TT2("dve", cim[:], cim[:], kk[:], ALU.subtract, pp, pp)
                TT2("dve", cim[:], cim[:], den[:], ALU.mult, pp, pp)
                crb = cr[:].unsqueeze(2).to_broadcast([128, 16, 16])
                cib = cim[:].unsqueeze(2).to_broadcast([128, 16, 16])
                TT2("dve", bt[0][:], bre[:], crb, ALU.mult, pp, pp)
                TT2("dve", bt[1][:], bim[:], cib, ALU.mult, pp, pp)
                TT2("dve", bbr[:], bt[0][:], bt[1][:], ALU.subtract, pp, pp)
                TT2("dve", bt[0][:], bim[:], crb, ALU.mult, pp, pp)
                TT2("dve", bt[1][:], bre[:], cib, ALU.mult, pp, pp)
                TT2("dve", bbi[:], bt[0][:], bt[1][:], ALU.add, pp, pp)
                MEMSET("pool", Zr[:], 0.0, [t_Z])
                MEMSET("pool", Zi[:], 0.0, [t_Z])
                for sc in range(16):
                    c0 = 32 * (sc % 4)
                    for Z_, bb_ in ((Zr, bbr), (Zi, bbi)):
                        CP("dve", Z_[0:64, sc, c0:c0 + 16], bb_[0:64, sc, :], pp + [t_Z], [t_Z])
                        CP("dve", Z_[64:128, sc, c0 + 16:c0 + 32], bb_[64:128, sc, :], pp + [t_Z], [t_Z])
                for sc in range(16):
                    for bi_, (Z_, BT_) in enumerate(((Zr, BBrT), (Zi, BBiT))):
                        bk = (sc * 2 + bi_) % 4
                        TR(ps[bk][:, 0:128], Z_[:, sc, :], id_f[:], [t_Z, t_const], [t_ps[bk]])
                        CP("act", BT_[:, sc, :], ps[bk][:, 0:128], [t_ps[bk]], [t_bb])
                CP("dve", CT[:, :, 0:1], c1[:].unsqueeze(2), pp, [t_tab])
                CP("dve", ST[:, :, 0:1], s1[:].unsqueeze(2), pp, [t_tab])
                n = 1
                while n < 512:
                    cn = CT[:, :, n - 1:n].to_broadcast([128, 16, n])
                    sn = ST[:, :, n - 1:n].to_broadcast([128, 16, n])
                    tt_ = [t_tab]
                    TT2("dve", tA[:, :, 0:n], CT[:, :, 0:n], cn, ALU.mult, tt_, pp)
                    TT2("pool", tB[:, :, 0:n], ST[:, :, 0:n], sn, ALU.mult, tt_, [t_Z])
                    TT2("dve", CT[:, :, n:2 * n], tA[:, :, 0:n], tB[:, :, 0:n], ALU.subtract, pp + [t_Z], tt_)
                    TT2("dve", tA[:, :, 0:n], ST[:, :, 0:n], cn, ALU.mult, tt_, pp)
                    TT2("pool", tB[:, :, 0:n], CT[:, :, 0:n], sn, ALU.mult, tt_, [t_Z])
                    TT2("dve", ST[:, :, n:2 * n], tA[:, :, 0:n], tB[:, :, 0:n], ALU.add, pp + [t_Z], tt_)
                    n *= 2
                MEMSET("pool", rhoF[:], 1.0, [t_rho])
                for sc in range(16):
                    ACT(rhoF[:, sc, :], rhoF[:, sc, :], AF.Identity, [t_rho], [t_rho], scale=rho[:, sc:sc + 1])
                for a in range(2):
                    for b in range(2):
                        MEMSET("pool", car[a][b][:], 0.0, [t_car[a][sc] for sc in range(16)])
                P.barrier()

            NB = 2
            uTt = [sb("uTt%d" % i, [128, 4, 512], BF16, fs) for i in range(3)]
            ygst = sb("ygst", [128, 4, 512], BF16, fs)
            f = [[sb("sf%d_%d" % (i, b), [128, 512], F32 if i < 8 else BF16, fs) for b in range(NB)] for i in range(12)]
            gx = sb("gx", [128, 512], F32, fs)
            gu = sb("gu", [128, 512], F32, fs)
            cu = sb("cu", [128, 4], F32, fs)
            t_u = [Tk("uTt%d" % i) for i in range(3)]
            t_f = [[Tk("sf%d_%d" % (i, b)) for b in range(NB)] for i in range(12)]
            t_gx, t_gu, t_ygst, t_cu = Tk("gx"), Tk("gu"), Tk("ygst"), Tk("cu")
            uT_v = uT.rearrange("(c p) t -> p c t", p=128)
            yg_v = ygT.rearrange("(c p) t -> p c t", p=128)
            P.dma("sp", uTt[0][:], uT_v[:, :, 0:512], reads=[t_uT], writes=[t_u[0]])
            tiles = [(tt, oc, sci) for tt in range(8) for oc in range(4) for sci in range(4)]

            def S1(i):
                tt, oc, sci = tiles[i]
                sc, b, ub = 4 * oc + sci, i % NB, tt % 3
                if oc == 0 and sci == 0 and tt + 1 < 8:
                    nb_ = (tt + 1) % 3
                    P.dma("sp", uTt[nb_][:], uT_v[:, :, (tt + 1) * 512:(tt + 2) * 512], reads=[t_uT], writes=[t_u[nb_]])
                pb = 2 * (i % 2)
                pur, pui = ps[pb][:, :], ps[pb + 1][:, :]
                C_, S_ = CT[:, sc, :], ST[:, sc, :]
                MM(pur, BBrT[:, sc, :], uTt[ub][:, oc, :], True, True, [t_bb, t_u[ub]], [t_ps[pb]])
                MM(pui, BBiT[:, sc, :], uTt[ub][:, oc, :], True, True, [t_bb, t_u[ub]], [t_ps[pb + 1]])
                TT2("dve", f[0][b][:], pur, C_, ALU.mult, [t_ps[pb], t_tab], [t_f[0][b]])
                TT2("dve", f[1][b][:], pui, S_, ALU.mult, [t_ps[pb + 1], t_tab], [t_f[1][b]])
                TT2("dve", f[2][b][:], pui, C_, ALU.mult, [t_ps[pb + 1], t_tab], [t_f[2][b]])
                TT2("dve", f[3][b][:], pur, S_, ALU.mult, [t_ps[pb], t_tab], [t_f[3][b]])

            def S2(i):
                b = i % NB
                TT2("pool", f[4][b][:], f[0][b][:], f[1][b][:], ALU.add, [t_f[0][b], t_f[1][b]], [t_f[4][b]])
                TT2("pool", f[5][b][:], f[2][b][:], f[3][b][:], ALU.subtract, [t_f[2][b], t_f[3][b]], [t_f[5][b]])

            def S3(i):
                tt, oc, sci = tiles[i]
                sc, b = 4 * oc + sci, i % NB
                cin, cout = car[tt % 2], car[(tt + 1) % 2]
                tcin, tcout = t_car[tt % 2], t_car[(tt + 1) % 2]
                P.op("dve", lambda e: e.tensor_tensor_scan(
                    out=f[6][b][:], data0=rhoF[:, sc, :], data1=f[4][b][:], initial=cin[0][:, sc:sc + 1],
                    op0=ALU.mult, op1=ALU.add), [t_rho, t_f[4][b], tcin[sc]], [t_f[6][b]])
                P.op("dve", lambda e: e.tensor_tensor_scan(
                    out=f[7][b][:], data0=rhoF[:, sc, :], data1=f[5][b][:], initial=cin[1][:, sc:sc + 1],
                    op0=ALU.mult, op1=ALU.add), [t_rho, t_f[5][b], tcin[sc]], [t_f[7][b]])
                zrL, ziL = f[6][b][:, 511:512], f[7][b][:, 511:512]
                cL, sL = CT[:, sc, 511:512], ST[:, sc, 511:512]
                TT2("dve", cu[:, 0:1], ziL, sL, ALU.mult, [t_f[7][b], t_tab], [t_cu])
                STT(cout[0][:, sc:sc + 1], zrL, cL, cu[:, 0:1], ALU.mult, ALU.subtract,
                    [t_f[6][b], t_tab, t_cu], [tcout[sc]])
                TT2("dve", cu[:, 1:2], ziL, cL, ALU.mult, [t_f[7][b], t_tab], [t_cu])
                STT(cout[1][:, sc:sc + 1], zrL, sL, cu[:, 1:2], ALU.mult, ALU.add,
                    [t_f[6][b], t_tab, t_cu], [tcout[sc]])
                TT2("dve", f[11][b][:], f[7][b][:], CT[:, sc, :], ALU.mult, [t_f[7][b], t_tab], [t_f[11][b]])

            def S4(i):
                tt, oc, sci = tiles[i]
                sc, b = 4 * oc + sci, i % NB
                C_, S_ = CT[:, sc, :], ST[:, sc, :]
                TT2("pool", f[8][b][:], f[6][b][:], C_, ALU.mult, [t_f[6][b], t_tab], [t_f[8][b]])
                TT2("pool", f[9][b][:], f[7][b][:], S_, ALU.mult, [t_f[7][b], t_tab], [t_f[9][b]])
                TT2("pool", f[10][b][:], f[6][b][:], S_, ALU.mult, [t_f[6][b], t_tab], [t_f[10][b]])

            def S5(i):
                tt, oc, sci = tiles[i]
                sc, b, ub = 4 * oc + sci, i % NB, tt % 3
                pyb = 4 + oc % 2
                py = ps[pyb][:, :]
                MM(py, CreT[:, sc, :], f[8][b][:], sci == 0, False, [t_cc, t_f[8][b]], [t_ps[pyb]])
                MM(py, nCreT[:, sc, :], f[9][b][:], False, False, [t_cc, t_f[9][b]], [t_ps[pyb]])
                MM(py, nCimT[:, sc, :], f[10][b][:], False, False, [t_cc, t_f[10][b]], [t_ps[pyb]])
                MM(py, nCimT[:, sc, :], f[11][b][:], False, False, [t_cc, t_f[11][b]], [t_ps[pyb]])
                if sci == 3:
                    MM(py, Dg[:, oc, :], uTt[ub][:, oc, :], False, True, [t_dg, t_u[ub]], [t_ps[pyb]])
                    CP("act", gx[:], py, [t_ps[pyb]], [t_gx])
                    ACT(gu[:], py, AF.Square, [t_ps[pyb]], [t_gu])
                    TS("pool", gu[:], gu[:], 0.044715, 1.0, ALU.mult, ALU.add, [t_gu], [t_gu])
                    TT2("pool", gu[:], gu[:], gx[:], ALU.mult, [t_gu, t_gx], [t_gu])
                    ACT(gu[:], gu[:], AF.Sigmoid, [t_gu], [t_gu], scale=1.5957691216057308)
                    TT2("pool", ygst[:, oc, :], gu[:], gx[:], ALU.mult, [t_gu, t_gx], [t_ygst])
                    if oc == 3:
                        P.dma("sp", yg_v[:, :, tt * 512:(tt + 1) * 512], ygst[:], reads=[t_ygst], writes=[t_ygT], chan=t_ygst)

            n_t = len(tiles)
            for k in range(n_t + 2):
                if k < n_t:
                    S1(k)
                    S2(k)
                if 0 <= k - 1 < n_t:
                    S3(k - 1)
                    S4(k - 1)
                if 0 <= k - 2 < n_t:
                    S5(k - 2)
            P.barrier()

    def merge_phase():
        TT = 256
        NT = S // TT
        with ExitStack() as fs:
            wF = sb("wF", [128, 8, 2048], BF16, fs)
            wnsa = sb("wnsa", [128, 4, 1024], BF16, fs)
            wglu = sb("wglu", [128, 4, 2048], BF16, fs)
            wout = sb("wout", [128, 8, 1024], BF16, fs)
            mg = sb("Fmg", [128, 8], F32, fs)
            xt = [sb("Fx%d" % i, [128, 8, TT], F32, fs) for i in range(2)]
            xn = sb("Fxn", [128, 8, TT], BF16, fs)
            sq = [sb("Fsq%d" % i, [128, TT], BF16, fs) for i in range(2)]
            rstd = sb("Frstd", [128, TT], F32, fs)
            ont = [sb("Fon%d" % i, [128, 4, TT], BF16, fs) for i in range(2)]
            ygt = [sb("Fyg%d" % i, [128, 4, TT], BF16, fs) for i in range(2)]
            sg3 = [[sb("Fsg%d_%d" % (i, b), [128, TT], F32, fs) for b in range(2)] for i in range(3)]
            ta = [sb("Fta%d" % b, [128, TT], F32, fs) for b in range(2)]
            tb = [sb("Ftb%d" % b, [128, TT], F32, fs) for b in range(2)]
            mrg = sb("Fmrg", [128, 8, TT], BF16, fs)
            sgA = sb("FsgA", [128, 16, TT], F32, fs)
            t_sgA = [Tk("FsgA%d" % j) for j in range(16)]
            ot = sb("Fot", [128, 8, TT], F32, fs)
            t_w, t_mg, t_xn, t_rstd, t_ot = [Tk(n) for n in ("Fw", "Fmg", "Fxn", "Frstd", "Fot")]
            t_xt = [Tk("Fxt0"), Tk("Fxt1")]
            t_sq = [Tk("Fsq0"), Tk("Fsq1")]
            t_on = [Tk("Fon0"), Tk("Fon1")]
            t_yg = [Tk("Fyg0"), Tk("Fyg1")]
            t_sg3 = [[Tk("Fsg%d_%d" % (i, b)) for b in range(2)] for i in range(3)]
            t_ta = [Tk("Fta0"), Tk("Fta1")]
            t_tb = [Tk("Ftb0"), Tk("Ftb1")]
            t_mrg = [Tk("Fmrg%d" % i) for i in range(8)]
            t_h = [[Tk("Fps%d_%d" % (i, b)) for b in range(2)] for i in range(6)]
            P.dma("sp", mg[:], din["mix_g"], writes=[t_mg])
            wF_v = din["w_inF"].rearrange("(k p) f -> p k f", p=128)
            for k in range(8):
                P.dma("pool", wF[:, k, :], wF_v[:, k, :], writes=[t_w])
            wn_v = din["w_nsa"].rearrange("(k p) f -> p k f", p=128)
            wgl_v = din["w_glu"].rearrange("(k p) f -> p k f", p=128)
            wo_v = din["w_out"].rearrange("(k p) f -> p k f", p=128)
            for k in range(4):
                P.dma("pool", wnsa[:, k, :], wn_v[:, k, :], writes=[t_w])
                P.dma("pool", wglu[:, k, :], wgl_v[:, k, :], writes=[t_w])
            for k in range(8):
                P.dma("pool", wout[:, k, :], wo_v[:, k, :], writes=[t_w])
            h1_v = h1T.rearrange("(c p) t -> p c t", p=128)
            h2_v = h2T.rearrange("(c p) t -> p c t", p=128)
            on_v = onsaT.rearrange("(c p) t -> p c t", p=128)
            yg_v = ygT.rearrange("(c p) t -> p c t", p=128)

            def load(i):
                b = i % 2
                tsl = slice(i * TT, (i + 1) * TT)
                P.dma("sp", xt[b][:], h1_v[:, :, tsl], reads=[t_h1T], writes=[t_xt[b]])
                P.dma("sp", ont[b][:], on_v[:, :, tsl], reads=[t_onsaT], writes=[t_on[b]])
                P.dma("sp", ygt[b][:], yg_v[:, :, tsl], reads=[t_ygT], writes=[t_yg[b]])

            sq8 = sb("Fsq8", [128, 8, TT], BF16, fs)
            t_sq8 = [Tk("Fsq8_%d" % c) for c in range(8)]

            def fnorm1(i):
                bb = i % 2
                for c in range(8):
                    TT2("pool", sq8[:, c, :], xt[bb][:, c, :], xt[bb][:, c, :], ALU.mult, [t_xt[bb]], [t_sq8[c]])

            def fnorm2(i):
                bb = i % 2
                pstat = ps[6][:, 0:TT]
                for c in range(8):
                    MM(pstat, ones_bf[:], sq8[:, c, :], c == 0, c == 7, [t_const, t_sq8[c]], [t_ps[6]])
                ACT(rstd[:], pstat, AF.Sqrt, [t_ps[6], t_const], [t_rstd], bias=epsc[:, 0:1], scale=1.0 / D)
                RECIP(rstd[:], rstd[:], [t_rstd], [t_rstd])
                for c in range(8):
                    STT(xn[:, c, :], xt[bb][:, c, :], mg[:, c:c + 1], rstd[:], ALU.mult, ALU.mult,
                        [t_xt[bb], t_mg, t_rstd], [t_xn])

            load(0)
            fnorm1(0)
            fnorm2(0)
            for i in range(NT):
                b = i % 2
                if i + 1 < NT:
                    load(i + 1)
                xb = xt[b]
                if i + 1 < NT:
                    fnorm1(i + 1)
                for j in range(16):
                    bk = j % 4
                    pg_ = ps[bk][:, 0:TT]
                    cs_ = slice(j * 128, (j + 1) * 128)
                    for k in range(8):
                        MM(pg_, wF[:, k, cs_], xn[:, k, :], k == 0, k == 7, [t_w, t_xn], [t_ps[bk]])
                    ACT(sgA[:, j, :], pg_, AF.Sigmoid, [t_ps[bk]], [t_sgA[j]])
                if i + 1 < NT:
                    fnorm2(i + 1)
                for dc in range(8):
                    hb = dc % 2
                    bks = (0, 1, 2) if hb == 0 else (3, 4, 5)
                    pyn, pval, pgt = [ps[j][:, 0:TT] for j in bks]
                    tp = [t_ps[j] for j in bks]
                    dsl = slice(dc * 128, (dc + 1) * 128)
                    dsl2 = slice(1024 + dc * 128, 1024 + (dc + 1) * 128)
                    for k in range(4):
                        MM(pyn, wnsa[:, k, dsl], ont[b][:, k, :], k == 0, k == 3, [t_w, t_on[b]], [tp[0]])
                    for k in range(4):
                        MM(pval, wglu[:, k, dsl], ygt[b][:, k, :], k == 0, k == 3, [t_w, t_yg[b]], [tp[1]])
                    for k in range(4):
                        MM(pgt, wglu[:, k, dsl2], ygt[b][:, k, :], k == 0, k == 3, [t_w, t_yg[b]], [tp[2]])
                    ACT(sg3[2][hb][:], pgt, AF.Sigmoid, [tp[2]], [t_sg3[2][hb]])
                    TT2("dve", ta[hb][:], pyn, sgA[:, dc, :], ALU.mult, [tp[0], t_sgA[dc]], [t_ta[hb]])
                    TT2("dve", tb[hb][:], pval, sg3[2][hb][:], ALU.mult, [tp[1], t_sg3[2][hb]], [t_tb[hb]])
                    TT2("pool", tb[hb][:], tb[hb][:], sgA[:, 8 + dc, :], ALU.mult, [t_tb[hb], t_sgA[8 + dc]], [t_tb[hb]])
                    TT2("pool", mrg[:, dc, :], ta[hb][:], tb[hb][:], ALU.add, [t_ta[hb], t_tb[hb]], [t_mrg[dc]])
                for dc in range(8):
                    bk = 7 if dc % 2 == 0 else 6
                    po = ps[bk][:, 0:TT]
                    for k in range(8):
                        MM(po, wout[:, k, dc * 128:(dc + 1) * 128], mrg[:, k, :], k == 0, k == 7,
                           [t_w, t_mrg[k]], [t_ps[bk]])
                    TT2("dve", ot[:, dc, :], po, xb[:, dc, :], ALU.add, [t_ps[bk], t_xt[b]], [t_ot])
                if "F3b" not in DBG:
                    P.dma("sp", h2_v[:, :, i * TT:(i + 1) * TT], ot[:], reads=[t_ot], writes=[t_h2T], chan=t_ot)
            P.barrier()

    if stage == "ffn1":
        ffn_phase("f1", xT, t_xT, outT, t_outT, din["f1_g"], din["f1_wg"], din["f1_wu"], din["f1_wd"])
    else:
        ffn_phase("f1", xT, t_xT, h1T, t_h1T, din["f1_g"], din["f1_wg"], din["f1_wu"], din["f1_wd"])
        mixer_attention()
        if stage not in ("B", "BC", "D"):
            ssm_phase()
        if stage not in ("B", "BC", "D", "E"):
            merge_phase()
        if stage == "full":
            ffn_phase("f2", h2T, t_h2T, outT, t_outT, din["f2_g"], din["f2_wg"], din["f2_wu"], din["f2_wd"])

    P.final_wait([t_outT])
    P.barrier()
    es.close()
    return nc, P


def _consts():
    f = np.float32
    c = {}
    c["c_id"] = np.eye(128, dtype=f)
    bd = np.zeros((128, 128), f)
    bd[:64, :64] = 1.0
    bd[64:, 64:] = 1.0
    c["c_bd"] = bd
    k = np.arange(S)
    c["c_E"] = (k[None, :] // 64 == np.arange(64)[:, None]).astype(f)
    ci = np.arange(256)
    j = np.arange(64)
    ovl = ((ci[:, None] * 16 < (j[None, :] + 1) * 64) & (ci[:, None] * 16 + 32 > j[None, :] * 64)).astype(f)
    ovl[255] = 0.0
    c["c_ovl"] = np.ascontiguousarray(ovl.reshape(2, 128, 64).transpose(1, 0, 2))
    t = np.arange(S)
    qblk = t // 64
    force = (j[None, :] == 0) | (j[None, :] == qblk[:, None]) | (j[None, :] == qblk[:, None] - 1)
    selB = np.where(j[None, :] <= qblk[:, None], 1000.0 * force.astype(f), -1e30).astype(f)
    c["c_selB"] = np.ascontiguousarray(selB.reshape(32, 128, 64).transpose(1, 0, 2))
    kl = np.arange(128)[:, None]
    tl = np.arange(512)[None, :]
    mwin = np.zeros((128, 8, 512), f)
    for r in range(8):
        dlt = 512 + tl - 128 * r - kl
        mwin[:, r, :] = ((dlt >= 0) & (dlt < 512)).astype(f)
    c["c_mwin"] = mwin
    mslc = np.zeros((128, 4, 512), f)
    for r in range(4):
        mslc[:, r, :] = ((tl - 128 * r - kl) >= 0).astype(f)
    c["c_mslc"] = mslc
    mcmp = np.zeros((128, 5, 512), f)
    for v in range(5):
        mcmp[:, v, :] = ((512 * v + tl - 16 * kl - 31) >= 0).astype(f)
    c["c_mcmp"] = mcmp
    return c


def _prep_shared(inp):
    f = np.float32
    A = lambda a: np.ascontiguousarray(a, dtype=f)
    sh = dict(_consts())
    for tag, nm in (("f1", "ffn1"), ("f2", "ffn2")):
        sh[tag + "_g"] = A(inp[nm + "_norm"][0].reshape(8, 128).T)
        sh[tag + "_wg"] = A(inp[nm + "_w_gate"][0])
        sh[tag + "_wu"] = A(inp[nm + "_w_up"][0])
        sh[tag + "_wd"] = A(inp[nm + "_w_down"][0])
    sh["mix_g"] = A(inp["mix_norm"][0].reshape(8, 128).T)
    w = inp["w_in"][0]
    q = w[:, 0:512]
    blocks = []
    for r in range(4):
        blocks.append(q[:, r * 64:(r + 1) * 64])
        blocks.append(q[:, (4 + r) * 64:(5 + r) * 64])
    blocks += [w[:, 768:896], w[:, 1024:1152], w[:, 512:640], w[:, 640:768], w[:, 1304:1816],
               w[:, 896:1024], w[:, 1152:1280], w[:, 1280:1304]]
    sh["w_inB"] = A(np.concatenate(blocks, axis=1))
    assert sh["w_inB"].shape == (1024, 1816)
    sh["w_inF"] = A(w[:, 1816:3864])
    rep = lambda v: np.concatenate([v, v])
    sh["gains"] = A(np.stack([rep(inp["q_norm"][0]), rep(inp["k_norm_slc"][0]),
                              rep(inp["k_norm_win"][0]), rep(inp["k_norm_cmp"][0])], axis=1))
    sh["posk"] = A(np.concatenate([inp["cmp_pos_k"][0].T] * 2, axis=0))
    sh["posv"] = A(np.concatenate([inp["cmp_pos_v"][0].T] * 2, axis=0))
    w1 = lambda a: A(np.concatenate([a.reshape(32, 64, 256).transpose(1, 0, 2)] * 2, axis=0))
    sh["w1k"] = w1(inp["cmp_k_w1"][0])
    sh["w1v"] = w1(inp["cmp_v_w1"][0])
    w2k = inp["cmp_k_w2"][0]
    sh["w2k"] = A(np.concatenate([w2k, w2k], axis=1).reshape(2, 128, 128).transpose(1, 0, 2))
    sh["w2v"] = A(inp["cmp_v_w2"][0].reshape(2, 128, 64).transpose(1, 0, 2))
    sh["w_nsa"] = A(inp["w_nsa_proj"][0])
    sh["w_glu"] = A(inp["ssm_glu_w"][0])
    sh["w_out"] = A(inp["w_out"][0])
    sm = lambda a: A(a.reshape(16, 128).T)
    sh["s_lr"] = sm(inp["ssm_lambda_re"][0])
    sh["s_li"] = sm(inp["ssm_lambda_im"][0])
    sh["s_ls"] = sm(np.repeat(inp["ssm_log_step"][0][:, None], 64, axis=1))
    sb_ = lambda a: A(a.reshape(16, 128, 16).transpose(1, 0, 2))
    sh["s_bre"] = sb_(inp["ssm_b_re"][0])
    sh["s_bim"] = sb_(inp["ssm_b_im"][0])

    def cplace(cc):
        o = np.zeros((128, 16, 128), f)
        for g in range(32):
            sc, gl = g // 2, g % 2
            c0 = 32 * (sc % 4) + 16 * gl
            o[gl * 64:(gl + 1) * 64, sc, c0:c0 + 16] = cc[g].T
        return o
    sh["s_creT"] = cplace(inp["ssm_c_re"][0])
    sh["s_cimT"] = cplace(inp["ssm_c_im"][0])
    sh["s_d"] = A(inp["ssm_d"][0].reshape(4, 128).T)
    return sh


IN_SHAPES = {
    "c_id": [128, 128], "c_bd": [128, 128], "c_E": [64, S], "c_ovl": [128, 2, 64], "c_selB": [128, 32, 64],
    "c_mwin": [128, 8, 512], "c_mslc": [128, 4, 512], "c_mcmp": [128, 5, 512],
    "f1_g": [128, 8], "f1_wg": [D, DFF], "f1_wu": [D, DFF], "f1_wd": [DFF, D],
    "f2_g": [128, 8], "f2_wg": [D, DFF], "f2_wu": [D, DFF], "f2_wd": [DFF, D],
    "mix_g": [128, 8], "w_inB": [D, 1816], "w_inF": [D, 2048], "gains": [128, 4],
    "posk": [128, 32], "posv": [128, 32], "w1k": [128, 32, 256], "w1v": [128, 32, 256],
    "w2k": [128, 2, 128], "w2v": [128, 2, 64], "w_nsa": [512, D], "w_glu": [512, 2 * D], "w_out": [D, D],
    "s_lr": [128, 16], "s_li": [128, 16], "s_ls": [128, 16], "s_bre": [128, 16, 16], "s_bim": [128, 16, 16],
    "s_creT": [128, 16, 128], "s_cimT": [128, 16, 128], "s_d": [128, 4],
}

STAGE = "full"


def kernel(**inputs):
    inp = {k: np.asarray(v) for k, v in inputs.items()}
    x = inp["x"]
    nc, P = build(STAGE)
    sh = _prep_shared(inp)
    in_maps = []
    for b in range(NCORES):
        m = dict(sh)
        m["xT"] = np.ascontiguousarray(x[b].T)
        in_maps.append(m)
    res = run_bass_kernel_spmd(nc, in_maps, core_ids=list(range(NCORES)))
    out = np.stack([np.ascontiguousarray(r["outT"].T) for r in res.results], axis=0)
    return out.astype(np.float32)
```
